# Optimizing a Trainium2 kernel written in Bass

```python
import math
import jax
import jax.numpy as jnp
from jax import lax
import numpy as np

D_MODEL = 2048
BATCH = 4
SEQ = 4096
DEPTH = 4

CTX_LEN = 256
GRID_W = 64

MLA_HEADS = 8
MLA_Q_RANK = 512
MLA_KV_RANK = 512
MLA_NOPE = 128
MLA_ROPE = 64
MLA_V = 128
MLA_WIDTH = MLA_HEADS * MLA_V
ROPE_THETA = 10000.0
Q_BLOCK = 128
ATTN_SCALE = (MLA_NOPE + MLA_ROPE) ** -0.5

RWKV_HEADS = 8
RWKV_HEAD = 64
RWKV_WIDTH = RWKV_HEADS * RWKV_HEAD
RWKV_LORA_W = 32
RWKV_LORA_A = 32
RWKV_LORA_G = 96
RWKV_GN_EPS = 64e-5
RWKV_SPLITS = (RWKV_WIDTH, RWKV_WIDTH, RWKV_WIDTH, 2 * RWKV_LORA_W, 2 * RWKV_LORA_A, RWKV_LORA_G)
RWKV_COLS = 3 * RWKV_WIDTH + 2 * RWKV_LORA_W + 2 * RWKV_LORA_A + RWKV_LORA_G

GDN_HEADS = 4
GDN_HEAD = 128
GDN_WIDTH = GDN_HEADS * GDN_HEAD
GDN_CONV = 5
GDN_CHUNK = 64
GDN_SPLITS = (3 * GDN_WIDTH, GDN_WIDTH, 2 * GDN_HEADS, 2 * GDN_HEADS)
GDN_COLS = 4 * GDN_WIDTH + 4 * GDN_HEADS

IN_SPLITS = (MLA_Q_RANK, MLA_KV_RANK, MLA_ROPE, RWKV_COLS, GDN_COLS)
IN_COLS = MLA_Q_RANK + MLA_KV_RANK + MLA_ROPE + RWKV_COLS + GDN_COLS
D_MIX = MLA_WIDTH + RWKV_WIDTH + GDN_WIDTH

D_FF = -(-(8 * D_MODEL) // (3 * 256)) * 256

kernel_name = 'hybrid_mla_rwkv7_gdn_dit_trunk'


def _split(x, sizes):
    return jnp.split(x, [int(s) for s in np.cumsum(sizes)[:-1]], axis=-1)


def _heads(t, n_heads):
    return t.reshape(t.shape[:-1] + (n_heads, t.shape[-1] // n_heads))


def _layer_norm(x, g, b, eps=1e-5):
    xf = x.astype(jnp.float32)
    mu = jnp.mean(xf, -1, keepdims=True)
    var = jnp.mean(jnp.square(xf - mu), -1, keepdims=True)
    return ((xf - mu) * lax.rsqrt(var + eps) * g + b).astype(x.dtype)


def _rms_norm(x, g, eps=1e-6):
    xf = x.astype(jnp.float32)
    return (xf * lax.rsqrt(jnp.mean(jnp.square(xf), -1, keepdims=True) + eps) * g).astype(x.dtype)


def _l2norm(x):
    return x * lax.rsqrt(jnp.sum(x * x, -1, keepdims=True) + 1e-12)


def _modulation(cvec, w_mod, b_mod):
    return jnp.split(jax.nn.silu(cvec) @ w_mod + b_mod, 6, axis=-1)


def _modulate(x, shift, scale):
    return x * (1.0 + scale) + shift


def _swiglu(h, w_gate, w_up, w_down):
    return (jax.nn.silu(h @ w_gate) * (h @ w_up)) @ w_down


def _axial_rope_table(n_tok):
    rows = n_tok // GRID_W
    row = jnp.repeat(jnp.arange(rows), GRID_W)
    col = jnp.tile(jnp.arange(GRID_W), rows)
    pos = jnp.stack([row, col], axis=-1).astype(jnp.float32)
    n_freq = MLA_ROPE // 4
    inv = ROPE_THETA ** (-jnp.arange(n_freq, dtype=jnp.float32) / n_freq)
    ang = pos[..., None] * inv
    return jnp.cos(ang), jnp.sin(ang)


def _apply_axial_rope(x, cos, sin):
    shp = x.shape
    xr = x.reshape(shp[:-1] + (2, 2, MLA_ROPE // 4))
    x1, x2 = xr[..., 0, :], xr[..., 1, :]
    out = jnp.stack([x1 * cos - x2 * sin, x1 * sin + x2 * cos], axis=-2)
    return out.reshape(shp).astype(x.dtype)


def _mla_q(cq, q_norm, w_uq, cos, sin):
    q = _heads(_rms_norm(cq, q_norm) @ w_uq, MLA_HEADS)
    q_nope, q_rot = q[..., :MLA_NOPE], q[..., MLA_NOPE:]
    if cos is not None:
        q_rot = _apply_axial_rope(q_rot, cos[:, None], sin[:, None])
    return jnp.concatenate([q_nope, q_rot], axis=-1)


def _mla_kv(ckv, kr, kv_norm, w_ukv, cos, sin):
    kv = _heads(_rms_norm(ckv, kv_norm) @ w_ukv, MLA_HEADS)
    k_nope, v = kv[..., :MLA_NOPE], kv[..., MLA_NOPE:]
    if cos is not None:
        kr = _apply_axial_rope(kr, cos, sin)
    k_rot = jnp.broadcast_to(kr[:, :, None, :], k_nope.shape[:-1] + (MLA_ROPE,)).astype(k_nope.dtype)
    return jnp.concatenate([k_nope, k_rot], axis=-1), v


def _attend(q, k, v):
    s = jnp.einsum('bqhd,bkhd->bhqk', q, k, preferred_element_type=jnp.float32) * ATTN_SCALE
    p = jax.nn.softmax(s, axis=-1).astype(v.dtype)
    return jnp.einsum('bhqk,bkhd->bqhd', p, v)


def _mla_latent_attention(q, k_all, v_all):
    B, T = q.shape[:2]
    qb = jnp.moveaxis(q.reshape(B, T // Q_BLOCK, Q_BLOCK, MLA_HEADS, q.shape[-1]), 1, 0)
    o = lax.map(lambda qi: _attend(qi, k_all, v_all), qb)
    return jnp.moveaxis(o, 0, 1).reshape(B, T, MLA_WIDTH)


def _bidirectional(run_dir, prep_ctx, prep_lat, s0):
    y_lat, y_ctx = [], []
    for d, reverse in ((0, False), (1, True)):
        yc, s_ctx = run_dir(prep_ctx, d, s0, reverse)
        yl, _ = run_dir(prep_lat, d, s_ctx, reverse)
        y_lat.append(yl)
        y_ctx.append(yc)
    return y_lat[0] + y_lat[1], y_ctx[0] + y_ctx[1]


def _bidir_shift(p):
    z = jnp.zeros_like(p[:, :1])
    return 0.5 * (jnp.concatenate([z, p[:, :-1]], axis=1) + jnp.concatenate([p[:, 1:], z], axis=1))


def _rwkv7_prep(p, mu, w0, w2, a0, a2, k_k, k_a):
    B, T = p.shape[:2]
    p = p + (_bidir_shift(p) - p) * mu
    r, k, v, wd, ad, gd = _split(p, RWKV_SPLITS)
    r, k, v = r.astype(jnp.float32), k.astype(jnp.float32), v.astype(jnp.float32)
    wd = wd.reshape(B, T, 2, RWKV_LORA_W).astype(jnp.float32)
    ad = ad.reshape(B, T, 2, RWKV_LORA_A).astype(jnp.float32)
    w_log = -jax.nn.softplus(-(w0 + jnp.einsum('btdr,drc->btdc', jnp.tanh(wd), w2))) - 0.5
    decay = jnp.exp(-jnp.exp(w_log))
    a_lr = jax.nn.sigmoid(a0 + jnp.einsum('btdr,drc->btdc', ad, a2))
    kk = _l2norm(_heads(k * k_k, RWKV_HEADS))
    k_dir = k[:, :, None, :] * (1.0 + (a_lr - 1.0) * k_a)
    return r, v, kk, decay, a_lr, k_dir, gd


def _wkv7_scan(s0, r, w, k, v, a, b):
    xs = tuple(jnp.moveaxis(t, 1, 0) for t in (r, w, k, v, a, b))

    def step(S, inp):
        r_t, w_t, k_t, v_t, a_t, b_t = inp
        sa = jnp.einsum('bhvk,bhk->bhv', S, a_t)
        S = S * w_t[:, :, None, :] + sa[..., None] * b_t[:, :, None, :] + v_t[..., None] * k_t[:, :, None, :]
        return S, jnp.einsum('bhvk,bhk->bhv', S, r_t)

    S, ys = lax.scan(step, s0, xs)
    return jnp.moveaxis(ys, 0, 1), S


def _rwkv7_dir(prep, d, s0, reverse):
    r, v, kk, decay, a_lr, k_dir, _ = prep
    hd = lambda t: _heads(t, RWKV_HEADS)
    seqs = (hd(r), hd(decay[:, :, d]), hd(k_dir[:, :, d]), hd(v), -kk, kk * hd(a_lr[:, :, d]))
    if reverse:
        seqs = tuple(jnp.flip(t, axis=1) for t in seqs)
    y, s = _wkv7_scan(s0, *seqs)
    if reverse:
        y = jnp.flip(y, axis=1)
    return y, s


def _rwkv7_out(prep, y, r_k, g2, gn_g, gn_b):
    r, v, kk, decay, a_lr, k_dir, gd = prep
    B, T = r.shape[:2]
    mu = jnp.mean(y, -1, keepdims=True)
    var = jnp.mean(jnp.square(y - mu), -1, keepdims=True)
    y = ((y - mu) * lax.rsqrt(var + RWKV_GN_EPS)).reshape(B, T, RWKV_WIDTH) * gn_g + gn_b
    bonus = jnp.einsum('bthn,btdhn,hn->bth', _heads(r, RWKV_HEADS),
                       k_dir.reshape(B, T, 2, RWKV_HEADS, RWKV_HEAD), r_k)[..., None] * _heads(v, RWKV_HEADS)
    g = jax.nn.sigmoid(gd) @ g2
    return (y + bonus.reshape(B, T, RWKV_WIDTH)) * g


def _dwconv_centred(x, w):
    ch = x.shape[-1]
    return lax.conv_general_dilated(x, w.astype(x.dtype)[:, None, :], window_strides=(1,),
                                    padding=[(GDN_CONV // 2, GDN_CONV // 2)],
                                    dimension_numbers=('NWC', 'WIO', 'NWC'), feature_group_count=ch)


def _gdn_prep(p, conv_w, a_log, dt_bias):
    B, T = p.shape[:2]
    qkv, z, a_raw, b_raw = _split(p, GDN_SPLITS)
    qkv = jax.nn.silu(_dwconv_centred(qkv, conv_w))
    q, k, v = jnp.split(qkv.astype(jnp.float32), 3, axis=-1)
    q = _l2norm(_heads(q, GDN_HEADS))
    k = _l2norm(_heads(k, GDN_HEADS))
    v = _heads(v, GDN_HEADS)
    a_raw = a_raw.reshape(B, T, 2, GDN_HEADS).astype(jnp.float32)
    g = -jnp.exp(a_log) * jax.nn.softplus(a_raw + dt_bias)
    beta = jax.nn.sigmoid(b_raw.reshape(B, T, 2, GDN_HEADS).astype(jnp.float32))
    return q, k, v, g, beta, z


def _gated_delta_chunked(s0, q, k, v, g, beta):
    B, T, H, DK = q.shape
    DV = v.shape[-1]
    n = T // GDN_CHUNK

    def to_chunks(t):
        return jnp.moveaxis(t.reshape((B, n, GDN_CHUNK, H) + t.shape[3:]), 3, 1)

    q = to_chunks(q * DK ** -0.5)
    k = to_chunks(k)
    v = to_chunks(v)
    g = jnp.cumsum(to_chunks(g), axis=-1)
    beta = to_chunks(beta)
    idx = jnp.arange(GDN_CHUNK)
    incl = idx[:, None] >= idx[None, :]
    strict = idx[:, None] > idx[None, :]
    decay = jnp.exp(jnp.where(incl, g[..., :, None] - g[..., None, :], -jnp.inf))
    kb = k * beta[..., None]
    m = jnp.where(strict, jnp.einsum('bhnid,bhnjd->bhnij', kb, k) * decay, 0.0)
    eye = jnp.eye(GDN_CHUNK, dtype=m.dtype)
    rhs = jnp.concatenate([v * beta[..., None], kb * jnp.exp(g)[..., None]], axis=-1)
    sol = lax.linalg.triangular_solve(m + eye, rhs, left_side=True, lower=True, unit_diagonal=True)
    u, w = sol[..., :DV], sol[..., DV:]
    a_intra = jnp.where(incl, jnp.einsum('bhnid,bhnjd->bhnij', q, k) * decay, 0.0)
    xs = tuple(jnp.moveaxis(t, 2, 0) for t in (q, k, u, w, g, a_intra))

    def step(S, inp):
        q_i, k_i, u_i, w_i, g_i, a_i = inp
        v_new = u_i - jnp.einsum('bhck,bhkv->bhcv', w_i, S)
        o = (jnp.einsum('bhck,bhkv->bhcv', q_i * jnp.exp(g_i)[..., None], S)
             + jnp.einsum('bhij,bhjv->bhiv', a_i, v_new))
        g_end = g_i[..., -1:]
        S = S * jnp.exp(g_end)[..., None] + jnp.einsum(
            'bhck,bhcv->bhkv', k_i * jnp.exp(g_end - g_i)[..., None], v_new)
        return S, o

    S, o = lax.scan(step, s0, xs)
    o = jnp.moveaxis(o, 0, 2).reshape(B, H, T, DV)
    return jnp.moveaxis(o, 1, 2), S


def _gdn_dir(prep, d, s0, reverse):
    q, k, v, g, beta, _ = prep
    seqs = (q, k, v, g[:, :, d], beta[:, :, d])
    if reverse:
        seqs = tuple(jnp.flip(t, axis=1) for t in seqs)
    o, s = _gated_delta_chunked(s0, *seqs)
    if reverse:
        o = jnp.flip(o, axis=1)
    return o, s


def _gdn_out(prep, o, norm_g):
    z = prep[5]
    B, T = o.shape[:2]
    o = _rms_norm(o, norm_g) * jax.nn.silu(_heads(z.astype(jnp.float32), GDN_HEADS))
    return o.reshape(B, T, GDN_WIDTH)


def _token_mixers(p_lat, p_ctx, cos, sin, mla_q_norm, mla_kv_norm, mla_w_uq, mla_w_ukv,
                  rwkv_mu, rwkv_w0, rwkv_w2, rwkv_a0, rwkv_a2, rwkv_g2, rwkv_k_k, rwkv_k_a, rwkv_r_k,
                  rwkv_gn_g, rwkv_gn_b, gdn_conv, gdn_a_log, gdn_dt_bias, gdn_norm, ctx_out):
    B, n_ctx = p_ctx.shape[:2]
    cq_l, ckv_l, kr_l, pr_l, pg_l = _split(p_lat, IN_SPLITS)
    cq_c, ckv_c, kr_c, pr_c, pg_c = _split(p_ctx, IN_SPLITS)

    k_c, v_c = _mla_kv(ckv_c, kr_c, mla_kv_norm, mla_w_ukv, None, None)
    k_l, v_l = _mla_kv(ckv_l, kr_l, mla_kv_norm, mla_w_ukv, cos, sin)
    q_l = _mla_q(cq_l, mla_q_norm, mla_w_uq, cos, sin)
    a_lat = _mla_latent_attention(q_l, jnp.concatenate([k_c, k_l], axis=1), jnp.concatenate([v_c, v_l], axis=1))

    prep_rc = _rwkv7_prep(pr_c, rwkv_mu, rwkv_w0, rwkv_w2, rwkv_a0, rwkv_a2, rwkv_k_k, rwkv_k_a)
    prep_rl = _rwkv7_prep(pr_l, rwkv_mu, rwkv_w0, rwkv_w2, rwkv_a0, rwkv_a2, rwkv_k_k, rwkv_k_a)
    s0_r = jnp.zeros((B, RWKV_HEADS, RWKV_HEAD, RWKV_HEAD), jnp.float32)
    y_rl, y_rc = _bidirectional(_rwkv7_dir, prep_rc, prep_rl, s0_r)
    b_lat = _rwkv7_out(prep_rl, y_rl, rwkv_r_k, rwkv_g2, rwkv_gn_g, rwkv_gn_b)

    prep_gc = _gdn_prep(pg_c, gdn_conv, gdn_a_log, gdn_dt_bias)
    prep_gl = _gdn_prep(pg_l, gdn_conv, gdn_a_log, gdn_dt_bias)
    s0_g = jnp.zeros((B, GDN_HEADS, GDN_HEAD, GDN_HEAD), jnp.float32)
    o_gl, o_gc = _bidirectional(_gdn_dir, prep_gc, prep_gl, s0_g)
    c_lat = _gdn_out(prep_gl, o_gl, gdn_norm)

    dt = p_lat.dtype
    mix_lat = jnp.concatenate([a_lat.astype(dt), b_lat.astype(dt), c_lat.astype(dt)], axis=-1)
    if not ctx_out:
        return mix_lat, None
    q_c = _mla_q(cq_c, mla_q_norm, mla_w_uq, None, None)
    a_ctx = _attend(q_c, k_c, v_c).reshape(B, n_ctx, MLA_WIDTH)
    b_ctx = _rwkv7_out(prep_rc, y_rc, rwkv_r_k, rwkv_g2, rwkv_gn_g, rwkv_gn_b)
    c_ctx_out = _gdn_out(prep_gc, o_gc, gdn_norm)
    mix_ctx = jnp.concatenate([a_ctx.astype(dt), b_ctx.astype(dt), c_ctx_out.astype(dt)], axis=-1)
    return mix_lat, mix_ctx


def setup_inputs(seed: int = 0) -> dict:
    key = jax.random.key(seed)
    keys = jax.random.split(key, 40)
    counter = [0]

    def nk():
        counter[0] += 1
        return keys[counter[0] - 1]

    def nrm(shape, std):
        return std * jax.random.normal(nk(), shape, jnp.float32)

    L, D = DEPTH, D_MODEL
    beta_init = (8.0 * DEPTH) ** -0.25
    inp = {}
    inp['x'] = nrm((BATCH, SEQ, D), 1.0)
    inp['c'] = nrm((BATCH, D), 1.0)
    inp['ctx'] = nrm((BATCH, CTX_LEN, D), 1.0)
    inp['c_ctx'] = nrm((D,), 1.0)
    inp['w_mod'] = nrm((L, D, 6 * D), 0.5 * D ** -0.5)
    inp['b_mod'] = nrm((L, 6 * D), 0.02)
    inp['w_in'] = nrm((L, D, IN_COLS), D ** -0.5)
    inp['mla_q_norm'] = 1.0 + nrm((L, MLA_Q_RANK), 0.02)
    inp['mla_kv_norm'] = 1.0 + nrm((L, MLA_KV_RANK), 0.02)
    inp['mla_w_uq'] = nrm((L, MLA_Q_RANK, MLA_HEADS * (MLA_NOPE + MLA_ROPE)), MLA_Q_RANK ** -0.5)
    inp['mla_w_ukv'] = nrm((L, MLA_KV_RANK, MLA_HEADS * (MLA_NOPE + MLA_V)), MLA_KV_RANK ** -0.5)
    inp['rwkv_mu'] = jax.random.uniform(nk(), (L, RWKV_COLS), jnp.float32)
    inp['rwkv_w0'] = jax.random.uniform(nk(), (L, 2, RWKV_WIDTH), jnp.float32, -6.0, -1.0)
    inp['rwkv_w2'] = nrm((L, 2, RWKV_LORA_W, RWKV_WIDTH), 0.1 * RWKV_LORA_W ** -0.5)
    inp['rwkv_a0'] = nrm((L, 2, RWKV_WIDTH), 0.1)
    inp['rwkv_a2'] = nrm((L, 2, RWKV_LORA_A, RWKV_WIDTH), RWKV_LORA_A ** -0.5)
    inp['rwkv_g2'] = nrm((L, RWKV_LORA_G, RWKV_WIDTH), RWKV_LORA_G ** -0.5)
    inp['rwkv_k_k'] = 0.85 + nrm((L, RWKV_WIDTH), 0.02)
    inp['rwkv_k_a'] = 1.0 + nrm((L, RWKV_WIDTH), 0.02)
    inp['rwkv_r_k'] = nrm((L, RWKV_HEADS, RWKV_HEAD), 0.1)
    inp['rwkv_gn_g'] = 1.0 + nrm((L, RWKV_WIDTH), 0.02)
    inp['rwkv_gn_b'] = nrm((L, RWKV_WIDTH), 0.02)
    inp['gdn_conv'] = nrm((L, GDN_CONV, 3 * GDN_WIDTH), GDN_CONV ** -0.5)
    inp['gdn_a_log'] = jnp.log(jax.random.uniform(nk(), (L, 2, GDN_HEADS), jnp.float32, 1.0, 16.0))
    dt = jnp.exp(jax.random.uniform(nk(), (L, 2, GDN_HEADS), jnp.float32, math.log(1e-3), math.log(1e-1)))
    inp['gdn_dt_bias'] = dt + jnp.log(-jnp.expm1(-dt))
    inp['gdn_norm'] = 1.0 + nrm((L, GDN_HEAD), 0.02)
    inp['w_out'] = nrm((L, D_MIX, D), beta_init * D_MIX ** -0.5)
    inp['ln1_g'] = 1.0 + nrm((L, D), 0.02)
    inp['ln1_b'] = nrm((L, D), 0.02)
    inp['ffn_w_gate'] = nrm((L, D, D_FF), D ** -0.5)
    inp['ffn_w_up'] = nrm((L, D, D_FF), D ** -0.5)
    inp['ffn_w_down'] = nrm((L, D_FF, D), beta_init * D_FF ** -0.5)
    inp['ln2_g'] = 1.0 + nrm((L, D), 0.02)
    inp['ln2_b'] = nrm((L, D), 0.02)
    return inp


def reference(x, c, ctx, c_ctx, w_mod, b_mod, w_in, mla_q_norm, mla_kv_norm, mla_w_uq, mla_w_ukv,
              rwkv_mu, rwkv_w0, rwkv_w2, rwkv_a0, rwkv_a2, rwkv_g2, rwkv_k_k, rwkv_k_a, rwkv_r_k,
              rwkv_gn_g, rwkv_gn_b, gdn_conv, gdn_a_log, gdn_dt_bias, gdn_norm, w_out, ln1_g, ln1_b,
              ffn_w_gate, ffn_w_up, ffn_w_down, ln2_g, ln2_b):
    alpha = (2.0 * DEPTH) ** 0.25
    cos, sin = _axial_rope_table(x.shape[1])
    cx = ctx
    for l in range(DEPTH):
        ctx_out = l < DEPTH - 1
        sh_m, sc_m, gt_m, sh_f, sc_f, gt_f = [t[:, None, :] for t in _modulation(c, w_mod[l], b_mod[l])]
        csh_m, csc_m, cgt_m, csh_f, csc_f, cgt_f = _modulation(c_ctx, w_mod[l], b_mod[l])
        p_lat = _modulate(x, sh_m, sc_m) @ w_in[l]
        p_ctx = _modulate(cx, csh_m, csc_m) @ w_in[l]
        mix_lat, mix_ctx = _token_mixers(
            p_lat, p_ctx, cos, sin, mla_q_norm[l], mla_kv_norm[l], mla_w_uq[l], mla_w_ukv[l],
            rwkv_mu[l], rwkv_w0[l], rwkv_w2[l], rwkv_a0[l], rwkv_a2[l], rwkv_g2[l], rwkv_k_k[l], rwkv_k_a[l],
            rwkv_r_k[l], rwkv_gn_g[l], rwkv_gn_b[l], gdn_conv[l], gdn_a_log[l], gdn_dt_bias[l], gdn_norm[l],
            ctx_out)
        x = _layer_norm(alpha * x + gt_m * (mix_lat @ w_out[l]), ln1_g[l], ln1_b[l])
        ffn = _swiglu(_modulate(x, sh_f, sc_f), ffn_w_gate[l], ffn_w_up[l], ffn_w_down[l])
        x = _layer_norm(alpha * x + gt_f * ffn, ln2_g[l], ln2_b[l])
        if ctx_out:
            cx = _layer_norm(alpha * cx + cgt_m * (mix_ctx @ w_out[l]), ln1_g[l], ln1_b[l])
            cffn = _swiglu(_modulate(cx, csh_f, csc_f), ffn_w_gate[l], ffn_w_up[l], ffn_w_down[l])
            cx = _layer_norm(alpha * cx + cgt_f * cffn, ln2_g[l], ln2_b[l])
    return x
```

```python
import math
from contextlib import ExitStack

import numpy as np
import concourse.bass as bass
import concourse.mybir as mybir
from concourse.bass_utils import run_bass_kernel_spmd

F32 = mybir.dt.float32
BF16 = mybir.dt.bfloat16
AF = mybir.ActivationFunctionType
ALU = mybir.AluOpType
AX = mybir.AxisListType

D = 2048
KD = D // 128
SEQ = 4096
CTX = 256
TT = SEQ + CTX
DEPTH = 4
D_FF = 5632
KF = D_FF // 128
IN_COLS = 4912
ALPHA = (2.0 * DEPTH) ** 0.25
ATTN_SCALE = 192 ** -0.5

CH = []
for i in range(4):
    CH.append(("cq%d" % i, 128 * i, 128))
for i in range(4):
    CH.append(("ckv%d" % i, 512 + 128 * i, 128))
CH += [("kr", 1024, 64), ("krsw", -1, 64), ("wd", 2624, 64), ("ad", 2688, 64)]
CH += [("gd", 2752, 96), ("ab", 4896, 16), ("pad0", -2, 0), ("pad1", -2, 0)]
for nm, c0 in (("r", 1088), ("k", 1600), ("v", 2112), ("gq", 2848), ("gk", 3360), ("gv", 3872), ("z", 4384)):
    for i in range(4):
        CH.append(("%s%d" % (nm, i), c0 + 128 * i, 128))
NCH = len(CH)
CHI = {c[0]: i for i, c in enumerate(CH)}
NCHP = NCH
NFM = 13
TOKC = {"r": 0, "k": 512, "v": 1024, "gq": 1536, "gk": 2048, "gv": 2560, "z": 3072, "ab": 3584}
NTOKC = 3600

TBLK = [(0, CTX, True)] + [(CTX + 512 * j, 512, False) for j in range(SEQ // 512)]
PC0 = 2
PL0 = 2 + CTX + 4
TTP = PL0 + SEQ + 2


def ppos(t):
    return PC0 + t if t < CTX else PL0 + (t - CTX)


def configure(seq):
    global SEQ, TT, TBLK, TTP
    SEQ = seq
    TT = SEQ + CTX
    TBLK = [(0, CTX, True)] + [(CTX + 512 * j, 512, False) for j in range(SEQ // 512)]
    TTP = PL0 + SEQ + 2


class T:
    __slots__ = ("h", "lw", "rd")

    def __init__(self, h=None):
        self.h = h
        self.lw = {}
        self.rd = {}

    def __getitem__(self, idx):
        return self.h[idx]


class Em:
    NDMA = 6

    def __init__(self, nc, st):
        self.nc = nc
        self.st = st
        self.eng = {"pe": nc.tensor, "act": nc.scalar, "dve": nc.vector, "pool": nc.gpsimd, "sp": nc.sync}
        self.sem = {}
        self.cnt = {}
        for k in ("pe", "act", "dve", "pool", "sp"):
            self.sem[k] = st.enter_context(nc.semaphore("s_" + k))
            self.cnt[k] = 0
        self.dcnt = {"sp": 0, "pool": 0, "act": 0}
        for q in self.dcnt:
            for i in range(self.NDMA):
                self.sem[(q, i)] = st.enter_context(nc.semaphore("d_%s%d" % (q, i)))
        self.seen = {k: {} for k in self.eng}
        self.dmax = {}
        self.ninst = 0
        self.uid = 0

    def sb(self, name, shape, dt, st=None):
        self.uid += 1
        return T((st or self.st).enter_context(self.nc.sbuf_tensor("%s_u%d" % (name, self.uid), list(shape), dt)))

    def ps(self, name, shape, dt=F32, st=None):
        self.uid += 1
        return T((st or self.st).enter_context(self.nc.psum_tensor("%s_u%d" % (name, self.uid), list(shape), dt)))

    def _wait(self, eng, deps):
        e = self.eng[eng]
        seen = self.seen[eng]
        for sk, v in deps.items():
            if sk == "pe" and eng == "pe":
                continue
            if seen.get(sk, 0) < v:
                e.wait_ge(self.sem[sk], v)
                seen[sk] = v
                self.ninst += 1

    @staticmethod
    def _merge(d, s):
        for k, v in s.items():
            if d.get(k, 0) < v:
                d[k] = v

    def _deps(self, reads, writes, partial):
        deps = {}
        for b in reads:
            self._merge(deps, b.lw)
        for b in writes:
            self._merge(deps, b.rd)
            if not partial:
                self._merge(deps, b.lw)
        return deps

    def _record(self, ev, reads, writes, partial):
        sk, v = ev
        for b in reads:
            if b.rd.get(sk, 0) < v:
                b.rd[sk] = v
        for b in writes:
            if partial and not b.rd:
                if b.lw.get(sk, 0) < v:
                    b.lw[sk] = v
            else:
                b.lw = {sk: v}
                b.rd = {}

    def op(self, eng, fn, reads=(), writes=(), partial=False):
        self._wait(eng, self._deps(reads, writes, partial))
        ins = fn(self.eng[eng])
        self.cnt[eng] += 1
        ins.then_inc(self.sem[eng], 1)
        self.ninst += 1
        self._record((eng, self.cnt[eng]), reads, writes, partial)

    def dma(self, q, out, in_, reads=(), writes=(), partial=False):
        n = self.dcnt[q]
        slot = n % self.NDMA
        sk = (q, slot)
        tgt = 16 * (n // self.NDMA + 1)
        deps = self._deps(reads, writes, partial)
        if tgt > 16:
            deps[sk] = max(deps.get(sk, 0), tgt - 16)
        self._wait(q, deps)
        self.eng[q].dma_start(out=out, in_=in_).then_inc(self.sem[sk], 16)
        self.dcnt[q] = n + 1
        self.dmax[sk] = tgt
        self.ninst += 1
        self._record((sk, tgt), reads, writes, partial)

    def barrier(self):
        allev = {k: v for k, v in self.cnt.items() if v > 0}
        allev.update(self.dmax)
        for eng in self.eng:
            d = {k: v for k, v in allev.items() if k != eng}
            self._wait(eng, d)
        for eng in ("act", "dve", "pool"):
            if self.cnt[eng] > 0:
                self._wait(eng, {eng: self.cnt[eng]})

    def mm(self, out_t, out_ap, lhsT, rhs, start, stop, reads):
        self.op("pe", lambda e: e.matmul(out_ap, lhsT, rhs, start=start, stop=stop),
                reads=reads, writes=[out_t])

    def act(self, eng_out_t, out_ap, in_ap, func, reads, bias=None, scale=None, accum=None, writes=None):
        kw = {}
        if bias is not None:
            kw["bias"] = bias
        if scale is not None:
            kw["scale"] = scale
        if accum is not None:
            kw["accum_out"] = accum
        self.op("act", lambda e: e.activation(out_ap, in_ap, func, **kw), reads=reads,
                writes=writes if writes is not None else [eng_out_t])


def _chunk_rows(ap2d):
    return ap2d.rearrange("(k p) t -> p k t", p=128)


class Prog:
    def __init__(self, nl=DEPTH, dbg=()):
        self.nl = nl
        self.dbg = set(dbg)
        nc = self.nc = bass.Bass("TRN2", target_bir_lowering=False)
        self.st = ExitStack()
        self.em = Em(nc, self.st)
        self.inputs = {}
        self.outs = {}

    def din(self, name, shape, dt=F32):
        self.inputs[name] = (shape, dt)
        return self.nc.dram_tensor(name, list(shape), dt, kind="ExternalInput").ap()

    def dscr(self, name, shape, dt=F32, out=False):
        kind = "ExternalOutput" if (out or name in self.dbg) else "Internal"
        if kind == "ExternalOutput":
            self.outs[name] = (shape, dt)
        return self.nc.dram_tensor(name, list(shape), dt, kind=kind).ap()


def declare_io(P):
    L = P.nl
    P.xT0 = P.din("xT0", [D, TT])
    P.cvec = P.din("cvec", [128, KD, 2])
    P.w_mod = P.din("w_mod", [L, D, 6 * D])
    P.b_modT = P.din("b_modT", [128, L, 96])
    P.w_in = P.din("w_in", [L, D, IN_COLS])
    P.w_in_krsw = P.din("w_in_krsw", [L, D, 64])
    P.wb_in = P.dscr("wb_in", [L, D, NCHP * 128], BF16)
    P.pT = P.dscr("pT", [NFM * 128, TTP])
    P.pTok = P.dscr("pTok", [TTP, NTOKC])


def phase_cast_in(P):
    em = P.em
    P.wb_in_t = [T() for _ in range(P.nl)]
    for l in range(P.nl):
        for i, (nm, c0, w) in enumerate(CH):
            if w == 0:
                continue
            src = P.w_in_krsw[l, :, :] if c0 < 0 else P.w_in[l, :, c0:c0 + w]
            for r0 in range(0, D, 512):
                em.dma("pool", P.wb_in[l, r0:r0 + 512, i * 128:i * 128 + w],
                       src[r0:r0 + 512, :], writes=[P.wb_in_t[l]], partial=True)


def phase_mod(P):
    em = P.em
    L = P.nl
    P.mod = em.sb("mod", [128, L, 96, 2], F32)
    P.mod1 = em.sb("mod1", [128, L, 96, 2], F32)
    P.modg = em.sb("modg", [128, L, 96, 2], F32)
    with ExitStack() as st:
        cv = em.sb("cv", [128, KD, 2], F32, st)
        sc = em.sb("sc", [128, KD, 2], F32, st)
        bm = em.sb("bm", [128, L, 96], F32, st)
        em.dma("sp", cv[:], P.cvec[:, :, :], writes=[cv])
        em.dma("sp", bm[:], P.b_modT[:, :, :], writes=[bm])
        em.act(sc, sc[:], cv[:], AF.Silu, reads=[cv])
        wt = [em.sb("wm%d" % i, [128, KD, 512], F32, st) for i in range(2)]
        pmf = [em.ps("pm%d" % i, [128, 512], F32, st) for i in range(2)]
        g = 0
        for l in range(L):
            wv = P.w_mod[l].rearrange("(k p) c -> p k c", p=128)
            for gi in range(24):
                w = wt[g % 2]
                p = pmf[g % 2]
                g += 1
                em.dma("sp", w[:], wv[:, :, gi * 512:(gi + 1) * 512], writes=[w])
                for c in range(4):
                    for k in range(KD):
                        em.mm(p, p[:, 2 * c:2 * c + 2], w[:, k, c * 128:(c + 1) * 128], sc[:, k, :],
                              k == 0, k == KD - 1, reads=[w, sc])
                em.op("dve", lambda e: e.tensor_tensor(
                    out=P.mod[:, l, gi * 4:(gi + 1) * 4, :], in0=p[:, 0:8].rearrange("p (c j) -> p c j", j=2),
                    in1=bm[:, l, gi * 4:(gi + 1) * 4].unsqueeze(2).to_broadcast([128, 4, 2]),
                    op=ALU.add), reads=[p, bm], writes=[P.mod], partial=True)
        em.op("dve", lambda e: e.tensor_scalar_add(out=P.mod1[:], in0=P.mod[:], scalar1=1.0),
              reads=[P.mod], writes=[P.mod1])
        em.op("dve", lambda e: e.tensor_scalar_mul(out=P.modg[:], in0=P.mod[:], scalar1=1.0 / ALPHA),
              reads=[P.mod], writes=[P.modg])
        em.barrier()


SH_M, SC_M, GT_M, SH_F, SC_F, GT_F = 0, 16, 32, 48, 64, 80


def phase_zero_pads(P):
    em = P.em
    with ExitStack() as st:
        z = em.sb("zpad", [128, NTOKC], F32, st)
        em.op("dve", lambda e: e.memset(z[:], 0.0), writes=[z])
        dummy = T()
        for a, b in ((0, PC0), (PC0 + CTX, PL0), (PL0 + SEQ, TTP)):
            em.dma("sp", P.pTok[a:b, :], z[0:b - a, :], reads=[z], writes=[dummy], partial=True)
            for c in range(NFM):
                em.dma("sp", P.pT[c * 128:(c + 1) * 128, a:b], z[:, 0:b - a], reads=[z], writes=[dummy], partial=True)
        em.barrier()


def phase_inproj(P, l, xsrc, xsrc_t):
    em = P.em
    with ExitStack() as st:
        xs = [em.sb("ip_xs%d" % i, [128, KD, 512], F32, st) for i in range(2)]
        xm = [em.sb("ip_xm%d" % i, [128, KD, 512], BF16, st) for i in range(2)]
        wt = [em.sb("ip_w%d" % i, [128, KD, 512], BF16, st) for i in range(2)]
        ps = [em.ps("ip_ps%d" % i, [128, 512], F32, st) for i in range(4)]
        sg = [em.sb("ip_sg%d" % i, [128, 512], F32, st) for i in range(4)]
        wv = P.wb_in[l].rearrange("(k p) c -> p k c", p=128)
        xv = _chunk_rows(xsrc)
        gcount = 0
        ccount = 0
        dummy = T()

        def evac(p, s, rows, cols):
            nonlocal ccount
            if ccount % 2 == 0:
                em.op("dve", lambda e: e.tensor_copy(out=s[0:rows, 0:cols], in_=p[0:rows, 0:cols]),
                      reads=[p], writes=[s])
            else:
                em.act(s, s[0:rows, 0:cols], p[0:rows, 0:cols], AF.Copy, reads=[p])
            ccount += 1

        for bi, (t0, n, isctx) in enumerate(TBLK):
            j = 1 if isctx else 0
            pp = ppos(t0)
            x_s, x_m = xs[bi % 2], xm[bi % 2]
            em.dma("sp", x_s[:, :, 0:n], xv[:, :, t0:t0 + n], reads=[xsrc_t[bi]], writes=[x_s])
            for k in range(KD):
                em.act(x_m, x_m[:, k, 0:n], x_s[:, k, 0:n], AF.Identity, reads=[x_s, P.mod, P.mod1],
                       bias=P.mod[:, l, SH_M + k, j:j + 1], scale=P.mod1[:, l, SC_M + k, j:j + 1],
                       writes=[x_m])
            for g in range(NCH // 4):
                w = wt[gcount % 2]
                gcount += 1
                em.dma("sp", w[:], wv[:, :, g * 512:(g + 1) * 512], reads=[P.wb_in_t[l]], writes=[w])
                if g < 4:
                    for c in range(4):
                        ci = g * 4 + c
                        nm, c0, wd = CH[ci]
                        if wd == 0:
                            continue
                        if nm == "ab":
                            for tt in range(n // 128):
                                p, s_ = ps[ccount % 4], sg[ccount % 4]
                                for k in range(KD):
                                    em.mm(p, p[:, 0:16], x_m[:, k, tt * 128:(tt + 1) * 128],
                                          w[:, k, c * 128:c * 128 + 16], k == 0, k == KD - 1, reads=[w, x_m])
                                evac(p, s_, 128, 16)
                                em.dma("pool", P.pTok[pp + tt * 128:pp + (tt + 1) * 128, TOKC["ab"]:TOKC["ab"] + 16],
                                       s_[:, 0:16], reads=[s_], writes=[dummy], partial=True)
                            continue
                        fi = ci if ci < 12 else 12
                        p, s_ = ps[ccount % 4], sg[ccount % 4]
                        for k in range(KD):
                            em.mm(p, p[0:wd, 0:n], w[:, k, c * 128:c * 128 + wd], x_m[:, k, 0:n],
                                  k == 0, k == KD - 1, reads=[w, x_m])
                        evac(p, s_, wd, n)
                        em.dma("pool", P.pT[fi * 128:fi * 128 + wd, pp:pp + n], s_[0:wd, 0:n],
                               reads=[s_], writes=[dummy], partial=True)
                else:
                    col0 = (g - 4) * 512
                    for tt in range(n // 128):
                        p, s_ = ps[ccount % 4], sg[ccount % 4]
                        for k in range(KD):
                            em.mm(p, p[:, :], x_m[:, k, tt * 128:(tt + 1) * 128], w[:, k, :],
                                  k == 0, k == KD - 1, reads=[w, x_m])
                        evac(p, s_, 128, 512)
                        em.dma("pool", P.pTok[pp + tt * 128:pp + (tt + 1) * 128, col0:col0 + 512], s_[:, :],
                               reads=[s_], writes=[dummy], partial=True)
        em.barrier()


def declare_dense(P):
    L = P.nl
    P.w_out = P.din("w_out", [L, D, D])
    P.w_gate = P.din("ffn_w_gate", [L, D, D_FF])
    P.w_up = P.din("ffn_w_up", [L, D, D_FF])
    P.w_down = P.din("ffn_w_down", [L, D_FF, D])
    P.lnT = P.din("lnT", [128, L, 4, KD])
    P.wb_out = P.dscr("wb_out", [L, D, D], BF16)
    P.wb_gate = P.dscr("wb_gate", [L, D, D_FF], BF16)
    P.wb_up = P.dscr("wb_up", [L, D, D_FF], BF16)
    P.wb_down = P.dscr("wb_down", [L, D_FF, D], BF16)
    P.mixT = P.dscr("mixT", [D, TT], BF16)
    P.xA = P.dscr("xA", [D, TT])
    P.xB = P.dscr("xB", [D, TT])
    P.xOut = P.dscr("xOut", [D, SEQ], out=True)


def phase_cast_dense(P):
    em = P.em
    P.wb_dense_t = [T() for _ in range(P.nl)]
    for l in range(P.nl):
        for src, dst, rows in ((P.w_out, P.wb_out, D), (P.w_gate, P.wb_gate, D), (P.w_up, P.wb_up, D),
                               (P.w_down, P.wb_down, D_FF)):
            for r0 in range(0, rows, 256):
                em.dma("pool", dst[l, r0:r0 + 256, :], src[l, r0:r0 + 256, :],
                       writes=[P.wb_dense_t[l]], partial=True)


def setup_consts(P):
    em = P.em
    P.ones_f = em.sb("ones_f", [128, 128], F32)
    P.ones_b = em.sb("ones_b", [128, 128], BF16)
    em.op("dve", lambda e: e.memset(P.ones_f[:], 1.0), writes=[P.ones_f])
    em.op("dve", lambda e: e.memset(P.ones_b[:], 1.0), writes=[P.ones_b])
    P.epsc = em.sb("epsc", [128, 4], F32)
    em.op("dve", lambda e: e.memset(P.epsc[:, 0:1], 1e-5 / (ALPHA * ALPHA)), writes=[P.epsc])
    em.op("dve", lambda e: e.memset(P.epsc[:, 1:2], 1e-6), writes=[P.epsc])
    em.op("dve", lambda e: e.memset(P.epsc[:, 2:3], 64e-5), writes=[P.epsc])
    em.op("dve", lambda e: e.memset(P.epsc[:, 3:4], 1e-12), writes=[P.epsc])
    P.ln = em.sb("ln", [128, P.nl, 4, KD], F32)
    em.dma("sp", P.ln[:], P.lnT[:, :, :, :], writes=[P.ln])


class ResLN:
    def __init__(self, P, st, tag):
        em = P.em
        self.P = P
        self.s1 = em.ps(tag + "_s1", [128, 512], F32, st)
        self.s2 = em.ps(tag + "_s2", [128, 512], F32, st)
        self.sq = [em.sb(tag + "_sq%d" % i, [128, 512], F32, st) for i in range(2)]
        self.mean = em.sb(tag + "_mean", [128, 512], F32, st)
        self.rstd = em.sb(tag + "_rstd", [128, 512], F32, st)
        self.tmp = [em.sb(tag + "_tmp%d" % i, [128, 512], F32, st) for i in range(2)]
        self.og = [em.sb(tag + "_og%d" % i, [128, 512], F32, st) for i in range(2)]
        self.cnt = 0

    def add_chunk(self, m, psum_t, xblk, n, l, gslot, j):
        P, em = self.P, self.P.em
        em.op("dve", lambda e: e.scalar_tensor_tensor(
            out=xblk[:, m, 0:n], in0=psum_t[:, 0:n], scalar=P.modg[:, l, gslot + m, j:j + 1],
            in1=xblk[:, m, 0:n], op0=ALU.mult, op1=ALU.add), reads=[psum_t, xblk, P.modg], writes=[xblk])
        sq = self.sq[m % 2]
        em.act(sq, sq[:, 0:n], xblk[:, m, 0:n], AF.Square, reads=[xblk])
        em.mm(self.s1, self.s1[:, 0:n], P.ones_f[:], xblk[:, m, 0:n], m == 0, m == KD - 1, reads=[xblk, P.ones_f])
        em.mm(self.s2, self.s2[:, 0:n], P.ones_f[:], sq[:, 0:n], m == 0, m == KD - 1, reads=[sq, P.ones_f])

    def finish(self, xblk, n, l, lnslot, xdst, xdst_t, c0, dst2=None):
        P, em = self.P, self.P.em
        mean, rstd = self.mean, self.rstd
        em.op("dve", lambda e: e.tensor_scalar_mul(out=mean[:, 0:n], in0=self.s1[:, 0:n], scalar1=1.0 / D),
              reads=[self.s1], writes=[mean])
        em.op("dve", lambda e: e.tensor_tensor(out=rstd[:, 0:n], in0=mean[:, 0:n], in1=mean[:, 0:n], op=ALU.mult),
              reads=[mean], writes=[rstd])
        em.op("dve", lambda e: e.scalar_tensor_tensor(
            out=rstd[:, 0:n], in0=self.s2[:, 0:n], scalar=1.0 / D, in1=rstd[:, 0:n],
            op0=ALU.mult, op1=ALU.subtract), reads=[self.s2, rstd], writes=[rstd])
        em.act(rstd, rstd[:, 0:n], rstd[:, 0:n], AF.Sqrt, reads=[rstd, P.epsc], bias=P.epsc[:, 0:1])
        em.op("dve", lambda e: e.reciprocal(out=rstd[:, 0:n], in_=rstd[:, 0:n]), reads=[rstd], writes=[rstd])
        xv = _chunk_rows(xdst)
        for m in range(KD):
            tmp = self.tmp[m % 2]
            og = self.og[m % 2]
            em.op("dve", lambda e: e.tensor_tensor(out=tmp[:, 0:n], in0=xblk[:, m, 0:n], in1=mean[:, 0:n],
                                                   op=ALU.subtract), reads=[xblk, mean], writes=[tmp])
            em.op("pool", lambda e: e.tensor_tensor(out=tmp[:, 0:n], in0=tmp[:, 0:n], in1=rstd[:, 0:n],
                                                    op=ALU.mult), reads=[tmp, rstd], writes=[tmp])
            em.act(og, og[:, 0:n], tmp[:, 0:n], AF.Identity, reads=[tmp, P.ln],
                   bias=P.ln[:, l, lnslot + 1, m:m + 1], scale=P.ln[:, l, lnslot, m:m + 1])
            em.dma("pool", xdst[m * 128:(m + 1) * 128, c0:c0 + n], og[:, 0:n], reads=[og],
                   writes=[xdst_t], partial=True)
            if dst2 is not None:
                d2, d2_t, c2 = dst2
                em.dma("sp", d2[m * 128:(m + 1) * 128, c2:c2 + n], og[:, 0:n], reads=[og],
                       writes=[d2_t], partial=True)


def phase_outproj(P, l, xsrc, xsrc_t, xdst, xdst_t, with_ctx):
    em = P.em
    with ExitStack() as st:
        xs = [em.sb("op_xs%d" % i, [128, KD, 512], F32, st) for i in range(2)]
        am = [em.sb("op_am%d" % i, [128, KD, 512], BF16, st) for i in range(2)]
        wt = [em.sb("op_w%d" % i, [128, KD, 512], BF16, st) for i in range(2)]
        ps = [em.ps("op_ps%d" % i, [128, 512], F32, st) for i in range(2)]
        rl = ResLN(P, st, "op")
        wv = P.wb_out[l].rearrange("(k p) c -> p k c", p=128)
        xv = _chunk_rows(xsrc)
        av = _chunk_rows(P.mixT)
        gc = 0
        cc = 0
        for bi, (t0, n, isctx) in enumerate(TBLK):
            if isctx and not with_ctx:
                continue
            j = 1 if isctx else 0
            x_s, a_m = xs[bi % 2], am[bi % 2]
            em.dma("sp", x_s[:, :, 0:n], xv[:, :, t0:t0 + n], reads=[xsrc_t[bi]], writes=[x_s])
            em.dma("sp", a_m[:, :, 0:n], av[:, :, t0:t0 + n], reads=[P.mixT_t[bi]], writes=[a_m])
            for g in range(4):
                w = wt[gc % 2]
                gc += 1
                em.dma("sp", w[:], wv[:, :, g * 512:(g + 1) * 512], reads=[P.wb_dense_t[l]], writes=[w])
                for c in range(4):
                    m = g * 4 + c
                    p = ps[cc % 2]
                    cc += 1
                    for k in range(KD):
                        em.mm(p, p[:, 0:n], w[:, k, c * 128:(c + 1) * 128], a_m[:, k, 0:n],
                              k == 0, k == KD - 1, reads=[w, a_m])
                    rl.add_chunk(m, p, x_s, n, l, GT_M, j)
            rl.finish(x_s, n, l, 0, xdst, xdst_t[bi], t0)
        em.barrier()


def phase_ffn(P, l, xsrc, xsrc_t, xdst, xdst_t, with_ctx, final=False):
    em = P.em
    with ExitStack() as st:
        xs = em.sb("ff_xs", [128, KD, 512], F32, st)
        xm = em.sb("ff_xm", [128, KD, 512], BF16, st)
        hh = em.sb("ff_h", [128, KF, 512], BF16, st)
        wg = [em.sb("ff_wg%d" % i, [128, KD, 256], BF16, st) for i in range(2)]
        wu = [em.sb("ff_wu%d" % i, [128, KD, 256], BF16, st) for i in range(2)]
        wd = [em.sb("ff_wd%d" % i, [128, KF, 128], BF16, st) for i in range(2)]
        pg = [em.ps("ff_pg%d" % i, [128, 512], F32, st) for i in range(2)]
        pu = [em.ps("ff_pu%d" % i, [128, 512], F32, st) for i in range(2)]
        pd = [em.ps("ff_pd%d" % i, [128, 512], F32, st) for i in range(2)]
        sg = [em.sb("ff_sg%d" % i, [128, 512], F32, st) for i in range(2)]
        rl = ResLN(P, st, "ff")
        wgv = P.wb_gate[l].rearrange("(k p) c -> p k c", p=128)
        wuv = P.wb_up[l].rearrange("(k p) c -> p k c", p=128)
        wdv = P.wb_down[l].rearrange("(k p) c -> p k c", p=128)
        xv = _chunk_rows(xsrc)
        gc = 0
        cc = 0
        dc = 0
        for bi, (t0, n, isctx) in enumerate(TBLK):
            if isctx and not with_ctx:
                continue
            j = 1 if isctx else 0
            em.dma("sp", xs[:, :, 0:n], xv[:, :, t0:t0 + n], reads=[xsrc_t[bi]], writes=[xs])
            for k in range(KD):
                em.act(xm, xm[:, k, 0:n], xs[:, k, 0:n], AF.Identity, reads=[xs, P.mod, P.mod1],
                       bias=P.mod[:, l, SH_F + k, j:j + 1], scale=P.mod1[:, l, SC_F + k, j:j + 1])
            for g in range(KF // 2):
                w_g, w_u = wg[gc % 2], wu[gc % 2]
                gc += 1
                em.dma("sp", w_g[:], wgv[:, :, g * 256:(g + 1) * 256], reads=[P.wb_dense_t[l]], writes=[w_g])
                em.dma("sp", w_u[:], wuv[:, :, g * 256:(g + 1) * 256], reads=[P.wb_dense_t[l]], writes=[w_u])
                for c in range(2):
                    m = g * 2 + c
                    p_g, p_u, s = pg[cc % 2], pu[cc % 2], sg[cc % 2]
                    cc += 1
                    for k in range(KD):
                        em.mm(p_g, p_g[:, 0:n], w_g[:, k, c * 128:(c + 1) * 128], xm[:, k, 0:n],
                              k == 0, k == KD - 1, reads=[w_g, xm])
                    for k in range(KD):
                        em.mm(p_u, p_u[:, 0:n], w_u[:, k, c * 128:(c + 1) * 128], xm[:, k, 0:n],
                              k == 0, k == KD - 1, reads=[w_u, xm])
                    em.act(s, s[:, 0:n], p_g[:, 0:n], AF.Silu, reads=[p_g])
                    em.op("dve", lambda e: e.tensor_tensor(out=hh[:, m, 0:n], in0=s[:, 0:n], in1=p_u[:, 0:n],
                                                           op=ALU.mult), reads=[s, p_u], writes=[hh], partial=True)
            for m in range(KD):
                w_d = wd[dc % 2]
                p_d = pd[dc % 2]
                dc += 1
                em.dma("sp", w_d[:], wdv[:, :, m * 128:(m + 1) * 128], reads=[P.wb_dense_t[l]], writes=[w_d])
                for k in range(KF):
                    em.mm(p_d, p_d[:, 0:n], w_d[:, k, :], hh[:, k, 0:n], k == 0, k == KF - 1, reads=[w_d, hh])
                rl.add_chunk(m, p_d, xs, n, l, GT_F, j)
            if final:
                rl.finish(xs, n, l, 2, P.xOut, xdst_t[bi], t0 - CTX)
            else:
                rl.finish(xs, n, l, 2, xdst, xdst_t[bi], t0)
        em.barrier()


NEG = -1.0e30


def declare_scan(P):
    L = P.nl
    P.masks_in = P.din("masks", [128, 8, 128])
    P.ident_in = P.din("ident", [128, 128])
    P.tri_in = P.din("tri", [128, 2, 128])
    P.bmasks_in = P.din("bmasks", [128, 4, 128])
    P.rw_row = P.din("rw_row", [L, 11, 512])
    P.rw_mu = P.din("rw_mu", [L, 1536])
    P.rw_muL = P.din("rw_muL", [128, L, 3])
    P.rw_w2 = P.din("rw_w2", [L, 64, 512])
    P.rw_a2 = P.din("rw_a2", [L, 64, 512])
    P.rw_g2 = P.din("rw_g2", [L, 96, 512])
    P.gd_conv = P.din("gd_conv", [L, 5, 1536])
    P.gd_row = P.din("gd_row", [L, 3, 512])
    P.yscr = P.dscr("yscr", [4, TTP, 512])
    P.auxs = P.dscr("auxs", [3, TTP, 512])


def setup_scan_consts(P):
    em = P.em
    P.masks = em.sb("masks_sb", [128, 8, 128], F32)
    P.ident = em.sb("ident", [128, 128], F32)
    P.tri = em.sb("tri", [128, 2, 128], F32)
    em.dma("sp", P.masks[:], P.masks_in[:, :, :], writes=[P.masks])
    em.dma("sp", P.ident[:], P.ident_in[:, :], writes=[P.ident])
    em.dma("sp", P.tri[:], P.tri_in[:, :, :], writes=[P.tri])
    P.bmasks = em.sb("bmasks_sb", [128, 4, 128], F32)
    em.dma("sp", P.bmasks[:], P.bmasks_in[:, :, :], writes=[P.bmasks])


def chunk_order(rev):
    nc_ctx = CTX // 128
    nc_lat = SEQ // 128
    ctx = [128 * i for i in range(nc_ctx)]
    lat = [CTX + 128 * i for i in range(nc_lat)]
    if rev:
        return ctx[::-1] + lat[::-1]
    return ctx + lat


class ScanCore:
    def __init__(self, P, st, N, tag):
        em = P.em
        self.P, self.N, self.H = P, N, 512 // N
        N_, H = N, self.H
        self.pb = [em.ps(tag + "_pb%d" % i, [128, 512], F32, st) for i in range(7)]
        self.fillb = em.ps(tag + "_fill", [128, 512], F32, st)
        self.fillr = em.sb(tag + "_fillr", [128, 512], BF16, st)
        em.op("dve", lambda e: e.memset(self.fillr[:], 0.0), writes=[self.fillr])
        self.pbi = 0
        f = lambda nm, shp: em.sb(tag + "_" + nm, shp, F32, st)
        self.XT = {k: f("xt_" + k, [N_, H, 128]) for k in ("a", "b", "k", "r", "R")}
        self.AT = [f("AT0", [128, H, 128])]
        self.A = [f("A0", [128, H, 128])]
        self.AkT = f("AkT", [128, H, 128])
        self.ArbT = f("ArbT", [128, H, 128])
        self.ArkT = f("ArkT", [128, H, 128])
        self.Z = [f("Z%d" % i, [128, H, 2 * N_]) for i in range(2)]
        self.GH = H // 2
        GH = self.GH
        self.J = [[[f("J%d%d%d" % (gi, i, k), [128, GH, 128]) for k in range(2)] for i in range(2)] for gi in range(2)]
        self.X = [f("X%d" % gi, [128, GH, 128]) for gi in range(2)]
        self.XTt = [f("XTt%d" % gi, [128, GH, 128]) for gi in range(2)]
        self.WpT = f("WpT", [N_, H, 128])
        self.U = f("U", [128, H, N_])
        self.ST = f("ST", [N_, H, N_])
        self.Ysb = [f("Ysb%d" % i, [128, 512]) for i in range(2)]
        self.yi = 0

    def bank(self):
        b = self.pb[self.pbi % 7]
        self.pbi += 1
        return b

    def reset_state(self):
        em = self.P.em
        em.op("dve", lambda e: e.memset(self.ST[:], 0.0), writes=[self.ST])

    def transpose_to(self, key, src):
        P, em, N, H = self.P, self.P.em, self.N, self.H
        dst = self.XT[key]
        for g in range(H // 4):
            pb = self.bank()
            for hh in range(4):
                h = g * 4 + hh
                em.op("pe", lambda e: e.transpose(pb[0:N, hh * 128:(hh + 1) * 128], src[:, h * N:(h + 1) * N],
                                                  P.ident[:]), reads=[src, P.ident], writes=[pb])
            em.act(dst, dst[:, g * 4:(g + 1) * 4, :], pb[0:N, :].rearrange("p (h t) -> p h t", h=4), AF.Copy,
                   reads=[pb])
        return dst

    def gram(self, lkey, rkey, dst, mask_ap_fn, mask_t):
        P, em, N, H = self.P, self.P.em, self.N, self.H
        L_, R_ = self.XT[lkey], self.XT[rkey]
        for g in range(H // 4):
            pb = self.bank()
            for hh in range(4):
                h = g * 4 + hh
                em.mm(pb, pb[:, hh * 128:(hh + 1) * 128], L_[:, h, :], R_[:, h, :], True, True, reads=[L_, R_])
            em.op("dve", lambda e: e.tensor_tensor(
                out=dst[:, g * 4:(g + 1) * 4, :], in0=pb[:, :].rearrange("p (h t) -> p h t", h=4),
                in1=mask_ap_fn(g), op=ALU.mult), reads=[pb, mask_t], writes=[dst], partial=True)

    def run_chunk(self, rev, ops, WcT, masks, ydst_ap):
        P, em, N, H = self.P, self.P.em, self.N, self.H
        hv = lambda t: t[:].rearrange("p (h n) -> p h n", n=N)
        same_bk = ops["gb"] is ops["gk"]
        self.transpose_to("a", ops["ga"])
        self.transpose_to("k", ops["gk"])
        if not same_bk:
            self.transpose_to("b", ops["gb"])
        bkey = "k" if same_bk else "b"
        self.transpose_to("r", ops["gr"])
        if ops["Rtil"] is ops["gr"]:
            Rkey = "r"
        else:
            self.transpose_to("R", ops["Rtil"])
            Rkey = "R"
        AT, A = self.AT[0], self.A[0]
        self.gram(bkey, "a", AT, *masks["AT"])
        self.gram("a", bkey, A, *masks["A"])
        if same_bk:
            AkT, ArbT = AT, None
        else:
            AkT, ArbT = self.AkT, self.ArbT
            self.gram("k", "a", AkT, *masks["AT"])
            self.gram("b", "r", ArbT, *masks["ArT"])
        ArkT = self.ArkT
        self.gram("k", "r", ArkT, *masks["ArT"])
        if same_bk:
            ArbT = ArkT
        V = ops["V"]
        Z = self.Z[0]
        pb = self.bank()
        for h in range(H):
            em.mm(pb, pb[:, h * N:(h + 1) * N], AkT[:, h, :], V[:, h * N:(h + 1) * N], True, True,
                  reads=[AkT, V])
        em.op("dve", lambda e: e.tensor_copy(out=Z[:, :, 0:N], in_=pb[:, :].rearrange("p (h n) -> p h n", n=N)),
              reads=[pb], writes=[Z], partial=True)
        em.op("pool", lambda e: e.tensor_copy(out=Z[:, :, N:2 * N], in_=hv(ops["Atil"])),
              reads=[ops["Atil"]], writes=[Z], partial=True)
        Zf = self.Z[1]
        HB = 512 // (2 * N)
        A0, AT0 = self.A[0], self.AT[0]
        GH = self.GH
        bm = lambda i: P.bmasks[:, i:i + 1, :].to_broadcast([128, GH, 128])
        idb = P.ident[:, :].unsqueeze(1).to_broadcast([128, GH, 128])

        NFILL = getattr(P, "nfill", 0)

        def mmg(dst_pb, lhs_t, rhs_t):
            for hh in range(GH):
                em.mm(dst_pb, dst_pb[:, hh * 128:(hh + 1) * 128], lhs_t[:, hh, :], rhs_t[:, hh, :], True, True,
                      reads=[lhs_t, rhs_t])
            for _ in range(NFILL):
                em.mm(self.fillb, self.fillb[:, :], P.ones_b[:], self.fillr[:, :], True, True, reads=[self.fillr])

        vg = lambda pb_: pb_[:, 0:GH * 128].rearrange("p (h t) -> p h t", h=GH)

        def inv_group(g):
            gs = slice(g * GH, (g + 1) * GH)
            X, XT = self.X[g], self.XTt[g]
            J = self.J[g]
            Ja, JaT = J[0]
            em.op("dve", lambda e: e.tensor_tensor(out=Ja[:], in0=A0[:, gs, :], in1=bm(0), op=ALU.mult),
                  reads=[A0, P.bmasks], writes=[Ja])
            em.op("pool", lambda e: e.tensor_tensor(out=JaT[:], in0=AT0[:, gs, :], in1=bm(0), op=ALU.mult),
                  reads=[AT0, P.bmasks], writes=[JaT])
            em.op("dve", lambda e: e.tensor_tensor(out=X[:], in0=Ja[:], in1=idb, op=ALU.add),
                  reads=[Ja, P.ident], writes=[X])
            em.op("pool", lambda e: e.tensor_tensor(out=XT[:], in0=JaT[:], in1=idb, op=ALU.add),
                  reads=[JaT, P.ident], writes=[XT])
            yield
            cur = 0
            for lev in range(3):
                Jc, JcT = J[cur]
                Jn, JnT = J[1 - cur]
                p1, p2 = self.bank(), self.bank()
                mmg(p1, JcT, Jc)
                mmg(p2, Jc, JcT)
                em.op("dve", lambda e: e.tensor_copy(out=Jn[:], in_=vg(p1)), reads=[p1], writes=[Jn])
                em.act(JnT, JnT[:], vg(p2), AF.Copy, reads=[p2])
                yield
                p3, p4 = self.bank(), self.bank()
                mmg(p3, JnT, X)
                mmg(p4, Jn, XT)
                em.op("dve", lambda e: e.tensor_tensor(out=X[:], in0=vg(p3), in1=X[:], op=ALU.add),
                      reads=[p3, X], writes=[X])
                em.op("dve", lambda e: e.tensor_tensor(out=XT[:], in0=vg(p4), in1=XT[:], op=ALU.add),
                      reads=[p4, XT], writes=[XT])
                yield
                cur = 1 - cur
            for bi in (1, 2, 3):
                Ao, AoT = J[0]
                Y, Y2 = J[1]
                em.op("dve", lambda e: e.tensor_tensor(out=Ao[:], in0=A0[:, gs, :], in1=bm(bi), op=ALU.mult),
                      reads=[A0, P.bmasks], writes=[Ao])
                em.op("pool", lambda e: e.tensor_tensor(out=AoT[:], in0=AT0[:, gs, :], in1=bm(bi), op=ALU.mult),
                      reads=[AT0, P.bmasks], writes=[AoT])
                last = bi == 3
                p2 = self.bank()
                mmg(p2, Ao, XT)
                if not last:
                    p1 = self.bank()
                    mmg(p1, AoT, X)
                    em.op("dve", lambda e: e.tensor_copy(out=Y[:], in_=vg(p1)), reads=[p1], writes=[Y])
                em.act(Y2, Y2[:], vg(p2), AF.Copy, reads=[p2])
                yield
                p4 = self.bank()
                mmg(p4, X, Y2)
                if not last:
                    p3 = self.bank()
                    mmg(p3, XT, Y)
                    em.op("dve", lambda e: e.tensor_tensor(out=X[:], in0=vg(p3), in1=X[:], op=ALU.add),
                          reads=[p3, X], writes=[X])
                em.op("dve", lambda e: e.tensor_tensor(out=XT[:], in0=vg(p4), in1=XT[:], op=ALU.add),
                      reads=[p4, XT], writes=[XT])
                yield
            nsub = max(1, GH // HB)
            hps = GH // nsub
            for sub in range(nsub):
                pb = self.bank()
                for hh in range(hps):
                    h4 = sub * hps + hh
                    h = g * GH + h4
                    em.mm(pb, pb[:, hh * 2 * N:(hh + 1) * 2 * N], XT[:, h4, :], Z[:, h, :], True, True,
                          reads=[XT, Z])
                h0 = g * GH + sub * hps
                em.act(Zf, Zf[:, h0:h0 + hps, :], pb[:, 0:hps * 2 * N].rearrange("p (h n) -> p h n", h=hps), AF.Copy,
                       reads=[pb], writes=[Zf])
            yield

        gens = [inv_group(0), inv_group(1)]
        alive = [True, True]
        while any(alive):
            for gi in range(2):
                if alive[gi]:
                    try:
                        next(gens[gi])
                    except StopIteration:
                        alive[gi] = False
        WpT = self.WpT
        for g in range(H // 4):
            pb = self.bank()
            for hh in range(4):
                h = g * 4 + hh
                em.op("pe", lambda e: e.transpose(pb[0:N, hh * 128:(hh + 1) * 128], Zf[:, h, N:2 * N],
                                                  P.ident[:]), reads=[Zf, P.ident], writes=[pb])
            em.act(WpT, WpT[:, g * 4:(g + 1) * 4, :], pb[0:N, :].rearrange("p (h t) -> p h t", h=4), AF.Copy,
                   reads=[pb], writes=[WpT])
        ST, U = self.ST, self.U
        RT = self.XT[Rkey]
        pb = self.bank()
        for h in range(H):
            em.mm(pb, pb[:, h * N:(h + 1) * N], WpT[:, h, :], ST[:, h, :], True, True, reads=[WpT, ST])
        em.op("dve", lambda e: e.tensor_tensor(out=U[:], in0=pb[:, :].rearrange("p (h n) -> p h n", n=N),
                                               in1=Zf[:, :, 0:N], op=ALU.add), reads=[pb, Zf], writes=[U])
        pb = self.bank()
        for h in range(H):
            o = pb[:, h * N:(h + 1) * N]
            em.mm(pb, o, RT[:, h, :], ST[:, h, :], True, False, reads=[RT, ST])
            em.mm(pb, o, ArbT[:, h, :], U[:, h, :], False, False, reads=[ArbT, U])
            em.mm(pb, o, ArkT[:, h, :], V[:, h * N:(h + 1) * N], False, True, reads=[ArkT, V])
        ysb = self.Ysb[self.yi % 2]
        self.yi += 1
        em.act(ysb, ysb[:], pb[:, :], AF.Copy, reads=[pb])
        em.dma("sp", ydst_ap, ysb[:], reads=[ysb], writes=[T()])
        pb = self.bank()
        Bh, Kh = ops["Bh"], ops["Kh"]
        for h in range(H):
            o = pb[0:N, h * N:(h + 1) * N]
            em.mm(pb, o, Bh[:, h * N:(h + 1) * N], U[:, h, :], True, False, reads=[Bh, U])
            em.mm(pb, o, Kh[:, h * N:(h + 1) * N], V[:, h * N:(h + 1) * N], False, True, reads=[Kh, V])
        em.op("dve", lambda e: e.tensor_tensor(out=ST[:], in0=ST[:],
                                               in1=WcT[:, :].unsqueeze(2).to_broadcast([N, H, N]), op=ALU.mult),
              reads=[ST, WcT], writes=[ST])
        em.op("dve", lambda e: e.tensor_tensor(out=ST[:], in0=pb[0:N, :].rearrange("p (h n) -> p h n", n=N),
                                               in1=ST[:], op=ALU.add), reads=[pb, ST], writes=[ST])


C0 = math.exp(-0.5)


def _bc_row(P, st, name, src_row_ap, width=512):
    em = P.em
    t = em.sb(name, [128, width], F32, st)
    em.dma("sp", t[:], src_row_ap.partition_broadcast(128), writes=[t])
    return t


def phase_rwkv(P, l, want_ctx_out=True):
    em = P.em
    dummy = T()
    with ExitStack() as st:
        core = ScanCore(P, st, 64, "rw")
        f = lambda nm, shp=(128, 512): em.sb("rw_" + nm, list(shp), F32, st)
        prm = {}
        for i, nm in enumerate(["w0_0", "w0_1", "a0_0", "a0_1", "k_k", "k_a", "r_k"]):
            prm[nm] = _bc_row(P, st, "rwp_" + nm, P.rw_row[l, i:i + 1, :])
        omm = f("omm", (128, 1536))
        hmu = f("hmu", (128, 1536))
        em.dma("sp", omm[:], P.rw_mu[l:l + 1, :].partition_broadcast(128), writes=[omm])
        em.op("dve", lambda e: e.tensor_scalar_mul(out=hmu[:], in0=omm[:], scalar1=0.5), reads=[omm], writes=[hmu])
        em.op("dve", lambda e: e.tensor_scalar(out=omm[:], in0=omm[:], scalar1=-1.0, scalar2=1.0, op0=ALU.mult,
                                               op1=ALU.add), reads=[omm], writes=[omm])
        muL = f("muL", (128, 3))
        ommL = f("ommL", (128, 3))
        hmuL = f("hmuL", (128, 3))
        em.dma("sp", muL[:], P.rw_muL[:, l, :], writes=[muL])
        em.op("dve", lambda e: e.tensor_scalar_mul(out=hmuL[:], in0=muL[:], scalar1=0.5), reads=[muL], writes=[hmuL])
        em.op("dve", lambda e: e.tensor_scalar(out=ommL[:], in0=muL[:], scalar1=-1.0, scalar2=1.0, op0=ALU.mult,
                                               op1=ALU.add), reads=[muL], writes=[ommL])
        w2 = f("w2", (64, 512))
        a2 = f("a2", (64, 512))
        g2 = f("g2", (96, 512))
        em.dma("sp", w2[:], P.rw_w2[l, :, :], writes=[w2])
        em.dma("sp", a2[:], P.rw_a2[l, :, :], writes=[a2])
        em.dma("sp", g2[:], P.rw_g2[l, :, :], writes=[g2])
        cen, prv, nxt = f("cen"), f("prv"), f("nxt")
        rp, kp, vp = f("rp"), f("kp"), f("vp")
        lo = f("lo", (128, 3, 130))
        loP = f("loP", (128, 3, 128))
        lot = f("lot", (128, 128))
        sgd = [f("sgd0"), f("sgd1")]
        alr = [f("alr0"), f("alr1")]
        gg = f("gg")
        kk = f("kk")
        kdir = [f("kdir0"), f("kdir1")]
        t1, t2, t3 = f("t1"), f("t2"), f("t3")
        ss = f("ss", (128, 8))
        E1, E1p, E2, E3 = f("E1"), f("E1p"), f("E2"), f("E3")
        at, bt, kt, rt, Bh, Kh, bb = f("at"), f("bt"), f("kt"), f("rt"), f("Bh"), f("Kh"), f("bb")
        WcT = f("WcT", (64, 8))
        aux = f("aux")
        for d in (0, 1):
            rev = d == 1
            core.reset_state()
            if not rev:
                mk = {"AT": (lambda g: P.masks[:, 1:2, :].to_broadcast([128, 4, 128]), P.masks),
                      "A": (lambda g: P.masks[:, 0:1, :].to_broadcast([128, 4, 128]), P.masks),
                      "ArT": (lambda g: P.masks[:, 3:4, :].to_broadcast([128, 4, 128]), P.masks)}
            else:
                mk = {"AT": (lambda g: P.masks[:, 0:1, :].to_broadcast([128, 4, 128]), P.masks),
                      "A": (lambda g: P.masks[:, 1:2, :].to_broadcast([128, 4, 128]), P.masks),
                      "ArT": (lambda g: P.masks[:, 2:3, :].to_broadcast([128, 4, 128]), P.masks)}
            for t0 in chunk_order(rev):
                pp = ppos(t0)
                for ci, dst in enumerate((rp, kp, vp)):
                    c0 = ci * 512
                    em.dma("sp", cen[:], P.pTok[pp:pp + 128, c0:c0 + 512], writes=[cen])
                    em.dma("sp", prv[:], P.pTok[pp - 1:pp + 127, c0:c0 + 512], writes=[prv])
                    em.dma("sp", nxt[:], P.pTok[pp + 1:pp + 129, c0:c0 + 512], writes=[nxt])
                    em.op("pool", lambda e: e.tensor_tensor(out=prv[:], in0=prv[:], in1=nxt[:], op=ALU.add),
                          reads=[prv, nxt], writes=[prv])
                    em.op("pool", lambda e: e.tensor_tensor(out=prv[:], in0=prv[:], in1=hmu[:, c0:c0 + 512],
                                                            op=ALU.mult), reads=[prv, hmu], writes=[prv])
                    em.op("dve", lambda e: e.tensor_tensor(out=cen[:], in0=cen[:], in1=omm[:, c0:c0 + 512],
                                                           op=ALU.mult), reads=[cen, omm], writes=[cen])
                    em.op("dve", lambda e: e.tensor_tensor(out=dst[:], in0=cen[:], in1=prv[:], op=ALU.add),
                          reads=[cen, prv], writes=[dst])
                for c, (fi, wdt) in enumerate(((10, 64), (11, 64), (12, 96))):
                    em.dma("sp", lo[0:wdt, c, :], P.pT[fi * 128:fi * 128 + wdt, pp - 1:pp + 129], writes=[lo],
                           partial=True)
                for c, wdt in enumerate((64, 64, 96)):
                    em.op("dve", lambda e: e.tensor_tensor(out=lot[0:wdt, :], in0=lo[0:wdt, c, 0:128],
                                                           in1=lo[0:wdt, c, 2:130], op=ALU.add),
                          reads=[lo], writes=[lot])
                    em.op("dve", lambda e: e.tensor_scalar_mul(out=lot[0:wdt, :], in0=lot[0:wdt, :],
                                                               scalar1=hmuL[0:wdt, c:c + 1]),
                          reads=[lot, hmuL], writes=[lot])
                    em.op("dve", lambda e: e.scalar_tensor_tensor(
                        out=loP[0:wdt, c, :], in0=lo[0:wdt, c, 1:129], scalar=ommL[0:wdt, c:c + 1],
                        in1=lot[0:wdt, :], op0=ALU.mult, op1=ALU.add), reads=[lo, ommL, lot], writes=[loP],
                        partial=True)
                em.act(loP, loP[0:64, 0, :], loP[0:64, 0, :], AF.Tanh, reads=[loP])
                em.act(loP, loP[0:96, 2, :], loP[0:96, 2, :], AF.Sigmoid, reads=[loP])
                for dd in (0, 1):
                    pb = core.bank()
                    em.mm(pb, pb[:, :], loP[dd * 32:(dd + 1) * 32, 0, :], w2[dd * 32:(dd + 1) * 32, :], True, True,
                          reads=[loP, w2])
                    em.op("dve", lambda e: e.tensor_tensor(out=sgd[dd][:], in0=pb[:, :], in1=prm["w0_%d" % dd][:],
                                                           op=ALU.add), reads=[pb, prm["w0_%d" % dd]],
                          writes=[sgd[dd]])
                    em.act(sgd[dd], sgd[dd][:], sgd[dd][:], AF.Sigmoid, reads=[sgd[dd]])
                    pb = core.bank()
                    em.mm(pb, pb[:, :], loP[dd * 32:(dd + 1) * 32, 1, :], a2[dd * 32:(dd + 1) * 32, :], True, True,
                          reads=[loP, a2])
                    em.op("dve", lambda e: e.tensor_tensor(out=alr[dd][:], in0=pb[:, :], in1=prm["a0_%d" % dd][:],
                                                           op=ALU.add), reads=[pb, prm["a0_%d" % dd]],
                          writes=[alr[dd]])
                    em.act(alr[dd], alr[dd][:], alr[dd][:], AF.Sigmoid, reads=[alr[dd]])
                em.op("dve", lambda e: e.tensor_tensor(out=kk[:], in0=kp[:], in1=prm["k_k"][:], op=ALU.mult),
                      reads=[kp, prm["k_k"]], writes=[kk])
                em.act(t1, t1[:], kk[:], AF.Square, reads=[kk])
                em.op("dve", lambda e: e.tensor_reduce(out=ss[:], in_=t1[:].rearrange("p (h n) -> p h n", n=64),
                                                       axis=AX.X, op=ALU.add), reads=[t1], writes=[ss])
                em.act(ss, ss[:], ss[:], AF.Sqrt, reads=[ss, P.epsc], bias=P.epsc[:, 3:4])
                em.op("dve", lambda e: e.reciprocal(out=ss[:], in_=ss[:]), reads=[ss], writes=[ss])
                em.op("dve", lambda e: e.tensor_tensor(
                    out=kk[:].rearrange("p (h n) -> p h n", n=64), in0=kk[:].rearrange("p (h n) -> p h n", n=64),
                    in1=ss[:, :].unsqueeze(2).to_broadcast([128, 8, 64]), op=ALU.mult), reads=[kk, ss], writes=[kk])
                for dd in (0, 1):
                    em.op("dve", lambda e: e.scalar_tensor_tensor(
                        out=t1[:], in0=alr[dd][:], scalar=-1.0, in1=prm["k_a"][:], op0=ALU.add, op1=ALU.mult),
                        reads=[alr[dd], prm["k_a"]], writes=[t1])
                    em.op("dve", lambda e: e.scalar_tensor_tensor(
                        out=kdir[dd][:], in0=t1[:], scalar=1.0, in1=kp[:], op0=ALU.add, op1=ALU.mult),
                        reads=[t1, kp], writes=[kdir[dd]])
                if d == 0:
                    pb = core.bank()
                    em.mm(pb, pb[:, :], loP[0:96, 2, :], g2[0:96, :], True, True, reads=[loP, g2])
                    em.act(gg, gg[:], pb[:, :], AF.Copy, reads=[pb])
                    em.dma("sp", P.auxs[1, pp:pp + 128, :], gg[:], reads=[gg], writes=[dummy])
                    em.op("pool", lambda e: e.tensor_tensor(out=t2[:], in0=kdir[0][:], in1=kdir[1][:], op=ALU.add),
                          reads=[kdir[0], kdir[1]], writes=[t2])
                    em.op("pool", lambda e: e.tensor_tensor(out=t2[:], in0=t2[:], in1=rp[:], op=ALU.mult),
                          reads=[t2, rp], writes=[t2])
                    em.op("pool", lambda e: e.tensor_tensor(out=t2[:], in0=t2[:], in1=prm["r_k"][:], op=ALU.mult),
                          reads=[t2, prm["r_k"]], writes=[t2])
                    em.op("dve", lambda e: e.tensor_reduce(out=ss[:], in_=t2[:].rearrange("p (h n) -> p h n", n=64),
                                                           axis=AX.X, op=ALU.add), reads=[t2], writes=[ss])
                    em.op("dve", lambda e: e.tensor_tensor(
                        out=aux[:].rearrange("p (h n) -> p h n", n=64), in0=vp[:].rearrange("p (h n) -> p h n", n=64),
                        in1=ss[:, :].unsqueeze(2).to_broadcast([128, 8, 64]), op=ALU.mult), reads=[vp, ss],
                        writes=[aux])
                    em.dma("sp", P.auxs[0, pp:pp + 128, :], aux[:], reads=[aux], writes=[dummy])
                sg_ = sgd[d]
                pc = core.bank()
                em.mm(pc, pc[:, :], P.tri[:, d, :], sg_[:], True, True, reads=[P.tri, sg_])
                ptot = core.bank()
                em.mm(ptot, ptot[:, :], P.ones_f[:], sg_[:], True, True, reads=[P.ones_f, sg_])
                pw = core.bank()
                for h in range(8):
                    em.mm(pw, pw[0:64, h:h + 1], sg_[:, h * 64:(h + 1) * 64], P.ones_f[:, 0:1], True, True,
                          reads=[sg_, P.ones_f])
                em.act(WcT, WcT[:], pw[0:64, 0:8], AF.Exp, reads=[pw], scale=-C0)
                em.op("dve", lambda e: e.tensor_copy(out=t1[:], in_=pc[:, :]), reads=[pc], writes=[t1])
                em.op("dve", lambda e: e.tensor_tensor(out=t2[:], in0=t1[:], in1=sg_[:], op=ALU.subtract),
                      reads=[t1, sg_], writes=[t2])
                em.op("dve", lambda e: e.tensor_tensor(out=t3[:], in0=ptot[:, :], in1=t1[:], op=ALU.subtract),
                      reads=[ptot, t1], writes=[t3])
                em.act(E1, E1[:], t1[:], AF.Exp, reads=[t1], scale=-C0)
                em.act(E2, E2[:], t1[:], AF.Exp, reads=[t1], scale=C0)
                em.act(E1p, E1p[:], t2[:], AF.Exp, reads=[t2], scale=-C0)
                em.act(E3, E3[:], t3[:], AF.Exp, reads=[t3], scale=-C0)
                kd, al = kdir[d], alr[d]
                em.op("dve", lambda e: e.scalar_tensor_tensor(out=at[:], in0=kk[:], scalar=-1.0, in1=E1p[:],
                                                              op0=ALU.mult, op1=ALU.mult), reads=[kk, E1p], writes=[at])
                em.op("pool", lambda e: e.tensor_tensor(out=bb[:], in0=kk[:], in1=al[:], op=ALU.mult),
                      reads=[kk, al], writes=[bb])
                em.op("pool", lambda e: e.tensor_tensor(out=bt[:], in0=bb[:], in1=E2[:], op=ALU.mult),
                      reads=[bb, E2], writes=[bt])
                em.op("pool", lambda e: e.tensor_tensor(out=Bh[:], in0=bb[:], in1=E3[:], op=ALU.mult),
                      reads=[bb, E3], writes=[Bh])
                em.op("dve", lambda e: e.tensor_tensor(out=kt[:], in0=kd[:], in1=E2[:], op=ALU.mult),
                      reads=[kd, E2], writes=[kt])
                em.op("pool", lambda e: e.tensor_tensor(out=Kh[:], in0=kd[:], in1=E3[:], op=ALU.mult),
                      reads=[kd, E3], writes=[Kh])
                em.op("dve", lambda e: e.tensor_tensor(out=rt[:], in0=rp[:], in1=E1[:], op=ALU.mult),
                      reads=[rp, E1], writes=[rt])
                ops = {"ga": at, "gb": bt, "gk": kt, "gr": rt, "V": vp, "Atil": at, "Bh": Bh, "Kh": Kh, "Rtil": rt}
                core.run_chunk(rev, ops, WcT, mk, P.yscr[d, pp:pp + 128, :])
        em.barrier()


def host_consts():
    idx = np.arange(128)
    r, c = idx[:, None], idx[None, :]
    m = np.zeros((128, 8, 128), np.float32)
    for i, cond in enumerate((c < r, c > r, c <= r, c >= r)):
        m[:, i, :] = cond.astype(np.float32)
        m[:, 4 + i, :] = np.where(cond, 0.0, NEG).astype(np.float32)
    tri = np.zeros((128, 2, 128), np.float32)
    tri[:, 0, :] = (r <= c)
    tri[:, 1, :] = (r >= c)
    bd = lambda b: ((r // b) == (c // b)).astype(np.float32)
    bmk = np.stack([bd(16), bd(32) - bd(16), bd(64) - bd(32), 1.0 - bd(64)], 1).astype(np.float32)
    return {"masks": m, "ident": np.eye(128, dtype=np.float32), "tri": tri, "bmasks": bmk}


def host_scan_inputs(inp, L):
    f32 = np.float32
    out = {}
    rw = np.zeros((L, 11, 512), f32)
    rw[:, 0] = inp["rwkv_w0"][:L, 0]
    rw[:, 1] = inp["rwkv_w0"][:L, 1]
    rw[:, 2] = inp["rwkv_a0"][:L, 0]
    rw[:, 3] = inp["rwkv_a0"][:L, 1]
    rw[:, 4] = inp["rwkv_k_k"][:L]
    rw[:, 5] = inp["rwkv_k_a"][:L]
    rw[:, 6] = inp["rwkv_r_k"][:L].reshape(L, 512)
    rw[:, 7] = inp["rwkv_gn_g"][:L]
    rw[:, 8] = inp["rwkv_gn_b"][:L]
    out["rw_row"] = rw
    mu = inp["rwkv_mu"][:L]
    out["rw_mu"] = np.ascontiguousarray(mu[:, 0:1536])
    muL = np.zeros((128, L, 3), f32)
    muL[0:64, :, 0] = mu[:, 1536:1600].T
    muL[0:64, :, 1] = mu[:, 1600:1664].T
    muL[0:96, :, 2] = mu[:, 1664:1760].T
    out["rw_muL"] = muL
    out["rw_w2"] = np.ascontiguousarray(inp["rwkv_w2"][:L].reshape(L, 64, 512))
    out["rw_a2"] = np.ascontiguousarray(inp["rwkv_a2"][:L].reshape(L, 64, 512))
    out["rw_g2"] = np.ascontiguousarray(inp["rwkv_g2"][:L])
    out["gd_conv"] = np.ascontiguousarray(inp["gdn_conv"][:L])
    gr = np.zeros((L, 3, 512), f32)
    gr[:, 0] = np.tile(inp["gdn_norm"][:L], (1, 4))
    gr[:, 1, 0:8] = inp["gdn_a_log"][:L].reshape(L, 8)
    gr[:, 1, 8:16] = inp["gdn_dt_bias"][:L].reshape(L, 8)
    out["gd_row"] = gr
    return out


def phase_gdn(P, l):
    em = P.em
    dummy = T()
    with ExitStack() as st:
        core = ScanCore(P, st, 128, "gd")
        f = lambda nm, shp=(128, 512): em.sb("gd_" + nm, list(shp), F32, st)
        cw = []
        for j in range(5):
            t = f("cw%d" % j, (128, 1536))
            em.dma("sp", t[:], P.gd_conv[l, j:j + 1, :].partition_broadcast(128), writes=[t])
            cw.append(t)
        prow = _bc_row(P, st, "gdp_row", P.gd_row[l, 1:2, :], 512)
        negea = f("negea", (128, 8))
        em.act(negea, negea[:], prow[:, 0:8], AF.Exp, reads=[prow])
        em.op("dve", lambda e: e.tensor_scalar_mul(out=negea[:], in0=negea[:], scalar1=-1.0), reads=[negea],
              writes=[negea])
        sh = [f("sh%d" % j) for j in range(5)]
        acc, tmp = f("acc"), f("tmp")
        qkv = [f("q"), f("k"), f("v")]
        ss = f("ss", (128, 4))
        ab = f("ab", (128, 16))
        gx, ge, gl = f("gx", (128, 8)), f("ge", (128, 8)), f("gl", (128, 8))
        gcol, beta = f("gcol", (128, 8)), f("beta", (128, 8))
        Gs, nG, eG, eTG, etot, nb, nbeG = (f(n, (128, 4)) for n in ("Gs", "nG", "eG", "eTG", "etot", "nb", "nbeG"))
        ka, Atil, Kh, Rtil, Vp, zt = f("ka"), f("Atil"), f("Kh"), f("Rtil"), f("Vp"), f("zt")
        diag = f("diag", (128, 4, 128))
        dtmp = f("dtmp", (128, 4, 128))
        Ds, DTs, DTi = f("Ds", (128, 4, 128)), f("DTs", (128, 4, 128)), f("DTi", (128, 4, 128))
        hv = lambda t: t[:].rearrange("p (h n) -> p h n", n=128)
        bc4 = lambda t: t[:, :].unsqueeze(2).to_broadcast([128, 4, 128])
        for d in (0, 1):
            rev = d == 1
            core.reset_state()
            mA, mAT, mATi = (4, 5, 7) if not rev else (5, 4, 6)
            mk = {"AT": (lambda g: DTs[:, :, :], DTs), "A": (lambda g: Ds[:, :, :], Ds),
                  "ArT": (lambda g: DTi[:, :, :], DTi)}
            for t0 in chunk_order(rev):
                pp = ppos(t0)
                for ci in range(3):
                    c0 = TOKC["gq"] + ci * 512
                    for j in range(5):
                        em.dma("sp", sh[j][:], P.pTok[pp + j - 2:pp + j - 2 + 128, c0:c0 + 512], writes=[sh[j]])
                    em.op("dve", lambda e: e.tensor_tensor(out=acc[:], in0=sh[0][:], in1=cw[0][:, ci * 512:(ci + 1) * 512],
                                                           op=ALU.mult), reads=[sh[0], cw[0]], writes=[acc])
                    for j in range(1, 5):
                        eng = "pool" if j % 2 else "dve"
                        em.op(eng, lambda e: e.tensor_tensor(out=sh[j][:], in0=sh[j][:],
                                                             in1=cw[j][:, ci * 512:(ci + 1) * 512], op=ALU.mult),
                              reads=[sh[j], cw[j]], writes=[sh[j]])
                        em.op("dve", lambda e: e.tensor_tensor(out=acc[:], in0=acc[:], in1=sh[j][:], op=ALU.add),
                              reads=[acc, sh[j]], writes=[acc])
                    em.act(qkv[ci], qkv[ci][:], acc[:], AF.Silu, reads=[acc])
                for ci, sc in ((0, 128 ** -0.5), (1, 1.0)):
                    x_ = qkv[ci]
                    em.act(tmp, tmp[:], x_[:], AF.Square, reads=[x_])
                    em.op("dve", lambda e: e.tensor_reduce(out=ss[:], in_=hv(tmp), axis=AX.X, op=ALU.add),
                          reads=[tmp], writes=[ss])
                    em.act(ss, ss[:], ss[:], AF.Sqrt, reads=[ss, P.epsc], bias=P.epsc[:, 3:4])
                    em.op("dve", lambda e: e.reciprocal(out=ss[:], in_=ss[:]), reads=[ss], writes=[ss])
                    if sc != 1.0:
                        em.op("dve", lambda e: e.tensor_scalar_mul(out=ss[:], in0=ss[:], scalar1=sc), reads=[ss],
                              writes=[ss])
                    em.op("dve", lambda e: e.tensor_tensor(out=hv(x_), in0=hv(x_), in1=bc4(ss), op=ALU.mult),
                          reads=[x_, ss], writes=[x_])
                q_, k_, v_ = qkv
                if d == 0:
                    em.dma("sp", zt[:], P.pTok[pp:pp + 128, TOKC["z"]:TOKC["z"] + 512], writes=[zt])
                    em.act(zt, zt[:], zt[:], AF.Silu, reads=[zt])
                    em.dma("sp", P.auxs[2, pp:pp + 128, :], zt[:], reads=[zt], writes=[dummy])
                em.dma("sp", ab[:], P.pTok[pp:pp + 128, TOKC["ab"]:TOKC["ab"] + 16], writes=[ab])
                em.op("dve", lambda e: e.tensor_tensor(out=gx[:], in0=ab[:, 0:8], in1=prow[:, 8:16], op=ALU.add),
                      reads=[ab, prow], writes=[gx])
                em.act(ge, ge[:], gx[:], AF.Abs, reads=[gx])
                em.act(ge, ge[:], ge[:], AF.Exp, reads=[ge], scale=-1.0)
                em.act(gl, gl[:], ge[:], AF.Ln, reads=[ge, P.ones_f], bias=P.ones_f[:, 0:1])
                em.op("dve", lambda e: e.scalar_tensor_tensor(out=gcol[:], in0=gx[:], scalar=0.0, in1=gl[:],
                                                              op0=ALU.max, op1=ALU.add), reads=[gx, gl], writes=[gcol])
                em.op("dve", lambda e: e.tensor_tensor(out=gcol[:], in0=gcol[:], in1=negea[:], op=ALU.mult),
                      reads=[gcol, negea], writes=[gcol])
                em.act(beta, beta[:], ab[:, 8:16], AF.Sigmoid, reads=[ab])
                gd_, bd_ = gcol[:, d * 4:(d + 1) * 4], beta[:, d * 4:(d + 1) * 4]
                pG = core.bank()
                em.mm(pG, pG[:, 0:4], P.tri[:, d, :], gd_, True, True, reads=[P.tri, gcol])
                pT_ = core.bank()
                em.mm(pT_, pT_[:, 0:4], P.ones_f[:], gd_, True, True, reads=[P.ones_f, gcol])
                em.op("dve", lambda e: e.tensor_copy(out=Gs[:], in_=pG[:, 0:4]), reads=[pG], writes=[Gs])
                em.op("dve", lambda e: e.tensor_scalar_mul(out=nG[:], in0=Gs[:], scalar1=-1.0), reads=[Gs], writes=[nG])
                em.act(eG, eG[:], Gs[:], AF.Exp, reads=[Gs])
                em.op("dve", lambda e: e.tensor_copy(out=etot[:], in_=pT_[:, 0:4]), reads=[pT_], writes=[etot])
                em.op("dve", lambda e: e.tensor_tensor(out=eTG[:], in0=etot[:], in1=Gs[:], op=ALU.subtract),
                      reads=[etot, Gs], writes=[eTG])
                em.act(etot, etot[:], etot[:], AF.Exp, reads=[etot])
                em.act(eTG, eTG[:], eTG[:], AF.Exp, reads=[eTG])
                em.op("dve", lambda e: e.tensor_scalar_mul(out=nb[:], in0=bd_, scalar1=-1.0), reads=[beta], writes=[nb])
                em.op("dve", lambda e: e.tensor_tensor(out=nbeG[:], in0=nb[:], in1=eG[:], op=ALU.mult),
                      reads=[nb, eG], writes=[nbeG])
                em.op("dve", lambda e: e.tensor_tensor(out=hv(ka), in0=hv(k_), in1=bc4(nb), op=ALU.mult),
                      reads=[k_, nb], writes=[ka])
                em.op("pool", lambda e: e.tensor_tensor(out=hv(Atil), in0=hv(k_), in1=bc4(nbeG), op=ALU.mult),
                      reads=[k_, nbeG], writes=[Atil])
                em.op("dve", lambda e: e.tensor_tensor(out=hv(Kh), in0=hv(k_), in1=bc4(eTG), op=ALU.mult),
                      reads=[k_, eTG], writes=[Kh])
                em.op("pool", lambda e: e.tensor_tensor(out=hv(Rtil), in0=hv(q_), in1=bc4(eG), op=ALU.mult),
                      reads=[q_, eG], writes=[Rtil])
                em.op("dve", lambda e: e.tensor_tensor(out=hv(Vp), in0=hv(v_),
                                                       in1=beta[:, d * 4:(d + 1) * 4].unsqueeze(2).to_broadcast([128, 4, 128]),
                                                       op=ALU.mult), reads=[v_, beta], writes=[Vp])
                em.op("dve", lambda e: e.tensor_tensor(out=diag[:], in0=P.ident[:, :].unsqueeze(1).to_broadcast([128, 4, 128]),
                                                       in1=bc4(Gs), op=ALU.mult), reads=[P.ident, Gs], writes=[diag])
                pR = core.bank()
                for h in range(4):
                    em.mm(pR, pR[:, h * 128:(h + 1) * 128], P.ones_f[:], diag[:, h, :], True, True,
                          reads=[P.ones_f, diag])
                pRv = pR[:, :].rearrange("p (h t) -> p h t", h=4)
                for dst, sgn, mi, bias_t in ((Ds, -1.0, mA, Gs), (DTs, 1.0, mAT, nG), (DTi, 1.0, mATi, nG)):
                    em.op("dve", lambda e: e.scalar_tensor_tensor(
                        out=dtmp[:], in0=pRv, scalar=sgn, in1=P.masks[:, mi:mi + 1, :].to_broadcast([128, 4, 128]),
                        op0=ALU.mult, op1=ALU.add), reads=[pR, P.masks], writes=[dtmp])
                    for h in range(4):
                        em.act(dst, dst[:, h, :], dtmp[:, h, :], AF.Exp, reads=[dtmp, bias_t],
                               bias=bias_t[:, h:h + 1], writes=[dst])
                ops = {"ga": ka, "gb": k_, "gk": k_, "gr": q_, "V": Vp, "Atil": Atil, "Bh": Kh, "Kh": Kh,
                       "Rtil": Rtil}
                core.run_chunk(rev, ops, etot, mk, P.yscr[2 + d, pp:pp + 128, :])
        em.barrier()


def phase_mix_out(P, l, with_ctx):
    em = P.em
    dummy = T()
    with ExitStack() as st:
        f = lambda nm, shp=(128, 512): em.sb("mo_" + nm, list(shp), F32, st)
        gn_g = _bc_row(P, st, "mo_gng", P.rw_row[l, 7:8, :])
        gn_b = _bc_row(P, st, "mo_gnb", P.rw_row[l, 8:9, :])
        nrm = _bc_row(P, st, "mo_nrm", P.gd_row[l, 0:1, :])
        ya, yb, bv, gg, cen, sq = f("ya"), f("yb"), f("bv"), f("gg"), f("cen"), f("sq")
        s8 = f("s8", (128, 8))
        ob = [em.sb("mo_ob%d" % i, [128, 4, 128], BF16, st) for i in range(2)]
        pb = [em.ps("mo_pb%d" % i, [128, 512], F32, st) for i in range(2)]
        cnt = 0
        tiles = ([128 * i for i in range(CTX // 128)] if with_ctx else []) + \
            [CTX + 128 * i for i in range(SEQ // 128)]
        for t0 in tiles:
            pp = ppos(t0)
            for mix, (ia, ib, nh, eps_col) in enumerate(((0, 1, 8, 2), (2, 3, 4, 1))):
                n = 512 // nh
                hv = lambda t: t[:].rearrange("p (h n) -> p h n", n=n)
                bc = lambda t: t[:, 0:nh].unsqueeze(2).to_broadcast([128, nh, n])
                em.dma("sp", ya[:], P.yscr[ia, pp:pp + 128, :], writes=[ya])
                em.dma("sp", yb[:], P.yscr[ib, pp:pp + 128, :], writes=[yb])
                em.op("dve", lambda e: e.tensor_tensor(out=ya[:], in0=ya[:], in1=yb[:], op=ALU.add),
                      reads=[ya, yb], writes=[ya])
                if mix == 0:
                    em.dma("sp", bv[:], P.auxs[0, pp:pp + 128, :], writes=[bv])
                    em.dma("sp", gg[:], P.auxs[1, pp:pp + 128, :], writes=[gg])
                    em.op("dve", lambda e: e.tensor_reduce(out=s8[:, 0:nh], in_=hv(ya), axis=AX.X, op=ALU.add),
                          reads=[ya], writes=[s8])
                    em.op("dve", lambda e: e.tensor_scalar_mul(out=s8[:, 0:nh], in0=s8[:, 0:nh], scalar1=1.0 / n),
                          reads=[s8], writes=[s8])
                    em.op("dve", lambda e: e.tensor_tensor(out=hv(cen), in0=hv(ya), in1=bc(s8), op=ALU.subtract),
                          reads=[ya, s8], writes=[cen])
                else:
                    em.dma("sp", gg[:], P.auxs[2, pp:pp + 128, :], writes=[gg])
                    em.op("dve", lambda e: e.tensor_copy(out=cen[:], in_=ya[:]), reads=[ya], writes=[cen])
                em.act(sq, sq[:], cen[:], AF.Square, reads=[cen])
                em.op("dve", lambda e: e.tensor_reduce(out=s8[:, 0:nh], in_=hv(sq), axis=AX.X, op=ALU.add),
                      reads=[sq], writes=[s8])
                em.act(s8, s8[:, 0:nh], s8[:, 0:nh], AF.Sqrt, reads=[s8, P.epsc], bias=P.epsc[:, eps_col:eps_col + 1],
                       scale=1.0 / n)
                em.op("dve", lambda e: e.reciprocal(out=s8[:, 0:nh], in_=s8[:, 0:nh]), reads=[s8], writes=[s8])
                em.op("dve", lambda e: e.tensor_tensor(out=hv(cen), in0=hv(cen), in1=bc(s8), op=ALU.mult),
                      reads=[cen, s8], writes=[cen])
                if mix == 0:
                    em.op("pool", lambda e: e.tensor_tensor(out=cen[:], in0=cen[:], in1=gn_g[:], op=ALU.mult),
                          reads=[cen, gn_g], writes=[cen])
                    em.op("pool", lambda e: e.tensor_tensor(out=cen[:], in0=cen[:], in1=gn_b[:], op=ALU.add),
                          reads=[cen, gn_b], writes=[cen])
                    em.op("pool", lambda e: e.tensor_tensor(out=cen[:], in0=cen[:], in1=bv[:], op=ALU.add),
                          reads=[cen, bv], writes=[cen])
                else:
                    em.op("pool", lambda e: e.tensor_tensor(out=cen[:], in0=cen[:], in1=nrm[:], op=ALU.mult),
                          reads=[cen, nrm], writes=[cen])
                em.op("dve", lambda e: e.tensor_tensor(out=cen[:], in0=cen[:], in1=gg[:], op=ALU.mult),
                      reads=[cen, gg], writes=[cen])
                p_ = pb[cnt % 2]
                o_ = ob[cnt % 2]
                cnt += 1
                for j in range(4):
                    em.op("pe", lambda e: e.transpose(p_[:, j * 128:(j + 1) * 128], cen[:, j * 128:(j + 1) * 128],
                                                      P.ident[:]), reads=[cen, P.ident], writes=[p_])
                em.act(o_, o_[:], p_[:, :].rearrange("p (j t) -> p j t", j=4), AF.Copy, reads=[p_])
                r0 = 1024 + mix * 512
                em.dma("sp", P.mixT[r0:r0 + 512, t0:t0 + 128].rearrange("(j p) t -> p j t", p=128), o_[:],
                       reads=[o_], writes=[dummy])
        em.barrier()


def declare_mla(P):
    L = P.nl
    P.w_uq = P.din("mla_w_uq", [L, 512, 1536])
    P.w_uq_sw = P.din("mla_w_uq_sw", [L, 512, 512])
    P.w_ukv = P.din("mla_w_ukv", [L, 512, 2048])
    P.mlaT = P.din("mlaT", [128, L, 2, 4])
    P.ropeT = P.din("ropeT", [64, 2, SEQ])
    P.wb_uq = P.dscr("wb_uq", [L, 512, 2048], BF16)
    P.wb_ukv = P.dscr("wb_ukv", [L, 512, 2048], BF16)
    P.Kn = P.dscr("Kn", [1024, TT], BF16)
    P.Kr = P.dscr("Kr", [128, TT], BF16)
    P.sel64_in = P.din("sel64", [128, 1])
    P.Vt = P.dscr("Vt", [TT, 1024], BF16)
    P.Qn = P.dscr("Qn", [1024, TT], BF16)
    P.Qr = P.dscr("Qr", [8, 128, TT], BF16)


def phase_cast_mla(P):
    em = P.em
    P.wb_mla_t = [T() for _ in range(P.nl)]
    for l in range(P.nl):
        d = P.wb_mla_t[l]
        em.dma("pool", P.wb_uq[l, :, 0:1536], P.w_uq[l, :, :], writes=[d], partial=True)
        em.dma("pool", P.wb_uq[l, :, 1536:2048], P.w_uq_sw[l, :, :], writes=[d], partial=True)
        em.dma("pool", P.wb_ukv[l, :, :], P.w_ukv[l, :, :], writes=[d], partial=True)


def phase_mla(P, l, with_ctx):
    em = P.em
    dummy = T()
    with ExitStack() as st:
        f = lambda nm, shp, dt=F32: em.sb("ml_" + nm, list(shp), dt, st)
        gains = f("gains", (128, 2, 4))
        em.dma("sp", gains[:], P.mlaT[:, l, :, :], writes=[gains])
        wkv = f("wkv", (128, 4, 2048), BF16)
        wq = f("wq", (128, 4, 2048), BF16)
        wmt = getattr(P, "wb_mla_t", None)
        wrd = [wmt[l]] if wmt else []
        em.dma("sp", wkv[:], P.wb_ukv[l].rearrange("(k p) c -> p k c", p=128), reads=wrd, writes=[wkv])
        em.dma("sp", wq[:], P.wb_uq[l].rearrange("(k p) c -> p k c", p=128), reads=wrd, writes=[wq])
        wvv = f("wvv", (128, 4, 1024), BF16)
        for k in range(4):
            em.dma("sp", wvv[:, k, :].rearrange("p (h v) -> p h v", v=128),
                   P.wb_ukv[l, k * 128:(k + 1) * 128, :].rearrange("p (h two v) -> p h two v", two=2, v=128)[:, :, 1, :],
                   reads=wrd, writes=[wvv], partial=True)
        cx = f("cx", (128, 4, 512))
        cn = f("cn", (128, 4, 512), BF16)
        sq = [f("sq%d" % i, (128, 512)) for i in range(2)]
        rstd = f("rstd", (128, 512))
        krt, ksw, kro = f("krt", (64, 512)), f("ksw", (64, 512)), f("kro", (64, 512))
        rope = f("rope", (64, 2, 512))
        krb = f("krb", (128, 512), BF16)
        sel = f("sel", (128, 1))
        em.dma("sp", sel[:], P.sel64_in[:, :], writes=[sel])
        zt = f("zt", (128, 512))
        sqr = f("sqr", (128, 512), BF16)
        em.op("dve", lambda e: e.memset(sqr[:], 0.0), writes=[sqr])
        sqb = f("sqb", (128, 512), BF16)
        em.op("dve", lambda e: e.memset(zt[:], 0.0), writes=[zt])
        kmax = f("kmax", (128, 8))
        bmax = f("bmax", (128, 1))
        ob = [f("ob%d" % i, (128, 512), BF16) for i in range(3)]
        qrb = [f("qrb%d" % i, (128, 512), BF16) for i in range(2)]
        qr32, qs32 = f("qr32", (64, 512)), f("qs32", (64, 512))
        ps = [em.ps("ml_ps%d" % i, [128, 512], F32, st) for i in range(6)]
        pi = [0]
        oi = [0]

        def bank():
            pi[0] += 1
            return ps[pi[0] % 6]

        def obuf():
            oi[0] += 1
            return ob[oi[0] % 3]

        em.op("dve", lambda e: e.memset(kmax[:], 0.0), writes=[kmax])
        em.op("dve", lambda e: e.tensor_scalar(out=krb[64:128, :], in0=zt[64:128, :], scalar1=sel[64:128, 0:1],
                                               scalar2=None, op0=ALU.add), reads=[zt, sel], writes=[krb], partial=True)

        def rmsnorm_block(row0, gi, pp, n):
            em.dma("sp", cx[:, :, 0:n], P.pT[row0:row0 + 512, pp:pp + n].rearrange("(k p) t -> p k t", p=128),
                   writes=[cx])
            pss = bank()
            for k in range(4):
                s_ = sq[k % 2]
                em.act(s_, s_[:, 0:n], cx[:, k, 0:n], AF.Square, reads=[cx])
                em.mm(pss, pss[:, 0:n], P.ones_f[:], s_[:, 0:n], k == 0, k == 3, reads=[P.ones_f, s_])
            em.act(rstd, rstd[:, 0:n], pss[:, 0:n], AF.Sqrt, reads=[pss, P.epsc], bias=P.epsc[:, 1:2],
                   scale=1.0 / 512)
            em.op("dve", lambda e: e.reciprocal(out=rstd[:, 0:n], in_=rstd[:, 0:n]), reads=[rstd], writes=[rstd])
            for k in range(4):
                em.op("dve", lambda e: e.scalar_tensor_tensor(
                    out=cn[:, k, 0:n], in0=cx[:, k, 0:n], scalar=gains[:, gi, k:k + 1], in1=rstd[:, 0:n],
                    op0=ALU.mult, op1=ALU.mult), reads=[cx, gains, rstd], writes=[cn], partial=True)

        def load_rope(t0, n):
            em.dma("sp", rope[:, :, 0:n], P.ropeT[:, :, t0 - CTX:t0 - CTX + n], writes=[rope])

        def apply_rope(dst, x_, xsw, n, isctx):
            if isctx:
                em.op("dve", lambda e: e.tensor_copy(out=dst[0:64, 0:n], in_=x_[0:64, 0:n]), reads=[x_], writes=[dst])
                return
            em.op("dve", lambda e: e.tensor_tensor(out=dst[0:64, 0:n], in0=x_[0:64, 0:n], in1=rope[:, 0, 0:n],
                                                   op=ALU.mult), reads=[x_, rope], writes=[dst])
            em.op("pool", lambda e: e.tensor_tensor(out=xsw[0:64, 0:n], in0=xsw[0:64, 0:n], in1=rope[:, 1, 0:n],
                                                    op=ALU.mult), reads=[xsw, rope], writes=[xsw])
            em.op("dve", lambda e: e.tensor_tensor(out=dst[0:64, 0:n], in0=dst[0:64, 0:n], in1=xsw[0:64, 0:n],
                                                   op=ALU.add), reads=[dst, xsw], writes=[dst])

        for (t0, n, isctx) in TBLK:
            pp = ppos(t0)
            rmsnorm_block(512, 1, pp, n)
            if not isctx:
                load_rope(t0, n)
            em.dma("sp", krt[:, 0:n], P.pT[8 * 128:8 * 128 + 64, pp:pp + n], writes=[krt])
            em.dma("sp", ksw[:, 0:n], P.pT[9 * 128:9 * 128 + 64, pp:pp + n], writes=[ksw])
            apply_rope(kro, krt, ksw, n, isctx)
            em.act(krb, krb[0:64, 0:n], kro[0:64, 0:n], AF.Copy, reads=[kro], writes=[krb])
            em.dma("pool", P.Kr[:, t0:t0 + n], krb[:, 0:n], reads=[krb], writes=[dummy])
            em.act(sqr, sqr[0:64, 0:n], kro[0:64, 0:n], AF.Square, reads=[kro])
            for h in range(8):
                pk = bank()
                for k in range(4):
                    em.mm(pk, pk[:, 0:n], wkv[:, k, h * 256:h * 256 + 128], cn[:, k, 0:n], k == 0, k == 3,
                          reads=[wkv, cn])
                o_ = obuf()
                em.op("dve", lambda e: e.tensor_copy(out=o_[:, 0:n], in_=pk[:, 0:n]), reads=[pk], writes=[o_])
                em.dma("pool", P.Kn[h * 128:(h + 1) * 128, t0:t0 + n], o_[:, 0:n], reads=[o_], writes=[dummy])
                em.act(sqb, sqb[:, 0:n], o_[:, 0:n], AF.Square, reads=[o_])
                pn = bank()
                em.mm(pn, pn[:, 0:n], P.ones_b[:], sqb[:, 0:n], True, False, reads=[P.ones_b, sqb])
                em.mm(pn, pn[:, 0:n], P.ones_b[:], sqr[:, 0:n], False, True, reads=[P.ones_b, sqr])
                em.op("dve", lambda e: e.tensor_reduce(out=bmax[:], in_=pn[:, 0:n], axis=AX.X, op=ALU.max),
                      reads=[pn], writes=[bmax])
                em.op("dve", lambda e: e.tensor_tensor(out=kmax[:, h:h + 1], in0=kmax[:, h:h + 1], in1=bmax[:],
                                                       op=ALU.max), reads=[kmax, bmax], writes=[kmax])
            for tt in range(n // 128):
                for g in range(2):
                    pv = bank()
                    for k in range(4):
                        em.mm(pv, pv[:, :], cn[:, k, tt * 128:(tt + 1) * 128], wvv[:, k, g * 512:(g + 1) * 512],
                              k == 0, k == 3, reads=[cn, wvv])
                    o_ = obuf()
                    em.act(o_, o_[:], pv[:, :], AF.Copy, reads=[pv])
                    em.dma("pool", P.Vt[t0 + tt * 128:t0 + (tt + 1) * 128, g * 512:(g + 1) * 512], o_[:],
                           reads=[o_], writes=[dummy])
        if getattr(P, "mla_stage", 9) < 1:
            em.barrier()
            return
        nkm = f("nkm", (128, 8))
        em.act(nkm, nkm[:], kmax[:], AF.Sqrt, reads=[kmax])
        em.op("dve", lambda e: e.tensor_scalar_mul(out=nkm[:], in0=nkm[:], scalar1=-1.0), reads=[nkm], writes=[nkm])
        qcnt = 0
        for (t0, n, isctx) in TBLK:
            if isctx and not with_ctx:
                continue
            pp = ppos(t0)
            rmsnorm_block(0, 0, pp, n)
            if not isctx:
                load_rope(t0, n)
            for h in range(8):
                pq, pr, pw = bank(), bank(), bank()
                for k in range(4):
                    em.mm(pq, pq[:, 0:n], wq[:, k, h * 192:h * 192 + 128], cn[:, k, 0:n], k == 0, k == 3,
                          reads=[wq, cn])
                for k in range(4):
                    em.mm(pr, pr[0:64, 0:n], wq[:, k, h * 192 + 128:h * 192 + 192], cn[:, k, 0:n], k == 0, k == 3,
                          reads=[wq, cn])
                if not isctx:
                    for k in range(4):
                        em.mm(pw, pw[0:64, 0:n], wq[:, k, 1536 + h * 64:1536 + (h + 1) * 64], cn[:, k, 0:n],
                              k == 0, k == 3, reads=[wq, cn])
                    em.op("dve", lambda e: e.tensor_copy(out=qs32[0:64, 0:n], in_=pw[0:64, 0:n]), reads=[pw],
                          writes=[qs32])
                em.act(qr32, qr32[0:64, 0:n], pr[0:64, 0:n], AF.Copy, reads=[pr])
                apply_rope(kro, qr32, qs32, n, isctx)
                em.act(sqb, sqb[:, 0:n], pq[:, 0:n], AF.Square, reads=[pq])
                em.act(sqr, sqr[0:64, 0:n], kro[0:64, 0:n], AF.Square, reads=[kro])
                pn = bank()
                em.mm(pn, pn[:, 0:n], P.ones_b[:], sqb[:, 0:n], True, False, reads=[P.ones_b, sqb])
                em.mm(pn, pn[:, 0:n], P.ones_b[:], sqr[:, 0:n], False, True, reads=[P.ones_b, sqr])
                qb = qrb[qcnt % 2]
                qcnt += 1
                em.act(sq[1], sq[1][64:128, 0:n], pn[64:128, 0:n], AF.Sqrt, reads=[pn],
                       scale=ATTN_SCALE * ATTN_SCALE)
                em.op("dve", lambda e: e.tensor_scalar(out=qb[64:128, 0:n], in0=sq[1][64:128, 0:n],
                                                       scalar1=nkm[64:128, h:h + 1], scalar2=sel[64:128, 0:1],
                                                       op0=ALU.mult, op1=ALU.mult),
                      reads=[sq[1], nkm, sel], writes=[qb], partial=True)
                em.act(qb, qb[0:64, 0:n], kro[0:64, 0:n], AF.Copy, reads=[kro], scale=ATTN_SCALE, writes=[qb])
                em.dma("pool", P.Qr[h, :, t0:t0 + n], qb[:, 0:n], reads=[qb], writes=[dummy])
                o_ = obuf()
                em.act(o_, o_[:, 0:n], pq[:, 0:n], AF.Copy, reads=[pq], scale=ATTN_SCALE)
                em.dma("pool", P.Qn[h * 128:(h + 1) * 128, t0:t0 + n], o_[:, 0:n], reads=[o_], writes=[dummy])
        em.barrier()
    if getattr(P, "mla_stage", 9) < 2:
        return
    with ExitStack() as st:
        f = lambda nm, shp, dt=BF16: em.sb("at_" + nm, list(shp), dt, st)
        NKT = TT // 128
        kr = f("kr", (128, TT))
        em.dma("sp", kr[:], P.Kr[:, :], writes=[kr])
        kn = [f("kn%d" % i, (128, TT)) for i in range(2)]
        vv = [f("vv%d" % i, (128, NKT, 128)) for i in range(2)]
        qn = [f("qn%d" % i, (128, 512)) for i in range(2)]
        qr = [f("qr%d" % i, (128, 512)) for i in range(2)]
        pt = [f("pt%d" % i, (128, 512)) for i in range(3)]
        rd = [f("rd%d" % i, (128, 512), F32) for i in range(2)]
        ao = [f("ao%d" % i, (128, 512)) for i in range(2)]
        pS = [em.ps("at_pS%d" % i, [128, 512], F32, st) for i in range(3)]
        pO = [em.ps("at_pO%d" % i, [128, 512], F32, st) for i in range(2)]
        pD = [em.ps("at_pD%d" % i, [128, 512], F32, st) for i in range(2)]
        sc = 0
        qc = 0
        for h in range(8):
            k_n, v_ = kn[h % 2], vv[h % 2]
            em.dma("sp", k_n[:], P.Kn[h * 128:(h + 1) * 128, :], writes=[k_n])
            em.dma("sp", v_[:], P.Vt[:, h * 128:(h + 1) * 128].rearrange("(c p) v -> p c v", p=128), writes=[v_])
            for (t0, n, isctx) in TBLK:
                if isctx and not with_ctx:
                    continue
                q_n, q_r = qn[qc % 2], qr[qc % 2]
                p_O, p_D, r_d, a_o = pO[qc % 2], pD[qc % 2], rd[qc % 2], ao[qc % 2]
                qc += 1
                em.dma("sp", q_n[:, 0:n], P.Qn[h * 128:(h + 1) * 128, t0:t0 + n], writes=[q_n])
                em.dma("sp", q_r[:, 0:n], P.Qr[h, :, t0:t0 + n], writes=[q_r])
                nkt = CTX // 128 if isctx else NKT
                LOOK = 2
                ring = {}

                def scores(kt):
                    nonlocal sc
                    p_S, p_t = pS[sc % 3], pt[sc % 3]
                    sc += 1
                    ks = slice(kt * 128, (kt + 1) * 128)
                    em.mm(p_S, p_S[:, 0:n], k_n[:, ks], q_n[:, 0:n], True, False, reads=[k_n, q_n])
                    em.mm(p_S, p_S[:, 0:n], kr[:, ks], q_r[:, 0:n], False, True, reads=[kr, q_r])
                    em.act(p_t, p_t[:, 0:n], p_S[:, 0:n], AF.Exp, reads=[p_S])
                    ring[kt] = p_t

                for kt in range(min(LOOK, nkt)):
                    scores(kt)
                for kt in range(nkt):
                    p_t = ring.pop(kt)
                    em.mm(p_O, p_O[:, 0:n], v_[:, kt, :], p_t[:, 0:n], kt == 0, kt == nkt - 1, reads=[v_, p_t])
                    em.mm(p_D, p_D[:, 0:n], P.ones_b[:], p_t[:, 0:n], kt == 0, kt == nkt - 1,
                          reads=[P.ones_b, p_t])
                    if kt + LOOK < nkt:
                        scores(kt + LOOK)
                em.op("dve", lambda e: e.reciprocal(out=r_d[:, 0:n], in_=p_D[:, 0:n]), reads=[p_D], writes=[r_d])
                em.op("dve", lambda e: e.tensor_tensor(out=a_o[:, 0:n], in0=p_O[:, 0:n], in1=r_d[:, 0:n],
                                                       op=ALU.mult), reads=[p_O, r_d], writes=[a_o])
                em.dma("sp", P.mixT[h * 128:(h + 1) * 128, t0:t0 + n], a_o[:, 0:n], reads=[a_o], writes=[dummy])
        em.barrier()


def host_mla_inputs(inp, L, seq):
    f32 = np.float32
    perm = np.arange(64).reshape(2, 2, 16)[:, ::-1, :].reshape(64)
    out = {}
    out["mla_w_uq"] = inp["mla_w_uq"][:L]
    cols = np.concatenate([h * 192 + 128 + perm for h in range(8)])
    out["mla_w_uq_sw"] = np.ascontiguousarray(inp["mla_w_uq"][:L][:, :, cols])
    out["mla_w_ukv"] = inp["mla_w_ukv"][:L]
    g = np.stack([inp["mla_q_norm"][:L].reshape(L, 4, 128), inp["mla_kv_norm"][:L].reshape(L, 4, 128)], 1)
    out["mlaT"] = np.ascontiguousarray(g.transpose(3, 0, 1, 2)).astype(f32)
    t = np.arange(seq)
    pos = np.stack([t // 64, t % 64], -1).astype(f32)
    inv = (10000.0 ** (-np.arange(16, dtype=f32) / 16)).astype(f32)
    ang = pos[..., None] * inv
    cos, sin = np.cos(ang), np.sin(ang)
    ct = np.zeros((64, seq), f32)
    stb = np.zeros((64, seq), f32)
    for a in range(2):
        for half in range(2):
            r0 = a * 32 + half * 16
            ct[r0:r0 + 16] = cos[:, a, :].T
            stb[r0:r0 + 16] = (sin[:, a, :].T) * (-1.0 if half == 0 else 1.0)
    out["ropeT"] = np.ascontiguousarray(np.stack([ct, stb], 1))
    return out, perm


def build_program(nl=DEPTH, dbg=()):
    P = Prog(nl=nl, dbg=dbg)
    em = P.em
    declare_io(P)
    declare_dense(P)
    declare_scan(P)
    declare_mla(P)
    setup_consts(P)
    setup_scan_consts(P)
    phase_zero_pads(P)
    phase_cast_in(P)
    phase_cast_dense(P)
    phase_cast_mla(P)
    phase_mod(P)
    nblk = len(TBLK)
    for l in range(nl):
        with_ctx = l < nl - 1
        xsrc = P.xT0 if l == 0 else P.xB
        ft = lambda: [T() for _ in range(nblk)]
        sel_ = getattr(build_program, "phases", "imrgodf")
        if "i" in sel_:
            phase_inproj(P, l, xsrc, ft())
        if "m" in sel_:
            phase_mla(P, l, with_ctx)
        if "r" in sel_:
            phase_rwkv(P, l)
        if "g" in sel_:
            phase_gdn(P, l)
        if "o" in sel_:
            phase_mix_out(P, l, with_ctx)
        P.mixT_t = ft()
        if "d" in sel_:
            phase_outproj(P, l, xsrc, ft(), P.xA, ft(), with_ctx)
        if "f" in sel_:
            phase_ffn(P, l, P.xA, ft(), P.xB, ft(), with_ctx, final=(l == nl - 1))
    em.barrier()
    return P


def host_inputs(inp, b, nl, seq):
    f32 = np.float32
    L = nl
    x = np.asarray(inp["x"][b][:seq], f32)
    ctx = np.asarray(inp["ctx"][b], f32)
    im = {}
    im["xT0"] = np.ascontiguousarray(np.concatenate([ctx, x], 0).T)
    im["cvec"] = np.ascontiguousarray(np.stack([np.asarray(inp["c"][b]).reshape(16, 128).T,
                                                np.asarray(inp["c_ctx"]).reshape(16, 128).T], -1).astype(f32))
    im["w_mod"] = np.asarray(inp["w_mod"][:L], f32)
    im["b_modT"] = np.ascontiguousarray(np.asarray(inp["b_mod"][:L], f32).reshape(L, 96, 128).transpose(2, 0, 1))
    im["w_in"] = np.asarray(inp["w_in"][:L], f32)
    mi, perm = host_mla_inputs(inp, L, seq)
    im["w_in_krsw"] = np.ascontiguousarray(im["w_in"][:, :, 1024 + perm])
    im.update(mi)
    e = np.zeros((128, 1), f32)
    e[64] = 1.0
    im["sel64"] = e
    im["w_out"] = np.asarray(inp["w_out"][:L], f32)
    im["ffn_w_gate"] = np.asarray(inp["ffn_w_gate"][:L], f32)
    im["ffn_w_up"] = np.asarray(inp["ffn_w_up"][:L], f32)
    im["ffn_w_down"] = np.asarray(inp["ffn_w_down"][:L], f32)
    lnT = np.stack([np.asarray(inp[k][:L], f32).reshape(L, 16, 128) for k in ("ln1_g", "ln1_b", "ln2_g", "ln2_b")], 1)
    im["lnT"] = np.ascontiguousarray(lnT.transpose(3, 0, 1, 2))
    im.update(host_consts())
    im.update(host_scan_inputs(inp, L))
    return im


def kernel(**inputs):
    nb = inputs["x"].shape[0]
    P = build_program(DEPTH)
    shared = None
    in_maps = []
    for b in range(nb):
        im = host_inputs(inputs, b, DEPTH, SEQ)
        if shared is None:
            shared = im
        else:
            for k in im:
                if k not in ("xT0", "cvec"):
                    im[k] = shared[k]
        in_maps.append({k: v for k, v in im.items() if k in P.inputs})
    res = run_bass_kernel_spmd(P.nc, in_maps, core_ids=list(range(nb)))
    out = np.stack([np.ascontiguousarray(np.asarray(r["xOut"], np.float32).T) for r in res.results], 0)
    return out
```

```python
import math
from contextlib import ExitStack

import numpy as np
import concourse.bass as bass
import concourse.mybir as mybir
from concourse.bass_utils import run_bass_kernel_spmd

F32 = mybir.dt.float32
BF16 = mybir.dt.bfloat16
AF = mybir.ActivationFunctionType
ALU = mybir.AluOpType
AX = mybir.AxisListType

D = 2048
KD = D // 128
SEQ = 4096
CTX = 256
TT = SEQ + CTX
DEPTH = 4
D_FF = 5632
KF = D_FF // 128
IN_COLS = 4912
ALPHA = (2.0 * DEPTH) ** 0.25
ATTN_SCALE = 192 ** -0.5

CH = []
for i in range(4):
    CH.append(("cq%d" % i, 128 * i, 128))
for i in range(4):
    CH.append(("ckv%d" % i, 512 + 128 * i, 128))
CH += [("kr", 1024, 64), ("krsw", -1, 64), ("wd", 2624, 64), ("ad", 2688, 64)]
CH += [("gd", 2752, 96), ("ab", 4896, 16), ("pad0", -2, 0), ("pad1", -2, 0)]
for nm, c0 in (("r", 1088), ("k", 1600), ("v", 2112), ("gq", 2848), ("gk", 3360), ("gv", 3872), ("z", 4384)):
    for i in range(4):
        CH.append(("%s%d" % (nm, i), c0 + 128 * i, 128))
NCH = len(CH)
CHI = {c[0]: i for i, c in enumerate(CH)}
NCHP = NCH
NFM = 13
TOKC = {"r": 0, "k": 512, "v": 1024, "gq": 1536, "gk": 2048, "gv": 2560, "z": 3072, "ab": 3584}
NTOKC = 3600

TBLK = [(0, CTX, True)] + [(CTX + 512 * j, 512, False) for j in range(SEQ // 512)]
PC0 = 2
PL0 = 2 + CTX + 4
TTP = PL0 + SEQ + 2


def ppos(t):
    return PC0 + t if t < CTX else PL0 + (t - CTX)


def configure(seq):
    global SEQ, TT, TBLK, TTP
    SEQ = seq
    TT = SEQ + CTX
    TBLK = [(0, CTX, True)] + [(CTX + 512 * j, 512, False) for j in range(SEQ // 512)]
    TTP = PL0 + SEQ + 2


class T:
    __slots__ = ("h", "lw", "rd", "pg")

    def __init__(self, h=None):
        self.h = h
        self.lw = {}
        self.rd = {}
        self.pg = {}

    def __getitem__(self, idx):
        return self.h[idx]


class Em:
    NDMA = 6

    def __init__(self, nc, st):
        self.nc = nc
        self.st = st
        self.eng = {"pe": nc.tensor, "act": nc.scalar, "dve": nc.vector, "pool": nc.gpsimd, "sp": nc.sync}
        self.sem = {}
        self.cnt = {}
        for k in ("pe", "act", "dve", "pool", "sp"):
            self.sem[k] = st.enter_context(nc.semaphore("s_" + k))
            self.cnt[k] = 0
        self.dcnt = {"sp": 0, "pool": 0, "act": 0}
        for q in self.dcnt:
            for i in range(self.NDMA):
                self.sem[(q, i)] = st.enter_context(nc.semaphore("d_%s%d" % (q, i)))
        self.seen = {k: {} for k in self.eng}
        self.dmax = {}
        self.ninst = 0
        self.uid = 0

    def sb(self, name, shape, dt, st=None):
        self.uid += 1
        return T((st or self.st).enter_context(self.nc.sbuf_tensor("%s_u%d" % (name, self.uid), list(shape), dt)))

    def ps(self, name, shape, dt=F32, st=None):
        self.uid += 1
        return T((st or self.st).enter_context(self.nc.psum_tensor("%s_u%d" % (name, self.uid), list(shape), dt)))

    def _wait(self, eng, deps):
        e = self.eng[eng]
        seen = self.seen[eng]
        for sk, v in deps.items():
            if sk == "pe" and eng == "pe":
                continue
            if seen.get(sk, 0) < v:
                e.wait_ge(self.sem[sk], v)
                seen[sk] = v
                self.ninst += 1

    @staticmethod
    def _merge(d, s):
        for k, v in s.items():
            if d.get(k, 0) < v:
                d[k] = v

    def _deps(self, reads, writes, partial):
        deps = {}
        for b in reads:
            self._merge(deps, b.lw)
        for b in writes:
            self._merge(deps, b.rd)
            if partial:
                self._merge(deps, b.pg)
            else:
                self._merge(deps, b.lw)
        return deps

    def _record(self, ev, reads, writes, partial):
        sk, v = ev
        for b in reads:
            if b.rd.get(sk, 0) < v:
                b.rd[sk] = v
        for b in writes:
            if partial and not b.rd:
                if b.lw.get(sk, 0) < v:
                    b.lw[sk] = v
            else:
                pg = dict(b.rd)
                self._merge(pg, b.lw)
                b.pg = pg
                b.lw = {sk: v}
                b.rd = {}

    def op(self, eng, fn, reads=(), writes=(), partial=False):
        self._wait(eng, self._deps(reads, writes, partial))
        ins = fn(self.eng[eng])
        self.cnt[eng] += 1
        ins.then_inc(self.sem[eng], 1)
        self.ninst += 1
        self._record((eng, self.cnt[eng]), reads, writes, partial)

    def dma(self, q, out, in_, reads=(), writes=(), partial=False):
        n = self.dcnt[q]
        slot = n % self.NDMA
        sk = (q, slot)
        tgt = 16 * (n // self.NDMA + 1)
        deps = self._deps(reads, writes, partial)
        if tgt > 16:
            deps[sk] = max(deps.get(sk, 0), tgt - 16)
        self._wait(q, deps)
        self.eng[q].dma_start(out=out, in_=in_).then_inc(self.sem[sk], 16)
        self.dcnt[q] = n + 1
        self.dmax[sk] = tgt
        self.ninst += 1
        self._record((sk, tgt), reads, writes, partial)

    def barrier(self):
        allev = {k: v for k, v in self.cnt.items() if v > 0}
        allev.update(self.dmax)
        for eng in self.eng:
            d = {k: v for k, v in allev.items() if k != eng}
            self._wait(eng, d)
        for eng in ("act", "dve", "pool"):
            if self.cnt[eng] > 0:
                self._wait(eng, {eng: self.cnt[eng]})

    def mm(self, out_t, out_ap, lhsT, rhs, start, stop, reads):
        self.op("pe", lambda e: e.matmul(out_ap, lhsT, rhs, start=start, stop=stop),
                reads=reads, writes=[out_t])

    def act(self, eng_out_t, out_ap, in_ap, func, reads, bias=None, scale=None, accum=None, writes=None):
        kw = {}
        if bias is not None:
            kw["bias"] = bias
        if scale is not None:
            kw["scale"] = scale
        if accum is not None:
            kw["accum_out"] = accum
        self.op("act", lambda e: e.activation(out_ap, in_ap, func, **kw), reads=reads,
                writes=writes if writes is not None else [eng_out_t])


def _chunk_rows(ap2d):
    return ap2d.rearrange("(k p) t -> p k t", p=128)


class Prog:
    def __init__(self, nl=DEPTH, dbg=()):
        self.nl = nl
        self.dbg = set(dbg)
        nc = self.nc = bass.Bass("TRN2", target_bir_lowering=False)
        self.st = ExitStack()
        self.em = Em(nc, self.st)
        self.inputs = {}
        self.outs = {}

    def din(self, name, shape, dt=F32):
        self.inputs[name] = (shape, dt)
        return self.nc.dram_tensor(name, list(shape), dt, kind="ExternalInput").ap()

    def dscr(self, name, shape, dt=F32, out=False):
        kind = "ExternalOutput" if (out or name in self.dbg) else "Internal"
        if kind == "ExternalOutput":
            self.outs[name] = (shape, dt)
        return self.nc.dram_tensor(name, list(shape), dt, kind=kind).ap()


def declare_io(P):
    L = P.nl
    P.xT0 = P.din("xT0", [D, TT])
    P.cvec = P.din("cvec", [128, KD, 2])
    P.w_mod = P.din("w_mod", [L, D, 6 * D])
    P.b_modT = P.din("b_modT", [128, L, 96])
    P.w_in = P.din("w_in", [L, D, IN_COLS])
    P.w_in_krsw = P.din("w_in_krsw", [L, D, 64])
    P.wb_in = P.dscr("wb_in", [L, D, NCHP * 128], BF16)
    P.pT = P.dscr("pT", [NFM * 128, TTP])
    P.pTok = P.dscr("pTok", [TTP, NTOKC])


def phase_cast_in(P):
    em = P.em
    P.wb_in_t = [T() for _ in range(P.nl)]
    for l in range(P.nl):
        for i, (nm, c0, w) in enumerate(CH):
            if w == 0:
                continue
            src = P.w_in_krsw[l, :, :] if c0 < 0 else P.w_in[l, :, c0:c0 + w]
            for r0 in range(0, D, 512):
                em.dma("pool", P.wb_in[l, r0:r0 + 512, i * 128:i * 128 + w],
                       src[r0:r0 + 512, :], writes=[P.wb_in_t[l]], partial=True)


def phase_mod(P):
    em = P.em
    L = P.nl
    P.mod = em.sb("mod", [128, L, 96, 2], F32)
    P.mod1 = em.sb("mod1", [128, L, 96, 2], F32)
    P.modg = em.sb("modg", [128, L, 96, 2], F32)
    with ExitStack() as st:
        cv = em.sb("cv", [128, KD, 2], F32, st)
        sc = em.sb("sc", [128, KD, 2], F32, st)
        bm = em.sb("bm", [128, L, 96], F32, st)
        em.dma("sp", cv[:], P.cvec[:, :, :], writes=[cv])
        em.dma("sp", bm[:], P.b_modT[:, :, :], writes=[bm])
        em.act(sc, sc[:], cv[:], AF.Silu, reads=[cv])
        wt = [em.sb("wm%d" % i, [128, KD, 512], F32, st) for i in range(2)]
        pmf = [em.ps("pm%d" % i, [128, 512], F32, st) for i in range(2)]
        g = 0
        for l in range(L):
            wv = P.w_mod[l].rearrange("(k p) c -> p k c", p=128)
            for gi in range(24):
                w = wt[g % 2]
                p = pmf[g % 2]
                g += 1
                em.dma("sp", w[:], wv[:, :, gi * 512:(gi + 1) * 512], writes=[w])
                for c in range(4):
                    for k in range(KD):
                        em.mm(p, p[:, 2 * c:2 * c + 2], w[:, k, c * 128:(c + 1) * 128], sc[:, k, :],
                              k == 0, k == KD - 1, reads=[w, sc])
                em.op("dve", lambda e: e.tensor_tensor(
                    out=P.mod[:, l, gi * 4:(gi + 1) * 4, :], in0=p[:, 0:8].rearrange("p (c j) -> p c j", j=2),
                    in1=bm[:, l, gi * 4:(gi + 1) * 4].unsqueeze(2).to_broadcast([128, 4, 2]),
                    op=ALU.add), reads=[p, bm], writes=[P.mod], partial=True)
        em.op("dve", lambda e: e.tensor_scalar_add(out=P.mod1[:], in0=P.mod[:], scalar1=1.0),
              reads=[P.mod], writes=[P.mod1])
        em.op("dve", lambda e: e.tensor_scalar_mul(out=P.modg[:], in0=P.mod[:], scalar1=1.0 / ALPHA),
              reads=[P.mod], writes=[P.modg])
        em.barrier()


SH_M, SC_M, GT_M, SH_F, SC_F, GT_F = 0, 16, 32, 48, 64, 80


def phase_zero_pads(P):
    em = P.em
    with ExitStack() as st:
        z = em.sb("zpad", [128, NTOKC], F32, st)
        em.op("dve", lambda e: e.memset(z[:], 0.0), writes=[z])
        dummy = T()
        for a, b in ((0, PC0), (PC0 + CTX, PL0), (PL0 + SEQ, TTP)):
            em.dma("sp", P.pTok[a:b, :], z[0:b - a, :], reads=[z], writes=[dummy], partial=True)
            for c in range(NFM):
                em.dma("sp", P.pT[c * 128:(c + 1) * 128, a:b], z[:, 0:b - a], reads=[z], writes=[dummy], partial=True)
        em.barrier()


def phase_inproj(P, l, xsrc, xsrc_t):
    em = P.em
    with ExitStack() as st:
        xs = [em.sb("ip_xs%d" % i, [128, KD, 512], F32, st) for i in range(2)]
        xm = [em.sb("ip_xm%d" % i, [128, KD, 512], BF16, st) for i in range(2)]
        wt = [em.sb("ip_w%d" % i, [128, KD, 512], BF16, st) for i in range(2)]
        ps = [em.ps("ip_ps%d" % i, [128, 512], F32, st) for i in range(4)]
        sg = [em.sb("ip_sg%d" % i, [128, 512], F32, st) for i in range(4)]
        wv = P.wb_in[l].rearrange("(k p) c -> p k c", p=128)
        xv = _chunk_rows(xsrc)
        gcount = 0
        ccount = 0
        dummy = T()

        def evac(p, s, rows, cols):
            nonlocal ccount
            if ccount % 2 == 0:
                em.op("dve", lambda e: e.tensor_copy(out=s[0:rows, 0:cols], in_=p[0:rows, 0:cols]),
                      reads=[p], writes=[s])
            else:
                em.act(s, s[0:rows, 0:cols], p[0:rows, 0:cols], AF.Copy, reads=[p])
            ccount += 1

        for bi, (t0, n, isctx) in enumerate(TBLK):
            j = 1 if isctx else 0
            pp = ppos(t0)
            x_s, x_m = xs[bi % 2], xm[bi % 2]
            em.dma("sp", x_s[:, :, 0:n], xv[:, :, t0:t0 + n], reads=[xsrc_t[bi]], writes=[x_s])
            for k in range(KD):
                em.act(x_m, x_m[:, k, 0:n], x_s[:, k, 0:n], AF.Identity, reads=[x_s, P.mod, P.mod1],
                       bias=P.mod[:, l, SH_M + k, j:j + 1], scale=P.mod1[:, l, SC_M + k, j:j + 1],
                       writes=[x_m])
            for g in range(NCH // 4):
                w = wt[gcount % 2]
                gcount += 1
                em.dma("sp", w[:], wv[:, :, g * 512:(g + 1) * 512], reads=[P.wb_in_t[l]], writes=[w])
                if g < 4:
                    for c in range(4):
                        ci = g * 4 + c
                        nm, c0, wd = CH[ci]
                        if wd == 0:
                            continue
                        if nm == "ab":
                            for tt in range(n // 128):
                                p, s_ = ps[ccount % 4], sg[ccount % 4]
                                for k in range(KD):
                                    em.mm(p, p[:, 0:16], x_m[:, k, tt * 128:(tt + 1) * 128],
                                          w[:, k, c * 128:c * 128 + 16], k == 0, k == KD - 1, reads=[w, x_m])
                                evac(p, s_, 128, 16)
                                em.dma("pool", P.pTok[pp + tt * 128:pp + (tt + 1) * 128, TOKC["ab"]:TOKC["ab"] + 16],
                                       s_[:, 0:16], reads=[s_], writes=[dummy], partial=True)
                            continue
                        fi = ci if ci < 12 else 12
                        p, s_ = ps[ccount % 4], sg[ccount % 4]
                        for k in range(KD):
                            em.mm(p, p[0:wd, 0:n], w[:, k, c * 128:c * 128 + wd], x_m[:, k, 0:n],
                                  k == 0, k == KD - 1, reads=[w, x_m])
                        evac(p, s_, wd, n)
                        em.dma("pool", P.pT[fi * 128:fi * 128 + wd, pp:pp + n], s_[0:wd, 0:n],
                               reads=[s_], writes=[dummy], partial=True)
                else:
                    col0 = (g - 4) * 512
                    for tt in range(n // 128):
                        p, s_ = ps[ccount % 4], sg[ccount % 4]
                        for k in range(KD):
                            em.mm(p, p[:, :], x_m[:, k, tt * 128:(tt + 1) * 128], w[:, k, :],
                                  k == 0, k == KD - 1, reads=[w, x_m])
                        evac(p, s_, 128, 512)
                        em.dma("pool", P.pTok[pp + tt * 128:pp + (tt + 1) * 128, col0:col0 + 512], s_[:, :],
                               reads=[s_], writes=[dummy], partial=True)
        em.barrier()


def declare_dense(P):
    L = P.nl
    P.w_out = P.din("w_out", [L, D, D])
    P.w_gate = P.din("ffn_w_gate", [L, D, D_FF])
    P.w_up = P.din("ffn_w_up", [L, D, D_FF])
    P.w_down = P.din("ffn_w_down", [L, D_FF, D])
    P.lnT = P.din("lnT", [128, L, 4, KD])
    P.wb_out = P.dscr("wb_out", [L, D, D], BF16)
    P.wb_gate = P.dscr("wb_gate", [L, D, D_FF], BF16)
    P.wb_up = P.dscr("wb_up", [L, D, D_FF], BF16)
    P.wb_down = P.dscr("wb_down", [L, D_FF, D], BF16)
    P.mixT = P.dscr("mixT", [D, TT], BF16)
    P.xA = P.dscr("xA", [D, TT])
    P.xB = P.dscr("xB", [D, TT])
    P.xOut = P.dscr("xOut", [D, SEQ], out=True)


def phase_cast_dense(P):
    em = P.em
    P.wb_dense_t = [T() for _ in range(P.nl)]
    for l in range(P.nl):
        for src, dst, rows in ((P.w_out, P.wb_out, D), (P.w_gate, P.wb_gate, D), (P.w_up, P.wb_up, D),
                               (P.w_down, P.wb_down, D_FF)):
            for r0 in range(0, rows, 256):
                em.dma("pool", dst[l, r0:r0 + 256, :], src[l, r0:r0 + 256, :],
                       writes=[P.wb_dense_t[l]], partial=True)


def setup_consts(P):
    em = P.em
    P.ones_f = em.sb("ones_f", [128, 128], F32)
    P.ones_b = em.sb("ones_b", [128, 128], BF16)
    em.op("dve", lambda e: e.memset(P.ones_f[:], 1.0), writes=[P.ones_f])
    em.op("dve", lambda e: e.memset(P.ones_b[:], 1.0), writes=[P.ones_b])
    P.epsc = em.sb("epsc", [128, 4], F32)
    em.op("dve", lambda e: e.memset(P.epsc[:, 0:1], 1e-5 / (ALPHA * ALPHA)), writes=[P.epsc])
    em.op("dve", lambda e: e.memset(P.epsc[:, 1:2], 1e-6), writes=[P.epsc])
    em.op("dve", lambda e: e.memset(P.epsc[:, 2:3], 64e-5), writes=[P.epsc])
    em.op("dve", lambda e: e.memset(P.epsc[:, 3:4], 1e-12), writes=[P.epsc])
    P.ln = em.sb("ln", [128, P.nl, 4, KD], F32)
    em.dma("sp", P.ln[:], P.lnT[:, :, :, :], writes=[P.ln])


class ResLN:
    def __init__(self, P, st, tag):
        em = P.em
        self.P = P
        self.s1 = em.ps(tag + "_s1", [128, 512], F32, st)
        self.s2 = em.ps(tag + "_s2", [128, 512], F32, st)
        self.sq = [em.sb(tag + "_sq%d" % i, [128, 512], F32, st) for i in range(2)]
        self.mean = em.sb(tag + "_mean", [128, 512], F32, st)
        self.rstd = em.sb(tag + "_rstd", [128, 512], F32, st)
        self.tmp = [em.sb(tag + "_tmp%d" % i, [128, 512], F32, st) for i in range(2)]
        self.og = [em.sb(tag + "_og%d" % i, [128, 512], F32, st) for i in range(2)]
        self.cnt = 0

    def add_chunk(self, m, psum_t, xblk, n, l, gslot, j):
        P, em = self.P, self.P.em
        em.op("dve", lambda e: e.scalar_tensor_tensor(
            out=xblk[:, m, 0:n], in0=psum_t[:, 0:n], scalar=P.modg[:, l, gslot + m, j:j + 1],
            in1=xblk[:, m, 0:n], op0=ALU.mult, op1=ALU.add), reads=[psum_t, xblk, P.modg], writes=[xblk])
        sq = self.sq[m % 2]
        em.act(sq, sq[:, 0:n], xblk[:, m, 0:n], AF.Square, reads=[xblk])
        em.mm(self.s1, self.s1[:, 0:n], P.ones_f[:], xblk[:, m, 0:n], m == 0, m == KD - 1, reads=[xblk, P.ones_f])
        em.mm(self.s2, self.s2[:, 0:n], P.ones_f[:], sq[:, 0:n], m == 0, m == KD - 1, reads=[sq, P.ones_f])

    def finish(self, xblk, n, l, lnslot, xdst, xdst_t, c0, dst2=None):
        P, em = self.P, self.P.em
        mean, rstd = self.mean, self.rstd
        em.op("dve", lambda e: e.tensor_scalar_mul(out=mean[:, 0:n], in0=self.s1[:, 0:n], scalar1=1.0 / D),
              reads=[self.s1], writes=[mean])
        em.op("dve", lambda e: e.tensor_tensor(out=rstd[:, 0:n], in0=mean[:, 0:n], in1=mean[:, 0:n], op=ALU.mult),
              reads=[mean], writes=[rstd])
        em.op("dve", lambda e: e.scalar_tensor_tensor(
            out=rstd[:, 0:n], in0=self.s2[:, 0:n], scalar=1.0 / D, in1=rstd[:, 0:n],
            op0=ALU.mult, op1=ALU.subtract), reads=[self.s2, rstd], writes=[rstd])
        em.act(rstd, rstd[:, 0:n], rstd[:, 0:n], AF.Sqrt, reads=[rstd, P.epsc], bias=P.epsc[:, 0:1])
        em.op("dve", lambda e: e.reciprocal(out=rstd[:, 0:n], in_=rstd[:, 0:n]), reads=[rstd], writes=[rstd])
        xv = _chunk_rows(xdst)
        for m in range(KD):
            tmp = self.tmp[m % 2]
            og = self.og[m % 2]
            em.op("dve", lambda e: e.tensor_tensor(out=tmp[:, 0:n], in0=xblk[:, m, 0:n], in1=mean[:, 0:n],
                                                   op=ALU.subtract), reads=[xblk, mean], writes=[tmp])
            em.op("pool", lambda e: e.tensor_tensor(out=tmp[:, 0:n], in0=tmp[:, 0:n], in1=rstd[:, 0:n],
                                                    op=ALU.mult), reads=[tmp, rstd], writes=[tmp])
            em.act(og, og[:, 0:n], tmp[:, 0:n], AF.Identity, reads=[tmp, P.ln],
                   bias=P.ln[:, l, lnslot + 1, m:m + 1], scale=P.ln[:, l, lnslot, m:m + 1])
            em.dma("pool", xdst[m * 128:(m + 1) * 128, c0:c0 + n], og[:, 0:n], reads=[og],
                   writes=[xdst_t], partial=True)
            if dst2 is not None:
                d2, d2_t, c2 = dst2
                em.dma("sp", d2[m * 128:(m + 1) * 128, c2:c2 + n], og[:, 0:n], reads=[og],
                       writes=[d2_t], partial=True)


def phase_outproj(P, l, xsrc, xsrc_t, xdst, xdst_t, with_ctx):
    em = P.em
    with ExitStack() as st:
        xs = [em.sb("op_xs%d" % i, [128, KD, 512], F32, st) for i in range(2)]
        am = [em.sb("op_am%d" % i, [128, KD, 512], BF16, st) for i in range(2)]
        wt = [em.sb("op_w%d" % i, [128, KD, 512], BF16, st) for i in range(2)]
        ps = [em.ps("op_ps%d" % i, [128, 512], F32, st) for i in range(2)]
        rl = ResLN(P, st, "op")
        wv = P.wb_out[l].rearrange("(k p) c -> p k c", p=128)
        xv = _chunk_rows(xsrc)
        av = _chunk_rows(P.mixT)
        gc = 0
        cc = 0
        for bi, (t0, n, isctx) in enumerate(TBLK):
            if isctx and not with_ctx:
                continue
            j = 1 if isctx else 0
            x_s, a_m = xs[bi % 2], am[bi % 2]
            em.dma("sp", x_s[:, :, 0:n], xv[:, :, t0:t0 + n], reads=[xsrc_t[bi]], writes=[x_s])
            em.dma("sp", a_m[:, :, 0:n], av[:, :, t0:t0 + n], reads=[P.mixT_t[bi]], writes=[a_m])
            for g in range(4):
                w = wt[gc % 2]
                gc += 1
                em.dma("sp", w[:], wv[:, :, g * 512:(g + 1) * 512], reads=[P.wb_dense_t[l]], writes=[w])
                for c in range(4):
                    m = g * 4 + c
                    p = ps[cc % 2]
                    cc += 1
                    for k in range(KD):
                        em.mm(p, p[:, 0:n], w[:, k, c * 128:(c + 1) * 128], a_m[:, k, 0:n],
                              k == 0, k == KD - 1, reads=[w, a_m])
                    rl.add_chunk(m, p, x_s, n, l, GT_M, j)
            rl.finish(x_s, n, l, 0, xdst, xdst_t[bi], t0)
        em.barrier()


def phase_ffn(P, l, xsrc, xsrc_t, xdst, xdst_t, with_ctx, final=False):
    em = P.em
    with ExitStack() as st:
        xs = em.sb("ff_xs", [128, KD, 512], F32, st)
        xm = em.sb("ff_xm", [128, KD, 512], BF16, st)
        hh = em.sb("ff_h", [128, KF, 512], BF16, st)
        wg = [em.sb("ff_wg%d" % i, [128, KD, 256], BF16, st) for i in range(2)]
        wu = [em.sb("ff_wu%d" % i, [128, KD, 256], BF16, st) for i in range(2)]
        wd = [em.sb("ff_wd%d" % i, [128, KF, 128], BF16, st) for i in range(2)]
        pg = [em.ps("ff_pg%d" % i, [128, 512], F32, st) for i in range(2)]
        pu = [em.ps("ff_pu%d" % i, [128, 512], F32, st) for i in range(2)]
        pd = [em.ps("ff_pd%d" % i, [128, 512], F32, st) for i in range(2)]
        sg = [em.sb("ff_sg%d" % i, [128, 512], F32, st) for i in range(2)]
        rl = ResLN(P, st, "ff")
        wgv = P.wb_gate[l].rearrange("(k p) c -> p k c", p=128)
        wuv = P.wb_up[l].rearrange("(k p) c -> p k c", p=128)
        wdv = P.wb_down[l].rearrange("(k p) c -> p k c", p=128)
        xv = _chunk_rows(xsrc)
        gc = 0
        cc = 0
        dc = 0
        for bi, (t0, n, isctx) in enumerate(TBLK):
            if isctx and not with_ctx:
                continue
            j = 1 if isctx else 0
            em.dma("sp", xs[:, :, 0:n], xv[:, :, t0:t0 + n], reads=[xsrc_t[bi]], writes=[xs])
            for k in range(KD):
                em.act(xm, xm[:, k, 0:n], xs[:, k, 0:n], AF.Identity, reads=[xs, P.mod, P.mod1],
                       bias=P.mod[:, l, SH_F + k, j:j + 1], scale=P.mod1[:, l, SC_F + k, j:j + 1])
            for g in range(KF // 2):
                w_g, w_u = wg[gc % 2], wu[gc % 2]
                gc += 1
                em.dma("sp", w_g[:], wgv[:, :, g * 256:(g + 1) * 256], reads=[P.wb_dense_t[l]], writes=[w_g])
                em.dma("sp", w_u[:], wuv[:, :, g * 256:(g + 1) * 256], reads=[P.wb_dense_t[l]], writes=[w_u])
                for c in range(2):
                    m = g * 2 + c
                    p_g, p_u, s = pg[cc % 2], pu[cc % 2], sg[cc % 2]
                    cc += 1
                    for k in range(KD):
                        em.mm(p_g, p_g[:, 0:n], w_g[:, k, c * 128:(c + 1) * 128], xm[:, k, 0:n],
                              k == 0, k == KD - 1, reads=[w_g, xm])
                    for k in range(KD):
                        em.mm(p_u, p_u[:, 0:n], w_u[:, k, c * 128:(c + 1) * 128], xm[:, k, 0:n],
                              k == 0, k == KD - 1, reads=[w_u, xm])
                    em.act(s, s[:, 0:n], p_g[:, 0:n], AF.Silu, reads=[p_g])
                    em.op("dve", lambda e: e.tensor_tensor(out=hh[:, m, 0:n], in0=s[:, 0:n], in1=p_u[:, 0:n],
                                                           op=ALU.mult), reads=[s, p_u], writes=[hh], partial=True)
            for m in range(KD):
                w_d = wd[dc % 2]
                p_d = pd[dc % 2]
                dc += 1
                em.dma("sp", w_d[:], wdv[:, :, m * 128:(m + 1) * 128], reads=[P.wb_dense_t[l]], writes=[w_d])
                for k in range(KF):
                    em.mm(p_d, p_d[:, 0:n], w_d[:, k, :], hh[:, k, 0:n], k == 0, k == KF - 1, reads=[w_d, hh])
                rl.add_chunk(m, p_d, xs, n, l, GT_F, j)
            if final:
                rl.finish(xs, n, l, 2, P.xOut, xdst_t[bi], t0 - CTX)
            else:
                rl.finish(xs, n, l, 2, xdst, xdst_t[bi], t0)
        em.barrier()


NEG = -1.0e30
PIPE = True


def declare_scan(P):
    L = P.nl
    P.masks_in = P.din("masks", [128, 8, 128])
    P.ident_in = P.din("ident", [128, 128])
    P.tri_in = P.din("tri", [128, 2, 128])
    P.bmasks_in = P.din("bmasks", [128, 4, 128])
    P.rw_row = P.din("rw_row", [L, 11, 512])
    P.rw_mu = P.din("rw_mu", [L, 1536])
    P.rw_muL = P.din("rw_muL", [128, L, 3])
    P.rw_w2 = P.din("rw_w2", [L, 64, 512])
    P.rw_a2 = P.din("rw_a2", [L, 64, 512])
    P.rw_g2 = P.din("rw_g2", [L, 96, 512])
    P.gd_conv = P.din("gd_conv", [L, 5, 1536])
    P.gd_row = P.din("gd_row", [L, 3, 512])
    P.yscr = P.dscr("yscr", [4, TTP, 512])
    P.auxs = P.dscr("auxs", [3, TTP, 512])


def setup_scan_consts(P):
    em = P.em
    P.masks = em.sb("masks_sb", [128, 8, 128], F32)
    P.ident = em.sb("ident", [128, 128], F32)
    P.tri = em.sb("tri", [128, 2, 128], F32)
    em.dma("sp", P.masks[:], P.masks_in[:, :, :], writes=[P.masks])
    em.dma("sp", P.ident[:], P.ident_in[:, :], writes=[P.ident])
    em.dma("sp", P.tri[:], P.tri_in[:, :, :], writes=[P.tri])
    P.bmasks = em.sb("bmasks_sb", [128, 4, 128], F32)
    em.dma("sp", P.bmasks[:], P.bmasks_in[:, :, :], writes=[P.bmasks])


def chunk_order(rev):
    nc_ctx = CTX // 128
    nc_lat = SEQ // 128
    ctx = [128 * i for i in range(nc_ctx)]
    lat = [CTX + 128 * i for i in range(nc_lat)]
    if rev:
        return ctx[::-1] + lat[::-1]
    return ctx + lat


class ScanCore:
    def __init__(self, P, st, N, tag):
        em = P.em
        self.P, self.N, self.H = P, N, 512 // N
        N_, H = N, self.H
        self.pb = [em.ps(tag + "_pb%d" % i, [128, 512], F32, st) for i in range(8)]
        self.pbi = 0
        f = lambda nm, shp: em.sb(tag + "_" + nm, shp, F32, st)
        self.XT = {k: f("xt_" + k, [N_, H, 128]) for k in ("a", "b", "k", "r", "R")}
        self.AT = [f("AT0", [128, H, 128])]
        self.A = [f("A0", [128, H, 128])]
        self.AkT = f("AkT", [128, H, 128])
        self.ArbT = f("ArbT", [128, H, 128])
        self.ArkT = f("ArkT", [128, H, 128])
        self.Z = [f("Z%d" % i, [128, H, 2 * N_]) for i in range(2)]
        self.GH = H // 2
        GH = self.GH
        fb = lambda nm, shp: em.sb(tag + "_" + nm, shp, BF16, st)
        self.J = [[[fb("J%d%d%d" % (gi, i, k), [128, GH, 128]) for k in range(2)] for i in range(2)] for gi in range(2)]
        self.X = [fb("X%d" % gi, [128, GH, 128]) for gi in range(2)]
        self.XTt = [fb("XTt%d" % gi, [128, GH, 128]) for gi in range(2)]
        self.Zb = [fb("Zb%d" % gi, [128, GH, 2 * N_]) for gi in range(2)]
        self.Zt = [f("Zt%d" % gi, [128, GH, 2 * N_]) for gi in range(2)]
        self.WpT = f("WpT", [N_, H, 128])
        self.U = f("U", [128, H, N_])
        self.ST = f("ST", [N_, H, N_])
        self.Ysb = [f("Ysb%d" % i, [128, 512]) for i in range(2)]
        self.yi = 0

    def bank(self):
        b = self.pb[self.pbi % 8]
        self.pbi += 1
        return b

    def reset_state(self):
        em = self.P.em
        em.op("dve", lambda e: e.memset(self.ST[:], 0.0), writes=[self.ST])

    def transpose_to(self, key, src):
        P, em, N, H = self.P, self.P.em, self.N, self.H
        dst = self.XT[key]
        for g in range(H // 4):
            pb = self.bank()
            for hh in range(4):
                h = g * 4 + hh
                em.op("pe", lambda e: e.transpose(pb[0:N, hh * 128:(hh + 1) * 128], src[:, h * N:(h + 1) * N],
                                                  P.ident[:]), reads=[src, P.ident], writes=[pb])
            em.act(dst, dst[:, g * 4:(g + 1) * 4, :], pb[0:N, :].rearrange("p (h t) -> p h t", h=4), AF.Copy,
                   reads=[pb])
        return dst

    def gram(self, lkey, rkey, dst, mask_ap_fn, mask_t):
        P, em, N, H = self.P, self.P.em, self.N, self.H
        L_, R_ = self.XT[lkey], self.XT[rkey]
        for g in range(H // 4):
            pb = self.bank()
            for hh in range(4):
                h = g * 4 + hh
                em.mm(pb, pb[:, hh * 128:(hh + 1) * 128], L_[:, h, :], R_[:, h, :], True, True, reads=[L_, R_])
            em.op("dve", lambda e: e.tensor_tensor(
                out=dst[:, g * 4:(g + 1) * 4, :], in0=pb[:, :].rearrange("p (h t) -> p h t", h=4),
                in1=mask_ap_fn(g), op=ALU.mult), reads=[pb, mask_t], writes=[dst], partial=True)

    def run_chunk(self, rev, ops, WcT, masks, ydst_ap, filler=None):
        P, em, N, H = self.P, self.P.em, self.N, self.H
        hv = lambda t: t[:].rearrange("p (h n) -> p h n", n=N)
        same_bk = ops["gb"] is ops["gk"]
        self.transpose_to("a", ops["ga"])
        self.transpose_to("k", ops["gk"])
        if not same_bk:
            self.transpose_to("b", ops["gb"])
        bkey = "k" if same_bk else "b"
        self.transpose_to("r", ops["gr"])
        if ops["Rtil"] is ops["gr"]:
            Rkey = "r"
        else:
            self.transpose_to("R", ops["Rtil"])
            Rkey = "R"
        AT, A = self.AT[0], self.A[0]
        self.gram(bkey, "a", AT, *masks["AT"])
        self.gram("a", bkey, A, *masks["A"])
        if same_bk:
            AkT, ArbT = AT, None
        else:
            AkT, ArbT = self.AkT, self.ArbT
            self.gram("k", "a", AkT, *masks["AT"])
            self.gram("b", "r", ArbT, *masks["ArT"])
        ArkT = self.ArkT
        self.gram("k", "r", ArkT, *masks["ArT"])
        if same_bk:
            ArbT = ArkT
        V = ops["V"]
        Z = self.Z[0]
        pb = self.bank()
        for h in range(H):
            em.mm(pb, pb[:, h * N:(h + 1) * N], AkT[:, h, :], V[:, h * N:(h + 1) * N], True, True,
                  reads=[AkT, V])
        em.op("dve", lambda e: e.tensor_copy(out=Z[:, :, 0:N], in_=pb[:, :].rearrange("p (h n) -> p h n", n=N)),
              reads=[pb], writes=[Z], partial=True)
        em.op("pool", lambda e: e.tensor_copy(out=Z[:, :, N:2 * N], in_=hv(ops["Atil"])),
              reads=[ops["Atil"]], writes=[Z], partial=True)
        Zf = self.Z[1]
        HB = 512 // (2 * N)
        A0, AT0 = self.A[0], self.AT[0]
        GH = self.GH
        bm = lambda i: P.bmasks[:, i:i + 1, :].to_broadcast([128, GH, 128])
        idb = P.ident[:, :].unsqueeze(1).to_broadcast([128, GH, 128])

        def mmg(dst_pb, lhs_t, rhs_t):
            for hh in range(GH):
                em.mm(dst_pb, dst_pb[:, hh * 128:(hh + 1) * 128], lhs_t[:, hh, :], rhs_t[:, hh, :], True, True,
                      reads=[lhs_t, rhs_t])

        vg = lambda pb_: pb_[:, 0:GH * 128].rearrange("p (h t) -> p h t", h=GH)

        def inv_group(g):
            gs = slice(g * GH, (g + 1) * GH)
            X, XT = self.X[g], self.XTt[g]
            J = self.J[g]
            Ja, JaT = J[0]
            em.op("dve", lambda e: e.tensor_tensor(out=Ja[:], in0=A0[:, gs, :], in1=bm(0), op=ALU.mult),
                  reads=[A0, P.bmasks], writes=[Ja])
            em.op("pool", lambda e: e.tensor_tensor(out=JaT[:], in0=AT0[:, gs, :], in1=bm(0), op=ALU.mult),
                  reads=[AT0, P.bmasks], writes=[JaT])
            em.op("dve", lambda e: e.tensor_tensor(out=X[:], in0=Ja[:], in1=idb, op=ALU.add),
                  reads=[Ja, P.ident], writes=[X])
            em.op("pool", lambda e: e.tensor_tensor(out=XT[:], in0=JaT[:], in1=idb, op=ALU.add),
                  reads=[JaT, P.ident], writes=[XT])
            yield
            cur = 0
            for lev in range(3):
                Jc, JcT = J[cur]
                Jn, JnT = J[1 - cur]
                p1, p2 = self.bank(), self.bank()
                mmg(p1, JcT, Jc)
                mmg(p2, Jc, JcT)
                em.op("dve", lambda e: e.tensor_copy(out=Jn[:], in_=vg(p1)), reads=[p1], writes=[Jn])
                em.act(JnT, JnT[:], vg(p2), AF.Copy, reads=[p2])
                yield
                p3, p4 = self.bank(), self.bank()
                mmg(p3, JnT, X)
                mmg(p4, Jn, XT)
                em.op("dve", lambda e: e.tensor_tensor(out=X[:], in0=vg(p3), in1=X[:], op=ALU.add),
                      reads=[p3, X], writes=[X])
                em.op("dve", lambda e: e.tensor_tensor(out=XT[:], in0=vg(p4), in1=XT[:], op=ALU.add),
                      reads=[p4, XT], writes=[XT])
                yield
                cur = 1 - cur
            for bi in (1, 2, 3):
                Ao, AoT = J[0]
                Y, Y2 = J[1]
                em.op("dve", lambda e: e.tensor_tensor(out=Ao[:], in0=A0[:, gs, :], in1=bm(bi), op=ALU.mult),
                      reads=[A0, P.bmasks], writes=[Ao])
                em.op("pool", lambda e: e.tensor_tensor(out=AoT[:], in0=AT0[:, gs, :], in1=bm(bi), op=ALU.mult),
                      reads=[AT0, P.bmasks], writes=[AoT])
                last = bi == 3
                p2 = self.bank()
                mmg(p2, Ao, XT)
                if not last:
                    p1 = self.bank()
                    mmg(p1, AoT, X)
                    em.op("dve", lambda e: e.tensor_copy(out=Y[:], in_=vg(p1)), reads=[p1], writes=[Y])
                em.act(Y2, Y2[:], vg(p2), AF.Copy, reads=[p2])
                yield
                p4 = self.bank()
                mmg(p4, X, Y2)
                if not last:
                    p3 = self.bank()
                    mmg(p3, XT, Y)
                    em.op("dve", lambda e: e.tensor_tensor(out=X[:], in0=vg(p3), in1=X[:], op=ALU.add),
                          reads=[p3, X], writes=[X])
                em.op("dve", lambda e: e.tensor_tensor(out=XT[:], in0=vg(p4), in1=XT[:], op=ALU.add),
                      reads=[p4, XT], writes=[XT])
                yield
            nsub = max(1, GH // HB)
            hps = GH // nsub
            Zb, Zt = self.Zb[g], self.Zt[g]
            em.op("pool", lambda e: e.tensor_copy(out=Zb[:], in_=Z[:, gs, :]), reads=[Z], writes=[Zb])

            def apply_x(src_b, accumulate):
                for sub in range(nsub):
                    pb = self.bank()
                    for hh in range(hps):
                        h4 = sub * hps + hh
                        em.mm(pb, pb[:, hh * 2 * N:(hh + 1) * 2 * N], XT[:, h4, :], src_b[:, h4, :], True, True,
                              reads=[XT, src_b])
                    h0 = g * GH + sub * hps
                    pv = pb[:, 0:hps * 2 * N].rearrange("p (h n) -> p h n", h=hps)
                    if accumulate:
                        em.op("dve", lambda e: e.tensor_tensor(out=Zf[:, h0:h0 + hps, :], in0=pv,
                                                               in1=Zf[:, h0:h0 + hps, :], op=ALU.add),
                              reads=[pb, Zf], writes=[Zf], partial=True)
                    else:
                        em.act(Zf, Zf[:, h0:h0 + hps, :], pv, AF.Copy, reads=[pb], writes=[Zf])

            apply_x(Zb, False)
            yield
            em.op("pool", lambda e: e.tensor_tensor(out=Zt[:], in0=Z[:, gs, :], in1=Zf[:, gs, :], op=ALU.subtract),
                  reads=[Z, Zf], writes=[Zt])
            for sub in range(nsub):
                pb = self.bank()
                for hh in range(hps):
                    h4 = sub * hps + hh
                    h = g * GH + h4
                    em.mm(pb, pb[:, hh * 2 * N:(hh + 1) * 2 * N], AT0[:, h, :], Zf[:, h, :], True, True,
                          reads=[AT0, Zf])
                h4s = slice(sub * hps, (sub + 1) * hps)
                em.op("dve", lambda e: e.tensor_tensor(
                    out=Zb[:, h4s, :], in0=pb[:, 0:hps * 2 * N].rearrange("p (h n) -> p h n", h=hps),
                    in1=Zt[:, h4s, :], op=ALU.add), reads=[pb, Zt], writes=[Zb], partial=True)
            yield
            apply_x(Zb, True)
            yield

        gens = [inv_group(0), inv_group(1)]
        alive = [True, True]
        while any(alive):
            for gi in range(2):
                if alive[gi]:
                    try:
                        next(gens[gi])
                    except StopIteration:
                        alive[gi] = False
            if filler is not None:
                next(filler, None)
        WpT = self.WpT
        for g in range(H // 4):
            pb = self.bank()
            for hh in range(4):
                h = g * 4 + hh
                em.op("pe", lambda e: e.transpose(pb[0:N, hh * 128:(hh + 1) * 128], Zf[:, h, N:2 * N],
                                                  P.ident[:]), reads=[Zf, P.ident], writes=[pb])
            em.act(WpT, WpT[:, g * 4:(g + 1) * 4, :], pb[0:N, :].rearrange("p (h t) -> p h t", h=4), AF.Copy,
                   reads=[pb], writes=[WpT])
        ST, U = self.ST, self.U
        RT = self.XT[Rkey]
        pb = self.bank()
        for h in range(H):
            em.mm(pb, pb[:, h * N:(h + 1) * N], WpT[:, h, :], ST[:, h, :], True, True, reads=[WpT, ST])
        em.op("dve", lambda e: e.tensor_tensor(out=U[:], in0=pb[:, :].rearrange("p (h n) -> p h n", n=N),
                                               in1=Zf[:, :, 0:N], op=ALU.add), reads=[pb, Zf], writes=[U])
        pb = self.bank()
        for h in range(H):
            o = pb[:, h * N:(h + 1) * N]
            em.mm(pb, o, RT[:, h, :], ST[:, h, :], True, False, reads=[RT, ST])
            em.mm(pb, o, ArbT[:, h, :], U[:, h, :], False, False, reads=[ArbT, U])
            em.mm(pb, o, ArkT[:, h, :], V[:, h * N:(h + 1) * N], False, True, reads=[ArkT, V])
        ysb = self.Ysb[self.yi % 2]
        self.yi += 1
        em.act(ysb, ysb[:], pb[:, :], AF.Copy, reads=[pb])
        em.dma("sp", ydst_ap, ysb[:], reads=[ysb], writes=[T()])
        pb = self.bank()
        Bh, Kh = ops["Bh"], ops["Kh"]
        for h in range(H):
            o = pb[0:N, h * N:(h + 1) * N]
            em.mm(pb, o, Bh[:, h * N:(h + 1) * N], U[:, h, :], True, False, reads=[Bh, U])
            em.mm(pb, o, Kh[:, h * N:(h + 1) * N], V[:, h * N:(h + 1) * N], False, True, reads=[Kh, V])
        em.op("dve", lambda e: e.tensor_tensor(out=ST[:], in0=ST[:],
                                               in1=WcT[:, :].unsqueeze(2).to_broadcast([N, H, N]), op=ALU.mult),
              reads=[ST, WcT], writes=[ST])
        em.op("dve", lambda e: e.tensor_tensor(out=ST[:], in0=pb[0:N, :].rearrange("p (h n) -> p h n", n=N),
                                               in1=ST[:], op=ALU.add), reads=[pb, ST], writes=[ST])


C0 = math.exp(-0.5)


def _bc_row(P, st, name, src_row_ap, width=512):
    em = P.em
    t = em.sb(name, [128, width], F32, st)
    em.dma("sp", t[:], src_row_ap.partition_broadcast(128), writes=[t])
    return t


def phase_rwkv(P, l, want_ctx_out=True):
    em = P.em
    dummy = T()
    with ExitStack() as st:
        core = ScanCore(P, st, 64, "rw")
        f = lambda nm, shp=(128, 512): em.sb("rw_" + nm, list(shp), F32, st)
        prm = {}
        for i, nm in enumerate(["w0_0", "w0_1", "a0_0", "a0_1", "k_k", "k_a", "r_k"]):
            prm[nm] = _bc_row(P, st, "rwp_" + nm, P.rw_row[l, i:i + 1, :])
        omm = f("omm", (128, 1536))
        hmu = f("hmu", (128, 1536))
        em.dma("sp", omm[:], P.rw_mu[l:l + 1, :].partition_broadcast(128), writes=[omm])
        em.op("dve", lambda e: e.tensor_scalar_mul(out=hmu[:], in0=omm[:], scalar1=0.5), reads=[omm], writes=[hmu])
        em.op("dve", lambda e: e.tensor_scalar(out=omm[:], in0=omm[:], scalar1=-1.0, scalar2=1.0, op0=ALU.mult,
                                               op1=ALU.add), reads=[omm], writes=[omm])
        muL = f("muL", (128, 3))
        ommL = f("ommL", (128, 3))
        hmuL = f("hmuL", (128, 3))
        em.dma("sp", muL[:], P.rw_muL[:, l, :], writes=[muL])
        em.op("dve", lambda e: e.tensor_scalar_mul(out=hmuL[:], in0=muL[:], scalar1=0.5), reads=[muL], writes=[hmuL])
        em.op("dve", lambda e: e.tensor_scalar(out=ommL[:], in0=muL[:], scalar1=-1.0, scalar2=1.0, op0=ALU.mult,
                                               op1=ALU.add), reads=[muL], writes=[ommL])
        w2 = f("w2", (64, 512))
        a2 = f("a2", (64, 512))
        g2 = f("g2", (96, 512))
        em.dma("sp", w2[:], P.rw_w2[l, :, :], writes=[w2])
        em.dma("sp", a2[:], P.rw_a2[l, :, :], writes=[a2])
        em.dma("sp", g2[:], P.rw_g2[l, :, :], writes=[g2])
        cen, prv, nxt = f("cen"), f("prv"), f("nxt")
        rp, kp = f("rp"), f("kp")
        vp2 = [f("vp0"), f("vp1")]
        lo = f("lo", (128, 3, 130))
        loP = f("loP", (128, 3, 128))
        lot = f("lot", (128, 128))
        sgd = [f("sgd0"), f("sgd1")]
        alr = [f("alr0"), f("alr1")]
        gg = f("gg")
        kk = f("kk")
        kdir = [f("kdir0"), f("kdir1")]
        t1, t2, t3 = f("t1"), f("t2"), f("t3")
        ss = f("ss", (128, 8))
        E1, E1p, E2, E3 = f("E1"), f("E1p"), f("E2"), f("E3")
        at, bt, kt, rt, bb = f("at"), f("bt"), f("kt"), f("rt"), f("bb")
        Bh2, Kh2 = [f("Bh0"), f("Bh1")], [f("Kh0"), f("Kh1")]
        WcT2 = [f("WcT0", (64, 8)), f("WcT1", (64, 8))]
        aux = f("aux")
        for d in (0, 1):
            rev = d == 1
            core.reset_state()
            if not rev:
                mk = {"AT": (lambda g: P.masks[:, 1:2, :].to_broadcast([128, 4, 128]), P.masks),
                      "A": (lambda g: P.masks[:, 0:1, :].to_broadcast([128, 4, 128]), P.masks),
                      "ArT": (lambda g: P.masks[:, 3:4, :].to_broadcast([128, 4, 128]), P.masks)}
            else:
                mk = {"AT": (lambda g: P.masks[:, 0:1, :].to_broadcast([128, 4, 128]), P.masks),
                      "A": (lambda g: P.masks[:, 1:2, :].to_broadcast([128, 4, 128]), P.masks),
                      "ArT": (lambda g: P.masks[:, 2:3, :].to_broadcast([128, 4, 128]), P.masks)}
            def prep(t0, slot, d=d):
                vp, Bh, Kh, WcT = vp2[slot], Bh2[slot], Kh2[slot], WcT2[slot]
                pp = ppos(t0)
                for ci, dst in enumerate((rp, kp, vp)):
                    c0 = ci * 512
                    em.dma("sp", cen[:], P.pTok[pp:pp + 128, c0:c0 + 512], writes=[cen])
                    em.dma("sp", prv[:], P.pTok[pp - 1:pp + 127, c0:c0 + 512], writes=[prv])
                    em.dma("sp", nxt[:], P.pTok[pp + 1:pp + 129, c0:c0 + 512], writes=[nxt])
                    em.op("pool", lambda e: e.tensor_tensor(out=prv[:], in0=prv[:], in1=nxt[:], op=ALU.add),
                          reads=[prv, nxt], writes=[prv])
                    em.op("pool", lambda e: e.tensor_tensor(out=prv[:], in0=prv[:], in1=hmu[:, c0:c0 + 512],
                                                            op=ALU.mult), reads=[prv, hmu], writes=[prv])
                    em.op("dve", lambda e: e.tensor_tensor(out=cen[:], in0=cen[:], in1=omm[:, c0:c0 + 512],
                                                           op=ALU.mult), reads=[cen, omm], writes=[cen])
                    em.op("dve", lambda e: e.tensor_tensor(out=dst[:], in0=cen[:], in1=prv[:], op=ALU.add),
                          reads=[cen, prv], writes=[dst])
                yield
                for c, (fi, wdt) in enumerate(((10, 64), (11, 64), (12, 96))):
                    em.dma("sp", lo[0:wdt, c, :], P.pT[fi * 128:fi * 128 + wdt, pp - 1:pp + 129], writes=[lo],
                           partial=True)
                for c, wdt in enumerate((64, 64, 96)):
                    em.op("dve", lambda e: e.tensor_tensor(out=lot[0:wdt, :], in0=lo[0:wdt, c, 0:128],
                                                           in1=lo[0:wdt, c, 2:130], op=ALU.add),
                          reads=[lo], writes=[lot])
                    em.op("dve", lambda e: e.tensor_scalar_mul(out=lot[0:wdt, :], in0=lot[0:wdt, :],
                                                               scalar1=hmuL[0:wdt, c:c + 1]),
                          reads=[lot, hmuL], writes=[lot])
                    em.op("dve", lambda e: e.scalar_tensor_tensor(
                        out=loP[0:wdt, c, :], in0=lo[0:wdt, c, 1:129], scalar=ommL[0:wdt, c:c + 1],
                        in1=lot[0:wdt, :], op0=ALU.mult, op1=ALU.add), reads=[lo, ommL, lot], writes=[loP],
                        partial=True)
                em.act(loP, loP[0:64, 0, :], loP[0:64, 0, :], AF.Tanh, reads=[loP])
                em.act(loP, loP[0:96, 2, :], loP[0:96, 2, :], AF.Sigmoid, reads=[loP])
                yield
                for dd in (0, 1):
                    pb = core.bank()
                    em.mm(pb, pb[:, :], loP[dd * 32:(dd + 1) * 32, 0, :], w2[dd * 32:(dd + 1) * 32, :], True, True,
                          reads=[loP, w2])
                    em.op("dve", lambda e: e.tensor_tensor(out=sgd[dd][:], in0=pb[:, :], in1=prm["w0_%d" % dd][:],
                                                           op=ALU.add), reads=[pb, prm["w0_%d" % dd]],
                          writes=[sgd[dd]])
                    em.act(sgd[dd], sgd[dd][:], sgd[dd][:], AF.Sigmoid, reads=[sgd[dd]])
                    pb = core.bank()
                    em.mm(pb, pb[:, :], loP[dd * 32:(dd + 1) * 32, 1, :], a2[dd * 32:(dd + 1) * 32, :], True, True,
                          reads=[loP, a2])
                    em.op("dve", lambda e: e.tensor_tensor(out=alr[dd][:], in0=pb[:, :], in1=prm["a0_%d" % dd][:],
                                                           op=ALU.add), reads=[pb, prm["a0_%d" % dd]],
                          writes=[alr[dd]])
                    em.act(alr[dd], alr[dd][:], alr[dd][:], AF.Sigmoid, reads=[alr[dd]])
                yield
                em.op("dve", lambda e: e.tensor_tensor(out=kk[:], in0=kp[:], in1=prm["k_k"][:], op=ALU.mult),
                      reads=[kp, prm["k_k"]], writes=[kk])
                em.act(t1, t1[:], kk[:], AF.Square, reads=[kk])
                em.op("dve", lambda e: e.tensor_reduce(out=ss[:], in_=t1[:].rearrange("p (h n) -> p h n", n=64),
                                                       axis=AX.X, op=ALU.add), reads=[t1], writes=[ss])
                em.act(ss, ss[:], ss[:], AF.Sqrt, reads=[ss, P.epsc], bias=P.epsc[:, 3:4])
                em.op("dve", lambda e: e.reciprocal(out=ss[:], in_=ss[:]), reads=[ss], writes=[ss])
                em.op("dve", lambda e: e.tensor_tensor(
                    out=kk[:].rearrange("p (h n) -> p h n", n=64), in0=kk[:].rearrange("p (h n) -> p h n", n=64),
                    in1=ss[:, :].unsqueeze(2).to_broadcast([128, 8, 64]), op=ALU.mult), reads=[kk, ss], writes=[kk])
                yield
                for dd in (0, 1):
                    em.op("dve", lambda e: e.scalar_tensor_tensor(
                        out=t1[:], in0=alr[dd][:], scalar=-1.0, in1=prm["k_a"][:], op0=ALU.add, op1=ALU.mult),
                        reads=[alr[dd], prm["k_a"]], writes=[t1])
                    em.op("dve", lambda e: e.scalar_tensor_tensor(
                        out=kdir[dd][:], in0=t1[:], scalar=1.0, in1=kp[:], op0=ALU.add, op1=ALU.mult),
                        reads=[t1, kp], writes=[kdir[dd]])
                yield
                if d == 0:
                    pb = core.bank()
                    em.mm(pb, pb[:, :], loP[0:96, 2, :], g2[0:96, :], True, True, reads=[loP, g2])
                    em.act(gg, gg[:], pb[:, :], AF.Copy, reads=[pb])
                    em.dma("sp", P.auxs[1, pp:pp + 128, :], gg[:], reads=[gg], writes=[dummy])
                    em.op("pool", lambda e: e.tensor_tensor(out=t2[:], in0=kdir[0][:], in1=kdir[1][:], op=ALU.add),
                          reads=[kdir[0], kdir[1]], writes=[t2])
                    em.op("pool", lambda e: e.tensor_tensor(out=t2[:], in0=t2[:], in1=rp[:], op=ALU.mult),
                          reads=[t2, rp], writes=[t2])
                    em.op("pool", lambda e: e.tensor_tensor(out=t2[:], in0=t2[:], in1=prm["r_k"][:], op=ALU.mult),
                          reads=[t2, prm["r_k"]], writes=[t2])
                    em.op("dve", lambda e: e.tensor_reduce(out=ss[:], in_=t2[:].rearrange("p (h n) -> p h n", n=64),
                                                           axis=AX.X, op=ALU.add), reads=[t2], writes=[ss])
                    em.op("dve", lambda e: e.tensor_tensor(
                        out=aux[:].rearrange("p (h n) -> p h n", n=64), in0=vp[:].rearrange("p (h n) -> p h n", n=64),
                        in1=ss[:, :].unsqueeze(2).to_broadcast([128, 8, 64]), op=ALU.mult), reads=[vp, ss],
                        writes=[aux])
                    em.dma("sp", P.auxs[0, pp:pp + 128, :], aux[:], reads=[aux], writes=[dummy])
                yield
                sg_ = sgd[d]
                pc = core.bank()
                em.mm(pc, pc[:, :], P.tri[:, d, :], sg_[:], True, True, reads=[P.tri, sg_])
                ptot = core.bank()
                em.mm(ptot, ptot[:, :], P.ones_f[:], sg_[:], True, True, reads=[P.ones_f, sg_])
                pw = core.bank()
                for h in range(8):
                    em.mm(pw, pw[0:64, h:h + 1], sg_[:, h * 64:(h + 1) * 64], P.ones_f[:, 0:1], True, True,
                          reads=[sg_, P.ones_f])
                em.act(WcT, WcT[:], pw[0:64, 0:8], AF.Exp, reads=[pw], scale=-C0)
                em.op("dve", lambda e: e.tensor_copy(out=t1[:], in_=pc[:, :]), reads=[pc], writes=[t1])
                em.op("dve", lambda e: e.tensor_tensor(out=t2[:], in0=t1[:], in1=sg_[:], op=ALU.subtract),
                      reads=[t1, sg_], writes=[t2])
                em.op("dve", lambda e: e.tensor_tensor(out=t3[:], in0=ptot[:, :], in1=t1[:], op=ALU.subtract),
                      reads=[ptot, t1], writes=[t3])
                yield
                em.act(E1, E1[:], t1[:], AF.Exp, reads=[t1], scale=-C0)
                em.act(E2, E2[:], t1[:], AF.Exp, reads=[t1], scale=C0)
                em.act(E1p, E1p[:], t2[:], AF.Exp, reads=[t2], scale=-C0)
                em.act(E3, E3[:], t3[:], AF.Exp, reads=[t3], scale=-C0)
                yield
                kd, al = kdir[d], alr[d]
                em.op("dve", lambda e: e.scalar_tensor_tensor(out=at[:], in0=kk[:], scalar=-1.0, in1=E1p[:],
                                                              op0=ALU.mult, op1=ALU.mult), reads=[kk, E1p], writes=[at])
                em.op("pool", lambda e: e.tensor_tensor(out=bb[:], in0=kk[:], in1=al[:], op=ALU.mult),
                      reads=[kk, al], writes=[bb])
                em.op("pool", lambda e: e.tensor_tensor(out=bt[:], in0=bb[:], in1=E2[:], op=ALU.mult),
                      reads=[bb, E2], writes=[bt])
                em.op("pool", lambda e: e.tensor_tensor(out=Bh[:], in0=bb[:], in1=E3[:], op=ALU.mult),
                      reads=[bb, E3], writes=[Bh])
                em.op("dve", lambda e: e.tensor_tensor(out=kt[:], in0=kd[:], in1=E2[:], op=ALU.mult),
                      reads=[kd, E2], writes=[kt])
                em.op("pool", lambda e: e.tensor_tensor(out=Kh[:], in0=kd[:], in1=E3[:], op=ALU.mult),
                      reads=[kd, E3], writes=[Kh])
                em.op("dve", lambda e: e.tensor_tensor(out=rt[:], in0=rp[:], in1=E1[:], op=ALU.mult),
                      reads=[rp, E1], writes=[rt])
                yield

            order = chunk_order(rev)
            for _ in prep(order[0], 0):
                pass
            for ci, t0 in enumerate(order):
                slot = ci % 2
                gnext = prep(order[ci + 1], 1 - slot) if ci + 1 < len(order) else None
                ops = {"ga": at, "gb": bt, "gk": kt, "gr": rt, "V": vp2[slot], "Atil": at, "Bh": Bh2[slot],
                       "Kh": Kh2[slot], "Rtil": rt}
                core.run_chunk(rev, ops, WcT2[slot], mk, P.yscr[d, ppos(t0):ppos(t0) + 128, :], filler=(gnext if PIPE else None))
                if gnext is not None:
                    for _ in gnext:
                        pass
        em.barrier()


def host_consts():
    idx = np.arange(128)
    r, c = idx[:, None], idx[None, :]
    m = np.zeros((128, 8, 128), np.float32)
    for i, cond in enumerate((c < r, c > r, c <= r, c >= r)):
        m[:, i, :] = cond.astype(np.float32)
        m[:, 4 + i, :] = np.where(cond, 0.0, NEG).astype(np.float32)
    tri = np.zeros((128, 2, 128), np.float32)
    tri[:, 0, :] = (r <= c)
    tri[:, 1, :] = (r >= c)
    bd = lambda b: ((r // b) == (c // b)).astype(np.float32)
    bmk = np.stack([bd(16), bd(32) - bd(16), bd(64) - bd(32), 1.0 - bd(64)], 1).astype(np.float32)
    return {"masks": m, "ident": np.eye(128, dtype=np.float32), "tri": tri, "bmasks": bmk}


def host_scan_inputs(inp, L):
    f32 = np.float32
    out = {}
    rw = np.zeros((L, 11, 512), f32)
    rw[:, 0] = inp["rwkv_w0"][:L, 0]
    rw[:, 1] = inp["rwkv_w0"][:L, 1]
    rw[:, 2] = inp["rwkv_a0"][:L, 0]
    rw[:, 3] = inp["rwkv_a0"][:L, 1]
    rw[:, 4] = inp["rwkv_k_k"][:L]
    rw[:, 5] = inp["rwkv_k_a"][:L]
    rw[:, 6] = inp["rwkv_r_k"][:L].reshape(L, 512)
    rw[:, 7] = inp["rwkv_gn_g"][:L]
    rw[:, 8] = inp["rwkv_gn_b"][:L]
    out["rw_row"] = rw
    mu = inp["rwkv_mu"][:L]
    out["rw_mu"] = np.ascontiguousarray(mu[:, 0:1536])
    muL = np.zeros((128, L, 3), f32)
    muL[0:64, :, 0] = mu[:, 1536:1600].T
    muL[0:64, :, 1] = mu[:, 1600:1664].T
    muL[0:96, :, 2] = mu[:, 1664:1760].T
    out["rw_muL"] = muL
    out["rw_w2"] = np.ascontiguousarray(inp["rwkv_w2"][:L].reshape(L, 64, 512))
    out["rw_a2"] = np.ascontiguousarray(inp["rwkv_a2"][:L].reshape(L, 64, 512))
    out["rw_g2"] = np.ascontiguousarray(inp["rwkv_g2"][:L])
    out["gd_conv"] = np.ascontiguousarray(inp["gdn_conv"][:L])
    gr = np.zeros((L, 3, 512), f32)
    gr[:, 0] = np.tile(inp["gdn_norm"][:L], (1, 4))
    gr[:, 1, 0:8] = inp["gdn_a_log"][:L].reshape(L, 8)
    gr[:, 1, 8:16] = inp["gdn_dt_bias"][:L].reshape(L, 8)
    out["gd_row"] = gr
    return out


def phase_gdn(P, l):
    em = P.em
    dummy = T()
    with ExitStack() as st:
        core = ScanCore(P, st, 128, "gd")
        f = lambda nm, shp=(128, 512): em.sb("gd_" + nm, list(shp), F32, st)
        cw = []
        for j in range(5):
            t = f("cw%d" % j, (128, 1536))
            em.dma("sp", t[:], P.gd_conv[l, j:j + 1, :].partition_broadcast(128), writes=[t])
            cw.append(t)
        prow = _bc_row(P, st, "gdp_row", P.gd_row[l, 1:2, :], 512)
        negea = f("negea", (128, 8))
        em.act(negea, negea[:], prow[:, 0:8], AF.Exp, reads=[prow])
        em.op("dve", lambda e: e.tensor_scalar_mul(out=negea[:], in0=negea[:], scalar1=-1.0), reads=[negea],
              writes=[negea])
        sh = [f("sh%d" % j) for j in range(5)]
        acc, tmp = f("acc"), f("tmp")
        qkv = [f("q"), f("k"), f("v")]
        ss = f("ss", (128, 4))
        ab = f("ab", (128, 16))
        gx, ge, gl = f("gx", (128, 8)), f("ge", (128, 8)), f("gl", (128, 8))
        gcol, beta = f("gcol", (128, 8)), f("beta", (128, 8))
        Gs, nG, eG, eTG, etot, nb, nbeG = (f(n, (128, 4)) for n in ("Gs", "nG", "eG", "eTG", "etot", "nb", "nbeG"))
        ka, Atil, Kh, Rtil, Vp, zt = f("ka"), f("Atil"), f("Kh"), f("Rtil"), f("Vp"), f("zt")
        diag = f("diag", (128, 4, 128))
        dtmp = f("dtmp", (128, 4, 128))
        Ds, DTs, DTi = f("Ds", (128, 4, 128)), f("DTs", (128, 4, 128)), f("DTi", (128, 4, 128))
        hv = lambda t: t[:].rearrange("p (h n) -> p h n", n=128)
        bc4 = lambda t: t[:, :].unsqueeze(2).to_broadcast([128, 4, 128])
        for d in (0, 1):
            rev = d == 1
            core.reset_state()
            mA, mAT, mATi = (4, 5, 7) if not rev else (5, 4, 6)
            mk = {"AT": (lambda g: DTs[:, :, :], DTs), "A": (lambda g: Ds[:, :, :], Ds),
                  "ArT": (lambda g: DTi[:, :, :], DTi)}
            for t0 in chunk_order(rev):
                pp = ppos(t0)
                for ci in range(3):
                    c0 = TOKC["gq"] + ci * 512
                    for j in range(5):
                        em.dma("sp", sh[j][:], P.pTok[pp + j - 2:pp + j - 2 + 128, c0:c0 + 512], writes=[sh[j]])
                    em.op("dve", lambda e: e.tensor_tensor(out=acc[:], in0=sh[0][:], in1=cw[0][:, ci * 512:(ci + 1) * 512],
                                                           op=ALU.mult), reads=[sh[0], cw[0]], writes=[acc])
                    for j in range(1, 5):
                        eng = "pool" if j % 2 else "dve"
                        em.op(eng, lambda e: e.tensor_tensor(out=sh[j][:], in0=sh[j][:],
                                                             in1=cw[j][:, ci * 512:(ci + 1) * 512], op=ALU.mult),
                              reads=[sh[j], cw[j]], writes=[sh[j]])
                        em.op("dve", lambda e: e.tensor_tensor(out=acc[:], in0=acc[:], in1=sh[j][:], op=ALU.add),
                              reads=[acc, sh[j]], writes=[acc])
                    em.act(qkv[ci], qkv[ci][:], acc[:], AF.Silu, reads=[acc])
                for ci, sc in ((0, 128 ** -0.5), (1, 1.0)):
                    x_ = qkv[ci]
                    em.act(tmp, tmp[:], x_[:], AF.Square, reads=[x_])
                    em.op("dve", lambda e: e.tensor_reduce(out=ss[:], in_=hv(tmp), axis=AX.X, op=ALU.add),
                          reads=[tmp], writes=[ss])
                    em.act(ss, ss[:], ss[:], AF.Sqrt, reads=[ss, P.epsc], bias=P.epsc[:, 3:4])
                    em.op("dve", lambda e: e.reciprocal(out=ss[:], in_=ss[:]), reads=[ss], writes=[ss])
                    if sc != 1.0:
                        em.op("dve", lambda e: e.tensor_scalar_mul(out=ss[:], in0=ss[:], scalar1=sc), reads=[ss],
                              writes=[ss])
                    em.op("dve", lambda e: e.tensor_tensor(out=hv(x_), in0=hv(x_), in1=bc4(ss), op=ALU.mult),
                          reads=[x_, ss], writes=[x_])
                q_, k_, v_ = qkv
                if d == 0:
                    em.dma("sp", zt[:], P.pTok[pp:pp + 128, TOKC["z"]:TOKC["z"] + 512], writes=[zt])
                    em.act(zt, zt[:], zt[:], AF.Silu, reads=[zt])
                    em.dma("sp", P.auxs[2, pp:pp + 128, :], zt[:], reads=[zt], writes=[dummy])
                em.dma("sp", ab[:], P.pTok[pp:pp + 128, TOKC["ab"]:TOKC["ab"] + 16], writes=[ab])
                em.op("dve", lambda e: e.tensor_tensor(out=gx[:], in0=ab[:, 0:8], in1=prow[:, 8:16], op=ALU.add),
                      reads=[ab, prow], writes=[gx])
                em.act(ge, ge[:], gx[:], AF.Abs, reads=[gx])
                em.act(ge, ge[:], ge[:], AF.Exp, reads=[ge], scale=-1.0)
                em.act(gl, gl[:], ge[:], AF.Ln, reads=[ge, P.ones_f], bias=P.ones_f[:, 0:1])
                em.op("dve", lambda e: e.scalar_tensor_tensor(out=gcol[:], in0=gx[:], scalar=0.0, in1=gl[:],
                                                              op0=ALU.max, op1=ALU.add), reads=[gx, gl], writes=[gcol])
                em.op("dve", lambda e: e.tensor_tensor(out=gcol[:], in0=gcol[:], in1=negea[:], op=ALU.mult),
                      reads=[gcol, negea], writes=[gcol])
                em.act(beta, beta[:], ab[:, 8:16], AF.Sigmoid, reads=[ab])
                gd_, bd_ = gcol[:, d * 4:(d + 1) * 4], beta[:, d * 4:(d + 1) * 4]
                pG = core.bank()
                em.mm(pG, pG[:, 0:4], P.tri[:, d, :], gd_, True, True, reads=[P.tri, gcol])
                pT_ = core.bank()
                em.mm(pT_, pT_[:, 0:4], P.ones_f[:], gd_, True, True, reads=[P.ones_f, gcol])
                em.op("dve", lambda e: e.tensor_copy(out=Gs[:], in_=pG[:, 0:4]), reads=[pG], writes=[Gs])
                em.op("dve", lambda e: e.tensor_scalar_mul(out=nG[:], in0=Gs[:], scalar1=-1.0), reads=[Gs], writes=[nG])
                em.act(eG, eG[:], Gs[:], AF.Exp, reads=[Gs])
                em.op("dve", lambda e: e.tensor_copy(out=etot[:], in_=pT_[:, 0:4]), reads=[pT_], writes=[etot])
                em.op("dve", lambda e: e.tensor_tensor(out=eTG[:], in0=etot[:], in1=Gs[:], op=ALU.subtract),
                      reads=[etot, Gs], writes=[eTG])
                em.act(etot, etot[:], etot[:], AF.Exp, reads=[etot])
                em.act(eTG, eTG[:], eTG[:], AF.Exp, reads=[eTG])
                em.op("dve", lambda e: e.tensor_scalar_mul(out=nb[:], in0=bd_, scalar1=-1.0), reads=[beta], writes=[nb])
                em.op("dve", lambda e: e.tensor_tensor(out=nbeG[:], in0=nb[:], in1=eG[:], op=ALU.mult),
                      reads=[nb, eG], writes=[nbeG])
                em.op("dve", lambda e: e.tensor_tensor(out=hv(ka), in0=hv(k_), in1=bc4(nb), op=ALU.mult),
                      reads=[k_, nb], writes=[ka])
                em.op("pool", lambda e: e.tensor_tensor(out=hv(Atil), in0=hv(k_), in1=bc4(nbeG), op=ALU.mult),
                      reads=[k_, nbeG], writes=[Atil])
                em.op("dve", lambda e: e.tensor_tensor(out=hv(Kh), in0=hv(k_), in1=bc4(eTG), op=ALU.mult),
                      reads=[k_, eTG], writes=[Kh])
                em.op("pool", lambda e: e.tensor_tensor(out=hv(Rtil), in0=hv(q_), in1=bc4(eG), op=ALU.mult),
                      reads=[q_, eG], writes=[Rtil])
                em.op("dve", lambda e: e.tensor_tensor(out=hv(Vp), in0=hv(v_),
                                                       in1=beta[:, d * 4:(d + 1) * 4].unsqueeze(2).to_broadcast([128, 4, 128]),
                                                       op=ALU.mult), reads=[v_, beta], writes=[Vp])
                em.op("dve", lambda e: e.tensor_tensor(out=diag[:], in0=P.ident[:, :].unsqueeze(1).to_broadcast([128, 4, 128]),
                                                       in1=bc4(Gs), op=ALU.mult), reads=[P.ident, Gs], writes=[diag])
                pR = core.bank()
                for h in range(4):
                    em.mm(pR, pR[:, h * 128:(h + 1) * 128], P.ones_f[:], diag[:, h, :], True, True,
                          reads=[P.ones_f, diag])
                pRv = pR[:, :].rearrange("p (h t) -> p h t", h=4)
                for dst, sgn, mi, bias_t in ((Ds, -1.0, mA, Gs), (DTs, 1.0, mAT, nG), (DTi, 1.0, mATi, nG)):
                    em.op("dve", lambda e: e.scalar_tensor_tensor(
                        out=dtmp[:], in0=pRv, scalar=sgn, in1=P.masks[:, mi:mi + 1, :].to_broadcast([128, 4, 128]),
                        op0=ALU.mult, op1=ALU.add), reads=[pR, P.masks], writes=[dtmp])
                    for h in range(4):
                        em.act(dst, dst[:, h, :], dtmp[:, h, :], AF.Exp, reads=[dtmp, bias_t],
                               bias=bias_t[:, h:h + 1], writes=[dst])
                ops = {"ga": ka, "gb": k_, "gk": k_, "gr": q_, "V": Vp, "Atil": Atil, "Bh": Kh, "Kh": Kh,
                       "Rtil": Rtil}
                core.run_chunk(rev, ops, etot, mk, P.yscr[2 + d, pp:pp + 128, :])
        em.barrier()


def phase_mix_out(P, l, with_ctx):
    em = P.em
    dummy = T()
    with ExitStack() as st:
        f = lambda nm, shp=(128, 512): em.sb("mo_" + nm, list(shp), F32, st)
        gn_g = _bc_row(P, st, "mo_gng", P.rw_row[l, 7:8, :])
        gn_b = _bc_row(P, st, "mo_gnb", P.rw_row[l, 8:9, :])
        nrm = _bc_row(P, st, "mo_nrm", P.gd_row[l, 0:1, :])
        ya, yb, bv, gg, cen, sq = f("ya"), f("yb"), f("bv"), f("gg"), f("cen"), f("sq")
        s8 = f("s8", (128, 8))
        ob = [em.sb("mo_ob%d" % i, [128, 4, 128], BF16, st) for i in range(2)]
        pb = [em.ps("mo_pb%d" % i, [128, 512], F32, st) for i in range(2)]
        cnt = 0
        tiles = ([128 * i for i in range(CTX // 128)] if with_ctx else []) + \
            [CTX + 128 * i for i in range(SEQ // 128)]
        for t0 in tiles:
            pp = ppos(t0)
            for mix, (ia, ib, nh, eps_col) in enumerate(((0, 1, 8, 2), (2, 3, 4, 1))):
                n = 512 // nh
                hv = lambda t: t[:].rearrange("p (h n) -> p h n", n=n)
                bc = lambda t: t[:, 0:nh].unsqueeze(2).to_broadcast([128, nh, n])
                em.dma("sp", ya[:], P.yscr[ia, pp:pp + 128, :], writes=[ya])
                em.dma("sp", yb[:], P.yscr[ib, pp:pp + 128, :], writes=[yb])
                em.op("dve", lambda e: e.tensor_tensor(out=ya[:], in0=ya[:], in1=yb[:], op=ALU.add),
                      reads=[ya, yb], writes=[ya])
                if mix == 0:
                    em.dma("sp", bv[:], P.auxs[0, pp:pp + 128, :], writes=[bv])
                    em.dma("sp", gg[:], P.auxs[1, pp:pp + 128, :], writes=[gg])
                    em.op("dve", lambda e: e.tensor_reduce(out=s8[:, 0:nh], in_=hv(ya), axis=AX.X, op=ALU.add),
                          reads=[ya], writes=[s8])
                    em.op("dve", lambda e: e.tensor_scalar_mul(out=s8[:, 0:nh], in0=s8[:, 0:nh], scalar1=1.0 / n),
                          reads=[s8], writes=[s8])
                    em.op("dve", lambda e: e.tensor_tensor(out=hv(cen), in0=hv(ya), in1=bc(s8), op=ALU.subtract),
                          reads=[ya, s8], writes=[cen])
                else:
                    em.dma("sp", gg[:], P.auxs[2, pp:pp + 128, :], writes=[gg])
                    em.op("dve", lambda e: e.tensor_copy(out=cen[:], in_=ya[:]), reads=[ya], writes=[cen])
                em.act(sq, sq[:], cen[:], AF.Square, reads=[cen])
                em.op("dve", lambda e: e.tensor_reduce(out=s8[:, 0:nh], in_=hv(sq), axis=AX.X, op=ALU.add),
                      reads=[sq], writes=[s8])
                em.act(s8, s8[:, 0:nh], s8[:, 0:nh], AF.Sqrt, reads=[s8, P.epsc], bias=P.epsc[:, eps_col:eps_col + 1],
                       scale=1.0 / n)
                em.op("dve", lambda e: e.reciprocal(out=s8[:, 0:nh], in_=s8[:, 0:nh]), reads=[s8], writes=[s8])
                em.op("dve", lambda e: e.tensor_tensor(out=hv(cen), in0=hv(cen), in1=bc(s8), op=ALU.mult),
                      reads=[cen, s8], writes=[cen])
                if mix == 0:
                    em.op("pool", lambda e: e.tensor_tensor(out=cen[:], in0=cen[:], in1=gn_g[:], op=ALU.mult),
                          reads=[cen, gn_g], writes=[cen])
                    em.op("pool", lambda e: e.tensor_tensor(out=cen[:], in0=cen[:], in1=gn_b[:], op=ALU.add),
                          reads=[cen, gn_b], writes=[cen])
                    em.op("pool", lambda e: e.tensor_tensor(out=cen[:], in0=cen[:], in1=bv[:], op=ALU.add),
                          reads=[cen, bv], writes=[cen])
                else:
                    em.op("pool", lambda e: e.tensor_tensor(out=cen[:], in0=cen[:], in1=nrm[:], op=ALU.mult),
                          reads=[cen, nrm], writes=[cen])
                em.op("dve", lambda e: e.tensor_tensor(out=cen[:], in0=cen[:], in1=gg[:], op=ALU.mult),
                      reads=[cen, gg], writes=[cen])
                p_ = pb[cnt % 2]
                o_ = ob[cnt % 2]
                cnt += 1
                for j in range(4):
                    em.op("pe", lambda e: e.transpose(p_[:, j * 128:(j + 1) * 128], cen[:, j * 128:(j + 1) * 128],
                                                      P.ident[:]), reads=[cen, P.ident], writes=[p_])
                em.act(o_, o_[:], p_[:, :].rearrange("p (j t) -> p j t", j=4), AF.Copy, reads=[p_])
                r0 = 1024 + mix * 512
                em.dma("sp", P.mixT[r0:r0 + 512, t0:t0 + 128].rearrange("(j p) t -> p j t", p=128), o_[:],
                       reads=[o_], writes=[dummy])
        em.barrier()


def declare_mla(P):
    L = P.nl
    P.w_uq = P.din("mla_w_uq", [L, 512, 1536])
    P.w_uq_sw = P.din("mla_w_uq_sw", [L, 512, 512])
    P.w_ukv = P.din("mla_w_ukv", [L, 512, 2048])
    P.mlaT = P.din("mlaT", [128, L, 2, 4])
    P.ropeT = P.din("ropeT", [64, 2, SEQ])
    P.wb_uq = P.dscr("wb_uq", [L, 512, 2048], BF16)
    P.wb_ukv = P.dscr("wb_ukv", [L, 512, 2048], BF16)
    P.Kn = P.dscr("Kn", [1024, TT], BF16)
    P.Kr = P.dscr("Kr", [128, TT], BF16)
    P.sel64_in = P.din("sel64", [128, 1])
    P.Vt = P.dscr("Vt", [TT, 1024], BF16)
    P.Qn = P.dscr("Qn", [1024, TT], BF16)
    P.Qr = P.dscr("Qr", [8, 128, TT], BF16)


def phase_cast_mla(P):
    em = P.em
    P.wb_mla_t = [T() for _ in range(P.nl)]
    for l in range(P.nl):
        d = P.wb_mla_t[l]
        em.dma("pool", P.wb_uq[l, :, 0:1536], P.w_uq[l, :, :], writes=[d], partial=True)
        em.dma("pool", P.wb_uq[l, :, 1536:2048], P.w_uq_sw[l, :, :], writes=[d], partial=True)
        em.dma("pool", P.wb_ukv[l, :, :], P.w_ukv[l, :, :], writes=[d], partial=True)


def phase_mla(P, l, with_ctx):
    em = P.em
    dummy = T()
    with ExitStack() as st:
        f = lambda nm, shp, dt=F32: em.sb("ml_" + nm, list(shp), dt, st)
        gains = f("gains", (128, 2, 4))
        em.dma("sp", gains[:], P.mlaT[:, l, :, :], writes=[gains])
        wkv = f("wkv", (128, 4, 2048), BF16)
        wq = f("wq", (128, 4, 2048), BF16)
        wmt = getattr(P, "wb_mla_t", None)
        wrd = [wmt[l]] if wmt else []
        em.dma("sp", wkv[:], P.wb_ukv[l].rearrange("(k p) c -> p k c", p=128), reads=wrd, writes=[wkv])
        em.dma("sp", wq[:], P.wb_uq[l].rearrange("(k p) c -> p k c", p=128), reads=wrd, writes=[wq])
        wvv = f("wvv", (128, 4, 1024), BF16)
        for k in range(4):
            em.dma("sp", wvv[:, k, :].rearrange("p (h v) -> p h v", v=128),
                   P.wb_ukv[l, k * 128:(k + 1) * 128, :].rearrange("p (h two v) -> p h two v", two=2, v=128)[:, :, 1, :],
                   reads=wrd, writes=[wvv], partial=True)
        cx = f("cx", (128, 4, 512))
        cn = f("cn", (128, 4, 512), BF16)
        sq = [f("sq%d" % i, (128, 512)) for i in range(2)]
        rstd = f("rstd", (128, 512))
        krt, ksw, kro = f("krt", (64, 512)), f("ksw", (64, 512)), f("kro", (64, 512))
        rope = f("rope", (64, 2, 512))
        krb = f("krb", (128, 512), BF16)
        sel = f("sel", (128, 1))
        em.dma("sp", sel[:], P.sel64_in[:, :], writes=[sel])
        zt = f("zt", (128, 512))
        sqr = f("sqr", (128, 512), BF16)
        em.op("dve", lambda e: e.memset(sqr[:], 0.0), writes=[sqr])
        sqb = f("sqb", (128, 512), BF16)
        em.op("dve", lambda e: e.memset(zt[:], 0.0), writes=[zt])
        kmax = f("kmax", (128, 8))
        bmax = f("bmax", (128, 1))
        ob = [f("ob%d" % i, (128, 512), BF16) for i in range(3)]
        qrb = [f("qrb%d" % i, (128, 512), BF16) for i in range(2)]
        qr32, qs32 = f("qr32", (64, 512)), f("qs32", (64, 512))
        ps = [em.ps("ml_ps%d" % i, [128, 512], F32, st) for i in range(6)]
        pi = [0]
        oi = [0]

        def bank():
            pi[0] += 1
            return ps[pi[0] % 6]

        def obuf():
            oi[0] += 1
            return ob[oi[0] % 3]

        em.op("dve", lambda e: e.memset(kmax[:], 0.0), writes=[kmax])
        em.op("dve", lambda e: e.tensor_scalar(out=krb[64:128, :], in0=zt[64:128, :], scalar1=sel[64:128, 0:1],
                                               scalar2=None, op0=ALU.add), reads=[zt, sel], writes=[krb], partial=True)

        def rmsnorm_block(row0, gi, pp, n):
            em.dma("sp", cx[:, :, 0:n], P.pT[row0:row0 + 512, pp:pp + n].rearrange("(k p) t -> p k t", p=128),
                   writes=[cx])
            pss = bank()
            for k in range(4):
                s_ = sq[k % 2]
                em.act(s_, s_[:, 0:n], cx[:, k, 0:n], AF.Square, reads=[cx])
                em.mm(pss, pss[:, 0:n], P.ones_f[:], s_[:, 0:n], k == 0, k == 3, reads=[P.ones_f, s_])
            em.act(rstd, rstd[:, 0:n], pss[:, 0:n], AF.Sqrt, reads=[pss, P.epsc], bias=P.epsc[:, 1:2],
                   scale=1.0 / 512)
            em.op("dve", lambda e: e.reciprocal(out=rstd[:, 0:n], in_=rstd[:, 0:n]), reads=[rstd], writes=[rstd])
            for k in range(4):
                em.op("dve", lambda e: e.scalar_tensor_tensor(
                    out=cn[:, k, 0:n], in0=cx[:, k, 0:n], scalar=gains[:, gi, k:k + 1], in1=rstd[:, 0:n],
                    op0=ALU.mult, op1=ALU.mult), reads=[cx, gains, rstd], writes=[cn], partial=True)

        def load_rope(t0, n):
            em.dma("sp", rope[:, :, 0:n], P.ropeT[:, :, t0 - CTX:t0 - CTX + n], writes=[rope])

        def apply_rope(dst, x_, xsw, n, isctx):
            if isctx:
                em.op("dve", lambda e: e.tensor_copy(out=dst[0:64, 0:n], in_=x_[0:64, 0:n]), reads=[x_], writes=[dst])
                return
            em.op("dve", lambda e: e.tensor_tensor(out=dst[0:64, 0:n], in0=x_[0:64, 0:n], in1=rope[:, 0, 0:n],
                                                   op=ALU.mult), reads=[x_, rope], writes=[dst])
            em.op("pool", lambda e: e.tensor_tensor(out=xsw[0:64, 0:n], in0=xsw[0:64, 0:n], in1=rope[:, 1, 0:n],
                                                    op=ALU.mult), reads=[xsw, rope], writes=[xsw])
            em.op("dve", lambda e: e.tensor_tensor(out=dst[0:64, 0:n], in0=dst[0:64, 0:n], in1=xsw[0:64, 0:n],
                                                   op=ALU.add), reads=[dst, xsw], writes=[dst])

        for (t0, n, isctx) in TBLK:
            pp = ppos(t0)
            rmsnorm_block(512, 1, pp, n)
            if not isctx:
                load_rope(t0, n)
            em.dma("sp", krt[:, 0:n], P.pT[8 * 128:8 * 128 + 64, pp:pp + n], writes=[krt])
            em.dma("sp", ksw[:, 0:n], P.pT[9 * 128:9 * 128 + 64, pp:pp + n], writes=[ksw])
            apply_rope(kro, krt, ksw, n, isctx)
            em.act(krb, krb[0:64, 0:n], kro[0:64, 0:n], AF.Copy, reads=[kro], writes=[krb])
            em.dma("pool", P.Kr[:, t0:t0 + n], krb[:, 0:n], reads=[krb], writes=[dummy])
            em.act(sqr, sqr[0:64, 0:n], kro[0:64, 0:n], AF.Square, reads=[kro])
            for h in range(8):
                pk = bank()
                for k in range(4):
                    em.mm(pk, pk[:, 0:n], wkv[:, k, h * 256:h * 256 + 128], cn[:, k, 0:n], k == 0, k == 3,
                          reads=[wkv, cn])
                o_ = obuf()
                em.op("dve", lambda e: e.tensor_copy(out=o_[:, 0:n], in_=pk[:, 0:n]), reads=[pk], writes=[o_])
                em.dma("pool", P.Kn[h * 128:(h + 1) * 128, t0:t0 + n], o_[:, 0:n], reads=[o_], writes=[dummy])
                em.act(sqb, sqb[:, 0:n], o_[:, 0:n], AF.Square, reads=[o_])
                pn = bank()
                em.mm(pn, pn[:, 0:n], P.ones_b[:], sqb[:, 0:n], True, False, reads=[P.ones_b, sqb])
                em.mm(pn, pn[:, 0:n], P.ones_b[:], sqr[:, 0:n], False, True, reads=[P.ones_b, sqr])
                em.op("dve", lambda e: e.tensor_reduce(out=bmax[:], in_=pn[:, 0:n], axis=AX.X, op=ALU.max),
                      reads=[pn], writes=[bmax])
                em.op("dve", lambda e: e.tensor_tensor(out=kmax[:, h:h + 1], in0=kmax[:, h:h + 1], in1=bmax[:],
                                                       op=ALU.max), reads=[kmax, bmax], writes=[kmax])
            for tt in range(n // 128):
                for g in range(2):
                    pv = bank()
                    for k in range(4):
                        em.mm(pv, pv[:, :], cn[:, k, tt * 128:(tt + 1) * 128], wvv[:, k, g * 512:(g + 1) * 512],
                              k == 0, k == 3, reads=[cn, wvv])
                    o_ = obuf()
                    em.act(o_, o_[:], pv[:, :], AF.Copy, reads=[pv])
                    em.dma("pool", P.Vt[t0 + tt * 128:t0 + (tt + 1) * 128, g * 512:(g + 1) * 512], o_[:],
                           reads=[o_], writes=[dummy])
        if getattr(P, "mla_stage", 9) < 1:
            em.barrier()
            return
        nkm = f("nkm", (128, 8))
        em.act(nkm, nkm[:], kmax[:], AF.Sqrt, reads=[kmax])
        em.op("dve", lambda e: e.tensor_scalar_mul(out=nkm[:], in0=nkm[:], scalar1=-1.0), reads=[nkm], writes=[nkm])
        qcnt = 0
        for (t0, n, isctx) in TBLK:
            if isctx and not with_ctx:
                continue
            pp = ppos(t0)
            rmsnorm_block(0, 0, pp, n)
            if not isctx:
                load_rope(t0, n)
            for h in range(8):
                pq, pr, pw = bank(), bank(), bank()
                for k in range(4):
                    em.mm(pq, pq[:, 0:n], wq[:, k, h * 192:h * 192 + 128], cn[:, k, 0:n], k == 0, k == 3,
                          reads=[wq, cn])
                for k in range(4):
                    em.mm(pr, pr[0:64, 0:n], wq[:, k, h * 192 + 128:h * 192 + 192], cn[:, k, 0:n], k == 0, k == 3,
                          reads=[wq, cn])
                if not isctx:
                    for k in range(4):
                        em.mm(pw, pw[0:64, 0:n], wq[:, k, 1536 + h * 64:1536 + (h + 1) * 64], cn[:, k, 0:n],
                              k == 0, k == 3, reads=[wq, cn])
                    em.op("dve", lambda e: e.tensor_copy(out=qs32[0:64, 0:n], in_=pw[0:64, 0:n]), reads=[pw],
                          writes=[qs32])
                em.act(qr32, qr32[0:64, 0:n], pr[0:64, 0:n], AF.Copy, reads=[pr])
                apply_rope(kro, qr32, qs32, n, isctx)
                em.act(sqb, sqb[:, 0:n], pq[:, 0:n], AF.Square, reads=[pq])
                em.act(sqr, sqr[0:64, 0:n], kro[0:64, 0:n], AF.Square, reads=[kro])
                pn = bank()
                em.mm(pn, pn[:, 0:n], P.ones_b[:], sqb[:, 0:n], True, False, reads=[P.ones_b, sqb])
                em.mm(pn, pn[:, 0:n], P.ones_b[:], sqr[:, 0:n], False, True, reads=[P.ones_b, sqr])
                qb = qrb[qcnt % 2]
                qcnt += 1
                em.act(sq[1], sq[1][64:128, 0:n], pn[64:128, 0:n], AF.Sqrt, reads=[pn],
                       scale=ATTN_SCALE * ATTN_SCALE)
                em.op("dve", lambda e: e.tensor_scalar(out=qb[64:128, 0:n], in0=sq[1][64:128, 0:n],
                                                       scalar1=nkm[64:128, h:h + 1], scalar2=sel[64:128, 0:1],
                                                       op0=ALU.mult, op1=ALU.mult),
                      reads=[sq[1], nkm, sel], writes=[qb], partial=True)
                em.act(qb, qb[0:64, 0:n], kro[0:64, 0:n], AF.Copy, reads=[kro], scale=ATTN_SCALE, writes=[qb])
                em.dma("pool", P.Qr[h, :, t0:t0 + n], qb[:, 0:n], reads=[qb], writes=[dummy])
                o_ = obuf()
                em.act(o_, o_[:, 0:n], pq[:, 0:n], AF.Copy, reads=[pq], scale=ATTN_SCALE)
                em.dma("pool", P.Qn[h * 128:(h + 1) * 128, t0:t0 + n], o_[:, 0:n], reads=[o_], writes=[dummy])
        em.barrier()
    if getattr(P, "mla_stage", 9) < 2:
        return
    with ExitStack() as st:
        f = lambda nm, shp, dt=BF16: em.sb("at_" + nm, list(shp), dt, st)
        NKT = TT // 128
        kr = f("kr", (128, TT))
        em.dma("sp", kr[:], P.Kr[:, :], writes=[kr])
        kn = [f("kn%d" % i, (128, TT)) for i in range(2)]
        vv = [f("vv%d" % i, (128, NKT, 128)) for i in range(2)]
        qn = [f("qn%d" % i, (128, 512)) for i in range(2)]
        qr = [f("qr%d" % i, (128, 512)) for i in range(2)]
        pt = [f("pt%d" % i, (128, 512)) for i in range(3)]
        rd = [f("rd%d" % i, (128, 512), F32) for i in range(2)]
        ao = [f("ao%d" % i, (128, 512)) for i in range(2)]
        pS = [em.ps("at_pS%d" % i, [128, 512], F32, st) for i in range(3)]
        pO = [em.ps("at_pO%d" % i, [128, 512], F32, st) for i in range(2)]
        pD = [em.ps("at_pD%d" % i, [128, 512], F32, st) for i in range(2)]
        sc = 0
        qc = 0
        for h in range(8):
            k_n, v_ = kn[h % 2], vv[h % 2]
            em.dma("sp", k_n[:], P.Kn[h * 128:(h + 1) * 128, :], writes=[k_n])
            em.dma("sp", v_[:], P.Vt[:, h * 128:(h + 1) * 128].rearrange("(c p) v -> p c v", p=128), writes=[v_])
            for (t0, n, isctx) in TBLK:
                if isctx and not with_ctx:
                    continue
                q_n, q_r = qn[qc % 2], qr[qc % 2]
                p_O, p_D, r_d, a_o = pO[qc % 2], pD[qc % 2], rd[qc % 2], ao[qc % 2]
                qc += 1
                em.dma("sp", q_n[:, 0:n], P.Qn[h * 128:(h + 1) * 128, t0:t0 + n], writes=[q_n])
                em.dma("sp", q_r[:, 0:n], P.Qr[h, :, t0:t0 + n], writes=[q_r])
                nkt = CTX // 128 if isctx else NKT
                LOOK = 2
                ring = {}

                def scores(kt):
                    nonlocal sc
                    p_S, p_t = pS[sc % 3], pt[sc % 3]
                    sc += 1
                    ks = slice(kt * 128, (kt + 1) * 128)
                    em.mm(p_S, p_S[:, 0:n], k_n[:, ks], q_n[:, 0:n], True, False, reads=[k_n, q_n])
                    em.mm(p_S, p_S[:, 0:n], kr[:, ks], q_r[:, 0:n], False, True, reads=[kr, q_r])
                    em.act(p_t, p_t[:, 0:n], p_S[:, 0:n], AF.Exp, reads=[p_S])
                    ring[kt] = p_t

                for kt in range(min(LOOK, nkt)):
                    scores(kt)
                for kt in range(nkt):
                    p_t = ring.pop(kt)
                    em.mm(p_O, p_O[:, 0:n], v_[:, kt, :], p_t[:, 0:n], kt == 0, kt == nkt - 1, reads=[v_, p_t])
                    em.mm(p_D, p_D[:, 0:n], P.ones_b[:], p_t[:, 0:n], kt == 0, kt == nkt - 1,
                          reads=[P.ones_b, p_t])
                    if kt + LOOK < nkt:
                        scores(kt + LOOK)
                em.op("dve", lambda e: e.reciprocal(out=r_d[:, 0:n], in_=p_D[:, 0:n]), reads=[p_D], writes=[r_d])
                em.op("dve", lambda e: e.tensor_tensor(out=a_o[:, 0:n], in0=p_O[:, 0:n], in1=r_d[:, 0:n],
                                                       op=ALU.mult), reads=[p_O, r_d], writes=[a_o])
                em.dma("sp", P.mixT[h * 128:(h + 1) * 128, t0:t0 + n], a_o[:, 0:n], reads=[a_o], writes=[dummy])
        em.barrier()


def host_mla_inputs(inp, L, seq):
    f32 = np.float32
    perm = np.arange(64).reshape(2, 2, 16)[:, ::-1, :].reshape(64)
    out = {}
    out["mla_w_uq"] = inp["mla_w_uq"][:L]
    cols = np.concatenate([h * 192 + 128 + perm for h in range(8)])
    out["mla_w_uq_sw"] = np.ascontiguousarray(inp["mla_w_uq"][:L][:, :, cols])
    out["mla_w_ukv"] = inp["mla_w_ukv"][:L]
    g = np.stack([inp["mla_q_norm"][:L].reshape(L, 4, 128), inp["mla_kv_norm"][:L].reshape(L, 4, 128)], 1)
    out["mlaT"] = np.ascontiguousarray(g.transpose(3, 0, 1, 2)).astype(f32)
    t = np.arange(seq)
    pos = np.stack([t // 64, t % 64], -1).astype(f32)
    inv = (10000.0 ** (-np.arange(16, dtype=f32) / 16)).astype(f32)
    ang = pos[..., None] * inv
    cos, sin = np.cos(ang), np.sin(ang)
    ct = np.zeros((64, seq), f32)
    stb = np.zeros((64, seq), f32)
    for a in range(2):
        for half in range(2):
            r0 = a * 32 + half * 16
            ct[r0:r0 + 16] = cos[:, a, :].T
            stb[r0:r0 + 16] = (sin[:, a, :].T) * (-1.0 if half == 0 else 1.0)
    out["ropeT"] = np.ascontiguousarray(np.stack([ct, stb], 1))
    return out, perm


def build_program(nl=DEPTH, dbg=()):
    P = Prog(nl=nl, dbg=dbg)
    em = P.em
    declare_io(P)
    declare_dense(P)
    declare_scan(P)
    declare_mla(P)
    setup_consts(P)
    setup_scan_consts(P)
    phase_zero_pads(P)
    phase_cast_in(P)
    phase_cast_dense(P)
    phase_cast_mla(P)
    phase_mod(P)
    nblk = len(TBLK)
    for l in range(nl):
        with_ctx = l < nl - 1
        xsrc = P.xT0 if l == 0 else P.xB
        ft = lambda: [T() for _ in range(nblk)]
        sel_ = getattr(build_program, "phases", "imrgodf")
        if "i" in sel_:
            phase_inproj(P, l, xsrc, ft())
        if "m" in sel_:
            phase_mla(P, l, with_ctx)
        if "r" in sel_:
            phase_rwkv(P, l)
        if "g" in sel_:
            phase_gdn(P, l)
        if "o" in sel_:
            phase_mix_out(P, l, with_ctx)
        P.mixT_t = ft()
        if "d" in sel_:
            phase_outproj(P, l, xsrc, ft(), P.xA, ft(), with_ctx)
        if "f" in sel_:
            phase_ffn(P, l, P.xA, ft(), P.xB, ft(), with_ctx, final=(l == nl - 1))
    em.barrier()
    return P


def host_inputs(inp, b, nl, seq):
    f32 = np.float32
    L = nl
    x = np.asarray(inp["x"][b][:seq], f32)
    ctx = np.asarray(inp["ctx"][b], f32)
    im = {}
    im["xT0"] = np.ascontiguousarray(np.concatenate([ctx, x], 0).T)
    im["cvec"] = np.ascontiguousarray(np.stack([np.asarray(inp["c"][b]).reshape(16, 128).T,
                                                np.asarray(inp["c_ctx"]).reshape(16, 128).T], -1).astype(f32))
    im["w_mod"] = np.asarray(inp["w_mod"][:L], f32)
    im["b_modT"] = np.ascontiguousarray(np.asarray(inp["b_mod"][:L], f32).reshape(L, 96, 128).transpose(2, 0, 1))
    im["w_in"] = np.asarray(inp["w_in"][:L], f32)
    mi, perm = host_mla_inputs(inp, L, seq)
    im["w_in_krsw"] = np.ascontiguousarray(im["w_in"][:, :, 1024 + perm])
    im.update(mi)
    e = np.zeros((128, 1), f32)
    e[64] = 1.0
    im["sel64"] = e
    im["w_out"] = np.asarray(inp["w_out"][:L], f32)
    im["ffn_w_gate"] = np.asarray(inp["ffn_w_gate"][:L], f32)
    im["ffn_w_up"] = np.asarray(inp["ffn_w_up"][:L], f32)
    im["ffn_w_down"] = np.asarray(inp["ffn_w_down"][:L], f32)
    lnT = np.stack([np.asarray(inp[k][:L], f32).reshape(L, 16, 128) for k in ("ln1_g", "ln1_b", "ln2_g", "ln2_b")], 1)
    im["lnT"] = np.ascontiguousarray(lnT.transpose(3, 0, 1, 2))
    im.update(host_consts())
    im.update(host_scan_inputs(inp, L))
    return im


def kernel(**inputs):
    nb = inputs["x"].shape[0]
    P = build_program(DEPTH)
    shared = None
    in_maps = []
    for b in range(nb):
        im = host_inputs(inputs, b, DEPTH, SEQ)
        if shared is None:
            shared = im
        else:
            for k in im:
                if k not in ("xT0", "cvec"):
                    im[k] = shared[k]
        in_maps.append({k: v for k, v in im.items() if k in P.inputs})
    res = run_bass_kernel_spmd(P.nc, in_maps, core_ids=list(range(nb)))
    out = np.stack([np.ascontiguousarray(np.asarray(r["xOut"], np.float32).T) for r in res.results], 0)
    return out
```

```python
import math
from contextlib import ExitStack

import numpy as np
import concourse.bass as bass
import concourse.mybir as mybir
from concourse.bass_utils import run_bass_kernel_spmd

F32 = mybir.dt.float32
BF16 = mybir.dt.bfloat16
AF = mybir.ActivationFunctionType
ALU = mybir.AluOpType
AX = mybir.AxisListType

D = 2048
KD = D // 128
SEQ = 4096
CTX = 256
TT = SEQ + CTX
DEPTH = 4
D_FF = 5632
KF = D_FF // 128
IN_COLS = 4912
ALPHA = (2.0 * DEPTH) ** 0.25
ATTN_SCALE = 192 ** -0.5

CH = []
for i in range(4):
    CH.append(("cq%d" % i, 128 * i, 128))
for i in range(4):
    CH.append(("ckv%d" % i, 512 + 128 * i, 128))
CH += [("kr", 1024, 64), ("krsw", -1, 64), ("wd", 2624, 64), ("ad", 2688, 64)]
CH += [("gd", 2752, 96), ("ab", 4896, 16), ("pad0", -2, 0), ("pad1", -2, 0)]
for nm, c0 in (("r", 1088), ("k", 1600), ("v", 2112), ("gq", 2848), ("gk", 3360), ("gv", 3872), ("z", 4384)):
    for i in range(4):
        CH.append(("%s%d" % (nm, i), c0 + 128 * i, 128))
NCH = len(CH)
CHI = {c[0]: i for i, c in enumerate(CH)}
NCHP = NCH
NFM = 13
TOKC = {"r": 0, "k": 512, "v": 1024, "gq": 1536, "gk": 2048, "gv": 2560, "z": 3072, "ab": 3584}
NTOKC = 3600

TBLK = [(0, CTX, True)] + [(CTX + 512 * j, 512, False) for j in range(SEQ // 512)]
PC0 = 2
PL0 = 2 + CTX + 4
TTP = PL0 + SEQ + 2


def ppos(t):
    return PC0 + t if t < CTX else PL0 + (t - CTX)


def configure(seq):
    global SEQ, TT, TBLK, TTP
    SEQ = seq
    TT = SEQ + CTX
    TBLK = [(0, CTX, True)] + [(CTX + 512 * j, 512, False) for j in range(SEQ // 512)]
    TTP = PL0 + SEQ + 2


class T:
    __slots__ = ("h", "lw", "rd", "pg")

    def __init__(self, h=None):
        self.h = h
        self.lw = {}
        self.rd = {}
        self.pg = {}

    def __getitem__(self, idx):
        return self.h[idx]


class Em:
    NDMA = 6

    def __init__(self, nc, st):
        self.nc = nc
        self.st = st
        self.eng = {"pe": nc.tensor, "act": nc.scalar, "dve": nc.vector, "pool": nc.gpsimd, "sp": nc.sync}
        self.sem = {}
        self.cnt = {}
        for k in ("pe", "act", "dve", "pool", "sp"):
            self.sem[k] = st.enter_context(nc.semaphore("s_" + k))
            self.cnt[k] = 0
        self.dcnt = {"sp": 0, "pool": 0, "act": 0}
        for q in self.dcnt:
            for i in range(self.NDMA):
                self.sem[(q, i)] = st.enter_context(nc.semaphore("d_%s%d" % (q, i)))
        self.seen = {k: {} for k in self.eng}
        self.dmax = {}
        self.ninst = 0
        self.uid = 0

    def sb(self, name, shape, dt, st=None):
        self.uid += 1
        return T((st or self.st).enter_context(self.nc.sbuf_tensor("%s_u%d" % (name, self.uid), list(shape), dt)))

    def ps(self, name, shape, dt=F32, st=None):
        self.uid += 1
        return T((st or self.st).enter_context(self.nc.psum_tensor("%s_u%d" % (name, self.uid), list(shape), dt)))

    def _wait(self, eng, deps):
        e = self.eng[eng]
        seen = self.seen[eng]
        for sk, v in deps.items():
            if sk == "pe" and eng == "pe":
                continue
            if seen.get(sk, 0) < v:
                e.wait_ge(self.sem[sk], v)
                seen[sk] = v
                self.ninst += 1

    @staticmethod
    def _merge(d, s):
        for k, v in s.items():
            if d.get(k, 0) < v:
                d[k] = v

    def _deps(self, reads, writes, partial):
        deps = {}
        for b in reads:
            self._merge(deps, b.lw)
        for b in writes:
            self._merge(deps, b.rd)
            if partial:
                self._merge(deps, b.pg)
            else:
                self._merge(deps, b.lw)
        return deps

    def _record(self, ev, reads, writes, partial):
        sk, v = ev
        for b in reads:
            if b.rd.get(sk, 0) < v:
                b.rd[sk] = v
        for b in writes:
            if partial and not b.rd:
                if b.lw.get(sk, 0) < v:
                    b.lw[sk] = v
            else:
                pg = dict(b.rd)
                self._merge(pg, b.lw)
                b.pg = pg
                b.lw = {sk: v}
                b.rd = {}

    def op(self, eng, fn, reads=(), writes=(), partial=False):
        self._wait(eng, self._deps(reads, writes, partial))
        ins = fn(self.eng[eng])
        self.cnt[eng] += 1
        ins.then_inc(self.sem[eng], 1)
        self.ninst += 1
        self._record((eng, self.cnt[eng]), reads, writes, partial)

    def dma(self, q, out, in_, reads=(), writes=(), partial=False):
        n = self.dcnt[q]
        slot = n % self.NDMA
        sk = (q, slot)
        tgt = 16 * (n // self.NDMA + 1)
        deps = self._deps(reads, writes, partial)
        if tgt > 16:
            deps[sk] = max(deps.get(sk, 0), tgt - 16)
        self._wait(q, deps)
        self.eng[q].dma_start(out=out, in_=in_).then_inc(self.sem[sk], 16)
        self.dcnt[q] = n + 1
        self.dmax[sk] = tgt
        self.ninst += 1
        self._record((sk, tgt), reads, writes, partial)

    def barrier(self):
        allev = {k: v for k, v in self.cnt.items() if v > 0}
        allev.update(self.dmax)
        for eng in self.eng:
            d = {k: v for k, v in allev.items() if k != eng}
            self._wait(eng, d)
        for eng in ("act", "dve", "pool"):
            if self.cnt[eng] > 0:
                self._wait(eng, {eng: self.cnt[eng]})

    def mm(self, out_t, out_ap, lhsT, rhs, start, stop, reads):
        self.op("pe", lambda e: e.matmul(out_ap, lhsT, rhs, start=start, stop=stop),
                reads=reads, writes=[out_t])

    def act(self, eng_out_t, out_ap, in_ap, func, reads, bias=None, scale=None, accum=None, writes=None):
        kw = {}
        if bias is not None:
            kw["bias"] = bias
        if scale is not None:
            kw["scale"] = scale
        if accum is not None:
            kw["accum_out"] = accum
        self.op("act", lambda e: e.activation(out_ap, in_ap, func, **kw), reads=reads,
                writes=writes if writes is not None else [eng_out_t])


def _chunk_rows(ap2d):
    return ap2d.rearrange("(k p) t -> p k t", p=128)


class Prog:
    def __init__(self, nl=DEPTH, dbg=()):
        self.nl = nl
        self.dbg = set(dbg)
        nc = self.nc = bass.Bass("TRN2", target_bir_lowering=False)
        self.st = ExitStack()
        self.em = Em(nc, self.st)
        self.inputs = {}
        self.outs = {}

    def din(self, name, shape, dt=F32):
        self.inputs[name] = (shape, dt)
        return self.nc.dram_tensor(name, list(shape), dt, kind="ExternalInput").ap()

    def dscr(self, name, shape, dt=F32, out=False):
        kind = "ExternalOutput" if (out or name in self.dbg) else "Internal"
        if kind == "ExternalOutput":
            self.outs[name] = (shape, dt)
        return self.nc.dram_tensor(name, list(shape), dt, kind=kind).ap()


def declare_io(P):
    L = P.nl
    P.xT0 = P.din("xT0", [D, TT])
    P.cvec = P.din("cvec", [128, KD, 2])
    P.w_mod = P.din("w_mod", [L, D, 6 * D])
    P.b_modT = P.din("b_modT", [128, L, 96])
    P.w_in = P.din("w_in", [L, D, IN_COLS])
    P.w_in_krsw = P.din("w_in_krsw", [L, D, 64])
    P.wb_in = P.dscr("wb_in", [L, D, NCHP * 128], BF16)
    P.pT = P.dscr("pT", [NFM * 128, TTP])
    P.pTok = P.dscr("pTok", [TTP, NTOKC])


def phase_cast_in(P):
    em = P.em
    P.wb_in_t = [T() for _ in range(P.nl)]
    for l in range(P.nl):
        for i, (nm, c0, w) in enumerate(CH):
            if w == 0:
                continue
            src = P.w_in_krsw[l, :, :] if c0 < 0 else P.w_in[l, :, c0:c0 + w]
            for r0 in range(0, D, 512):
                em.dma("pool", P.wb_in[l, r0:r0 + 512, i * 128:i * 128 + w],
                       src[r0:r0 + 512, :], writes=[P.wb_in_t[l]], partial=True)


def phase_mod(P):
    em = P.em
    L = P.nl
    P.mod = em.sb("mod", [128, L, 96, 2], F32)
    P.mod1 = em.sb("mod1", [128, L, 96, 2], F32)
    P.modg = em.sb("modg", [128, L, 96, 2], F32)
    with ExitStack() as st:
        cv = em.sb("cv", [128, KD, 2], F32, st)
        sc = em.sb("sc", [128, KD, 2], F32, st)
        bm = em.sb("bm", [128, L, 96], F32, st)
        em.dma("sp", cv[:], P.cvec[:, :, :], writes=[cv])
        em.dma("sp", bm[:], P.b_modT[:, :, :], writes=[bm])
        em.act(sc, sc[:], cv[:], AF.Silu, reads=[cv])
        wt = [em.sb("wm%d" % i, [128, KD, 512], F32, st) for i in range(2)]
        pmf = [em.ps("pm%d" % i, [128, 512], F32, st) for i in range(2)]
        g = 0
        for l in range(L):
            wv = P.w_mod[l].rearrange("(k p) c -> p k c", p=128)
            for gi in range(24):
                w = wt[g % 2]
                p = pmf[g % 2]
                g += 1
                em.dma("sp", w[:], wv[:, :, gi * 512:(gi + 1) * 512], writes=[w])
                for c in range(4):
                    for k in range(KD):
                        em.mm(p, p[:, 2 * c:2 * c + 2], w[:, k, c * 128:(c + 1) * 128], sc[:, k, :],
                              k == 0, k == KD - 1, reads=[w, sc])
                em.op("dve", lambda e: e.tensor_tensor(
                    out=P.mod[:, l, gi * 4:(gi + 1) * 4, :], in0=p[:, 0:8].rearrange("p (c j) -> p c j", j=2),
                    in1=bm[:, l, gi * 4:(gi + 1) * 4].unsqueeze(2).to_broadcast([128, 4, 2]),
                    op=ALU.add), reads=[p, bm], writes=[P.mod], partial=True)
        em.op("dve", lambda e: e.tensor_scalar_add(out=P.mod1[:], in0=P.mod[:], scalar1=1.0),
              reads=[P.mod], writes=[P.mod1])
        em.op("dve", lambda e: e.tensor_scalar_mul(out=P.modg[:], in0=P.mod[:], scalar1=1.0 / ALPHA),
              reads=[P.mod], writes=[P.modg])
        em.barrier()


SH_M, SC_M, GT_M, SH_F, SC_F, GT_F = 0, 16, 32, 48, 64, 80


def phase_zero_pads(P):
    em = P.em
    with ExitStack() as st:
        z = em.sb("zpad", [128, NTOKC], F32, st)
        em.op("dve", lambda e: e.memset(z[:], 0.0), writes=[z])
        dummy = T()
        for a, b in ((0, PC0), (PC0 + CTX, PL0), (PL0 + SEQ, TTP)):
            em.dma("sp", P.pTok[a:b, :], z[0:b - a, :], reads=[z], writes=[dummy], partial=True)
            for c in range(NFM):
                em.dma("sp", P.pT[c * 128:(c + 1) * 128, a:b], z[:, 0:b - a], reads=[z], writes=[dummy], partial=True)
        em.barrier()


def phase_inproj(P, l, xsrc, xsrc_t):
    em = P.em
    with ExitStack() as st:
        xs = [em.sb("ip_xs%d" % i, [128, KD, 512], F32, st) for i in range(2)]
        xm = [em.sb("ip_xm%d" % i, [128, KD, 512], BF16, st) for i in range(2)]
        wt = [em.sb("ip_w%d" % i, [128, KD, 512], BF16, st) for i in range(2)]
        ps = [em.ps("ip_ps%d" % i, [128, 512], F32, st) for i in range(4)]
        sg = [em.sb("ip_sg%d" % i, [128, 512], F32, st) for i in range(4)]
        wv = P.wb_in[l].rearrange("(k p) c -> p k c", p=128)
        xv = _chunk_rows(xsrc)
        gcount = 0
        ccount = 0
        dummy = T()

        def evac(p, s, rows, cols):
            nonlocal ccount
            if ccount % 2 == 0:
                em.op("dve", lambda e: e.tensor_copy(out=s[0:rows, 0:cols], in_=p[0:rows, 0:cols]),
                      reads=[p], writes=[s])
            else:
                em.act(s, s[0:rows, 0:cols], p[0:rows, 0:cols], AF.Copy, reads=[p])
            ccount += 1

        for bi, (t0, n, isctx) in enumerate(TBLK):
            j = 1 if isctx else 0
            pp = ppos(t0)
            x_s, x_m = xs[bi % 2], xm[bi % 2]
            em.dma("sp", x_s[:, :, 0:n], xv[:, :, t0:t0 + n], reads=[xsrc_t[bi]], writes=[x_s])
            for k in range(KD):
                em.act(x_m, x_m[:, k, 0:n], x_s[:, k, 0:n], AF.Identity, reads=[x_s, P.mod, P.mod1],
                       bias=P.mod[:, l, SH_M + k, j:j + 1], scale=P.mod1[:, l, SC_M + k, j:j + 1],
                       writes=[x_m])
            for g in range(NCH // 4):
                w = wt[gcount % 2]
                gcount += 1
                em.dma("sp", w[:], wv[:, :, g * 512:(g + 1) * 512], reads=[P.wb_in_t[l]], writes=[w])
                if g < 4:
                    for c in range(4):
                        ci = g * 4 + c
                        nm, c0, wd = CH[ci]
                        if wd == 0:
                            continue
                        if nm == "ab":
                            for tt in range(n // 128):
                                p, s_ = ps[ccount % 4], sg[ccount % 4]
                                for k in range(KD):
                                    em.mm(p, p[:, 0:16], x_m[:, k, tt * 128:(tt + 1) * 128],
                                          w[:, k, c * 128:c * 128 + 16], k == 0, k == KD - 1, reads=[w, x_m])
                                evac(p, s_, 128, 16)
                                em.dma("pool", P.pTok[pp + tt * 128:pp + (tt + 1) * 128, TOKC["ab"]:TOKC["ab"] + 16],
                                       s_[:, 0:16], reads=[s_], writes=[dummy], partial=True)
                            continue
                        fi = ci if ci < 12 else 12
                        p, s_ = ps[ccount % 4], sg[ccount % 4]
                        for k in range(KD):
                            em.mm(p, p[0:wd, 0:n], w[:, k, c * 128:c * 128 + wd], x_m[:, k, 0:n],
                                  k == 0, k == KD - 1, reads=[w, x_m])
                        evac(p, s_, wd, n)
                        em.dma("pool", P.pT[fi * 128:fi * 128 + wd, pp:pp + n], s_[0:wd, 0:n],
                               reads=[s_], writes=[dummy], partial=True)
                else:
                    col0 = (g - 4) * 512
                    for tt in range(n // 128):
                        p, s_ = ps[ccount % 4], sg[ccount % 4]
                        for k in range(KD):
                            em.mm(p, p[:, :], x_m[:, k, tt * 128:(tt + 1) * 128], w[:, k, :],
                                  k == 0, k == KD - 1, reads=[w, x_m])
                        evac(p, s_, 128, 512)
                        em.dma("pool", P.pTok[pp + tt * 128:pp + (tt + 1) * 128, col0:col0 + 512], s_[:, :],
                               reads=[s_], writes=[dummy], partial=True)
        em.barrier()


def declare_dense(P):
    L = P.nl
    P.w_out = P.din("w_out", [L, D, D])
    P.w_gate = P.din("ffn_w_gate", [L, D, D_FF])
    P.w_up = P.din("ffn_w_up", [L, D, D_FF])
    P.w_down = P.din("ffn_w_down", [L, D_FF, D])
    P.lnT = P.din("lnT", [128, L, 4, KD])
    P.wb_out = P.dscr("wb_out", [L, D, D], BF16)
    P.wb_gate = P.dscr("wb_gate", [L, D, D_FF], BF16)
    P.wb_up = P.dscr("wb_up", [L, D, D_FF], BF16)
    P.wb_down = P.dscr("wb_down", [L, D_FF, D], BF16)
    P.mixT = P.dscr("mixT", [D, TT], BF16)
    P.xA = P.dscr("xA", [D, TT])
    P.xB = P.dscr("xB", [D, TT])
    P.xOut = P.dscr("xOut", [D, SEQ], out=True)


def phase_cast_dense(P):
    em = P.em
    P.wb_dense_t = [T() for _ in range(P.nl)]
    for l in range(P.nl):
        for src, dst, rows in ((P.w_out, P.wb_out, D), (P.w_gate, P.wb_gate, D), (P.w_up, P.wb_up, D),
                               (P.w_down, P.wb_down, D_FF)):
            for r0 in range(0, rows, 256):
                em.dma("pool", dst[l, r0:r0 + 256, :], src[l, r0:r0 + 256, :],
                       writes=[P.wb_dense_t[l]], partial=True)


def setup_consts(P):
    em = P.em
    P.ones_f = em.sb("ones_f", [128, 128], F32)
    P.ones_b = em.sb("ones_b", [128, 128], BF16)
    em.op("dve", lambda e: e.memset(P.ones_f[:], 1.0), writes=[P.ones_f])
    em.op("dve", lambda e: e.memset(P.ones_b[:], 1.0), writes=[P.ones_b])
    P.epsc = em.sb("epsc", [128, 4], F32)
    em.op("dve", lambda e: e.memset(P.epsc[:, 0:1], 1e-5 / (ALPHA * ALPHA)), writes=[P.epsc])
    em.op("dve", lambda e: e.memset(P.epsc[:, 1:2], 1e-6), writes=[P.epsc])
    em.op("dve", lambda e: e.memset(P.epsc[:, 2:3], 64e-5), writes=[P.epsc])
    em.op("dve", lambda e: e.memset(P.epsc[:, 3:4], 1e-12), writes=[P.epsc])
    P.ln = em.sb("ln", [128, P.nl, 4, KD], F32)
    em.dma("sp", P.ln[:], P.lnT[:, :, :, :], writes=[P.ln])


class ResLN:
    def __init__(self, P, st, tag):
        em = P.em
        self.P = P
        self.s1 = em.ps(tag + "_s1", [128, 512], F32, st)
        self.s2 = em.ps(tag + "_s2", [128, 512], F32, st)
        self.sq = [em.sb(tag + "_sq%d" % i, [128, 512], F32, st) for i in range(2)]
        self.mean = em.sb(tag + "_mean", [128, 512], F32, st)
        self.rstd = em.sb(tag + "_rstd", [128, 512], F32, st)
        self.tmp = [em.sb(tag + "_tmp%d" % i, [128, 512], F32, st) for i in range(2)]
        self.og = [em.sb(tag + "_og%d" % i, [128, 512], F32, st) for i in range(2)]
        self.cnt = 0

    def add_chunk(self, m, psum_t, xblk, n, l, gslot, j):
        P, em = self.P, self.P.em
        em.op("dve", lambda e: e.scalar_tensor_tensor(
            out=xblk[:, m, 0:n], in0=psum_t[:, 0:n], scalar=P.modg[:, l, gslot + m, j:j + 1],
            in1=xblk[:, m, 0:n], op0=ALU.mult, op1=ALU.add), reads=[psum_t, xblk, P.modg], writes=[xblk])
        sq = self.sq[m % 2]
        em.act(sq, sq[:, 0:n], xblk[:, m, 0:n], AF.Square, reads=[xblk])
        em.mm(self.s1, self.s1[:, 0:n], P.ones_f[:], xblk[:, m, 0:n], m == 0, m == KD - 1, reads=[xblk, P.ones_f])
        em.mm(self.s2, self.s2[:, 0:n], P.ones_f[:], sq[:, 0:n], m == 0, m == KD - 1, reads=[sq, P.ones_f])

    def finish(self, xblk, n, l, lnslot, xdst, xdst_t, c0, dst2=None):
        P, em = self.P, self.P.em
        mean, rstd = self.mean, self.rstd
        em.op("dve", lambda e: e.tensor_scalar_mul(out=mean[:, 0:n], in0=self.s1[:, 0:n], scalar1=1.0 / D),
              reads=[self.s1], writes=[mean])
        em.op("dve", lambda e: e.tensor_tensor(out=rstd[:, 0:n], in0=mean[:, 0:n], in1=mean[:, 0:n], op=ALU.mult),
              reads=[mean], writes=[rstd])
        em.op("dve", lambda e: e.scalar_tensor_tensor(
            out=rstd[:, 0:n], in0=self.s2[:, 0:n], scalar=1.0 / D, in1=rstd[:, 0:n],
            op0=ALU.mult, op1=ALU.subtract), reads=[self.s2, rstd], writes=[rstd])
        em.act(rstd, rstd[:, 0:n], rstd[:, 0:n], AF.Sqrt, reads=[rstd, P.epsc], bias=P.epsc[:, 0:1])
        em.op("dve", lambda e: e.reciprocal(out=rstd[:, 0:n], in_=rstd[:, 0:n]), reads=[rstd], writes=[rstd])
        xv = _chunk_rows(xdst)
        for m in range(KD):
            tmp = self.tmp[m % 2]
            og = self.og[m % 2]
            em.op("dve", lambda e: e.tensor_tensor(out=tmp[:, 0:n], in0=xblk[:, m, 0:n], in1=mean[:, 0:n],
                                                   op=ALU.subtract), reads=[xblk, mean], writes=[tmp])
            em.op("pool", lambda e: e.tensor_tensor(out=tmp[:, 0:n], in0=tmp[:, 0:n], in1=rstd[:, 0:n],
                                                    op=ALU.mult), reads=[tmp, rstd], writes=[tmp])
            em.act(og, og[:, 0:n], tmp[:, 0:n], AF.Identity, reads=[tmp, P.ln],
                   bias=P.ln[:, l, lnslot + 1, m:m + 1], scale=P.ln[:, l, lnslot, m:m + 1])
            em.dma("pool", xdst[m * 128:(m + 1) * 128, c0:c0 + n], og[:, 0:n], reads=[og],
                   writes=[xdst_t], partial=True)
            if dst2 is not None:
                d2, d2_t, c2 = dst2
                em.dma("sp", d2[m * 128:(m + 1) * 128, c2:c2 + n], og[:, 0:n], reads=[og],
                       writes=[d2_t], partial=True)


def phase_outproj(P, l, xsrc, xsrc_t, xdst, xdst_t, with_ctx):
    em = P.em
    with ExitStack() as st:
        xs = [em.sb("op_xs%d" % i, [128, KD, 512], F32, st) for i in range(2)]
        am = [em.sb("op_am%d" % i, [128, KD, 512], BF16, st) for i in range(2)]
        wt = [em.sb("op_w%d" % i, [128, KD, 512], BF16, st) for i in range(2)]
        ps = [em.ps("op_ps%d" % i, [128, 512], F32, st) for i in range(2)]
        rl = ResLN(P, st, "op")
        wv = P.wb_out[l].rearrange("(k p) c -> p k c", p=128)
        xv = _chunk_rows(xsrc)
        av = _chunk_rows(P.mixT)
        gc = 0
        cc = 0
        for bi, (t0, n, isctx) in enumerate(TBLK):
            if isctx and not with_ctx:
                continue
            j = 1 if isctx else 0
            x_s, a_m = xs[bi % 2], am[bi % 2]
            em.dma("sp", x_s[:, :, 0:n], xv[:, :, t0:t0 + n], reads=[xsrc_t[bi]], writes=[x_s])
            em.dma("sp", a_m[:, :, 0:n], av[:, :, t0:t0 + n], reads=[P.mixT_t[bi]], writes=[a_m])
            for g in range(4):
                w = wt[gc % 2]
                gc += 1
                em.dma("sp", w[:], wv[:, :, g * 512:(g + 1) * 512], reads=[P.wb_dense_t[l]], writes=[w])
                for c in range(4):
                    m = g * 4 + c
                    p = ps[cc % 2]
                    cc += 1
                    for k in range(KD):
                        em.mm(p, p[:, 0:n], w[:, k, c * 128:(c + 1) * 128], a_m[:, k, 0:n],
                              k == 0, k == KD - 1, reads=[w, a_m])
                    rl.add_chunk(m, p, x_s, n, l, GT_M, j)
            rl.finish(x_s, n, l, 0, xdst, xdst_t[bi], t0)
        em.barrier()


def phase_ffn(P, l, xsrc, xsrc_t, xdst, xdst_t, with_ctx, final=False):
    em = P.em
    with ExitStack() as st:
        xs = em.sb("ff_xs", [128, KD, 512], F32, st)
        xm = em.sb("ff_xm", [128, KD, 512], BF16, st)
        hh = em.sb("ff_h", [128, KF, 512], BF16, st)
        wg = [em.sb("ff_wg%d" % i, [128, KD, 256], BF16, st) for i in range(2)]
        wu = [em.sb("ff_wu%d" % i, [128, KD, 256], BF16, st) for i in range(2)]
        wd = [em.sb("ff_wd%d" % i, [128, KF, 128], BF16, st) for i in range(2)]
        pg = [em.ps("ff_pg%d" % i, [128, 512], F32, st) for i in range(2)]
        pu = [em.ps("ff_pu%d" % i, [128, 512], F32, st) for i in range(2)]
        pd = [em.ps("ff_pd%d" % i, [128, 512], F32, st) for i in range(2)]
        sg = [em.sb("ff_sg%d" % i, [128, 512], F32, st) for i in range(2)]
        rl = ResLN(P, st, "ff")
        wgv = P.wb_gate[l].rearrange("(k p) c -> p k c", p=128)
        wuv = P.wb_up[l].rearrange("(k p) c -> p k c", p=128)
        wdv = P.wb_down[l].rearrange("(k p) c -> p k c", p=128)
        xv = _chunk_rows(xsrc)
        gc = 0
        cc = 0
        dc = 0
        for bi, (t0, n, isctx) in enumerate(TBLK):
            if isctx and not with_ctx:
                continue
            j = 1 if isctx else 0
            em.dma("sp", xs[:, :, 0:n], xv[:, :, t0:t0 + n], reads=[xsrc_t[bi]], writes=[xs])
            for k in range(KD):
                em.act(xm, xm[:, k, 0:n], xs[:, k, 0:n], AF.Identity, reads=[xs, P.mod, P.mod1],
                       bias=P.mod[:, l, SH_F + k, j:j + 1], scale=P.mod1[:, l, SC_F + k, j:j + 1])
            for g in range(KF // 2):
                w_g, w_u = wg[gc % 2], wu[gc % 2]
                gc += 1
                em.dma("sp", w_g[:], wgv[:, :, g * 256:(g + 1) * 256], reads=[P.wb_dense_t[l]], writes=[w_g])
                em.dma("sp", w_u[:], wuv[:, :, g * 256:(g + 1) * 256], reads=[P.wb_dense_t[l]], writes=[w_u])
                for c in range(2):
                    m = g * 2 + c
                    p_g, p_u, s = pg[cc % 2], pu[cc % 2], sg[cc % 2]
                    cc += 1
                    for k in range(KD):
                        em.mm(p_g, p_g[:, 0:n], w_g[:, k, c * 128:(c + 1) * 128], xm[:, k, 0:n],
                              k == 0, k == KD - 1, reads=[w_g, xm])
                    for k in range(KD):
                        em.mm(p_u, p_u[:, 0:n], w_u[:, k, c * 128:(c + 1) * 128], xm[:, k, 0:n],
                              k == 0, k == KD - 1, reads=[w_u, xm])
                    em.act(s, s[:, 0:n], p_g[:, 0:n], AF.Silu, reads=[p_g])
                    em.op("dve", lambda e: e.tensor_tensor(out=hh[:, m, 0:n], in0=s[:, 0:n], in1=p_u[:, 0:n],
                                                           op=ALU.mult), reads=[s, p_u], writes=[hh], partial=True)
            for m in range(KD):
                w_d = wd[dc % 2]
                p_d = pd[dc % 2]
                dc += 1
                em.dma("sp", w_d[:], wdv[:, :, m * 128:(m + 1) * 128], reads=[P.wb_dense_t[l]], writes=[w_d])
                for k in range(KF):
                    em.mm(p_d, p_d[:, 0:n], w_d[:, k, :], hh[:, k, 0:n], k == 0, k == KF - 1, reads=[w_d, hh])
                rl.add_chunk(m, p_d, xs, n, l, GT_F, j)
            if final:
                rl.finish(xs, n, l, 2, P.xOut, xdst_t[bi], t0 - CTX)
            else:
                rl.finish(xs, n, l, 2, xdst, xdst_t[bi], t0)
        em.barrier()


NEG = -1.0e30
PIPE = True


def declare_scan(P):
    L = P.nl
    P.masks_in = P.din("masks", [128, 8, 128])
    P.ident_in = P.din("ident", [128, 128])
    P.tri_in = P.din("tri", [128, 2, 128])
    P.bmasks_in = P.din("bmasks", [128, 4, 128])
    P.rw_row = P.din("rw_row", [L, 11, 512])
    P.rw_mu = P.din("rw_mu", [L, 1536])
    P.rw_muL = P.din("rw_muL", [128, L, 3])
    P.rw_w2 = P.din("rw_w2", [L, 64, 512])
    P.rw_a2 = P.din("rw_a2", [L, 64, 512])
    P.rw_g2 = P.din("rw_g2", [L, 96, 512])
    P.gd_conv = P.din("gd_conv", [L, 5, 1536])
    P.gd_row = P.din("gd_row", [L, 3, 512])
    P.yscr = P.dscr("yscr", [4, TTP, 512])
    P.auxs = P.dscr("auxs", [3, TTP, 512])


def setup_scan_consts(P):
    em = P.em
    P.masks = em.sb("masks_sb", [128, 8, 128], F32)
    P.ident = em.sb("ident", [128, 128], F32)
    P.tri = em.sb("tri", [128, 2, 128], F32)
    em.dma("sp", P.masks[:], P.masks_in[:, :, :], writes=[P.masks])
    em.dma("sp", P.ident[:], P.ident_in[:, :], writes=[P.ident])
    em.dma("sp", P.tri[:], P.tri_in[:, :, :], writes=[P.tri])
    P.bmasks = em.sb("bmasks_sb", [128, 4, 128], F32)
    em.dma("sp", P.bmasks[:], P.bmasks_in[:, :, :], writes=[P.bmasks])


def chunk_order(rev):
    nc_ctx = CTX // 128
    nc_lat = SEQ // 128
    ctx = [128 * i for i in range(nc_ctx)]
    lat = [CTX + 128 * i for i in range(nc_lat)]
    if rev:
        return ctx[::-1] + lat[::-1]
    return ctx + lat


class ScanCore:
    def __init__(self, P, st, N, tag):
        em = P.em
        self.P, self.N, self.H = P, N, 512 // N
        N_, H = N, self.H
        self.pb = [em.ps(tag + "_pb%d" % i, [128, 512], F32, st) for i in range(8)]
        self.pbi = 0
        f = lambda nm, shp: em.sb(tag + "_" + nm, shp, F32, st)
        self.XT = {k: f("xt_" + k, [N_, H, 128]) for k in ("a", "b", "k", "r", "R")}
        self.AT = [f("AT0", [128, H, 128])]
        self.A = [f("A0", [128, H, 128])]
        self.AkT = f("AkT", [128, H, 128])
        self.ArbT = f("ArbT", [128, H, 128])
        self.ArkT = f("ArkT", [128, H, 128])
        self.Z = [f("Z%d" % i, [128, H, 2 * N_]) for i in range(2)]
        self.GH = H // 2
        GH = self.GH
        fb = lambda nm, shp: em.sb(tag + "_" + nm, shp, BF16, st)
        self.J = [[[fb("J%d%d%d" % (gi, i, k), [128, GH, 128]) for k in range(2)] for i in range(2)] for gi in range(2)]
        self.X = [fb("X%d" % gi, [128, GH, 128]) for gi in range(2)]
        self.XTt = [fb("XTt%d" % gi, [128, GH, 128]) for gi in range(2)]
        self.Zb = [fb("Zb%d" % gi, [128, GH, 2 * N_]) for gi in range(2)]
        self.Zt = [f("Zt%d" % gi, [128, GH, 2 * N_]) for gi in range(2)]
        self.WpT = f("WpT", [N_, H, 128])
        self.U = f("U", [128, H, N_])
        self.ST = f("ST", [N_, H, N_])
        self.Ysb = [f("Ysb%d" % i, [128, 512]) for i in range(2)]
        self.yi = 0

    def bank(self):
        b = self.pb[self.pbi % 8]
        self.pbi += 1
        return b

    def reset_state(self):
        em = self.P.em
        em.op("dve", lambda e: e.memset(self.ST[:], 0.0), writes=[self.ST])

    def transpose_to(self, key, src):
        P, em, N, H = self.P, self.P.em, self.N, self.H
        dst = self.XT[key]
        for g in range(H // 4):
            pb = self.bank()
            for hh in range(4):
                h = g * 4 + hh
                em.op("pe", lambda e: e.transpose(pb[0:N, hh * 128:(hh + 1) * 128], src[:, h * N:(h + 1) * N],
                                                  P.ident[:]), reads=[src, P.ident], writes=[pb])
            em.act(dst, dst[:, g * 4:(g + 1) * 4, :], pb[0:N, :].rearrange("p (h t) -> p h t", h=4), AF.Copy,
                   reads=[pb])
        return dst

    def gram(self, lkey, rkey, dst, mask_ap_fn, mask_t):
        P, em, N, H = self.P, self.P.em, self.N, self.H
        L_, R_ = self.XT[lkey], self.XT[rkey]
        for g in range(H // 4):
            pb = self.bank()
            for hh in range(4):
                h = g * 4 + hh
                em.mm(pb, pb[:, hh * 128:(hh + 1) * 128], L_[:, h, :], R_[:, h, :], True, True, reads=[L_, R_])
            em.op("dve", lambda e: e.tensor_tensor(
                out=dst[:, g * 4:(g + 1) * 4, :], in0=pb[:, :].rearrange("p (h t) -> p h t", h=4),
                in1=mask_ap_fn(g), op=ALU.mult), reads=[pb, mask_t], writes=[dst], partial=True)

    def run_chunk(self, rev, ops, WcT, masks, ydst_ap, filler=None):
        P, em, N, H = self.P, self.P.em, self.N, self.H
        hv = lambda t: t[:].rearrange("p (h n) -> p h n", n=N)
        same_bk = ops["gb"] is ops["gk"]
        self.transpose_to("a", ops["ga"])
        self.transpose_to("k", ops["gk"])
        if not same_bk:
            self.transpose_to("b", ops["gb"])
        bkey = "k" if same_bk else "b"
        self.transpose_to("r", ops["gr"])
        if ops["Rtil"] is ops["gr"]:
            Rkey = "r"
        else:
            self.transpose_to("R", ops["Rtil"])
            Rkey = "R"
        AT, A = self.AT[0], self.A[0]
        self.gram(bkey, "a", AT, *masks["AT"])
        self.gram("a", bkey, A, *masks["A"])
        if same_bk:
            AkT, ArbT = AT, None
        else:
            AkT, ArbT = self.AkT, self.ArbT
            self.gram("k", "a", AkT, *masks["AT"])
            self.gram("b", "r", ArbT, *masks["ArT"])
        ArkT = self.ArkT
        self.gram("k", "r", ArkT, *masks["ArT"])
        if same_bk:
            ArbT = ArkT
        V = ops["V"]
        Z = self.Z[0]
        pb = self.bank()
        for h in range(H):
            em.mm(pb, pb[:, h * N:(h + 1) * N], AkT[:, h, :], V[:, h * N:(h + 1) * N], True, True,
                  reads=[AkT, V])
        em.op("dve", lambda e: e.tensor_copy(out=Z[:, :, 0:N], in_=pb[:, :].rearrange("p (h n) -> p h n", n=N)),
              reads=[pb], writes=[Z], partial=True)
        em.op("pool", lambda e: e.tensor_copy(out=Z[:, :, N:2 * N], in_=hv(ops["Atil"])),
              reads=[ops["Atil"]], writes=[Z], partial=True)
        Zf = self.Z[1]
        HB = 512 // (2 * N)
        A0, AT0 = self.A[0], self.AT[0]
        GH = self.GH
        bm = lambda i: P.bmasks[:, i:i + 1, :].to_broadcast([128, GH, 128])
        idb = P.ident[:, :].unsqueeze(1).to_broadcast([128, GH, 128])

        def mmg(dst_pb, lhs_t, rhs_t):
            for hh in range(GH):
                em.mm(dst_pb, dst_pb[:, hh * 128:(hh + 1) * 128], lhs_t[:, hh, :], rhs_t[:, hh, :], True, True,
                      reads=[lhs_t, rhs_t])

        vg = lambda pb_: pb_[:, 0:GH * 128].rearrange("p (h t) -> p h t", h=GH)

        def inv_group(g):
            gs = slice(g * GH, (g + 1) * GH)
            X, XT = self.X[g], self.XTt[g]
            J = self.J[g]
            Ja, JaT = J[0]
            em.op("dve", lambda e: e.tensor_tensor(out=Ja[:], in0=A0[:, gs, :], in1=bm(0), op=ALU.mult),
                  reads=[A0, P.bmasks], writes=[Ja])
            em.op("pool", lambda e: e.tensor_tensor(out=JaT[:], in0=AT0[:, gs, :], in1=bm(0), op=ALU.mult),
                  reads=[AT0, P.bmasks], writes=[JaT])
            em.op("dve", lambda e: e.tensor_tensor(out=X[:], in0=Ja[:], in1=idb, op=ALU.add),
                  reads=[Ja, P.ident], writes=[X])
            em.op("pool", lambda e: e.tensor_tensor(out=XT[:], in0=JaT[:], in1=idb, op=ALU.add),
                  reads=[JaT, P.ident], writes=[XT])
            yield
            cur = 0
            for lev in range(3):
                Jc, JcT = J[cur]
                Jn, JnT = J[1 - cur]
                p1, p2 = self.bank(), self.bank()
                mmg(p1, JcT, Jc)
                mmg(p2, Jc, JcT)
                em.op("dve", lambda e: e.tensor_copy(out=Jn[:], in_=vg(p1)), reads=[p1], writes=[Jn])
                em.act(JnT, JnT[:], vg(p2), AF.Copy, reads=[p2])
                yield
                p3, p4 = self.bank(), self.bank()
                mmg(p3, JnT, X)
                mmg(p4, Jn, XT)
                em.op("dve", lambda e: e.tensor_tensor(out=X[:], in0=vg(p3), in1=X[:], op=ALU.add),
                      reads=[p3, X], writes=[X])
                em.op("dve", lambda e: e.tensor_tensor(out=XT[:], in0=vg(p4), in1=XT[:], op=ALU.add),
                      reads=[p4, XT], writes=[XT])
                yield
                cur = 1 - cur
            for bi in (1, 2, 3):
                Ao, AoT = J[0]
                Y, Y2 = J[1]
                em.op("dve", lambda e: e.tensor_tensor(out=Ao[:], in0=A0[:, gs, :], in1=bm(bi), op=ALU.mult),
                      reads=[A0, P.bmasks], writes=[Ao])
                em.op("pool", lambda e: e.tensor_tensor(out=AoT[:], in0=AT0[:, gs, :], in1=bm(bi), op=ALU.mult),
                      reads=[AT0, P.bmasks], writes=[AoT])
                last = bi == 3
                p2 = self.bank()
                mmg(p2, Ao, XT)
                if not last:
                    p1 = self.bank()
                    mmg(p1, AoT, X)
                    em.op("dve", lambda e: e.tensor_copy(out=Y[:], in_=vg(p1)), reads=[p1], writes=[Y])
                em.act(Y2, Y2[:], vg(p2), AF.Copy, reads=[p2])
                yield
                p4 = self.bank()
                mmg(p4, X, Y2)
                if not last:
                    p3 = self.bank()
                    mmg(p3, XT, Y)
                    em.op("dve", lambda e: e.tensor_tensor(out=X[:], in0=vg(p3), in1=X[:], op=ALU.add),
                          reads=[p3, X], writes=[X])
                em.op("dve", lambda e: e.tensor_tensor(out=XT[:], in0=vg(p4), in1=XT[:], op=ALU.add),
                      reads=[p4, XT], writes=[XT])
                yield
            nsub = max(1, GH // HB)
            hps = GH // nsub
            Zb, Zt = self.Zb[g], self.Zt[g]
            em.op("pool", lambda e: e.tensor_copy(out=Zb[:], in_=Z[:, gs, :]), reads=[Z], writes=[Zb])

            def apply_x(src_b, accumulate):
                for sub in range(nsub):
                    pb = self.bank()
                    for hh in range(hps):
                        h4 = sub * hps + hh
                        em.mm(pb, pb[:, hh * 2 * N:(hh + 1) * 2 * N], XT[:, h4, :], src_b[:, h4, :], True, True,
                              reads=[XT, src_b])
                    h0 = g * GH + sub * hps
                    pv = pb[:, 0:hps * 2 * N].rearrange("p (h n) -> p h n", h=hps)
                    if accumulate:
                        em.op("dve", lambda e: e.tensor_tensor(out=Zf[:, h0:h0 + hps, :], in0=pv,
                                                               in1=Zf[:, h0:h0 + hps, :], op=ALU.add),
                              reads=[pb, Zf], writes=[Zf], partial=True)
                    else:
                        em.act(Zf, Zf[:, h0:h0 + hps, :], pv, AF.Copy, reads=[pb], writes=[Zf])

            apply_x(Zb, False)
            yield
            em.op("pool", lambda e: e.tensor_tensor(out=Zt[:], in0=Z[:, gs, :], in1=Zf[:, gs, :], op=ALU.subtract),
                  reads=[Z, Zf], writes=[Zt])
            for sub in range(nsub):
                pb = self.bank()
                for hh in range(hps):
                    h4 = sub * hps + hh
                    h = g * GH + h4
                    em.mm(pb, pb[:, hh * 2 * N:(hh + 1) * 2 * N], AT0[:, h, :], Zf[:, h, :], True, True,
                          reads=[AT0, Zf])
                h4s = slice(sub * hps, (sub + 1) * hps)
                em.op("dve", lambda e: e.tensor_tensor(
                    out=Zb[:, h4s, :], in0=pb[:, 0:hps * 2 * N].rearrange("p (h n) -> p h n", h=hps),
                    in1=Zt[:, h4s, :], op=ALU.add), reads=[pb, Zt], writes=[Zb], partial=True)
            yield
            apply_x(Zb, True)
            yield

        gens = [inv_group(0), inv_group(1)]
        alive = [True, True]
        while any(alive):
            for gi in range(2):
                if alive[gi]:
                    try:
                        next(gens[gi])
                    except StopIteration:
                        alive[gi] = False
            if filler is not None:
                next(filler, None)
        WpT = self.WpT
        for g in range(H // 4):
            pb = self.bank()
            for hh in range(4):
                h = g * 4 + hh
                em.op("pe", lambda e: e.transpose(pb[0:N, hh * 128:(hh + 1) * 128], Zf[:, h, N:2 * N],
                                                  P.ident[:]), reads=[Zf, P.ident], writes=[pb])
            em.act(WpT, WpT[:, g * 4:(g + 1) * 4, :], pb[0:N, :].rearrange("p (h t) -> p h t", h=4), AF.Copy,
                   reads=[pb], writes=[WpT])
        ST, U = self.ST, self.U
        RT = self.XT[Rkey]
        pb = self.bank()
        for h in range(H):
            em.mm(pb, pb[:, h * N:(h + 1) * N], WpT[:, h, :], ST[:, h, :], True, True, reads=[WpT, ST])
        em.op("dve", lambda e: e.tensor_tensor(out=U[:], in0=pb[:, :].rearrange("p (h n) -> p h n", n=N),
                                               in1=Zf[:, :, 0:N], op=ALU.add), reads=[pb, Zf], writes=[U])
        pb = self.bank()
        for h in range(H):
            o = pb[:, h * N:(h + 1) * N]
            em.mm(pb, o, RT[:, h, :], ST[:, h, :], True, False, reads=[RT, ST])
            em.mm(pb, o, ArbT[:, h, :], U[:, h, :], False, False, reads=[ArbT, U])
            em.mm(pb, o, ArkT[:, h, :], V[:, h * N:(h + 1) * N], False, True, reads=[ArkT, V])
        ysb = self.Ysb[self.yi % 2]
        self.yi += 1
        em.act(ysb, ysb[:], pb[:, :], AF.Copy, reads=[pb])
        em.dma("sp", ydst_ap, ysb[:], reads=[ysb], writes=[T()])
        pb = self.bank()
        Bh, Kh = ops["Bh"], ops["Kh"]
        for h in range(H):
            o = pb[0:N, h * N:(h + 1) * N]
            em.mm(pb, o, Bh[:, h * N:(h + 1) * N], U[:, h, :], True, False, reads=[Bh, U])
            em.mm(pb, o, Kh[:, h * N:(h + 1) * N], V[:, h * N:(h + 1) * N], False, True, reads=[Kh, V])
        em.op("dve", lambda e: e.tensor_tensor(out=ST[:], in0=ST[:],
                                               in1=WcT[:, :].unsqueeze(2).to_broadcast([N, H, N]), op=ALU.mult),
              reads=[ST, WcT], writes=[ST])
        em.op("dve", lambda e: e.tensor_tensor(out=ST[:], in0=pb[0:N, :].rearrange("p (h n) -> p h n", n=N),
                                               in1=ST[:], op=ALU.add), reads=[pb, ST], writes=[ST])


C0 = math.exp(-0.5)


def _bc_row(P, st, name, src_row_ap, width=512):
    em = P.em
    t = em.sb(name, [128, width], F32, st)
    em.dma("sp", t[:], src_row_ap.partition_broadcast(128), writes=[t])
    return t


def phase_rwkv(P, l, want_ctx_out=True):
    em = P.em
    dummy = T()
    with ExitStack() as st:
        core = ScanCore(P, st, 64, "rw")
        f = lambda nm, shp=(128, 512): em.sb("rw_" + nm, list(shp), F32, st)
        prm = {}
        for i, nm in enumerate(["w0_0", "w0_1", "a0_0", "a0_1", "k_k", "k_a", "r_k"]):
            prm[nm] = _bc_row(P, st, "rwp_" + nm, P.rw_row[l, i:i + 1, :])
        omm = f("omm", (128, 1536))
        hmu = f("hmu", (128, 1536))
        em.dma("sp", omm[:], P.rw_mu[l:l + 1, :].partition_broadcast(128), writes=[omm])
        em.op("dve", lambda e: e.tensor_scalar_mul(out=hmu[:], in0=omm[:], scalar1=0.5), reads=[omm], writes=[hmu])
        em.op("dve", lambda e: e.tensor_scalar(out=omm[:], in0=omm[:], scalar1=-1.0, scalar2=1.0, op0=ALU.mult,
                                               op1=ALU.add), reads=[omm], writes=[omm])
        muL = f("muL", (128, 3))
        ommL = f("ommL", (128, 3))
        hmuL = f("hmuL", (128, 3))
        em.dma("sp", muL[:], P.rw_muL[:, l, :], writes=[muL])
        em.op("dve", lambda e: e.tensor_scalar_mul(out=hmuL[:], in0=muL[:], scalar1=0.5), reads=[muL], writes=[hmuL])
        em.op("dve", lambda e: e.tensor_scalar(out=ommL[:], in0=muL[:], scalar1=-1.0, scalar2=1.0, op0=ALU.mult,
                                               op1=ALU.add), reads=[muL], writes=[ommL])
        w2 = f("w2", (64, 512))
        a2 = f("a2", (64, 512))
        g2 = f("g2", (96, 512))
        em.dma("sp", w2[:], P.rw_w2[l, :, :], writes=[w2])
        em.dma("sp", a2[:], P.rw_a2[l, :, :], writes=[a2])
        em.dma("sp", g2[:], P.rw_g2[l, :, :], writes=[g2])
        cen, prv, nxt = f("cen"), f("prv"), f("nxt")
        rp, kp = f("rp"), f("kp")
        vp2 = [f("vp0"), f("vp1")]
        lo = f("lo", (128, 3, 130))
        loP = f("loP", (128, 3, 128))
        lot = f("lot", (128, 128))
        sgd = [f("sgd0"), f("sgd1")]
        alr = [f("alr0"), f("alr1")]
        gg = f("gg")
        kk = f("kk")
        kdir = [f("kdir0"), f("kdir1")]
        t1, t2, t3 = f("t1"), f("t2"), f("t3")
        ss = f("ss", (128, 8))
        E1, E1p, E2, E3 = f("E1"), f("E1p"), f("E2"), f("E3")
        at, bt, kt, rt, bb = f("at"), f("bt"), f("kt"), f("rt"), f("bb")
        Bh2, Kh2 = [f("Bh0"), f("Bh1")], [f("Kh0"), f("Kh1")]
        WcT2 = [f("WcT0", (64, 8)), f("WcT1", (64, 8))]
        aux = f("aux")
        for d in (0, 1):
            rev = d == 1
            core.reset_state()
            if not rev:
                mk = {"AT": (lambda g: P.masks[:, 1:2, :].to_broadcast([128, 4, 128]), P.masks),
                      "A": (lambda g: P.masks[:, 0:1, :].to_broadcast([128, 4, 128]), P.masks),
                      "ArT": (lambda g: P.masks[:, 3:4, :].to_broadcast([128, 4, 128]), P.masks)}
            else:
                mk = {"AT": (lambda g: P.masks[:, 0:1, :].to_broadcast([128, 4, 128]), P.masks),
                      "A": (lambda g: P.masks[:, 1:2, :].to_broadcast([128, 4, 128]), P.masks),
                      "ArT": (lambda g: P.masks[:, 2:3, :].to_broadcast([128, 4, 128]), P.masks)}
            def prep(t0, slot, d=d):
                vp, Bh, Kh, WcT = vp2[slot], Bh2[slot], Kh2[slot], WcT2[slot]
                pp = ppos(t0)
                for ci, dst in enumerate((rp, kp, vp)):
                    c0 = ci * 512
                    em.dma("sp", cen[:], P.pTok[pp:pp + 128, c0:c0 + 512], writes=[cen])
                    em.dma("sp", prv[:], P.pTok[pp - 1:pp + 127, c0:c0 + 512], writes=[prv])
                    em.dma("sp", nxt[:], P.pTok[pp + 1:pp + 129, c0:c0 + 512], writes=[nxt])
                    em.op("pool", lambda e: e.tensor_tensor(out=prv[:], in0=prv[:], in1=nxt[:], op=ALU.add),
                          reads=[prv, nxt], writes=[prv])
                    em.op("pool", lambda e: e.tensor_tensor(out=prv[:], in0=prv[:], in1=hmu[:, c0:c0 + 512],
                                                            op=ALU.mult), reads=[prv, hmu], writes=[prv])
                    em.op("dve", lambda e: e.tensor_tensor(out=cen[:], in0=cen[:], in1=omm[:, c0:c0 + 512],
                                                           op=ALU.mult), reads=[cen, omm], writes=[cen])
                    em.op("dve", lambda e: e.tensor_tensor(out=dst[:], in0=cen[:], in1=prv[:], op=ALU.add),
                          reads=[cen, prv], writes=[dst])
                yield
                for c, (fi, wdt) in enumerate(((10, 64), (11, 64), (12, 96))):
                    em.dma("sp", lo[0:wdt, c, :], P.pT[fi * 128:fi * 128 + wdt, pp - 1:pp + 129], writes=[lo],
                           partial=True)
                for c, wdt in enumerate((64, 64, 96)):
                    em.op("dve", lambda e: e.tensor_tensor(out=lot[0:wdt, :], in0=lo[0:wdt, c, 0:128],
                                                           in1=lo[0:wdt, c, 2:130], op=ALU.add),
                          reads=[lo], writes=[lot])
                    em.op("dve", lambda e: e.tensor_scalar_mul(out=lot[0:wdt, :], in0=lot[0:wdt, :],
                                                               scalar1=hmuL[0:wdt, c:c + 1]),
                          reads=[lot, hmuL], writes=[lot])
                    em.op("dve", lambda e: e.scalar_tensor_tensor(
                        out=loP[0:wdt, c, :], in0=lo[0:wdt, c, 1:129], scalar=ommL[0:wdt, c:c + 1],
                        in1=lot[0:wdt, :], op0=ALU.mult, op1=ALU.add), reads=[lo, ommL, lot], writes=[loP],
                        partial=True)
                em.act(loP, loP[0:64, 0, :], loP[0:64, 0, :], AF.Tanh, reads=[loP])
                em.act(loP, loP[0:96, 2, :], loP[0:96, 2, :], AF.Sigmoid, reads=[loP])
                yield
                for dd in (0, 1):
                    pb = core.bank()
                    em.mm(pb, pb[:, :], loP[dd * 32:(dd + 1) * 32, 0, :], w2[dd * 32:(dd + 1) * 32, :], True, True,
                          reads=[loP, w2])
                    em.op("dve", lambda e: e.tensor_tensor(out=sgd[dd][:], in0=pb[:, :], in1=prm["w0_%d" % dd][:],
                                                           op=ALU.add), reads=[pb, prm["w0_%d" % dd]],
                          writes=[sgd[dd]])
                    em.act(sgd[dd], sgd[dd][:], sgd[dd][:], AF.Sigmoid, reads=[sgd[dd]])
                    pb = core.bank()
                    em.mm(pb, pb[:, :], loP[dd * 32:(dd + 1) * 32, 1, :], a2[dd * 32:(dd + 1) * 32, :], True, True,
                          reads=[loP, a2])
                    em.op("dve", lambda e: e.tensor_tensor(out=alr[dd][:], in0=pb[:, :], in1=prm["a0_%d" % dd][:],
                                                           op=ALU.add), reads=[pb, prm["a0_%d" % dd]],
                          writes=[alr[dd]])
                    em.act(alr[dd], alr[dd][:], alr[dd][:], AF.Sigmoid, reads=[alr[dd]])
                yield
                em.op("dve", lambda e: e.tensor_tensor(out=kk[:], in0=kp[:], in1=prm["k_k"][:], op=ALU.mult),
                      reads=[kp, prm["k_k"]], writes=[kk])
                em.act(t1, t1[:], kk[:], AF.Square, reads=[kk])
                em.op("dve", lambda e: e.tensor_reduce(out=ss[:], in_=t1[:].rearrange("p (h n) -> p h n", n=64),
                                                       axis=AX.X, op=ALU.add), reads=[t1], writes=[ss])
                em.act(ss, ss[:], ss[:], AF.Sqrt, reads=[ss, P.epsc], bias=P.epsc[:, 3:4])
                em.op("dve", lambda e: e.reciprocal(out=ss[:], in_=ss[:]), reads=[ss], writes=[ss])
                em.op("dve", lambda e: e.tensor_tensor(
                    out=kk[:].rearrange("p (h n) -> p h n", n=64), in0=kk[:].rearrange("p (h n) -> p h n", n=64),
                    in1=ss[:, :].unsqueeze(2).to_broadcast([128, 8, 64]), op=ALU.mult), reads=[kk, ss], writes=[kk])
                yield
                for dd in (0, 1):
                    em.op("dve", lambda e: e.scalar_tensor_tensor(
                        out=t1[:], in0=alr[dd][:], scalar=-1.0, in1=prm["k_a"][:], op0=ALU.add, op1=ALU.mult),
                        reads=[alr[dd], prm["k_a"]], writes=[t1])
                    em.op("dve", lambda e: e.scalar_tensor_tensor(
                        out=kdir[dd][:], in0=t1[:], scalar=1.0, in1=kp[:], op0=ALU.add, op1=ALU.mult),
                        reads=[t1, kp], writes=[kdir[dd]])
                yield
                if d == 0:
                    pb = core.bank()
                    em.mm(pb, pb[:, :], loP[0:96, 2, :], g2[0:96, :], True, True, reads=[loP, g2])
                    em.act(gg, gg[:], pb[:, :], AF.Copy, reads=[pb])
                    em.dma("sp", P.auxs[1, pp:pp + 128, :], gg[:], reads=[gg], writes=[dummy])
                    em.op("pool", lambda e: e.tensor_tensor(out=t2[:], in0=kdir[0][:], in1=kdir[1][:], op=ALU.add),
                          reads=[kdir[0], kdir[1]], writes=[t2])
                    em.op("pool", lambda e: e.tensor_tensor(out=t2[:], in0=t2[:], in1=rp[:], op=ALU.mult),
                          reads=[t2, rp], writes=[t2])
                    em.op("pool", lambda e: e.tensor_tensor(out=t2[:], in0=t2[:], in1=prm["r_k"][:], op=ALU.mult),
                          reads=[t2, prm["r_k"]], writes=[t2])
                    em.op("dve", lambda e: e.tensor_reduce(out=ss[:], in_=t2[:].rearrange("p (h n) -> p h n", n=64),
                                                           axis=AX.X, op=ALU.add), reads=[t2], writes=[ss])
                    em.op("dve", lambda e: e.tensor_tensor(
                        out=aux[:].rearrange("p (h n) -> p h n", n=64), in0=vp[:].rearrange("p (h n) -> p h n", n=64),
                        in1=ss[:, :].unsqueeze(2).to_broadcast([128, 8, 64]), op=ALU.mult), reads=[vp, ss],
                        writes=[aux])
                    em.dma("sp", P.auxs[0, pp:pp + 128, :], aux[:], reads=[aux], writes=[dummy])
                yield
                sg_ = sgd[d]
                pc = core.bank()
                em.mm(pc, pc[:, :], P.tri[:, d, :], sg_[:], True, True, reads=[P.tri, sg_])
                ptot = core.bank()
                em.mm(ptot, ptot[:, :], P.ones_f[:], sg_[:], True, True, reads=[P.ones_f, sg_])
                pw = core.bank()
                for h in range(8):
                    em.mm(pw, pw[0:64, h:h + 1], sg_[:, h * 64:(h + 1) * 64], P.ones_f[:, 0:1], True, True,
                          reads=[sg_, P.ones_f])
                em.act(WcT, WcT[:], pw[0:64, 0:8], AF.Exp, reads=[pw], scale=-C0)
                em.op("dve", lambda e: e.tensor_copy(out=t1[:], in_=pc[:, :]), reads=[pc], writes=[t1])
                em.op("dve", lambda e: e.tensor_tensor(out=t2[:], in0=t1[:], in1=sg_[:], op=ALU.subtract),
                      reads=[t1, sg_], writes=[t2])
                em.op("dve", lambda e: e.tensor_tensor(out=t3[:], in0=ptot[:, :], in1=t1[:], op=ALU.subtract),
                      reads=[ptot, t1], writes=[t3])
                yield
                em.act(E1, E1[:], t1[:], AF.Exp, reads=[t1], scale=-C0)
                em.act(E2, E2[:], t1[:], AF.Exp, reads=[t1], scale=C0)
                em.act(E1p, E1p[:], t2[:], AF.Exp, reads=[t2], scale=-C0)
                em.act(E3, E3[:], t3[:], AF.Exp, reads=[t3], scale=-C0)
                yield
                kd, al = kdir[d], alr[d]
                em.op("dve", lambda e: e.scalar_tensor_tensor(out=at[:], in0=kk[:], scalar=-1.0, in1=E1p[:],
                                                              op0=ALU.mult, op1=ALU.mult), reads=[kk, E1p], writes=[at])
                em.op("pool", lambda e: e.tensor_tensor(out=bb[:], in0=kk[:], in1=al[:], op=ALU.mult),
                      reads=[kk, al], writes=[bb])
                em.op("pool", lambda e: e.tensor_tensor(out=bt[:], in0=bb[:], in1=E2[:], op=ALU.mult),
                      reads=[bb, E2], writes=[bt])
                em.op("pool", lambda e: e.tensor_tensor(out=Bh[:], in0=bb[:], in1=E3[:], op=ALU.mult),
                      reads=[bb, E3], writes=[Bh])
                em.op("dve", lambda e: e.tensor_tensor(out=kt[:], in0=kd[:], in1=E2[:], op=ALU.mult),
                      reads=[kd, E2], writes=[kt])
                em.op("pool", lambda e: e.tensor_tensor(out=Kh[:], in0=kd[:], in1=E3[:], op=ALU.mult),
                      reads=[kd, E3], writes=[Kh])
                em.op("dve", lambda e: e.tensor_tensor(out=rt[:], in0=rp[:], in1=E1[:], op=ALU.mult),
                      reads=[rp, E1], writes=[rt])
                yield

            order = chunk_order(rev)
            for _ in prep(order[0], 0):
                pass
            for ci, t0 in enumerate(order):
                slot = ci % 2
                gnext = prep(order[ci + 1], 1 - slot) if ci + 1 < len(order) else None
                ops = {"ga": at, "gb": bt, "gk": kt, "gr": rt, "V": vp2[slot], "Atil": at, "Bh": Bh2[slot],
                       "Kh": Kh2[slot], "Rtil": rt}
                core.run_chunk(rev, ops, WcT2[slot], mk, P.yscr[d, ppos(t0):ppos(t0) + 128, :], filler=(gnext if PIPE else None))
                if gnext is not None:
                    for _ in gnext:
                        pass
        em.barrier()


def host_consts():
    idx = np.arange(128)
    r, c = idx[:, None], idx[None, :]
    m = np.zeros((128, 8, 128), np.float32)
    for i, cond in enumerate((c < r, c > r, c <= r, c >= r)):
        m[:, i, :] = cond.astype(np.float32)
        m[:, 4 + i, :] = np.where(cond, 0.0, NEG).astype(np.float32)
    tri = np.zeros((128, 2, 128), np.float32)
    tri[:, 0, :] = (r <= c)
    tri[:, 1, :] = (r >= c)
    bd = lambda b: ((r // b) == (c // b)).astype(np.float32)
    bmk = np.stack([bd(16), bd(32) - bd(16), bd(64) - bd(32), 1.0 - bd(64)], 1).astype(np.float32)
    return {"masks": m, "ident": np.eye(128, dtype=np.float32), "tri": tri, "bmasks": bmk}


def host_scan_inputs(inp, L):
    f32 = np.float32
    out = {}
    rw = np.zeros((L, 11, 512), f32)
    rw[:, 0] = inp["rwkv_w0"][:L, 0]
    rw[:, 1] = inp["rwkv_w0"][:L, 1]
    rw[:, 2] = inp["rwkv_a0"][:L, 0]
    rw[:, 3] = inp["rwkv_a0"][:L, 1]
    rw[:, 4] = inp["rwkv_k_k"][:L]
    rw[:, 5] = inp["rwkv_k_a"][:L]
    rw[:, 6] = inp["rwkv_r_k"][:L].reshape(L, 512)
    rw[:, 7] = inp["rwkv_gn_g"][:L]
    rw[:, 8] = inp["rwkv_gn_b"][:L]
    out["rw_row"] = rw
    mu = inp["rwkv_mu"][:L]
    out["rw_mu"] = np.ascontiguousarray(mu[:, 0:1536])
    muL = np.zeros((128, L, 3), f32)
    muL[0:64, :, 0] = mu[:, 1536:1600].T
    muL[0:64, :, 1] = mu[:, 1600:1664].T
    muL[0:96, :, 2] = mu[:, 1664:1760].T
    out["rw_muL"] = muL
    out["rw_w2"] = np.ascontiguousarray(inp["rwkv_w2"][:L].reshape(L, 64, 512))
    out["rw_a2"] = np.ascontiguousarray(inp["rwkv_a2"][:L].reshape(L, 64, 512))
    out["rw_g2"] = np.ascontiguousarray(inp["rwkv_g2"][:L])
    out["gd_conv"] = np.ascontiguousarray(inp["gdn_conv"][:L])
    gr = np.zeros((L, 3, 512), f32)
    gr[:, 0] = np.tile(inp["gdn_norm"][:L], (1, 4))
    gr[:, 1, 0:8] = inp["gdn_a_log"][:L].reshape(L, 8)
    gr[:, 1, 8:16] = inp["gdn_dt_bias"][:L].reshape(L, 8)
    out["gd_row"] = gr
    return out


def phase_gdn(P, l):
    em = P.em
    dummy = T()
    with ExitStack() as st:
        core = ScanCore(P, st, 128, "gd")
        f = lambda nm, shp=(128, 512): em.sb("gd_" + nm, list(shp), F32, st)
        cw = []
        for j in range(5):
            t = f("cw%d" % j, (128, 1536))
            em.dma("sp", t[:], P.gd_conv[l, j:j + 1, :].partition_broadcast(128), writes=[t])
            cw.append(t)
        prow = _bc_row(P, st, "gdp_row", P.gd_row[l, 1:2, :], 512)
        negea = f("negea", (128, 8))
        em.act(negea, negea[:], prow[:, 0:8], AF.Exp, reads=[prow])
        em.op("dve", lambda e: e.tensor_scalar_mul(out=negea[:], in0=negea[:], scalar1=-1.0), reads=[negea],
              writes=[negea])
        sh = [f("sh%d" % j) for j in range(5)]
        acc, tmp = f("acc"), f("tmp")
        qkv = [f("q"), f("k"), f("v")]
        ss = f("ss", (128, 4))
        ab = f("ab", (128, 16))
        gx, ge, gl = f("gx", (128, 8)), f("ge", (128, 8)), f("gl", (128, 8))
        gcol, beta = f("gcol", (128, 8)), f("beta", (128, 8))
        Gs, nG, eG, eTG, nb, nbeG = (f(n, (128, 4)) for n in ("Gs", "nG", "eG", "eTG", "nb", "nbeG"))
        ka, Atil, Rtil, zt = f("ka"), f("Atil"), f("Rtil"), f("zt")
        Kh2, Vp2 = [f("Kh0"), f("Kh1")], [f("Vp0"), f("Vp1")]
        etot2 = [f("etot0", (128, 4)), f("etot1", (128, 4))]
        diag = f("diag", (128, 4, 128))
        dtmp = f("dtmp", (128, 4, 128))
        Ds, DTs, DTi = f("Ds", (128, 4, 128)), f("DTs", (128, 4, 128)), f("DTi", (128, 4, 128))
        hv = lambda t: t[:].rearrange("p (h n) -> p h n", n=128)
        bc4 = lambda t: t[:, :].unsqueeze(2).to_broadcast([128, 4, 128])
        for d in (0, 1):
            rev = d == 1
            core.reset_state()
            mA, mAT, mATi = (4, 5, 7) if not rev else (5, 4, 6)
            mk = {"AT": (lambda g: DTs[:, :, :], DTs), "A": (lambda g: Ds[:, :, :], Ds),
                  "ArT": (lambda g: DTi[:, :, :], DTi)}
            def prep(t0, slot, d=d, mA=mA, mAT=mAT, mATi=mATi):
                Kh, Vp, etot = Kh2[slot], Vp2[slot], etot2[slot]
                pp = ppos(t0)
                for ci in range(3):
                    c0 = TOKC["gq"] + ci * 512
                    for j in range(5):
                        em.dma("sp", sh[j][:], P.pTok[pp + j - 2:pp + j - 2 + 128, c0:c0 + 512], writes=[sh[j]])
                    em.op("dve", lambda e: e.tensor_tensor(out=acc[:], in0=sh[0][:], in1=cw[0][:, ci * 512:(ci + 1) * 512],
                                                           op=ALU.mult), reads=[sh[0], cw[0]], writes=[acc])
                    for j in range(1, 5):
                        eng = "pool" if j % 2 else "dve"
                        em.op(eng, lambda e: e.tensor_tensor(out=sh[j][:], in0=sh[j][:],
                                                             in1=cw[j][:, ci * 512:(ci + 1) * 512], op=ALU.mult),
                              reads=[sh[j], cw[j]], writes=[sh[j]])
                        em.op("dve", lambda e: e.tensor_tensor(out=acc[:], in0=acc[:], in1=sh[j][:], op=ALU.add),
                              reads=[acc, sh[j]], writes=[acc])
                    em.act(qkv[ci], qkv[ci][:], acc[:], AF.Silu, reads=[acc])
                    yield
                for ci, sc in ((0, 128 ** -0.5), (1, 1.0)):
                    x_ = qkv[ci]
                    em.act(tmp, tmp[:], x_[:], AF.Square, reads=[x_])
                    em.op("dve", lambda e: e.tensor_reduce(out=ss[:], in_=hv(tmp), axis=AX.X, op=ALU.add),
                          reads=[tmp], writes=[ss])
                    em.act(ss, ss[:], ss[:], AF.Sqrt, reads=[ss, P.epsc], bias=P.epsc[:, 3:4])
                    em.op("dve", lambda e: e.reciprocal(out=ss[:], in_=ss[:]), reads=[ss], writes=[ss])
                    if sc != 1.0:
                        em.op("dve", lambda e: e.tensor_scalar_mul(out=ss[:], in0=ss[:], scalar1=sc), reads=[ss],
                              writes=[ss])
                    em.op("dve", lambda e: e.tensor_tensor(out=hv(x_), in0=hv(x_), in1=bc4(ss), op=ALU.mult),
                          reads=[x_, ss], writes=[x_])
                q_, k_, v_ = qkv
                if d == 0:
                    em.dma("sp", zt[:], P.pTok[pp:pp + 128, TOKC["z"]:TOKC["z"] + 512], writes=[zt])
                    em.act(zt, zt[:], zt[:], AF.Silu, reads=[zt])
                    em.dma("sp", P.auxs[2, pp:pp + 128, :], zt[:], reads=[zt], writes=[dummy])
                yield
                em.dma("sp", ab[:], P.pTok[pp:pp + 128, TOKC["ab"]:TOKC["ab"] + 16], writes=[ab])
                em.op("dve", lambda e: e.tensor_tensor(out=gx[:], in0=ab[:, 0:8], in1=prow[:, 8:16], op=ALU.add),
                      reads=[ab, prow], writes=[gx])
                em.act(ge, ge[:], gx[:], AF.Abs, reads=[gx])
                em.act(ge, ge[:], ge[:], AF.Exp, reads=[ge], scale=-1.0)
                em.act(gl, gl[:], ge[:], AF.Ln, reads=[ge, P.ones_f], bias=P.ones_f[:, 0:1])
                em.op("dve", lambda e: e.scalar_tensor_tensor(out=gcol[:], in0=gx[:], scalar=0.0, in1=gl[:],
                                                              op0=ALU.max, op1=ALU.add), reads=[gx, gl], writes=[gcol])
                em.op("dve", lambda e: e.tensor_tensor(out=gcol[:], in0=gcol[:], in1=negea[:], op=ALU.mult),
                      reads=[gcol, negea], writes=[gcol])
                em.act(beta, beta[:], ab[:, 8:16], AF.Sigmoid, reads=[ab])
                gd_, bd_ = gcol[:, d * 4:(d + 1) * 4], beta[:, d * 4:(d + 1) * 4]
                pG = core.bank()
                em.mm(pG, pG[:, 0:4], P.tri[:, d, :], gd_, True, True, reads=[P.tri, gcol])
                pT_ = core.bank()
                em.mm(pT_, pT_[:, 0:4], P.ones_f[:], gd_, True, True, reads=[P.ones_f, gcol])
                em.op("dve", lambda e: e.tensor_copy(out=Gs[:], in_=pG[:, 0:4]), reads=[pG], writes=[Gs])
                em.op("dve", lambda e: e.tensor_scalar_mul(out=nG[:], in0=Gs[:], scalar1=-1.0), reads=[Gs], writes=[nG])
                em.act(eG, eG[:], Gs[:], AF.Exp, reads=[Gs])
                em.op("dve", lambda e: e.tensor_copy(out=etot[:], in_=pT_[:, 0:4]), reads=[pT_], writes=[etot])
                em.op("dve", lambda e: e.tensor_tensor(out=eTG[:], in0=etot[:], in1=Gs[:], op=ALU.subtract),
                      reads=[etot, Gs], writes=[eTG])
                em.act(etot, etot[:], etot[:], AF.Exp, reads=[etot])
                em.act(eTG, eTG[:], eTG[:], AF.Exp, reads=[eTG])
                yield
                em.op("dve", lambda e: e.tensor_scalar_mul(out=nb[:], in0=bd_, scalar1=-1.0), reads=[beta], writes=[nb])
                em.op("dve", lambda e: e.tensor_tensor(out=nbeG[:], in0=nb[:], in1=eG[:], op=ALU.mult),
                      reads=[nb, eG], writes=[nbeG])
                em.op("dve", lambda e: e.tensor_tensor(out=hv(ka), in0=hv(k_), in1=bc4(nb), op=ALU.mult),
                      reads=[k_, nb], writes=[ka])
                em.op("pool", lambda e: e.tensor_tensor(out=hv(Atil), in0=hv(k_), in1=bc4(nbeG), op=ALU.mult),
                      reads=[k_, nbeG], writes=[Atil])
                em.op("dve", lambda e: e.tensor_tensor(out=hv(Kh), in0=hv(k_), in1=bc4(eTG), op=ALU.mult),
                      reads=[k_, eTG], writes=[Kh])
                em.op("pool", lambda e: e.tensor_tensor(out=hv(Rtil), in0=hv(q_), in1=bc4(eG), op=ALU.mult),
                      reads=[q_, eG], writes=[Rtil])
                em.op("dve", lambda e: e.tensor_tensor(out=hv(Vp), in0=hv(v_),
                                                       in1=beta[:, d * 4:(d + 1) * 4].unsqueeze(2).to_broadcast([128, 4, 128]),
                                                       op=ALU.mult), reads=[v_, beta], writes=[Vp])
                yield
                em.op("dve", lambda e: e.tensor_tensor(out=diag[:], in0=P.ident[:, :].unsqueeze(1).to_broadcast([128, 4, 128]),
                                                       in1=bc4(Gs), op=ALU.mult), reads=[P.ident, Gs], writes=[diag])
                pR = core.bank()
                for h in range(4):
                    em.mm(pR, pR[:, h * 128:(h + 1) * 128], P.ones_f[:], diag[:, h, :], True, True,
                          reads=[P.ones_f, diag])
                pRv = pR[:, :].rearrange("p (h t) -> p h t", h=4)
                for dst, sgn, mi, bias_t in ((Ds, -1.0, mA, Gs), (DTs, 1.0, mAT, nG), (DTi, 1.0, mATi, nG)):
                    em.op("dve", lambda e: e.scalar_tensor_tensor(
                        out=dtmp[:], in0=pRv, scalar=sgn, in1=P.masks[:, mi:mi + 1, :].to_broadcast([128, 4, 128]),
                        op0=ALU.mult, op1=ALU.add), reads=[pR, P.masks], writes=[dtmp])
                    for h in range(4):
                        em.act(dst, dst[:, h, :], dtmp[:, h, :], AF.Exp, reads=[dtmp, bias_t],
                               bias=bias_t[:, h:h + 1], writes=[dst])
                yield

            order = chunk_order(rev)
            for _ in prep(order[0], 0):
                pass
            k_, q_ = qkv[1], qkv[0]
            for ci, t0 in enumerate(order):
                slot = ci % 2
                gnext = prep(order[ci + 1], 1 - slot) if ci + 1 < len(order) else None
                ops = {"ga": ka, "gb": k_, "gk": k_, "gr": q_, "V": Vp2[slot], "Atil": Atil, "Bh": Kh2[slot],
                       "Kh": Kh2[slot], "Rtil": Rtil}
                core.run_chunk(rev, ops, etot2[slot], mk, P.yscr[2 + d, ppos(t0):ppos(t0) + 128, :],
                               filler=(gnext if PIPE else None))
                if gnext is not None:
                    for _ in gnext:
                        pass
        em.barrier()


def phase_mix_out(P, l, with_ctx):
    em = P.em
    dummy = T()
    with ExitStack() as st:
        f = lambda nm, shp=(128, 512): em.sb("mo_" + nm, list(shp), F32, st)
        gn_g = _bc_row(P, st, "mo_gng", P.rw_row[l, 7:8, :])
        gn_b = _bc_row(P, st, "mo_gnb", P.rw_row[l, 8:9, :])
        nrm = _bc_row(P, st, "mo_nrm", P.gd_row[l, 0:1, :])
        ya, yb, bv, gg, cen, sq = f("ya"), f("yb"), f("bv"), f("gg"), f("cen"), f("sq")
        s8 = f("s8", (128, 8))
        ob = [em.sb("mo_ob%d" % i, [128, 4, 128], BF16, st) for i in range(2)]
        pb = [em.ps("mo_pb%d" % i, [128, 512], F32, st) for i in range(2)]
        cnt = 0
        tiles = ([128 * i for i in range(CTX // 128)] if with_ctx else []) + \
            [CTX + 128 * i for i in range(SEQ // 128)]
        for t0 in tiles:
            pp = ppos(t0)
            for mix, (ia, ib, nh, eps_col) in enumerate(((0, 1, 8, 2), (2, 3, 4, 1))):
                n = 512 // nh
                hv = lambda t: t[:].rearrange("p (h n) -> p h n", n=n)
                bc = lambda t: t[:, 0:nh].unsqueeze(2).to_broadcast([128, nh, n])
                em.dma("sp", ya[:], P.yscr[ia, pp:pp + 128, :], writes=[ya])
                em.dma("sp", yb[:], P.yscr[ib, pp:pp + 128, :], writes=[yb])
                em.op("dve", lambda e: e.tensor_tensor(out=ya[:], in0=ya[:], in1=yb[:], op=ALU.add),
                      reads=[ya, yb], writes=[ya])
                if mix == 0:
                    em.dma("sp", bv[:], P.auxs[0, pp:pp + 128, :], writes=[bv])
                    em.dma("sp", gg[:], P.auxs[1, pp:pp + 128, :], writes=[gg])
                    em.op("dve", lambda e: e.tensor_reduce(out=s8[:, 0:nh], in_=hv(ya), axis=AX.X, op=ALU.add),
                          reads=[ya], writes=[s8])
                    em.op("dve", lambda e: e.tensor_scalar_mul(out=s8[:, 0:nh], in0=s8[:, 0:nh], scalar1=1.0 / n),
                          reads=[s8], writes=[s8])
                    em.op("dve", lambda e: e.tensor_tensor(out=hv(cen), in0=hv(ya), in1=bc(s8), op=ALU.subtract),
                          reads=[ya, s8], writes=[cen])
                else:
                    em.dma("sp", gg[:], P.auxs[2, pp:pp + 128, :], writes=[gg])
                    em.op("dve", lambda e: e.tensor_copy(out=cen[:], in_=ya[:]), reads=[ya], writes=[cen])
                em.act(sq, sq[:], cen[:], AF.Square, reads=[cen])
                em.op("dve", lambda e: e.tensor_reduce(out=s8[:, 0:nh], in_=hv(sq), axis=AX.X, op=ALU.add),
                      reads=[sq], writes=[s8])
                em.act(s8, s8[:, 0:nh], s8[:, 0:nh], AF.Sqrt, reads=[s8, P.epsc], bias=P.epsc[:, eps_col:eps_col + 1],
                       scale=1.0 / n)
                em.op("dve", lambda e: e.reciprocal(out=s8[:, 0:nh], in_=s8[:, 0:nh]), reads=[s8], writes=[s8])
                em.op("dve", lambda e: e.tensor_tensor(out=hv(cen), in0=hv(cen), in1=bc(s8), op=ALU.mult),
                      reads=[cen, s8], writes=[cen])
                if mix == 0:
                    em.op("pool", lambda e: e.tensor_tensor(out=cen[:], in0=cen[:], in1=gn_g[:], op=ALU.mult),
                          reads=[cen, gn_g], writes=[cen])
                    em.op("pool", lambda e: e.tensor_tensor(out=cen[:], in0=cen[:], in1=gn_b[:], op=ALU.add),
                          reads=[cen, gn_b], writes=[cen])
                    em.op("pool", lambda e: e.tensor_tensor(out=cen[:], in0=cen[:], in1=bv[:], op=ALU.add),
                          reads=[cen, bv], writes=[cen])
                else:
                    em.op("pool", lambda e: e.tensor_tensor(out=cen[:], in0=cen[:], in1=nrm[:], op=ALU.mult),
                          reads=[cen, nrm], writes=[cen])
                em.op("dve", lambda e: e.tensor_tensor(out=cen[:], in0=cen[:], in1=gg[:], op=ALU.mult),
                      reads=[cen, gg], writes=[cen])
                p_ = pb[cnt % 2]
                o_ = ob[cnt % 2]
                cnt += 1
                for j in range(4):
                    em.op("pe", lambda e: e.transpose(p_[:, j * 128:(j + 1) * 128], cen[:, j * 128:(j + 1) * 128],
                                                      P.ident[:]), reads=[cen, P.ident], writes=[p_])
                em.act(o_, o_[:], p_[:, :].rearrange("p (j t) -> p j t", j=4), AF.Copy, reads=[p_])
                r0 = 1024 + mix * 512
                em.dma("sp", P.mixT[r0:r0 + 512, t0:t0 + 128].rearrange("(j p) t -> p j t", p=128), o_[:],
                       reads=[o_], writes=[dummy])
        em.barrier()


def declare_mla(P):
    L = P.nl
    P.w_uq = P.din("mla_w_uq", [L, 512, 1536])
    P.w_uq_sw = P.din("mla_w_uq_sw", [L, 512, 512])
    P.w_ukv = P.din("mla_w_ukv", [L, 512, 2048])
    P.mlaT = P.din("mlaT", [128, L, 2, 4])
    P.ropeT = P.din("ropeT", [64, 2, SEQ])
    P.wb_uq = P.dscr("wb_uq", [L, 512, 2048], BF16)
    P.wb_ukv = P.dscr("wb_ukv", [L, 512, 2048], BF16)
    P.Kn = P.dscr("Kn", [1024, TT], BF16)
    P.Kr = P.dscr("Kr", [128, TT], BF16)
    P.sel64_in = P.din("sel64", [128, 1])
    P.Vt = P.dscr("Vt", [TT, 1024], BF16)
    P.Qn = P.dscr("Qn", [1024, TT], BF16)
    P.Qr = P.dscr("Qr", [8, 128, TT], BF16)


def phase_cast_mla(P):
    em = P.em
    P.wb_mla_t = [T() for _ in range(P.nl)]
    for l in range(P.nl):
        d = P.wb_mla_t[l]
        em.dma("pool", P.wb_uq[l, :, 0:1536], P.w_uq[l, :, :], writes=[d], partial=True)
        em.dma("pool", P.wb_uq[l, :, 1536:2048], P.w_uq_sw[l, :, :], writes=[d], partial=True)
        em.dma("pool", P.wb_ukv[l, :, :], P.w_ukv[l, :, :], writes=[d], partial=True)


def phase_mla(P, l, with_ctx):
    em = P.em
    dummy = T()
    with ExitStack() as st:
        f = lambda nm, shp, dt=F32: em.sb("ml_" + nm, list(shp), dt, st)
        gains = f("gains", (128, 2, 4))
        em.dma("sp", gains[:], P.mlaT[:, l, :, :], writes=[gains])
        wkv = f("wkv", (128, 4, 2048), BF16)
        wq = f("wq", (128, 4, 2048), BF16)
        wmt = getattr(P, "wb_mla_t", None)
        wrd = [wmt[l]] if wmt else []
        em.dma("sp", wkv[:], P.wb_ukv[l].rearrange("(k p) c -> p k c", p=128), reads=wrd, writes=[wkv])
        em.dma("sp", wq[:], P.wb_uq[l].rearrange("(k p) c -> p k c", p=128), reads=wrd, writes=[wq])
        wvv = f("wvv", (128, 4, 1024), BF16)
        for k in range(4):
            em.dma("sp", wvv[:, k, :].rearrange("p (h v) -> p h v", v=128),
                   P.wb_ukv[l, k * 128:(k + 1) * 128, :].rearrange("p (h two v) -> p h two v", two=2, v=128)[:, :, 1, :],
                   reads=wrd, writes=[wvv], partial=True)
        cx = f("cx", (128, 4, 512))
        cn = f("cn", (128, 4, 512), BF16)
        sq = [f("sq%d" % i, (128, 512)) for i in range(2)]
        rstd = f("rstd", (128, 512))
        krt, ksw, kro = f("krt", (64, 512)), f("ksw", (64, 512)), f("kro", (64, 512))
        rope = f("rope", (64, 2, 512))
        krb = f("krb", (128, 512), BF16)
        sel = f("sel", (128, 1))
        em.dma("sp", sel[:], P.sel64_in[:, :], writes=[sel])
        zt = f("zt", (128, 512))
        sqr = f("sqr", (128, 512), BF16)
        em.op("dve", lambda e: e.memset(sqr[:], 0.0), writes=[sqr])
        sqb = f("sqb", (128, 512), BF16)
        em.op("dve", lambda e: e.memset(zt[:], 0.0), writes=[zt])
        kmax = f("kmax", (128, 8))
        bmax = f("bmax", (128, 1))
        ob = [f("ob%d" % i, (128, 512), BF16) for i in range(3)]
        qrb = [f("qrb%d" % i, (128, 512), BF16) for i in range(2)]
        qr32, qs32 = f("qr32", (64, 512)), f("qs32", (64, 512))
        ps = [em.ps("ml_ps%d" % i, [128, 512], F32, st) for i in range(6)]
        pi = [0]
        oi = [0]

        def bank():
            pi[0] += 1
            return ps[pi[0] % 6]

        def obuf():
            oi[0] += 1
            return ob[oi[0] % 3]

        em.op("dve", lambda e: e.memset(kmax[:], 0.0), writes=[kmax])
        em.op("dve", lambda e: e.tensor_scalar(out=krb[64:128, :], in0=zt[64:128, :], scalar1=sel[64:128, 0:1],
                                               scalar2=None, op0=ALU.add), reads=[zt, sel], writes=[krb], partial=True)

        def rmsnorm_block(row0, gi, pp, n):
            em.dma("sp", cx[:, :, 0:n], P.pT[row0:row0 + 512, pp:pp + n].rearrange("(k p) t -> p k t", p=128),
                   writes=[cx])
            pss = bank()
            for k in range(4):
                s_ = sq[k % 2]
                em.act(s_, s_[:, 0:n], cx[:, k, 0:n], AF.Square, reads=[cx])
                em.mm(pss, pss[:, 0:n], P.ones_f[:], s_[:, 0:n], k == 0, k == 3, reads=[P.ones_f, s_])
            em.act(rstd, rstd[:, 0:n], pss[:, 0:n], AF.Sqrt, reads=[pss, P.epsc], bias=P.epsc[:, 1:2],
                   scale=1.0 / 512)
            em.op("dve", lambda e: e.reciprocal(out=rstd[:, 0:n], in_=rstd[:, 0:n]), reads=[rstd], writes=[rstd])
            for k in range(4):
                em.op("dve", lambda e: e.scalar_tensor_tensor(
                    out=cn[:, k, 0:n], in0=cx[:, k, 0:n], scalar=gains[:, gi, k:k + 1], in1=rstd[:, 0:n],
                    op0=ALU.mult, op1=ALU.mult), reads=[cx, gains, rstd], writes=[cn], partial=True)

        def load_rope(t0, n):
            em.dma("sp", rope[:, :, 0:n], P.ropeT[:, :, t0 - CTX:t0 - CTX + n], writes=[rope])

        def apply_rope(dst, x_, xsw, n, isctx):
            if isctx:
                em.op("dve", lambda e: e.tensor_copy(out=dst[0:64, 0:n], in_=x_[0:64, 0:n]), reads=[x_], writes=[dst])
                return
            em.op("dve", lambda e: e.tensor_tensor(out=dst[0:64, 0:n], in0=x_[0:64, 0:n], in1=rope[:, 0, 0:n],
                                                   op=ALU.mult), reads=[x_, rope], writes=[dst])
            em.op("pool", lambda e: e.tensor_tensor(out=xsw[0:64, 0:n], in0=xsw[0:64, 0:n], in1=rope[:, 1, 0:n],
                                                    op=ALU.mult), reads=[xsw, rope], writes=[xsw])
            em.op("dve", lambda e: e.tensor_tensor(out=dst[0:64, 0:n], in0=dst[0:64, 0:n], in1=xsw[0:64, 0:n],
                                                   op=ALU.add), reads=[dst, xsw], writes=[dst])

        for (t0, n, isctx) in TBLK:
            pp = ppos(t0)
            rmsnorm_block(512, 1, pp, n)
            if not isctx:
                load_rope(t0, n)
            em.dma("sp", krt[:, 0:n], P.pT[8 * 128:8 * 128 + 64, pp:pp + n], writes=[krt])
            em.dma("sp", ksw[:, 0:n], P.pT[9 * 128:9 * 128 + 64, pp:pp + n], writes=[ksw])
            apply_rope(kro, krt, ksw, n, isctx)
            em.act(krb, krb[0:64, 0:n], kro[0:64, 0:n], AF.Copy, reads=[kro], writes=[krb])
            em.dma("pool", P.Kr[:, t0:t0 + n], krb[:, 0:n], reads=[krb], writes=[dummy])
            em.act(sqr, sqr[0:64, 0:n], kro[0:64, 0:n], AF.Square, reads=[kro])
            for h in range(8):
                pk = bank()
                for k in range(4):
                    em.mm(pk, pk[:, 0:n], wkv[:, k, h * 256:h * 256 + 128], cn[:, k, 0:n], k == 0, k == 3,
                          reads=[wkv, cn])
                o_ = obuf()
                em.op("dve", lambda e: e.tensor_copy(out=o_[:, 0:n], in_=pk[:, 0:n]), reads=[pk], writes=[o_])
                em.dma("pool", P.Kn[h * 128:(h + 1) * 128, t0:t0 + n], o_[:, 0:n], reads=[o_], writes=[dummy])
                em.act(sqb, sqb[:, 0:n], o_[:, 0:n], AF.Square, reads=[o_])
                pn = bank()
                em.mm(pn, pn[:, 0:n], P.ones_b[:], sqb[:, 0:n], True, False, reads=[P.ones_b, sqb])
                em.mm(pn, pn[:, 0:n], P.ones_b[:], sqr[:, 0:n], False, True, reads=[P.ones_b, sqr])
                em.op("dve", lambda e: e.tensor_reduce(out=bmax[:], in_=pn[:, 0:n], axis=AX.X, op=ALU.max),
                      reads=[pn], writes=[bmax])
                em.op("dve", lambda e: e.tensor_tensor(out=kmax[:, h:h + 1], in0=kmax[:, h:h + 1], in1=bmax[:],
                                                       op=ALU.max), reads=[kmax, bmax], writes=[kmax])
            for tt in range(n // 128):
                for g in range(2):
                    pv = bank()
                    for k in range(4):
                        em.mm(pv, pv[:, :], cn[:, k, tt * 128:(tt + 1) * 128], wvv[:, k, g * 512:(g + 1) * 512],
                              k == 0, k == 3, reads=[cn, wvv])
                    o_ = obuf()
                    em.act(o_, o_[:], pv[:, :], AF.Copy, reads=[pv])
                    em.dma("pool", P.Vt[t0 + tt * 128:t0 + (tt + 1) * 128, g * 512:(g + 1) * 512], o_[:],
                           reads=[o_], writes=[dummy])
        if getattr(P, "mla_stage", 9) < 1:
            em.barrier()
            return
        nkm = f("nkm", (128, 8))
        em.act(nkm, nkm[:], kmax[:], AF.Sqrt, reads=[kmax])
        em.op("dve", lambda e: e.tensor_scalar_mul(out=nkm[:], in0=nkm[:], scalar1=-1.0), reads=[nkm], writes=[nkm])
        qcnt = 0
        for (t0, n, isctx) in TBLK:
            if isctx and not with_ctx:
                continue
            pp = ppos(t0)
            rmsnorm_block(0, 0, pp, n)
            if not isctx:
                load_rope(t0, n)
            for h in range(8):
                pq, pr, pw = bank(), bank(), bank()
                for k in range(4):
                    em.mm(pq, pq[:, 0:n], wq[:, k, h * 192:h * 192 + 128], cn[:, k, 0:n], k == 0, k == 3,
                          reads=[wq, cn])
                for k in range(4):
                    em.mm(pr, pr[0:64, 0:n], wq[:, k, h * 192 + 128:h * 192 + 192], cn[:, k, 0:n], k == 0, k == 3,
                          reads=[wq, cn])
                if not isctx:
                    for k in range(4):
                        em.mm(pw, pw[0:64, 0:n], wq[:, k, 1536 + h * 64:1536 + (h + 1) * 64], cn[:, k, 0:n],
                              k == 0, k == 3, reads=[wq, cn])
                    em.op("dve", lambda e: e.tensor_copy(out=qs32[0:64, 0:n], in_=pw[0:64, 0:n]), reads=[pw],
                          writes=[qs32])
                em.act(qr32, qr32[0:64, 0:n], pr[0:64, 0:n], AF.Copy, reads=[pr])
                apply_rope(kro, qr32, qs32, n, isctx)
                em.act(sqb, sqb[:, 0:n], pq[:, 0:n], AF.Square, reads=[pq])
                em.act(sqr, sqr[0:64, 0:n], kro[0:64, 0:n], AF.Square, reads=[kro])
                pn = bank()
                em.mm(pn, pn[:, 0:n], P.ones_b[:], sqb[:, 0:n], True, False, reads=[P.ones_b, sqb])
                em.mm(pn, pn[:, 0:n], P.ones_b[:], sqr[:, 0:n], False, True, reads=[P.ones_b, sqr])
                qb = qrb[qcnt % 2]
                qcnt += 1
                em.act(sq[1], sq[1][64:128, 0:n], pn[64:128, 0:n], AF.Sqrt, reads=[pn],
                       scale=ATTN_SCALE * ATTN_SCALE)
                em.op("dve", lambda e: e.tensor_scalar(out=qb[64:128, 0:n], in0=sq[1][64:128, 0:n],
                                                       scalar1=nkm[64:128, h:h + 1], scalar2=sel[64:128, 0:1],
                                                       op0=ALU.mult, op1=ALU.mult),
                      reads=[sq[1], nkm, sel], writes=[qb], partial=True)
                em.act(qb, qb[0:64, 0:n], kro[0:64, 0:n], AF.Copy, reads=[kro], scale=ATTN_SCALE, writes=[qb])
                em.dma("pool", P.Qr[h, :, t0:t0 + n], qb[:, 0:n], reads=[qb], writes=[dummy])
                o_ = obuf()
                em.act(o_, o_[:, 0:n], pq[:, 0:n], AF.Copy, reads=[pq], scale=ATTN_SCALE)
                em.dma("pool", P.Qn[h * 128:(h + 1) * 128, t0:t0 + n], o_[:, 0:n], reads=[o_], writes=[dummy])
        em.barrier()
    if getattr(P, "mla_stage", 9) < 2:
        return
    with ExitStack() as st:
        f = lambda nm, shp, dt=BF16: em.sb("at_" + nm, list(shp), dt, st)
        NKT = TT // 128
        kr = f("kr", (128, TT))
        em.dma("sp", kr[:], P.Kr[:, :], writes=[kr])
        kn = [f("kn%d" % i, (128, TT)) for i in range(2)]
        vv = [f("vv%d" % i, (128, NKT, 128)) for i in range(2)]
        qn = [f("qn%d" % i, (128, 512)) for i in range(2)]
        qr = [f("qr%d" % i, (128, 512)) for i in range(2)]
        pt = [f("pt%d" % i, (128, 512)) for i in range(3)]
        rd = [f("rd%d" % i, (128, 512), F32) for i in range(2)]
        ao = [f("ao%d" % i, (128, 512)) for i in range(2)]
        pS = [em.ps("at_pS%d" % i, [128, 512], F32, st) for i in range(3)]
        pO = [em.ps("at_pO%d" % i, [128, 512], F32, st) for i in range(2)]
        pD = [em.ps("at_pD%d" % i, [128, 512], F32, st) for i in range(2)]
        sc = 0
        qc = 0
        for h in range(8):
            k_n, v_ = kn[h % 2], vv[h % 2]
            em.dma("sp", k_n[:], P.Kn[h * 128:(h + 1) * 128, :], writes=[k_n])
            em.dma("sp", v_[:], P.Vt[:, h * 128:(h + 1) * 128].rearrange("(c p) v -> p c v", p=128), writes=[v_])
            for (t0, n, isctx) in TBLK:
                if isctx and not with_ctx:
                    continue
                q_n, q_r = qn[qc % 2], qr[qc % 2]
                p_O, p_D, r_d, a_o = pO[qc % 2], pD[qc % 2], rd[qc % 2], ao[qc % 2]
                qc += 1
                em.dma("sp", q_n[:, 0:n], P.Qn[h * 128:(h + 1) * 128, t0:t0 + n], writes=[q_n])
                em.dma("sp", q_r[:, 0:n], P.Qr[h, :, t0:t0 + n], writes=[q_r])
                nkt = CTX // 128 if isctx else NKT
                LOOK = 2
                ring = {}

                def scores(kt):
                    nonlocal sc
                    p_S, p_t = pS[sc % 3], pt[sc % 3]
                    sc += 1
                    ks = slice(kt * 128, (kt + 1) * 128)
                    em.mm(p_S, p_S[:, 0:n], k_n[:, ks], q_n[:, 0:n], True, False, reads=[k_n, q_n])
                    em.mm(p_S, p_S[:, 0:n], kr[:, ks], q_r[:, 0:n], False, True, reads=[kr, q_r])
                    em.act(p_t, p_t[:, 0:n], p_S[:, 0:n], AF.Exp, reads=[p_S])
                    ring[kt] = p_t

                for kt in range(min(LOOK, nkt)):
                    scores(kt)
                for kt in range(nkt):
                    p_t = ring.pop(kt)
                    em.mm(p_O, p_O[:, 0:n], v_[:, kt, :], p_t[:, 0:n], kt == 0, kt == nkt - 1, reads=[v_, p_t])
                    em.mm(p_D, p_D[:, 0:n], P.ones_b[:], p_t[:, 0:n], kt == 0, kt == nkt - 1,
                          reads=[P.ones_b, p_t])
                    if kt + LOOK < nkt:
                        scores(kt + LOOK)
                em.op("dve", lambda e: e.reciprocal(out=r_d[:, 0:n], in_=p_D[:, 0:n]), reads=[p_D], writes=[r_d])
                em.op("dve", lambda e: e.tensor_tensor(out=a_o[:, 0:n], in0=p_O[:, 0:n], in1=r_d[:, 0:n],
                                                       op=ALU.mult), reads=[p_O, r_d], writes=[a_o])
                em.dma("sp", P.mixT[h * 128:(h + 1) * 128, t0:t0 + n], a_o[:, 0:n], reads=[a_o], writes=[dummy])
        em.barrier()


def host_mla_inputs(inp, L, seq):
    f32 = np.float32
    perm = np.arange(64).reshape(2, 2, 16)[:, ::-1, :].reshape(64)
    out = {}
    out["mla_w_uq"] = inp["mla_w_uq"][:L]
    cols = np.concatenate([h * 192 + 128 + perm for h in range(8)])
    out["mla_w_uq_sw"] = np.ascontiguousarray(inp["mla_w_uq"][:L][:, :, cols])
    out["mla_w_ukv"] = inp["mla_w_ukv"][:L]
    g = np.stack([inp["mla_q_norm"][:L].reshape(L, 4, 128), inp["mla_kv_norm"][:L].reshape(L, 4, 128)], 1)
    out["mlaT"] = np.ascontiguousarray(g.transpose(3, 0, 1, 2)).astype(f32)
    t = np.arange(seq)
    pos = np.stack([t // 64, t % 64], -1).astype(f32)
    inv = (10000.0 ** (-np.arange(16, dtype=f32) / 16)).astype(f32)
    ang = pos[..., None] * inv
    cos, sin = np.cos(ang), np.sin(ang)
    ct = np.zeros((64, seq), f32)
    stb = np.zeros((64, seq), f32)
    for a in range(2):
        for half in range(2):
            r0 = a * 32 + half * 16
            ct[r0:r0 + 16] = cos[:, a, :].T
            stb[r0:r0 + 16] = (sin[:, a, :].T) * (-1.0 if half == 0 else 1.0)
    out["ropeT"] = np.ascontiguousarray(np.stack([ct, stb], 1))
    return out, perm


def build_program(nl=DEPTH, dbg=()):
    P = Prog(nl=nl, dbg=dbg)
    em = P.em
    declare_io(P)
    declare_dense(P)
    declare_scan(P)
    declare_mla(P)
    setup_consts(P)
    setup_scan_consts(P)
    phase_zero_pads(P)
    phase_cast_in(P)
    phase_cast_dense(P)
    phase_cast_mla(P)
    phase_mod(P)
    nblk = len(TBLK)
    for l in range(nl):
        with_ctx = l < nl - 1
        xsrc = P.xT0 if l == 0 else P.xB
        ft = lambda: [T() for _ in range(nblk)]
        sel_ = getattr(build_program, "phases", "imrgodf")
        if "i" in sel_:
            phase_inproj(P, l, xsrc, ft())
        if "m" in sel_:
            phase_mla(P, l, with_ctx)
        if "r" in sel_:
            phase_rwkv(P, l)
        if "g" in sel_:
            phase_gdn(P, l)
        if "o" in sel_:
            phase_mix_out(P, l, with_ctx)
        P.mixT_t = ft()
        if "d" in sel_:
            phase_outproj(P, l, xsrc, ft(), P.xA, ft(), with_ctx)
        if "f" in sel_:
            phase_ffn(P, l, P.xA, ft(), P.xB, ft(), with_ctx, final=(l == nl - 1))
    em.barrier()
    return P


def host_inputs(inp, b, nl, seq):
    f32 = np.float32
    L = nl
    x = np.asarray(inp["x"][b][:seq], f32)
    ctx = np.asarray(inp["ctx"][b], f32)
    im = {}
    im["xT0"] = np.ascontiguousarray(np.concatenate([ctx, x], 0).T)
    im["cvec"] = np.ascontiguousarray(np.stack([np.asarray(inp["c"][b]).reshape(16, 128).T,
                                                np.asarray(inp["c_ctx"]).reshape(16, 128).T], -1).astype(f32))
    im["w_mod"] = np.asarray(inp["w_mod"][:L], f32)
    im["b_modT"] = np.ascontiguousarray(np.asarray(inp["b_mod"][:L], f32).reshape(L, 96, 128).transpose(2, 0, 1))
    im["w_in"] = np.asarray(inp["w_in"][:L], f32)
    mi, perm = host_mla_inputs(inp, L, seq)
    im["w_in_krsw"] = np.ascontiguousarray(im["w_in"][:, :, 1024 + perm])
    im.update(mi)
    e = np.zeros((128, 1), f32)
    e[64] = 1.0
    im["sel64"] = e
    im["w_out"] = np.asarray(inp["w_out"][:L], f32)
    im["ffn_w_gate"] = np.asarray(inp["ffn_w_gate"][:L], f32)
    im["ffn_w_up"] = np.asarray(inp["ffn_w_up"][:L], f32)
    im["ffn_w_down"] = np.asarray(inp["ffn_w_down"][:L], f32)
    lnT = np.stack([np.asarray(inp[k][:L], f32).reshape(L, 16, 128) for k in ("ln1_g", "ln1_b", "ln2_g", "ln2_b")], 1)
    im["lnT"] = np.ascontiguousarray(lnT.transpose(3, 0, 1, 2))
    im.update(host_consts())
    im.update(host_scan_inputs(inp, L))
    return im


def kernel(**inputs):
    nb = inputs["x"].shape[0]
    P = build_program(DEPTH)
    shared = None
    in_maps = []
    for b in range(nb):
        im = host_inputs(inputs, b, DEPTH, SEQ)
        if shared is None:
            shared = im
        else:
            for k in im:
                if k not in ("xT0", "cvec"):
                    im[k] = shared[k]
        in_maps.append({k: v for k, v in im.items() if k in P.inputs})
    res = run_bass_kernel_spmd(P.nc, in_maps, core_ids=list(range(nb)))
    out = np.stack([np.ascontiguousarray(np.asarray(r["xOut"], np.float32).T) for r in res.results], 0)
    return out
```

```python
import math
from contextlib import ExitStack

import numpy as np
import concourse.bass as bass
import concourse.mybir as mybir
from concourse.bass_utils import run_bass_kernel_spmd

F32 = mybir.dt.float32
BF16 = mybir.dt.bfloat16
AF = mybir.ActivationFunctionType
ALU = mybir.AluOpType
AX = mybir.AxisListType

D = 2048
KD = D // 128
SEQ = 4096
CTX = 256
TT = SEQ + CTX
DEPTH = 4
D_FF = 5632
KF = D_FF // 128
IN_COLS = 4912
ALPHA = (2.0 * DEPTH) ** 0.25
ATTN_SCALE = 192 ** -0.5

CH = []
for i in range(4):
    CH.append(("cq%d" % i, 128 * i, 128))
for i in range(4):
    CH.append(("ckv%d" % i, 512 + 128 * i, 128))
CH += [("kr", 1024, 64), ("krsw", -1, 64), ("wd", 2624, 64), ("ad", 2688, 64)]
CH += [("gd", 2752, 96), ("ab", 4896, 16), ("pad0", -2, 0), ("pad1", -2, 0)]
for nm, c0 in (("r", 1088), ("k", 1600), ("v", 2112), ("gq", 2848), ("gk", 3360), ("gv", 3872), ("z", 4384)):
    for i in range(4):
        CH.append(("%s%d" % (nm, i), c0 + 128 * i, 128))
NCH = len(CH)
CHI = {c[0]: i for i, c in enumerate(CH)}
NCHP = NCH
NFM = 13
TOKC = {"r": 0, "k": 512, "v": 1024, "gq": 1536, "gk": 2048, "gv": 2560, "z": 3072, "ab": 3584}
NTOKC = 3600

TBLK = [(0, CTX, True)] + [(CTX + 512 * j, 512, False) for j in range(SEQ // 512)]
PC0 = 2
PL0 = 2 + CTX + 4
TTP = PL0 + SEQ + 2


def ppos(t):
    return PC0 + t if t < CTX else PL0 + (t - CTX)


def configure(seq):
    global SEQ, TT, TBLK, TTP
    SEQ = seq
    TT = SEQ + CTX
    TBLK = [(0, CTX, True)] + [(CTX + 512 * j, 512, False) for j in range(SEQ // 512)]
    TTP = PL0 + SEQ + 2


class T:
    __slots__ = ("h", "lw", "rd", "pg")

    def __init__(self, h=None):
        self.h = h
        self.lw = {}
        self.rd = {}
        self.pg = {}

    def __getitem__(self, idx):
        return self.h[idx]


class Em:
    NDMA = 6

    def __init__(self, nc, st):
        self.nc = nc
        self.st = st
        self.eng = {"pe": nc.tensor, "act": nc.scalar, "dve": nc.vector, "pool": nc.gpsimd, "sp": nc.sync}
        self.sem = {}
        self.cnt = {}
        for k in ("pe", "act", "dve", "pool", "sp"):
            self.sem[k] = st.enter_context(nc.semaphore("s_" + k))
            self.cnt[k] = 0
        self.dcnt = {"sp": 0, "pool": 0, "act": 0}
        for q in self.dcnt:
            for i in range(self.NDMA):
                self.sem[(q, i)] = st.enter_context(nc.semaphore("d_%s%d" % (q, i)))
        self.seen = {k: {} for k in self.eng}
        self.dmax = {}
        self.ninst = 0
        self.uid = 0

    def sb(self, name, shape, dt, st=None):
        self.uid += 1
        return T((st or self.st).enter_context(self.nc.sbuf_tensor("%s_u%d" % (name, self.uid), list(shape), dt)))

    def ps(self, name, shape, dt=F32, st=None):
        self.uid += 1
        return T((st or self.st).enter_context(self.nc.psum_tensor("%s_u%d" % (name, self.uid), list(shape), dt)))

    def _wait(self, eng, deps):
        e = self.eng[eng]
        seen = self.seen[eng]
        for sk, v in deps.items():
            if sk == "pe" and eng == "pe":
                continue
            if seen.get(sk, 0) < v:
                e.wait_ge(self.sem[sk], v)
                seen[sk] = v
                self.ninst += 1

    @staticmethod
    def _merge(d, s):
        for k, v in s.items():
            if d.get(k, 0) < v:
                d[k] = v

    def _deps(self, reads, writes, partial):
        deps = {}
        for b in reads:
            self._merge(deps, b.lw)
        for b in writes:
            self._merge(deps, b.rd)
            if partial:
                self._merge(deps, b.pg)
            else:
                self._merge(deps, b.lw)
        return deps

    def _record(self, ev, reads, writes, partial):
        sk, v = ev
        for b in reads:
            if b.rd.get(sk, 0) < v:
                b.rd[sk] = v
        for b in writes:
            if partial and not b.rd:
                if b.lw.get(sk, 0) < v:
                    b.lw[sk] = v
            else:
                pg = dict(b.rd)
                self._merge(pg, b.lw)
                b.pg = pg
                b.lw = {sk: v}
                b.rd = {}

    def op(self, eng, fn, reads=(), writes=(), partial=False):
        self._wait(eng, self._deps(reads, writes, partial))
        ins = fn(self.eng[eng])
        self.cnt[eng] += 1
        ins.then_inc(self.sem[eng], 1)
        self.ninst += 1
        self._record((eng, self.cnt[eng]), reads, writes, partial)

    def dma(self, q, out, in_, reads=(), writes=(), partial=False):
        n = self.dcnt[q]
        slot = n % self.NDMA
        sk = (q, slot)
        tgt = 16 * (n // self.NDMA + 1)
        deps = self._deps(reads, writes, partial)
        if tgt > 16:
            deps[sk] = max(deps.get(sk, 0), tgt - 16)
        self._wait(q, deps)
        self.eng[q].dma_start(out=out, in_=in_).then_inc(self.sem[sk], 16)
        self.dcnt[q] = n + 1
        self.dmax[sk] = tgt
        self.ninst += 1
        self._record((sk, tgt), reads, writes, partial)

    def barrier(self):
        allev = {k: v for k, v in self.cnt.items() if v > 0}
        allev.update(self.dmax)
        for eng in self.eng:
            d = {k: v for k, v in allev.items() if k != eng}
            self._wait(eng, d)
        for eng in ("act", "dve", "pool"):
            if self.cnt[eng] > 0:
                self._wait(eng, {eng: self.cnt[eng]})

    def mm(self, out_t, out_ap, lhsT, rhs, start, stop, reads):
        self.op("pe", lambda e: e.matmul(out_ap, lhsT, rhs, start=start, stop=stop),
                reads=reads, writes=[out_t])

    def act(self, eng_out_t, out_ap, in_ap, func, reads, bias=None, scale=None, accum=None, writes=None):
        kw = {}
        if bias is not None:
            kw["bias"] = bias
        if scale is not None:
            kw["scale"] = scale
        if accum is not None:
            kw["accum_out"] = accum
        self.op("act", lambda e: e.activation(out_ap, in_ap, func, **kw), reads=reads,
                writes=writes if writes is not None else [eng_out_t])


def _chunk_rows(ap2d):
    return ap2d.rearrange("(k p) t -> p k t", p=128)


class Prog:
    def __init__(self, nl=DEPTH, dbg=()):
        self.nl = nl
        self.dbg = set(dbg)
        nc = self.nc = bass.Bass("TRN2", target_bir_lowering=False)
        self.st = ExitStack()
        self.em = Em(nc, self.st)
        self.inputs = {}
        self.outs = {}

    def din(self, name, shape, dt=F32):
        self.inputs[name] = (shape, dt)
        return self.nc.dram_tensor(name, list(shape), dt, kind="ExternalInput").ap()

    def dscr(self, name, shape, dt=F32, out=False):
        kind = "ExternalOutput" if (out or name in self.dbg) else "Internal"
        if kind == "ExternalOutput":
            self.outs[name] = (shape, dt)
        return self.nc.dram_tensor(name, list(shape), dt, kind=kind).ap()


def declare_io(P):
    L = P.nl
    P.xT0 = P.din("xT0", [D, TT])
    P.cvec = P.din("cvec", [128, KD, 2])
    P.w_mod = P.din("w_mod", [L, D, 6 * D])
    P.b_modT = P.din("b_modT", [128, L, 96])
    P.w_in = P.din("w_in", [L, D, IN_COLS])
    P.w_in_krsw = P.din("w_in_krsw", [L, D, 64])
    P.wb_in = P.dscr("wb_in", [L, D, NCHP * 128], BF16)
    P.pT = P.dscr("pT", [NFM * 128, TTP])
    P.pTok = P.dscr("pTok", [TTP, NTOKC])


def phase_cast_in(P):
    em = P.em
    P.wb_in_t = [T() for _ in range(P.nl)]
    for l in range(P.nl):
        for i, (nm, c0, w) in enumerate(CH):
            if w == 0:
                continue
            src = P.w_in_krsw[l, :, :] if c0 < 0 else P.w_in[l, :, c0:c0 + w]
            for r0 in range(0, D, 512):
                em.dma("pool", P.wb_in[l, r0:r0 + 512, i * 128:i * 128 + w],
                       src[r0:r0 + 512, :], writes=[P.wb_in_t[l]], partial=True)


def phase_mod(P):
    em = P.em
    L = P.nl
    P.mod = em.sb("mod", [128, L, 96, 2], F32)
    P.mod1 = em.sb("mod1", [128, L, 96, 2], F32)
    P.modg = em.sb("modg", [128, L, 96, 2], F32)
    with ExitStack() as st:
        cv = em.sb("cv", [128, KD, 2], F32, st)
        sc = em.sb("sc", [128, KD, 2], F32, st)
        bm = em.sb("bm", [128, L, 96], F32, st)
        em.dma("sp", cv[:], P.cvec[:, :, :], writes=[cv])
        em.dma("sp", bm[:], P.b_modT[:, :, :], writes=[bm])
        em.act(sc, sc[:], cv[:], AF.Silu, reads=[cv])
        wt = [em.sb("wm%d" % i, [128, KD, 512], F32, st) for i in range(2)]
        pmf = [em.ps("pm%d" % i, [128, 512], F32, st) for i in range(2)]
        g = 0
        for l in range(L):
            wv = P.w_mod[l].rearrange("(k p) c -> p k c", p=128)
            for gi in range(24):
                w = wt[g % 2]
                p = pmf[g % 2]
                g += 1
                em.dma("sp", w[:], wv[:, :, gi * 512:(gi + 1) * 512], writes=[w])
                for c in range(4):
                    for k in range(KD):
                        em.mm(p, p[:, 2 * c:2 * c + 2], w[:, k, c * 128:(c + 1) * 128], sc[:, k, :],
                              k == 0, k == KD - 1, reads=[w, sc])
                em.op("dve", lambda e: e.tensor_tensor(
                    out=P.mod[:, l, gi * 4:(gi + 1) * 4, :], in0=p[:, 0:8].rearrange("p (c j) -> p c j", j=2),
                    in1=bm[:, l, gi * 4:(gi + 1) * 4].unsqueeze(2).to_broadcast([128, 4, 2]),
                    op=ALU.add), reads=[p, bm], writes=[P.mod], partial=True)
        em.op("dve", lambda e: e.tensor_scalar_add(out=P.mod1[:], in0=P.mod[:], scalar1=1.0),
              reads=[P.mod], writes=[P.mod1])
        em.op("dve", lambda e: e.tensor_scalar_mul(out=P.modg[:], in0=P.mod[:], scalar1=1.0 / ALPHA),
              reads=[P.mod], writes=[P.modg])
        em.barrier()


SH_M, SC_M, GT_M, SH_F, SC_F, GT_F = 0, 16, 32, 48, 64, 80


def phase_zero_pads(P):
    em = P.em
    with ExitStack() as st:
        z = em.sb("zpad", [128, NTOKC], F32, st)
        em.op("dve", lambda e: e.memset(z[:], 0.0), writes=[z])
        dummy = T()
        for a, b in ((0, PC0), (PC0 + CTX, PL0), (PL0 + SEQ, TTP)):
            em.dma("sp", P.pTok[a:b, :], z[0:b - a, :], reads=[z], writes=[dummy], partial=True)
            for c in range(NFM):
                em.dma("sp", P.pT[c * 128:(c + 1) * 128, a:b], z[:, 0:b - a], reads=[z], writes=[dummy], partial=True)
        em.barrier()


def phase_inproj(P, l, xsrc, xsrc_t):
    em = P.em
    with ExitStack() as st:
        xs = [em.sb("ip_xs%d" % i, [128, KD, 512], F32, st) for i in range(2)]
        xm = [em.sb("ip_xm%d" % i, [128, KD, 512], BF16, st) for i in range(2)]
        wt = [em.sb("ip_w%d" % i, [128, KD, 512], BF16, st) for i in range(2)]
        ps = [em.ps("ip_ps%d" % i, [128, 512], F32, st) for i in range(4)]
        sg = [em.sb("ip_sg%d" % i, [128, 512], F32, st) for i in range(4)]
        wv = P.wb_in[l].rearrange("(k p) c -> p k c", p=128)
        xv = _chunk_rows(xsrc)
        gcount = 0
        ccount = 0
        dummy = T()

        def evac(p, s, rows, cols):
            nonlocal ccount
            if ccount % 2 == 0:
                em.op("dve", lambda e: e.tensor_copy(out=s[0:rows, 0:cols], in_=p[0:rows, 0:cols]),
                      reads=[p], writes=[s])
            else:
                em.act(s, s[0:rows, 0:cols], p[0:rows, 0:cols], AF.Copy, reads=[p])
            ccount += 1

        for bi, (t0, n, isctx) in enumerate(TBLK):
            j = 1 if isctx else 0
            pp = ppos(t0)
            x_s, x_m = xs[bi % 2], xm[bi % 2]
            em.dma("sp", x_s[:, :, 0:n], xv[:, :, t0:t0 + n], reads=[xsrc_t[bi]], writes=[x_s])
            for k in range(KD):
                em.act(x_m, x_m[:, k, 0:n], x_s[:, k, 0:n], AF.Identity, reads=[x_s, P.mod, P.mod1],
                       bias=P.mod[:, l, SH_M + k, j:j + 1], scale=P.mod1[:, l, SC_M + k, j:j + 1],
                       writes=[x_m])
            for g in range(NCH // 4):
                w = wt[gcount % 2]
                gcount += 1
                em.dma("sp", w[:], wv[:, :, g * 512:(g + 1) * 512], reads=[P.wb_in_t[l]], writes=[w])
                if g < 4:
                    for c in range(4):
                        ci = g * 4 + c
                        nm, c0, wd = CH[ci]
                        if wd == 0:
                            continue
                        if nm == "ab":
                            for tt in range(n // 128):
                                p, s_ = ps[ccount % 4], sg[ccount % 4]
                                for k in range(KD):
                                    em.mm(p, p[:, 0:16], x_m[:, k, tt * 128:(tt + 1) * 128],
                                          w[:, k, c * 128:c * 128 + 16], k == 0, k == KD - 1, reads=[w, x_m])
                                evac(p, s_, 128, 16)
                                em.dma("pool", P.pTok[pp + tt * 128:pp + (tt + 1) * 128, TOKC["ab"]:TOKC["ab"] + 16],
                                       s_[:, 0:16], reads=[s_], writes=[dummy], partial=True)
                            continue
                        fi = ci if ci < 12 else 12
                        p, s_ = ps[ccount % 4], sg[ccount % 4]
                        for k in range(KD):
                            em.mm(p, p[0:wd, 0:n], w[:, k, c * 128:c * 128 + wd], x_m[:, k, 0:n],
                                  k == 0, k == KD - 1, reads=[w, x_m])
                        evac(p, s_, wd, n)
                        em.dma("pool", P.pT[fi * 128:fi * 128 + wd, pp:pp + n], s_[0:wd, 0:n],
                               reads=[s_], writes=[dummy], partial=True)
                else:
                    col0 = (g - 4) * 512
                    for tt in range(n // 128):
                        p, s_ = ps[ccount % 4], sg[ccount % 4]
                        for k in range(KD):
                            em.mm(p, p[:, :], x_m[:, k, tt * 128:(tt + 1) * 128], w[:, k, :],
                                  k == 0, k == KD - 1, reads=[w, x_m])
                        evac(p, s_, 128, 512)
                        em.dma("pool", P.pTok[pp + tt * 128:pp + (tt + 1) * 128, col0:col0 + 512], s_[:, :],
                               reads=[s_], writes=[dummy], partial=True)
        em.barrier()


def declare_dense(P):
    L = P.nl
    P.w_out = P.din("w_out", [L, D, D])
    P.w_gate = P.din("ffn_w_gate", [L, D, D_FF])
    P.w_up = P.din("ffn_w_up", [L, D, D_FF])
    P.w_down = P.din("ffn_w_down", [L, D_FF, D])
    P.lnT = P.din("lnT", [128, L, 4, KD])
    P.wb_out = P.dscr("wb_out", [L, D, D], BF16)
    P.wb_gate = P.dscr("wb_gate", [L, KF // 2, 128, KD, 256], BF16)
    P.wb_up = P.dscr("wb_up", [L, KF // 2, 128, KD, 256], BF16)
    P.wb_down = P.dscr("wb_down", [L, KD, 128, KF, 128], BF16)
    P.mixT = P.dscr("mixT", [D, TT], BF16)
    P.xA = P.dscr("xA", [D, TT])
    P.xB = P.dscr("xB", [D, TT])
    P.xOut = P.dscr("xOut", [D, SEQ], out=True)


def phase_cast_dense(P):
    em = P.em
    P.wb_dense_t = [T() for _ in range(P.nl)]
    for l in range(P.nl):
        for r0 in range(0, D, 256):
            em.dma("pool", P.wb_out[l, r0:r0 + 256, :], P.w_out[l, r0:r0 + 256, :],
                   writes=[P.wb_dense_t[l]], partial=True)
        for src, dst in ((P.w_gate, P.wb_gate), (P.w_up, P.wb_up)):
            for g in range(KF // 2):
                sv = src[l, :, g * 256:(g + 1) * 256].rearrange("(k p) c -> p k c", p=128)
                for k0 in range(0, KD, 8):
                    em.dma("pool", dst[l, g, :, k0:k0 + 8, :], sv[:, k0:k0 + 8, :],
                           writes=[P.wb_dense_t[l]], partial=True)
        for m in range(KD):
            sv = P.w_down[l, :, m * 128:(m + 1) * 128].rearrange("(k p) c -> p k c", p=128)
            for k0 in range(0, KF, 11):
                em.dma("pool", P.wb_down[l, m, :, k0:k0 + 11, :], sv[:, k0:k0 + 11, :],
                       writes=[P.wb_dense_t[l]], partial=True)


def setup_consts(P):
    em = P.em
    P.ones_f = em.sb("ones_f", [128, 128], F32)
    P.ones_b = em.sb("ones_b", [128, 128], BF16)
    em.op("dve", lambda e: e.memset(P.ones_f[:], 1.0), writes=[P.ones_f])
    em.op("dve", lambda e: e.memset(P.ones_b[:], 1.0), writes=[P.ones_b])
    P.epsc = em.sb("epsc", [128, 4], F32)
    em.op("dve", lambda e: e.memset(P.epsc[:, 0:1], 1e-5 / (ALPHA * ALPHA)), writes=[P.epsc])
    em.op("dve", lambda e: e.memset(P.epsc[:, 1:2], 1e-6), writes=[P.epsc])
    em.op("dve", lambda e: e.memset(P.epsc[:, 2:3], 64e-5), writes=[P.epsc])
    em.op("dve", lambda e: e.memset(P.epsc[:, 3:4], 1e-12), writes=[P.epsc])
    P.ln = em.sb("ln", [128, P.nl, 4, KD], F32)
    em.dma("sp", P.ln[:], P.lnT[:, :, :, :], writes=[P.ln])


class ResLN:
    def __init__(self, P, st, tag):
        em = P.em
        self.P = P
        self.s1 = em.ps(tag + "_s1", [128, 512], F32, st)
        self.s2 = em.ps(tag + "_s2", [128, 512], F32, st)
        self.sq = [em.sb(tag + "_sq%d" % i, [128, 512], F32, st) for i in range(2)]
        self.mean = em.sb(tag + "_mean", [128, 512], F32, st)
        self.rstd = em.sb(tag + "_rstd", [128, 512], F32, st)
        self.tmp = [em.sb(tag + "_tmp%d" % i, [128, 512], F32, st) for i in range(2)]
        self.og = [em.sb(tag + "_og%d" % i, [128, 512], F32, st) for i in range(2)]
        self.cnt = 0

    def add_chunk(self, m, psum_t, xblk, n, l, gslot, j):
        P, em = self.P, self.P.em
        em.op("dve", lambda e: e.scalar_tensor_tensor(
            out=xblk[:, m, 0:n], in0=psum_t[:, 0:n], scalar=P.modg[:, l, gslot + m, j:j + 1],
            in1=xblk[:, m, 0:n], op0=ALU.mult, op1=ALU.add), reads=[psum_t, xblk, P.modg], writes=[xblk])
        sq = self.sq[m % 2]
        em.act(sq, sq[:, 0:n], xblk[:, m, 0:n], AF.Square, reads=[xblk])
        em.mm(self.s1, self.s1[:, 0:n], P.ones_f[:], xblk[:, m, 0:n], m == 0, m == KD - 1, reads=[xblk, P.ones_f])
        em.mm(self.s2, self.s2[:, 0:n], P.ones_f[:], sq[:, 0:n], m == 0, m == KD - 1, reads=[sq, P.ones_f])

    def finish(self, xblk, n, l, lnslot, xdst, xdst_t, c0, dst2=None):
        P, em = self.P, self.P.em
        mean, rstd = self.mean, self.rstd
        em.op("dve", lambda e: e.tensor_scalar_mul(out=mean[:, 0:n], in0=self.s1[:, 0:n], scalar1=1.0 / D),
              reads=[self.s1], writes=[mean])
        em.op("dve", lambda e: e.tensor_tensor(out=rstd[:, 0:n], in0=mean[:, 0:n], in1=mean[:, 0:n], op=ALU.mult),
              reads=[mean], writes=[rstd])
        em.op("dve", lambda e: e.scalar_tensor_tensor(
            out=rstd[:, 0:n], in0=self.s2[:, 0:n], scalar=1.0 / D, in1=rstd[:, 0:n],
            op0=ALU.mult, op1=ALU.subtract), reads=[self.s2, rstd], writes=[rstd])
        em.act(rstd, rstd[:, 0:n], rstd[:, 0:n], AF.Sqrt, reads=[rstd, P.epsc], bias=P.epsc[:, 0:1])
        em.op("dve", lambda e: e.reciprocal(out=rstd[:, 0:n], in_=rstd[:, 0:n]), reads=[rstd], writes=[rstd])
        xv = _chunk_rows(xdst)
        for m in range(KD):
            tmp = self.tmp[m % 2]
            og = self.og[m % 2]
            em.op("dve", lambda e: e.tensor_tensor(out=tmp[:, 0:n], in0=xblk[:, m, 0:n], in1=mean[:, 0:n],
                                                   op=ALU.subtract), reads=[xblk, mean], writes=[tmp])
            em.op("pool", lambda e: e.tensor_tensor(out=tmp[:, 0:n], in0=tmp[:, 0:n], in1=rstd[:, 0:n],
                                                    op=ALU.mult), reads=[tmp, rstd], writes=[tmp])
            em.act(og, og[:, 0:n], tmp[:, 0:n], AF.Identity, reads=[tmp, P.ln],
                   bias=P.ln[:, l, lnslot + 1, m:m + 1], scale=P.ln[:, l, lnslot, m:m + 1])
            em.dma("pool", xdst[m * 128:(m + 1) * 128, c0:c0 + n], og[:, 0:n], reads=[og],
                   writes=[xdst_t], partial=True)
            if dst2 is not None:
                d2, d2_t, c2 = dst2
                em.dma("sp", d2[m * 128:(m + 1) * 128, c2:c2 + n], og[:, 0:n], reads=[og],
                       writes=[d2_t], partial=True)


def phase_outproj(P, l, xsrc, xsrc_t, xdst, xdst_t, with_ctx):
    em = P.em
    with ExitStack() as st:
        xs = [em.sb("op_xs%d" % i, [128, KD, 512], F32, st) for i in range(2)]
        am = [em.sb("op_am%d" % i, [128, KD, 512], BF16, st) for i in range(2)]
        wt = [em.sb("op_w%d" % i, [128, KD, 512], BF16, st) for i in range(2)]
        ps = [em.ps("op_ps%d" % i, [128, 512], F32, st) for i in range(2)]
        rl = ResLN(P, st, "op")
        wv = P.wb_out[l].rearrange("(k p) c -> p k c", p=128)
        xv = _chunk_rows(xsrc)
        av = _chunk_rows(P.mixT)
        gc = 0
        cc = 0
        for bi, (t0, n, isctx) in enumerate(TBLK):
            if isctx and not with_ctx:
                continue
            j = 1 if isctx else 0
            x_s, a_m = xs[bi % 2], am[bi % 2]
            em.dma("sp", x_s[:, :, 0:n], xv[:, :, t0:t0 + n], reads=[xsrc_t[bi]], writes=[x_s])
            em.dma("sp", a_m[:, :, 0:n], av[:, :, t0:t0 + n], reads=[P.mixT_t[bi]], writes=[a_m])
            for g in range(4):
                w = wt[gc % 2]
                gc += 1
                em.dma("sp", w[:], wv[:, :, g * 512:(g + 1) * 512], reads=[P.wb_dense_t[l]], writes=[w])
                for c in range(4):
                    m = g * 4 + c
                    p = ps[cc % 2]
                    cc += 1
                    for k in range(KD):
                        em.mm(p, p[:, 0:n], w[:, k, c * 128:(c + 1) * 128], a_m[:, k, 0:n],
                              k == 0, k == KD - 1, reads=[w, a_m])
                    rl.add_chunk(m, p, x_s, n, l, GT_M, j)
            rl.finish(x_s, n, l, 0, xdst, xdst_t[bi], t0)
        em.barrier()


def phase_ffn(P, l, xsrc, xsrc_t, xdst, xdst_t, with_ctx, final=False):
    em = P.em
    with ExitStack() as st:
        xs = em.sb("ff_xs", [128, KD, 512], F32, st)
        xm = em.sb("ff_xm", [128, KD, 512], BF16, st)
        hh = em.sb("ff_h", [128, KF, 512], BF16, st)
        wg = [em.sb("ff_wg%d" % i, [128, KD, 256], BF16, st) for i in range(2)]
        wu = [em.sb("ff_wu%d" % i, [128, KD, 256], BF16, st) for i in range(2)]
        wd = [em.sb("ff_wd%d" % i, [128, KF, 128], BF16, st) for i in range(2)]
        pg = [em.ps("ff_pg%d" % i, [128, 512], F32, st) for i in range(2)]
        pu = [em.ps("ff_pu%d" % i, [128, 512], F32, st) for i in range(2)]
        pd = [em.ps("ff_pd%d" % i, [128, 512], F32, st) for i in range(2)]
        sg = [em.sb("ff_sg%d" % i, [128, 512], F32, st) for i in range(2)]
        rl = ResLN(P, st, "ff")
        xv = _chunk_rows(xsrc)
        gc = 0
        cc = 0
        dc = 0
        for bi, (t0, n, isctx) in enumerate(TBLK):
            if isctx and not with_ctx:
                continue
            j = 1 if isctx else 0
            em.dma("sp", xs[:, :, 0:n], xv[:, :, t0:t0 + n], reads=[xsrc_t[bi]], writes=[xs])
            for k in range(KD):
                em.act(xm, xm[:, k, 0:n], xs[:, k, 0:n], AF.Identity, reads=[xs, P.mod, P.mod1],
                       bias=P.mod[:, l, SH_F + k, j:j + 1], scale=P.mod1[:, l, SC_F + k, j:j + 1])
            for g in range(KF // 2):
                w_g, w_u = wg[gc % 2], wu[gc % 2]
                gc += 1
                em.dma("sp", w_g[:], P.wb_gate[l, g], reads=[P.wb_dense_t[l]], writes=[w_g])
                em.dma("sp", w_u[:], P.wb_up[l, g], reads=[P.wb_dense_t[l]], writes=[w_u])
                for c in range(2):
                    m = g * 2 + c
                    p_g, p_u, s = pg[cc % 2], pu[cc % 2], sg[cc % 2]
                    cc += 1
                    for k in range(KD):
                        em.mm(p_g, p_g[:, 0:n], w_g[:, k, c * 128:(c + 1) * 128], xm[:, k, 0:n],
                              k == 0, k == KD - 1, reads=[w_g, xm])
                    for k in range(KD):
                        em.mm(p_u, p_u[:, 0:n], w_u[:, k, c * 128:(c + 1) * 128], xm[:, k, 0:n],
                              k == 0, k == KD - 1, reads=[w_u, xm])
                    em.act(s, s[:, 0:n], p_g[:, 0:n], AF.Silu, reads=[p_g])
                    em.op("dve", lambda e: e.tensor_tensor(out=hh[:, m, 0:n], in0=s[:, 0:n], in1=p_u[:, 0:n],
                                                           op=ALU.mult), reads=[s, p_u], writes=[hh], partial=True)
            for m in range(KD):
                w_d = wd[dc % 2]
                p_d = pd[dc % 2]
                dc += 1
                em.dma("sp", w_d[:], P.wb_down[l, m], reads=[P.wb_dense_t[l]], writes=[w_d])
                for k in range(KF):
                    em.mm(p_d, p_d[:, 0:n], w_d[:, k, :], hh[:, k, 0:n], k == 0, k == KF - 1, reads=[w_d, hh])
                rl.add_chunk(m, p_d, xs, n, l, GT_F, j)
            if final:
                rl.finish(xs, n, l, 2, P.xOut, xdst_t[bi], t0 - CTX)
            else:
                rl.finish(xs, n, l, 2, xdst, xdst_t[bi], t0)
        em.barrier()


NEG = -1.0e30
PIPE = True


def declare_scan(P):
    L = P.nl
    P.masks_in = P.din("masks", [128, 8, 128])
    P.ident_in = P.din("ident", [128, 128])
    P.tri_in = P.din("tri", [128, 2, 128])
    P.bmasks_in = P.din("bmasks", [128, 4, 128])
    P.rw_row = P.din("rw_row", [L, 11, 512])
    P.rw_mu = P.din("rw_mu", [L, 1536])
    P.rw_muL = P.din("rw_muL", [128, L, 3])
    P.rw_w2 = P.din("rw_w2", [L, 64, 512])
    P.rw_a2 = P.din("rw_a2", [L, 64, 512])
    P.rw_g2 = P.din("rw_g2", [L, 96, 512])
    P.gd_conv = P.din("gd_conv", [L, 5, 1536])
    P.gd_row = P.din("gd_row", [L, 3, 512])
    P.yscr = P.dscr("yscr", [4, TTP, 512])
    P.auxs = P.dscr("auxs", [3, TTP, 512])


def setup_scan_consts(P):
    em = P.em
    P.masks = em.sb("masks_sb", [128, 8, 128], F32)
    P.ident = em.sb("ident", [128, 128], F32)
    P.tri = em.sb("tri", [128, 2, 128], F32)
    em.dma("sp", P.masks[:], P.masks_in[:, :, :], writes=[P.masks])
    em.dma("sp", P.ident[:], P.ident_in[:, :], writes=[P.ident])
    em.dma("sp", P.tri[:], P.tri_in[:, :, :], writes=[P.tri])
    P.bmasks = em.sb("bmasks_sb", [128, 4, 128], F32)
    em.dma("sp", P.bmasks[:], P.bmasks_in[:, :, :], writes=[P.bmasks])


def chunk_order(rev):
    nc_ctx = CTX // 128
    nc_lat = SEQ // 128
    ctx = [128 * i for i in range(nc_ctx)]
    lat = [CTX + 128 * i for i in range(nc_lat)]
    if rev:
        return ctx[::-1] + lat[::-1]
    return ctx + lat


class ScanCore:
    def __init__(self, P, st, N, tag):
        em = P.em
        self.P, self.N, self.H = P, N, 512 // N
        N_, H = N, self.H
        self.pb = [em.ps(tag + "_pb%d" % i, [128, 512], F32, st) for i in range(8)]
        self.pbi = 0
        f = lambda nm, shp: em.sb(tag + "_" + nm, shp, F32, st)
        self.XT = {k: f("xt_" + k, [N_, H, 128]) for k in ("a", "b", "k", "r", "R")}
        self.AT = [f("AT0", [128, H, 128])]
        self.A = [f("A0", [128, H, 128])]
        self.AkT = f("AkT", [128, H, 128])
        self.ArbT = f("ArbT", [128, H, 128])
        self.ArkT = f("ArkT", [128, H, 128])
        self.Z = [f("Z%d" % i, [128, H, 2 * N_]) for i in range(2)]
        self.GH = H // 2
        GH = self.GH
        fb = lambda nm, shp: em.sb(tag + "_" + nm, shp, BF16, st)
        self.J = [[[fb("J%d%d%d" % (gi, i, k), [128, GH, 128]) for k in range(2)] for i in range(2)] for gi in range(2)]
        self.X = [fb("X%d" % gi, [128, GH, 128]) for gi in range(2)]
        self.XTt = [fb("XTt%d" % gi, [128, GH, 128]) for gi in range(2)]
        self.Zb = [fb("Zb%d" % gi, [128, GH, 2 * N_]) for gi in range(2)]
        self.Zt = [f("Zt%d" % gi, [128, GH, 2 * N_]) for gi in range(2)]
        self.WpT = f("WpT", [N_, H, 128])
        self.U = f("U", [128, H, N_])
        self.ST = f("ST", [N_, H, N_])
        self.Ysb = [f("Ysb%d" % i, [128, 512]) for i in range(2)]
        self.yi = 0

    def bank(self):
        b = self.pb[self.pbi % 8]
        self.pbi += 1
        return b

    def reset_state(self):
        em = self.P.em
        em.op("dve", lambda e: e.memset(self.ST[:], 0.0), writes=[self.ST])

    def transpose_to(self, key, src):
        P, em, N, H = self.P, self.P.em, self.N, self.H
        dst = self.XT[key]
        for g in range(H // 4):
            pb = self.bank()
            for hh in range(4):
                h = g * 4 + hh
                em.op("pe", lambda e: e.transpose(pb[0:N, hh * 128:(hh + 1) * 128], src[:, h * N:(h + 1) * N],
                                                  P.ident[:]), reads=[src, P.ident], writes=[pb])
            em.act(dst, dst[:, g * 4:(g + 1) * 4, :], pb[0:N, :].rearrange("p (h t) -> p h t", h=4), AF.Copy,
                   reads=[pb])
        return dst

    def gram(self, lkey, rkey, dst, mask_ap_fn, mask_t):
        P, em, N, H = self.P, self.P.em, self.N, self.H
        L_, R_ = self.XT[lkey], self.XT[rkey]
        for g in range(H // 4):
            pb = self.bank()
            for hh in range(4):
                h = g * 4 + hh
                em.mm(pb, pb[:, hh * 128:(hh + 1) * 128], L_[:, h, :], R_[:, h, :], True, True, reads=[L_, R_])
            em.op("dve", lambda e: e.tensor_tensor(
                out=dst[:, g * 4:(g + 1) * 4, :], in0=pb[:, :].rearrange("p (h t) -> p h t", h=4),
                in1=mask_ap_fn(g), op=ALU.mult), reads=[pb, mask_t], writes=[dst], partial=True)

    def run_chunk(self, rev, ops, WcT, masks, ydst_ap, filler=None):
        P, em, N, H = self.P, self.P.em, self.N, self.H
        hv = lambda t: t[:].rearrange("p (h n) -> p h n", n=N)
        same_bk = ops["gb"] is ops["gk"]
        self.transpose_to("a", ops["ga"])
        self.transpose_to("k", ops["gk"])
        if not same_bk:
            self.transpose_to("b", ops["gb"])
        bkey = "k" if same_bk else "b"
        self.transpose_to("r", ops["gr"])
        if ops["Rtil"] is ops["gr"]:
            Rkey = "r"
        else:
            self.transpose_to("R", ops["Rtil"])
            Rkey = "R"
        AT, A = self.AT[0], self.A[0]
        self.gram(bkey, "a", AT, *masks["AT"])
        self.gram("a", bkey, A, *masks["A"])
        if same_bk:
            AkT, ArbT = AT, None
        else:
            AkT, ArbT = self.AkT, self.ArbT
            self.gram("k", "a", AkT, *masks["AT"])
            self.gram("b", "r", ArbT, *masks["ArT"])
        ArkT = self.ArkT
        self.gram("k", "r", ArkT, *masks["ArT"])
        if same_bk:
            ArbT = ArkT
        V = ops["V"]
        Z = self.Z[0]
        pb = self.bank()
        for h in range(H):
            em.mm(pb, pb[:, h * N:(h + 1) * N], AkT[:, h, :], V[:, h * N:(h + 1) * N], True, True,
                  reads=[AkT, V])
        em.op("dve", lambda e: e.tensor_copy(out=Z[:, :, 0:N], in_=pb[:, :].rearrange("p (h n) -> p h n", n=N)),
              reads=[pb], writes=[Z], partial=True)
        em.op("pool", lambda e: e.tensor_copy(out=Z[:, :, N:2 * N], in_=hv(ops["Atil"])),
              reads=[ops["Atil"]], writes=[Z], partial=True)
        Zf = self.Z[1]
        HB = 512 // (2 * N)
        A0, AT0 = self.A[0], self.AT[0]
        GH = self.GH
        bm = lambda i: P.bmasks[:, i:i + 1, :].to_broadcast([128, GH, 128])
        idb = P.ident[:, :].unsqueeze(1).to_broadcast([128, GH, 128])

        def mmg(dst_pb, lhs_t, rhs_t):
            for hh in range(GH):
                em.mm(dst_pb, dst_pb[:, hh * 128:(hh + 1) * 128], lhs_t[:, hh, :], rhs_t[:, hh, :], True, True,
                      reads=[lhs_t, rhs_t])

        vg = lambda pb_: pb_[:, 0:GH * 128].rearrange("p (h t) -> p h t", h=GH)

        def inv_group(g):
            gs = slice(g * GH, (g + 1) * GH)
            X, XT = self.X[g], self.XTt[g]
            J = self.J[g]
            Ja, JaT = J[0]
            em.op("dve", lambda e: e.tensor_tensor(out=Ja[:], in0=A0[:, gs, :], in1=bm(0), op=ALU.mult),
                  reads=[A0, P.bmasks], writes=[Ja])
            em.op("pool", lambda e: e.tensor_tensor(out=JaT[:], in0=AT0[:, gs, :], in1=bm(0), op=ALU.mult),
                  reads=[AT0, P.bmasks], writes=[JaT])
            em.op("dve", lambda e: e.tensor_tensor(out=X[:], in0=Ja[:], in1=idb, op=ALU.add),
                  reads=[Ja, P.ident], writes=[X])
            em.op("pool", lambda e: e.tensor_tensor(out=XT[:], in0=JaT[:], in1=idb, op=ALU.add),
                  reads=[JaT, P.ident], writes=[XT])
            yield
            cur = 0
            for lev in range(3):
                Jc, JcT = J[cur]
                Jn, JnT = J[1 - cur]
                p1, p2 = self.bank(), self.bank()
                mmg(p1, JcT, Jc)
                mmg(p2, Jc, JcT)
                em.op("dve", lambda e: e.tensor_copy(out=Jn[:], in_=vg(p1)), reads=[p1], writes=[Jn])
                em.act(JnT, JnT[:], vg(p2), AF.Copy, reads=[p2])
                yield
                p3, p4 = self.bank(), self.bank()
                mmg(p3, JnT, X)
                mmg(p4, Jn, XT)
                em.op("dve", lambda e: e.tensor_tensor(out=X[:], in0=vg(p3), in1=X[:], op=ALU.add),
                      reads=[p3, X], writes=[X])
                em.op("dve", lambda e: e.tensor_tensor(out=XT[:], in0=vg(p4), in1=XT[:], op=ALU.add),
                      reads=[p4, XT], writes=[XT])
                yield
                cur = 1 - cur
            for bi in (1, 2, 3):
                Ao, AoT = J[0]
                Y, Y2 = J[1]
                em.op("dve", lambda e: e.tensor_tensor(out=Ao[:], in0=A0[:, gs, :], in1=bm(bi), op=ALU.mult),
                      reads=[A0, P.bmasks], writes=[Ao])
                em.op("pool", lambda e: e.tensor_tensor(out=AoT[:], in0=AT0[:, gs, :], in1=bm(bi), op=ALU.mult),
                      reads=[AT0, P.bmasks], writes=[AoT])
                last = bi == 3
                p2 = self.bank()
                mmg(p2, Ao, XT)
                if not last:
                    p1 = self.bank()
                    mmg(p1, AoT, X)
                    em.op("dve", lambda e: e.tensor_copy(out=Y[:], in_=vg(p1)), reads=[p1], writes=[Y])
                em.act(Y2, Y2[:], vg(p2), AF.Copy, reads=[p2])
                yield
                p4 = self.bank()
                mmg(p4, X, Y2)
                if not last:
                    p3 = self.bank()
                    mmg(p3, XT, Y)
                    em.op("dve", lambda e: e.tensor_tensor(out=X[:], in0=vg(p3), in1=X[:], op=ALU.add),
                          reads=[p3, X], writes=[X])
                em.op("dve", lambda e: e.tensor_tensor(out=XT[:], in0=vg(p4), in1=XT[:], op=ALU.add),
                      reads=[p4, XT], writes=[XT])
                yield
            nsub = max(1, GH // HB)
            hps = GH // nsub
            Zb, Zt = self.Zb[g], self.Zt[g]
            em.op("pool", lambda e: e.tensor_copy(out=Zb[:], in_=Z[:, gs, :]), reads=[Z], writes=[Zb])

            def apply_x(src_b, accumulate):
                for sub in range(nsub):
                    pb = self.bank()
                    for hh in range(hps):
                        h4 = sub * hps + hh
                        em.mm(pb, pb[:, hh * 2 * N:(hh + 1) * 2 * N], XT[:, h4, :], src_b[:, h4, :], True, True,
                              reads=[XT, src_b])
                    h0 = g * GH + sub * hps
                    pv = pb[:, 0:hps * 2 * N].rearrange("p (h n) -> p h n", h=hps)
                    if accumulate:
                        em.op("dve", lambda e: e.tensor_tensor(out=Zf[:, h0:h0 + hps, :], in0=pv,
                                                               in1=Zf[:, h0:h0 + hps, :], op=ALU.add),
                              reads=[pb, Zf], writes=[Zf], partial=True)
                    else:
                        em.act(Zf, Zf[:, h0:h0 + hps, :], pv, AF.Copy, reads=[pb], writes=[Zf])

            apply_x(Zb, False)
            yield
            em.op("pool", lambda e: e.tensor_tensor(out=Zt[:], in0=Z[:, gs, :], in1=Zf[:, gs, :], op=ALU.subtract),
                  reads=[Z, Zf], writes=[Zt])
            for sub in range(nsub):
                pb = self.bank()
                for hh in range(hps):
                    h4 = sub * hps + hh
                    h = g * GH + h4
                    em.mm(pb, pb[:, hh * 2 * N:(hh + 1) * 2 * N], AT0[:, h, :], Zf[:, h, :], True, True,
                          reads=[AT0, Zf])
                h4s = slice(sub * hps, (sub + 1) * hps)
                em.op("dve", lambda e: e.tensor_tensor(
                    out=Zb[:, h4s, :], in0=pb[:, 0:hps * 2 * N].rearrange("p (h n) -> p h n", h=hps),
                    in1=Zt[:, h4s, :], op=ALU.add), reads=[pb, Zt], writes=[Zb], partial=True)
            yield
            apply_x(Zb, True)
            yield

        gens = [inv_group(0), inv_group(1)]
        alive = [True, True]
        while any(alive):
            for gi in range(2):
                if alive[gi]:
                    try:
                        next(gens[gi])
                    except StopIteration:
                        alive[gi] = False
            if filler is not None:
                next(filler, None)
        WpT = self.WpT
        for g in range(H // 4):
            pb = self.bank()
            for hh in range(4):
                h = g * 4 + hh
                em.op("pe", lambda e: e.transpose(pb[0:N, hh * 128:(hh + 1) * 128], Zf[:, h, N:2 * N],
                                                  P.ident[:]), reads=[Zf, P.ident], writes=[pb])
            em.act(WpT, WpT[:, g * 4:(g + 1) * 4, :], pb[0:N, :].rearrange("p (h t) -> p h t", h=4), AF.Copy,
                   reads=[pb], writes=[WpT])
        ST, U = self.ST, self.U
        RT = self.XT[Rkey]
        pb = self.bank()
        for h in range(H):
            em.mm(pb, pb[:, h * N:(h + 1) * N], WpT[:, h, :], ST[:, h, :], True, True, reads=[WpT, ST])
        em.op("dve", lambda e: e.tensor_tensor(out=U[:], in0=pb[:, :].rearrange("p (h n) -> p h n", n=N),
                                               in1=Zf[:, :, 0:N], op=ALU.add), reads=[pb, Zf], writes=[U])
        pb = self.bank()
        for h in range(H):
            o = pb[:, h * N:(h + 1) * N]
            em.mm(pb, o, RT[:, h, :], ST[:, h, :], True, False, reads=[RT, ST])
            em.mm(pb, o, ArbT[:, h, :], U[:, h, :], False, False, reads=[ArbT, U])
            em.mm(pb, o, ArkT[:, h, :], V[:, h * N:(h + 1) * N], False, True, reads=[ArkT, V])
        ysb = self.Ysb[self.yi % 2]
        self.yi += 1
        em.act(ysb, ysb[:], pb[:, :], AF.Copy, reads=[pb])
        em.dma("sp", ydst_ap, ysb[:], reads=[ysb], writes=[T()])
        pb = self.bank()
        Bh, Kh = ops["Bh"], ops["Kh"]
        for h in range(H):
            o = pb[0:N, h * N:(h + 1) * N]
            em.mm(pb, o, Bh[:, h * N:(h + 1) * N], U[:, h, :], True, False, reads=[Bh, U])
            em.mm(pb, o, Kh[:, h * N:(h + 1) * N], V[:, h * N:(h + 1) * N], False, True, reads=[Kh, V])
        em.op("dve", lambda e: e.tensor_tensor(out=ST[:], in0=ST[:],
                                               in1=WcT[:, :].unsqueeze(2).to_broadcast([N, H, N]), op=ALU.mult),
              reads=[ST, WcT], writes=[ST])
        em.op("dve", lambda e: e.tensor_tensor(out=ST[:], in0=pb[0:N, :].rearrange("p (h n) -> p h n", n=N),
                                               in1=ST[:], op=ALU.add), reads=[pb, ST], writes=[ST])


C0 = math.exp(-0.5)


def _bc_row(P, st, name, src_row_ap, width=512):
    em = P.em
    t = em.sb(name, [128, width], F32, st)
    em.dma("sp", t[:], src_row_ap.partition_broadcast(128), writes=[t])
    return t


def phase_rwkv(P, l, want_ctx_out=True):
    em = P.em
    dummy = T()
    with ExitStack() as st:
        core = ScanCore(P, st, 64, "rw")
        f = lambda nm, shp=(128, 512): em.sb("rw_" + nm, list(shp), F32, st)
        prm = {}
        for i, nm in enumerate(["w0_0", "w0_1", "a0_0", "a0_1", "k_k", "k_a", "r_k"]):
            prm[nm] = _bc_row(P, st, "rwp_" + nm, P.rw_row[l, i:i + 1, :])
        omm = f("omm", (128, 1536))
        hmu = f("hmu", (128, 1536))
        em.dma("sp", omm[:], P.rw_mu[l:l + 1, :].partition_broadcast(128), writes=[omm])
        em.op("dve", lambda e: e.tensor_scalar_mul(out=hmu[:], in0=omm[:], scalar1=0.5), reads=[omm], writes=[hmu])
        em.op("dve", lambda e: e.tensor_scalar(out=omm[:], in0=omm[:], scalar1=-1.0, scalar2=1.0, op0=ALU.mult,
                                               op1=ALU.add), reads=[omm], writes=[omm])
        muL = f("muL", (128, 3))
        ommL = f("ommL", (128, 3))
        hmuL = f("hmuL", (128, 3))
        em.dma("sp", muL[:], P.rw_muL[:, l, :], writes=[muL])
        em.op("dve", lambda e: e.tensor_scalar_mul(out=hmuL[:], in0=muL[:], scalar1=0.5), reads=[muL], writes=[hmuL])
        em.op("dve", lambda e: e.tensor_scalar(out=ommL[:], in0=muL[:], scalar1=-1.0, scalar2=1.0, op0=ALU.mult,
                                               op1=ALU.add), reads=[muL], writes=[ommL])
        w2 = f("w2", (64, 512))
        a2 = f("a2", (64, 512))
        g2 = f("g2", (96, 512))
        em.dma("sp", w2[:], P.rw_w2[l, :, :], writes=[w2])
        em.dma("sp", a2[:], P.rw_a2[l, :, :], writes=[a2])
        em.dma("sp", g2[:], P.rw_g2[l, :, :], writes=[g2])
        cen, prv, nxt = f("cen"), f("prv"), f("nxt")
        rp, kp = f("rp"), f("kp")
        vp2 = [f("vp0"), f("vp1")]
        lo = f("lo", (128, 3, 130))
        loP = f("loP", (128, 3, 128))
        lot = f("lot", (128, 128))
        sgd = [f("sgd0"), f("sgd1")]
        alr = [f("alr0"), f("alr1")]
        gg = f("gg")
        kk = f("kk")
        kdir = [f("kdir0"), f("kdir1")]
        t1, t2, t3 = f("t1"), f("t2"), f("t3")
        ss = f("ss", (128, 8))
        E1, E1p, E2, E3 = f("E1"), f("E1p"), f("E2"), f("E3")
        at, bt, kt, rt, bb = f("at"), f("bt"), f("kt"), f("rt"), f("bb")
        Bh2, Kh2 = [f("Bh0"), f("Bh1")], [f("Kh0"), f("Kh1")]
        WcT2 = [f("WcT0", (64, 8)), f("WcT1", (64, 8))]
        aux = f("aux")
        for d in (0, 1):
            rev = d == 1
            core.reset_state()
            if not rev:
                mk = {"AT": (lambda g: P.masks[:, 1:2, :].to_broadcast([128, 4, 128]), P.masks),
                      "A": (lambda g: P.masks[:, 0:1, :].to_broadcast([128, 4, 128]), P.masks),
                      "ArT": (lambda g: P.masks[:, 3:4, :].to_broadcast([128, 4, 128]), P.masks)}
            else:
                mk = {"AT": (lambda g: P.masks[:, 0:1, :].to_broadcast([128, 4, 128]), P.masks),
                      "A": (lambda g: P.masks[:, 1:2, :].to_broadcast([128, 4, 128]), P.masks),
                      "ArT": (lambda g: P.masks[:, 2:3, :].to_broadcast([128, 4, 128]), P.masks)}
            def prep(t0, slot, d=d):
                vp, Bh, Kh, WcT = vp2[slot], Bh2[slot], Kh2[slot], WcT2[slot]
                pp = ppos(t0)
                for ci, dst in enumerate((rp, kp, vp)):
                    c0 = ci * 512
                    em.dma("sp", cen[:], P.pTok[pp:pp + 128, c0:c0 + 512], writes=[cen])
                    em.dma("sp", prv[:], P.pTok[pp - 1:pp + 127, c0:c0 + 512], writes=[prv])
                    em.dma("sp", nxt[:], P.pTok[pp + 1:pp + 129, c0:c0 + 512], writes=[nxt])
                    em.op("pool", lambda e: e.tensor_tensor(out=prv[:], in0=prv[:], in1=nxt[:], op=ALU.add),
                          reads=[prv, nxt], writes=[prv])
                    em.op("pool", lambda e: e.tensor_tensor(out=prv[:], in0=prv[:], in1=hmu[:, c0:c0 + 512],
                                                            op=ALU.mult), reads=[prv, hmu], writes=[prv])
                    em.op("dve", lambda e: e.tensor_tensor(out=cen[:], in0=cen[:], in1=omm[:, c0:c0 + 512],
                                                           op=ALU.mult), reads=[cen, omm], writes=[cen])
                    em.op("dve", lambda e: e.tensor_tensor(out=dst[:], in0=cen[:], in1=prv[:], op=ALU.add),
                          reads=[cen, prv], writes=[dst])
                yield
                for c, (fi, wdt) in enumerate(((10, 64), (11, 64), (12, 96))):
                    em.dma("sp", lo[0:wdt, c, :], P.pT[fi * 128:fi * 128 + wdt, pp - 1:pp + 129], writes=[lo],
                           partial=True)
                for c, wdt in enumerate((64, 64, 96)):
                    em.op("dve", lambda e: e.tensor_tensor(out=lot[0:wdt, :], in0=lo[0:wdt, c, 0:128],
                                                           in1=lo[0:wdt, c, 2:130], op=ALU.add),
                          reads=[lo], writes=[lot])
                    em.op("dve", lambda e: e.tensor_scalar_mul(out=lot[0:wdt, :], in0=lot[0:wdt, :],
                                                               scalar1=hmuL[0:wdt, c:c + 1]),
                          reads=[lot, hmuL], writes=[lot])
                    em.op("dve", lambda e: e.scalar_tensor_tensor(
                        out=loP[0:wdt, c, :], in0=lo[0:wdt, c, 1:129], scalar=ommL[0:wdt, c:c + 1],
                        in1=lot[0:wdt, :], op0=ALU.mult, op1=ALU.add), reads=[lo, ommL, lot], writes=[loP],
                        partial=True)
                em.act(loP, loP[0:64, 0, :], loP[0:64, 0, :], AF.Tanh, reads=[loP])
                em.act(loP, loP[0:96, 2, :], loP[0:96, 2, :], AF.Sigmoid, reads=[loP])
                yield
                for dd in (0, 1):
                    pb = core.bank()
                    em.mm(pb, pb[:, :], loP[dd * 32:(dd + 1) * 32, 0, :], w2[dd * 32:(dd + 1) * 32, :], True, True,
                          reads=[loP, w2])
                    em.op("dve", lambda e: e.tensor_tensor(out=sgd[dd][:], in0=pb[:, :], in1=prm["w0_%d" % dd][:],
                                                           op=ALU.add), reads=[pb, prm["w0_%d" % dd]],
                          writes=[sgd[dd]])
                    em.act(sgd[dd], sgd[dd][:], sgd[dd][:], AF.Sigmoid, reads=[sgd[dd]])
                    pb = core.bank()
                    em.mm(pb, pb[:, :], loP[dd * 32:(dd + 1) * 32, 1, :], a2[dd * 32:(dd + 1) * 32, :], True, True,
                          reads=[loP, a2])
                    em.op("dve", lambda e: e.tensor_tensor(out=alr[dd][:], in0=pb[:, :], in1=prm["a0_%d" % dd][:],
                                                           op=ALU.add), reads=[pb, prm["a0_%d" % dd]],
                          writes=[alr[dd]])
                    em.act(alr[dd], alr[dd][:], alr[dd][:], AF.Sigmoid, reads=[alr[dd]])
                yield
                em.op("dve", lambda e: e.tensor_tensor(out=kk[:], in0=kp[:], in1=prm["k_k"][:], op=ALU.mult),
                      reads=[kp, prm["k_k"]], writes=[kk])
                em.act(t1, t1[:], kk[:], AF.Square, reads=[kk])
                em.op("dve", lambda e: e.tensor_reduce(out=ss[:], in_=t1[:].rearrange("p (h n) -> p h n", n=64),
                                                       axis=AX.X, op=ALU.add), reads=[t1], writes=[ss])
                em.act(ss, ss[:], ss[:], AF.Sqrt, reads=[ss, P.epsc], bias=P.epsc[:, 3:4])
                em.op("dve", lambda e: e.reciprocal(out=ss[:], in_=ss[:]), reads=[ss], writes=[ss])
                em.op("dve", lambda e: e.tensor_tensor(
                    out=kk[:].rearrange("p (h n) -> p h n", n=64), in0=kk[:].rearrange("p (h n) -> p h n", n=64),
                    in1=ss[:, :].unsqueeze(2).to_broadcast([128, 8, 64]), op=ALU.mult), reads=[kk, ss], writes=[kk])
                yield
                for dd in (0, 1):
                    em.op("dve", lambda e: e.scalar_tensor_tensor(
                        out=t1[:], in0=alr[dd][:], scalar=-1.0, in1=prm["k_a"][:], op0=ALU.add, op1=ALU.mult),
                        reads=[alr[dd], prm["k_a"]], writes=[t1])
                    em.op("dve", lambda e: e.scalar_tensor_tensor(
                        out=kdir[dd][:], in0=t1[:], scalar=1.0, in1=kp[:], op0=ALU.add, op1=ALU.mult),
                        reads=[t1, kp], writes=[kdir[dd]])
                yield
                if d == 0:
                    pb = core.bank()
                    em.mm(pb, pb[:, :], loP[0:96, 2, :], g2[0:96, :], True, True, reads=[loP, g2])
                    em.act(gg, gg[:], pb[:, :], AF.Copy, reads=[pb])
                    em.dma("sp", P.auxs[1, pp:pp + 128, :], gg[:], reads=[gg], writes=[dummy])
                    em.op("pool", lambda e: e.tensor_tensor(out=t2[:], in0=kdir[0][:], in1=kdir[1][:], op=ALU.add),
                          reads=[kdir[0], kdir[1]], writes=[t2])
                    em.op("pool", lambda e: e.tensor_tensor(out=t2[:], in0=t2[:], in1=rp[:], op=ALU.mult),
                          reads=[t2, rp], writes=[t2])
                    em.op("pool", lambda e: e.tensor_tensor(out=t2[:], in0=t2[:], in1=prm["r_k"][:], op=ALU.mult),
                          reads=[t2, prm["r_k"]], writes=[t2])
                    em.op("dve", lambda e: e.tensor_reduce(out=ss[:], in_=t2[:].rearrange("p (h n) -> p h n", n=64),
                                                           axis=AX.X, op=ALU.add), reads=[t2], writes=[ss])
                    em.op("dve", lambda e: e.tensor_tensor(
                        out=aux[:].rearrange("p (h n) -> p h n", n=64), in0=vp[:].rearrange("p (h n) -> p h n", n=64),
                        in1=ss[:, :].unsqueeze(2).to_broadcast([128, 8, 64]), op=ALU.mult), reads=[vp, ss],
                        writes=[aux])
                    em.dma("sp", P.auxs[0, pp:pp + 128, :], aux[:], reads=[aux], writes=[dummy])
                yield
                sg_ = sgd[d]
                pc = core.bank()
                em.mm(pc, pc[:, :], P.tri[:, d, :], sg_[:], True, True, reads=[P.tri, sg_])
                ptot = core.bank()
                em.mm(ptot, ptot[:, :], P.ones_f[:], sg_[:], True, True, reads=[P.ones_f, sg_])
                pw = core.bank()
                for h in range(8):
                    em.mm(pw, pw[0:64, h:h + 1], sg_[:, h * 64:(h + 1) * 64], P.ones_f[:, 0:1], True, True,
                          reads=[sg_, P.ones_f])
                em.act(WcT, WcT[:], pw[0:64, 0:8], AF.Exp, reads=[pw], scale=-C0)
                em.op("dve", lambda e: e.tensor_copy(out=t1[:], in_=pc[:, :]), reads=[pc], writes=[t1])
                em.op("dve", lambda e: e.tensor_tensor(out=t2[:], in0=t1[:], in1=sg_[:], op=ALU.subtract),
                      reads=[t1, sg_], writes=[t2])
                em.op("dve", lambda e: e.tensor_tensor(out=t3[:], in0=ptot[:, :], in1=t1[:], op=ALU.subtract),
                      reads=[ptot, t1], writes=[t3])
                yield
                em.act(E1, E1[:], t1[:], AF.Exp, reads=[t1], scale=-C0)
                em.act(E2, E2[:], t1[:], AF.Exp, reads=[t1], scale=C0)
                em.act(E1p, E1p[:], t2[:], AF.Exp, reads=[t2], scale=-C0)
                em.act(E3, E3[:], t3[:], AF.Exp, reads=[t3], scale=-C0)
                yield
                kd, al = kdir[d], alr[d]
                em.op("dve", lambda e: e.scalar_tensor_tensor(out=at[:], in0=kk[:], scalar=-1.0, in1=E1p[:],
                                                              op0=ALU.mult, op1=ALU.mult), reads=[kk, E1p], writes=[at])
                em.op("pool", lambda e: e.tensor_tensor(out=bb[:], in0=kk[:], in1=al[:], op=ALU.mult),
                      reads=[kk, al], writes=[bb])
                em.op("pool", lambda e: e.tensor_tensor(out=bt[:], in0=bb[:], in1=E2[:], op=ALU.mult),
                      reads=[bb, E2], writes=[bt])
                em.op("pool", lambda e: e.tensor_tensor(out=Bh[:], in0=bb[:], in1=E3[:], op=ALU.mult),
                      reads=[bb, E3], writes=[Bh])
                em.op("dve", lambda e: e.tensor_tensor(out=kt[:], in0=kd[:], in1=E2[:], op=ALU.mult),
                      reads=[kd, E2], writes=[kt])
                em.op("pool", lambda e: e.tensor_tensor(out=Kh[:], in0=kd[:], in1=E3[:], op=ALU.mult),
                      reads=[kd, E3], writes=[Kh])
                em.op("dve", lambda e: e.tensor_tensor(out=rt[:], in0=rp[:], in1=E1[:], op=ALU.mult),
                      reads=[rp, E1], writes=[rt])
                yield

            order = chunk_order(rev)
            for _ in prep(order[0], 0):
                pass
            for ci, t0 in enumerate(order):
                slot = ci % 2
                gnext = prep(order[ci + 1], 1 - slot) if ci + 1 < len(order) else None
                ops = {"ga": at, "gb": bt, "gk": kt, "gr": rt, "V": vp2[slot], "Atil": at, "Bh": Bh2[slot],
                       "Kh": Kh2[slot], "Rtil": rt}
                core.run_chunk(rev, ops, WcT2[slot], mk, P.yscr[d, ppos(t0):ppos(t0) + 128, :], filler=(gnext if PIPE else None))
                if gnext is not None:
                    for _ in gnext:
                        pass
        em.barrier()


def host_consts():
    idx = np.arange(128)
    r, c = idx[:, None], idx[None, :]
    m = np.zeros((128, 8, 128), np.float32)
    for i, cond in enumerate((c < r, c > r, c <= r, c >= r)):
        m[:, i, :] = cond.astype(np.float32)
        m[:, 4 + i, :] = np.where(cond, 0.0, NEG).astype(np.float32)
    tri = np.zeros((128, 2, 128), np.float32)
    tri[:, 0, :] = (r <= c)
    tri[:, 1, :] = (r >= c)
    bd = lambda b: ((r // b) == (c // b)).astype(np.float32)
    bmk = np.stack([bd(16), bd(32) - bd(16), bd(64) - bd(32), 1.0 - bd(64)], 1).astype(np.float32)
    return {"masks": m, "ident": np.eye(128, dtype=np.float32), "tri": tri, "bmasks": bmk}


def host_scan_inputs(inp, L):
    f32 = np.float32
    out = {}
    rw = np.zeros((L, 11, 512), f32)
    rw[:, 0] = inp["rwkv_w0"][:L, 0]
    rw[:, 1] = inp["rwkv_w0"][:L, 1]
    rw[:, 2] = inp["rwkv_a0"][:L, 0]
    rw[:, 3] = inp["rwkv_a0"][:L, 1]
    rw[:, 4] = inp["rwkv_k_k"][:L]
    rw[:, 5] = inp["rwkv_k_a"][:L]
    rw[:, 6] = inp["rwkv_r_k"][:L].reshape(L, 512)
    rw[:, 7] = inp["rwkv_gn_g"][:L]
    rw[:, 8] = inp["rwkv_gn_b"][:L]
    out["rw_row"] = rw
    mu = inp["rwkv_mu"][:L]
    out["rw_mu"] = np.ascontiguousarray(mu[:, 0:1536])
    muL = np.zeros((128, L, 3), f32)
    muL[0:64, :, 0] = mu[:, 1536:1600].T
    muL[0:64, :, 1] = mu[:, 1600:1664].T
    muL[0:96, :, 2] = mu[:, 1664:1760].T
    out["rw_muL"] = muL
    out["rw_w2"] = np.ascontiguousarray(inp["rwkv_w2"][:L].reshape(L, 64, 512))
    out["rw_a2"] = np.ascontiguousarray(inp["rwkv_a2"][:L].reshape(L, 64, 512))
    out["rw_g2"] = np.ascontiguousarray(inp["rwkv_g2"][:L])
    out["gd_conv"] = np.ascontiguousarray(inp["gdn_conv"][:L])
    gr = np.zeros((L, 3, 512), f32)
    gr[:, 0] = np.tile(inp["gdn_norm"][:L], (1, 4))
    gr[:, 1, 0:8] = inp["gdn_a_log"][:L].reshape(L, 8)
    gr[:, 1, 8:16] = inp["gdn_dt_bias"][:L].reshape(L, 8)
    out["gd_row"] = gr
    return out


def phase_gdn(P, l):
    em = P.em
    dummy = T()
    with ExitStack() as st:
        core = ScanCore(P, st, 128, "gd")
        f = lambda nm, shp=(128, 512): em.sb("gd_" + nm, list(shp), F32, st)
        cw = []
        for j in range(5):
            t = f("cw%d" % j, (128, 1536))
            em.dma("sp", t[:], P.gd_conv[l, j:j + 1, :].partition_broadcast(128), writes=[t])
            cw.append(t)
        prow = _bc_row(P, st, "gdp_row", P.gd_row[l, 1:2, :], 512)
        negea = f("negea", (128, 8))
        em.act(negea, negea[:], prow[:, 0:8], AF.Exp, reads=[prow])
        em.op("dve", lambda e: e.tensor_scalar_mul(out=negea[:], in0=negea[:], scalar1=-1.0), reads=[negea],
              writes=[negea])
        sh = [f("sh%d" % j) for j in range(5)]
        acc, tmp = f("acc"), f("tmp")
        qkv = [f("q"), f("k"), f("v")]
        ss = f("ss", (128, 4))
        ab = f("ab", (128, 16))
        gx, ge, gl = f("gx", (128, 8)), f("ge", (128, 8)), f("gl", (128, 8))
        gcol, beta = f("gcol", (128, 8)), f("beta", (128, 8))
        Gs, nG, eG, eTG, nb, nbeG = (f(n, (128, 4)) for n in ("Gs", "nG", "eG", "eTG", "nb", "nbeG"))
        ka, Atil, Rtil, zt = f("ka"), f("Atil"), f("Rtil"), f("zt")
        Kh2, Vp2 = [f("Kh0"), f("Kh1")], [f("Vp0"), f("Vp1")]
        etot2 = [f("etot0", (128, 4)), f("etot1", (128, 4))]
        diag = f("diag", (128, 4, 128))
        dtmp = f("dtmp", (128, 4, 128))
        Ds, DTs, DTi = f("Ds", (128, 4, 128)), f("DTs", (128, 4, 128)), f("DTi", (128, 4, 128))
        hv = lambda t: t[:].rearrange("p (h n) -> p h n", n=128)
        bc4 = lambda t: t[:, :].unsqueeze(2).to_broadcast([128, 4, 128])
        for d in (0, 1):
            rev = d == 1
            core.reset_state()
            mA, mAT, mATi = (4, 5, 7) if not rev else (5, 4, 6)
            mk = {"AT": (lambda g: DTs[:, :, :], DTs), "A": (lambda g: Ds[:, :, :], Ds),
                  "ArT": (lambda g: DTi[:, :, :], DTi)}
            def prep(t0, slot, d=d, mA=mA, mAT=mAT, mATi=mATi):
                Kh, Vp, etot = Kh2[slot], Vp2[slot], etot2[slot]
                pp = ppos(t0)
                for ci in range(3):
                    c0 = TOKC["gq"] + ci * 512
                    for j in range(5):
                        em.dma("sp", sh[j][:], P.pTok[pp + j - 2:pp + j - 2 + 128, c0:c0 + 512], writes=[sh[j]])
                    em.op("dve", lambda e: e.tensor_tensor(out=acc[:], in0=sh[0][:], in1=cw[0][:, ci * 512:(ci + 1) * 512],
                                                           op=ALU.mult), reads=[sh[0], cw[0]], writes=[acc])
                    for j in range(1, 5):
                        eng = "pool" if j % 2 else "dve"
                        em.op(eng, lambda e: e.tensor_tensor(out=sh[j][:], in0=sh[j][:],
                                                             in1=cw[j][:, ci * 512:(ci + 1) * 512], op=ALU.mult),
                              reads=[sh[j], cw[j]], writes=[sh[j]])
                        em.op("dve", lambda e: e.tensor_tensor(out=acc[:], in0=acc[:], in1=sh[j][:], op=ALU.add),
                              reads=[acc, sh[j]], writes=[acc])
                    em.act(qkv[ci], qkv[ci][:], acc[:], AF.Silu, reads=[acc])
                    yield
                for ci, sc in ((0, 128 ** -0.5), (1, 1.0)):
                    x_ = qkv[ci]
                    em.act(tmp, tmp[:], x_[:], AF.Square, reads=[x_])
                    em.op("dve", lambda e: e.tensor_reduce(out=ss[:], in_=hv(tmp), axis=AX.X, op=ALU.add),
                          reads=[tmp], writes=[ss])
                    em.act(ss, ss[:], ss[:], AF.Sqrt, reads=[ss, P.epsc], bias=P.epsc[:, 3:4])
                    em.op("dve", lambda e: e.reciprocal(out=ss[:], in_=ss[:]), reads=[ss], writes=[ss])
                    if sc != 1.0:
                        em.op("dve", lambda e: e.tensor_scalar_mul(out=ss[:], in0=ss[:], scalar1=sc), reads=[ss],
                              writes=[ss])
                    em.op("dve", lambda e: e.tensor_tensor(out=hv(x_), in0=hv(x_), in1=bc4(ss), op=ALU.mult),
                          reads=[x_, ss], writes=[x_])
                q_, k_, v_ = qkv
                if d == 0:
                    em.dma("sp", zt[:], P.pTok[pp:pp + 128, TOKC["z"]:TOKC["z"] + 512], writes=[zt])
                    em.act(zt, zt[:], zt[:], AF.Silu, reads=[zt])
                    em.dma("sp", P.auxs[2, pp:pp + 128, :], zt[:], reads=[zt], writes=[dummy])
                yield
                em.dma("sp", ab[:], P.pTok[pp:pp + 128, TOKC["ab"]:TOKC["ab"] + 16], writes=[ab])
                em.op("dve", lambda e: e.tensor_tensor(out=gx[:], in0=ab[:, 0:8], in1=prow[:, 8:16], op=ALU.add),
                      reads=[ab, prow], writes=[gx])
                em.act(ge, ge[:], gx[:], AF.Abs, reads=[gx])
                em.act(ge, ge[:], ge[:], AF.Exp, reads=[ge], scale=-1.0)
                em.act(gl, gl[:], ge[:], AF.Ln, reads=[ge, P.ones_f], bias=P.ones_f[:, 0:1])
                em.op("dve", lambda e: e.scalar_tensor_tensor(out=gcol[:], in0=gx[:], scalar=0.0, in1=gl[:],
                                                              op0=ALU.max, op1=ALU.add), reads=[gx, gl], writes=[gcol])
                em.op("dve", lambda e: e.tensor_tensor(out=gcol[:], in0=gcol[:], in1=negea[:], op=ALU.mult),
                      reads=[gcol, negea], writes=[gcol])
                em.act(beta, beta[:], ab[:, 8:16], AF.Sigmoid, reads=[ab])
                gd_, bd_ = gcol[:, d * 4:(d + 1) * 4], beta[:, d * 4:(d + 1) * 4]
                pG = core.bank()
                em.mm(pG, pG[:, 0:4], P.tri[:, d, :], gd_, True, True, reads=[P.tri, gcol])
                pT_ = core.bank()
                em.mm(pT_, pT_[:, 0:4], P.ones_f[:], gd_, True, True, reads=[P.ones_f, gcol])
                em.op("dve", lambda e: e.tensor_copy(out=Gs[:], in_=pG[:, 0:4]), reads=[pG], writes=[Gs])
                em.op("dve", lambda e: e.tensor_scalar_mul(out=nG[:], in0=Gs[:], scalar1=-1.0), reads=[Gs], writes=[nG])
                em.act(eG, eG[:], Gs[:], AF.Exp, reads=[Gs])
                em.op("dve", lambda e: e.tensor_copy(out=etot[:], in_=pT_[:, 0:4]), reads=[pT_], writes=[etot])
                em.op("dve", lambda e: e.tensor_tensor(out=eTG[:], in0=etot[:], in1=Gs[:], op=ALU.subtract),
                      reads=[etot, Gs], writes=[eTG])
                em.act(etot, etot[:], etot[:], AF.Exp, reads=[etot])
                em.act(eTG, eTG[:], eTG[:], AF.Exp, reads=[eTG])
                yield
                em.op("dve", lambda e: e.tensor_scalar_mul(out=nb[:], in0=bd_, scalar1=-1.0), reads=[beta], writes=[nb])
                em.op("dve", lambda e: e.tensor_tensor(out=nbeG[:], in0=nb[:], in1=eG[:], op=ALU.mult),
                      reads=[nb, eG], writes=[nbeG])
                em.op("dve", lambda e: e.tensor_tensor(out=hv(ka), in0=hv(k_), in1=bc4(nb), op=ALU.mult),
                      reads=[k_, nb], writes=[ka])
                em.op("pool", lambda e: e.tensor_tensor(out=hv(Atil), in0=hv(k_), in1=bc4(nbeG), op=ALU.mult),
                      reads=[k_, nbeG], writes=[Atil])
                em.op("dve", lambda e: e.tensor_tensor(out=hv(Kh), in0=hv(k_), in1=bc4(eTG), op=ALU.mult),
                      reads=[k_, eTG], writes=[Kh])
                em.op("pool", lambda e: e.tensor_tensor(out=hv(Rtil), in0=hv(q_), in1=bc4(eG), op=ALU.mult),
                      reads=[q_, eG], writes=[Rtil])
                em.op("dve", lambda e: e.tensor_tensor(out=hv(Vp), in0=hv(v_),
                                                       in1=beta[:, d * 4:(d + 1) * 4].unsqueeze(2).to_broadcast([128, 4, 128]),
                                                       op=ALU.mult), reads=[v_, beta], writes=[Vp])
                yield
                em.op("dve", lambda e: e.tensor_tensor(out=diag[:], in0=P.ident[:, :].unsqueeze(1).to_broadcast([128, 4, 128]),
                                                       in1=bc4(Gs), op=ALU.mult), reads=[P.ident, Gs], writes=[diag])
                pR = core.bank()
                for h in range(4):
                    em.mm(pR, pR[:, h * 128:(h + 1) * 128], P.ones_f[:], diag[:, h, :], True, True,
                          reads=[P.ones_f, diag])
                pRv = pR[:, :].rearrange("p (h t) -> p h t", h=4)
                for dst, sgn, mi, bias_t in ((Ds, -1.0, mA, Gs), (DTs, 1.0, mAT, nG), (DTi, 1.0, mATi, nG)):
                    em.op("dve", lambda e: e.scalar_tensor_tensor(
                        out=dtmp[:], in0=pRv, scalar=sgn, in1=P.masks[:, mi:mi + 1, :].to_broadcast([128, 4, 128]),
                        op0=ALU.mult, op1=ALU.add), reads=[pR, P.masks], writes=[dtmp])
                    for h in range(4):
                        em.act(dst, dst[:, h, :], dtmp[:, h, :], AF.Exp, reads=[dtmp, bias_t],
                               bias=bias_t[:, h:h + 1], writes=[dst])
                yield

            order = chunk_order(rev)
            for _ in prep(order[0], 0):
                pass
            k_, q_ = qkv[1], qkv[0]
            for ci, t0 in enumerate(order):
                slot = ci % 2
                gnext = prep(order[ci + 1], 1 - slot) if ci + 1 < len(order) else None
                ops = {"ga": ka, "gb": k_, "gk": k_, "gr": q_, "V": Vp2[slot], "Atil": Atil, "Bh": Kh2[slot],
                       "Kh": Kh2[slot], "Rtil": Rtil}
                core.run_chunk(rev, ops, etot2[slot], mk, P.yscr[2 + d, ppos(t0):ppos(t0) + 128, :],
                               filler=(gnext if PIPE else None))
                if gnext is not None:
                    for _ in gnext:
                        pass
        em.barrier()


def phase_mix_out(P, l, with_ctx):
    em = P.em
    dummy = T()
    with ExitStack() as st:
        f = lambda nm, shp=(128, 512): em.sb("mo_" + nm, list(shp), F32, st)
        gn_g = _bc_row(P, st, "mo_gng", P.rw_row[l, 7:8, :])
        gn_b = _bc_row(P, st, "mo_gnb", P.rw_row[l, 8:9, :])
        nrm = _bc_row(P, st, "mo_nrm", P.gd_row[l, 0:1, :])
        ya, yb, bv, gg, cen, sq = f("ya"), f("yb"), f("bv"), f("gg"), f("cen"), f("sq")
        s8 = f("s8", (128, 8))
        ob = [em.sb("mo_ob%d" % i, [128, 4, 128], BF16, st) for i in range(2)]
        pb = [em.ps("mo_pb%d" % i, [128, 512], F32, st) for i in range(2)]
        cnt = 0
        tiles = ([128 * i for i in range(CTX // 128)] if with_ctx else []) + \
            [CTX + 128 * i for i in range(SEQ // 128)]
        for t0 in tiles:
            pp = ppos(t0)
            for mix, (ia, ib, nh, eps_col) in enumerate(((0, 1, 8, 2), (2, 3, 4, 1))):
                n = 512 // nh
                hv = lambda t: t[:].rearrange("p (h n) -> p h n", n=n)
                bc = lambda t: t[:, 0:nh].unsqueeze(2).to_broadcast([128, nh, n])
                em.dma("sp", ya[:], P.yscr[ia, pp:pp + 128, :], writes=[ya])
                em.dma("sp", yb[:], P.yscr[ib, pp:pp + 128, :], writes=[yb])
                em.op("dve", lambda e: e.tensor_tensor(out=ya[:], in0=ya[:], in1=yb[:], op=ALU.add),
                      reads=[ya, yb], writes=[ya])
                if mix == 0:
                    em.dma("sp", bv[:], P.auxs[0, pp:pp + 128, :], writes=[bv])
                    em.dma("sp", gg[:], P.auxs[1, pp:pp + 128, :], writes=[gg])
                    em.op("dve", lambda e: e.tensor_reduce(out=s8[:, 0:nh], in_=hv(ya), axis=AX.X, op=ALU.add),
                          reads=[ya], writes=[s8])
                    em.op("dve", lambda e: e.tensor_scalar_mul(out=s8[:, 0:nh], in0=s8[:, 0:nh], scalar1=1.0 / n),
                          reads=[s8], writes=[s8])
                    em.op("dve", lambda e: e.tensor_tensor(out=hv(cen), in0=hv(ya), in1=bc(s8), op=ALU.subtract),
                          reads=[ya, s8], writes=[cen])
                else:
                    em.dma("sp", gg[:], P.auxs[2, pp:pp + 128, :], writes=[gg])
                    em.op("dve", lambda e: e.tensor_copy(out=cen[:], in_=ya[:]), reads=[ya], writes=[cen])
                em.act(sq, sq[:], cen[:], AF.Square, reads=[cen])
                em.op("dve", lambda e: e.tensor_reduce(out=s8[:, 0:nh], in_=hv(sq), axis=AX.X, op=ALU.add),
                      reads=[sq], writes=[s8])
                em.act(s8, s8[:, 0:nh], s8[:, 0:nh], AF.Sqrt, reads=[s8, P.epsc], bias=P.epsc[:, eps_col:eps_col + 1],
                       scale=1.0 / n)
                em.op("dve", lambda e: e.reciprocal(out=s8[:, 0:nh], in_=s8[:, 0:nh]), reads=[s8], writes=[s8])
                em.op("dve", lambda e: e.tensor_tensor(out=hv(cen), in0=hv(cen), in1=bc(s8), op=ALU.mult),
                      reads=[cen, s8], writes=[cen])
                if mix == 0:
                    em.op("pool", lambda e: e.tensor_tensor(out=cen[:], in0=cen[:], in1=gn_g[:], op=ALU.mult),
                          reads=[cen, gn_g], writes=[cen])
                    em.op("pool", lambda e: e.tensor_tensor(out=cen[:], in0=cen[:], in1=gn_b[:], op=ALU.add),
                          reads=[cen, gn_b], writes=[cen])
                    em.op("pool", lambda e: e.tensor_tensor(out=cen[:], in0=cen[:], in1=bv[:], op=ALU.add),
                          reads=[cen, bv], writes=[cen])
                else:
                    em.op("pool", lambda e: e.tensor_tensor(out=cen[:], in0=cen[:], in1=nrm[:], op=ALU.mult),
                          reads=[cen, nrm], writes=[cen])
                em.op("dve", lambda e: e.tensor_tensor(out=cen[:], in0=cen[:], in1=gg[:], op=ALU.mult),
                      reads=[cen, gg], writes=[cen])
                p_ = pb[cnt % 2]
                o_ = ob[cnt % 2]
                cnt += 1
                for j in range(4):
                    em.op("pe", lambda e: e.transpose(p_[:, j * 128:(j + 1) * 128], cen[:, j * 128:(j + 1) * 128],
                                                      P.ident[:]), reads=[cen, P.ident], writes=[p_])
                em.act(o_, o_[:], p_[:, :].rearrange("p (j t) -> p j t", j=4), AF.Copy, reads=[p_])
                r0 = 1024 + mix * 512
                em.dma("sp", P.mixT[r0:r0 + 512, t0:t0 + 128].rearrange("(j p) t -> p j t", p=128), o_[:],
                       reads=[o_], writes=[dummy])
        em.barrier()


def declare_mla(P):
    L = P.nl
    P.w_uq = P.din("mla_w_uq", [L, 512, 1536])
    P.w_uq_sw = P.din("mla_w_uq_sw", [L, 512, 512])
    P.w_ukv = P.din("mla_w_ukv", [L, 512, 2048])
    P.mlaT = P.din("mlaT", [128, L, 2, 4])
    P.ropeT = P.din("ropeT", [64, 2, SEQ])
    P.wb_uq = P.dscr("wb_uq", [L, 512, 2048], BF16)
    P.wb_ukv = P.dscr("wb_ukv", [L, 512, 2048], BF16)
    P.Kn = P.dscr("Kn", [1024, TT], BF16)
    P.Kr = P.dscr("Kr", [128, TT], BF16)
    P.sel64_in = P.din("sel64", [128, 1])
    P.Vt = P.dscr("Vt", [TT, 1024], BF16)
    P.Qn = P.dscr("Qn", [1024, TT], BF16)
    P.Qr = P.dscr("Qr", [8, 128, TT], BF16)


def phase_cast_mla(P):
    em = P.em
    P.wb_mla_t = [T() for _ in range(P.nl)]
    for l in range(P.nl):
        d = P.wb_mla_t[l]
        em.dma("pool", P.wb_uq[l, :, 0:1536], P.w_uq[l, :, :], writes=[d], partial=True)
        em.dma("pool", P.wb_uq[l, :, 1536:2048], P.w_uq_sw[l, :, :], writes=[d], partial=True)
        em.dma("pool", P.wb_ukv[l, :, :], P.w_ukv[l, :, :], writes=[d], partial=True)


def phase_mla(P, l, with_ctx):
    em = P.em
    dummy = T()
    with ExitStack() as st:
        f = lambda nm, shp, dt=F32: em.sb("ml_" + nm, list(shp), dt, st)
        gains = f("gains", (128, 2, 4))
        em.dma("sp", gains[:], P.mlaT[:, l, :, :], writes=[gains])
        wkv = f("wkv", (128, 4, 2048), BF16)
        wq = f("wq", (128, 4, 2048), BF16)
        wmt = getattr(P, "wb_mla_t", None)
        wrd = [wmt[l]] if wmt else []
        em.dma("sp", wkv[:], P.wb_ukv[l].rearrange("(k p) c -> p k c", p=128), reads=wrd, writes=[wkv])
        em.dma("sp", wq[:], P.wb_uq[l].rearrange("(k p) c -> p k c", p=128), reads=wrd, writes=[wq])
        wvv = f("wvv", (128, 4, 1024), BF16)
        for k in range(4):
            em.dma("sp", wvv[:, k, :].rearrange("p (h v) -> p h v", v=128),
                   P.wb_ukv[l, k * 128:(k + 1) * 128, :].rearrange("p (h two v) -> p h two v", two=2, v=128)[:, :, 1, :],
                   reads=wrd, writes=[wvv], partial=True)
        cx = f("cx", (128, 4, 512))
        cn = f("cn", (128, 4, 512), BF16)
        sq = [f("sq%d" % i, (128, 512)) for i in range(2)]
        rstd = f("rstd", (128, 512))
        krt, ksw, kro = f("krt", (64, 512)), f("ksw", (64, 512)), f("kro", (64, 512))
        rope = f("rope", (64, 2, 512))
        krb = f("krb", (128, 512), BF16)
        sel = f("sel", (128, 1))
        em.dma("sp", sel[:], P.sel64_in[:, :], writes=[sel])
        zt = f("zt", (128, 512))
        sqr = f("sqr", (128, 512), BF16)
        em.op("dve", lambda e: e.memset(sqr[:], 0.0), writes=[sqr])
        sqb = f("sqb", (128, 512), BF16)
        em.op("dve", lambda e: e.memset(zt[:], 0.0), writes=[zt])
        kmax = f("kmax", (128, 8))
        bmax = f("bmax", (128, 1))
        ob = [f("ob%d" % i, (128, 512), BF16) for i in range(3)]
        qrb = [f("qrb%d" % i, (128, 512), BF16) for i in range(2)]
        qr32, qs32 = f("qr32", (64, 512)), f("qs32", (64, 512))
        ps = [em.ps("ml_ps%d" % i, [128, 512], F32, st) for i in range(6)]
        pi = [0]
        oi = [0]

        def bank():
            pi[0] += 1
            return ps[pi[0] % 6]

        def obuf():
            oi[0] += 1
            return ob[oi[0] % 3]

        em.op("dve", lambda e: e.memset(kmax[:], 0.0), writes=[kmax])
        em.op("dve", lambda e: e.tensor_scalar(out=krb[64:128, :], in0=zt[64:128, :], scalar1=sel[64:128, 0:1],
                                               scalar2=None, op0=ALU.add), reads=[zt, sel], writes=[krb], partial=True)

        def rmsnorm_block(row0, gi, pp, n):
            em.dma("sp", cx[:, :, 0:n], P.pT[row0:row0 + 512, pp:pp + n].rearrange("(k p) t -> p k t", p=128),
                   writes=[cx])
            pss = bank()
            for k in range(4):
                s_ = sq[k % 2]
                em.act(s_, s_[:, 0:n], cx[:, k, 0:n], AF.Square, reads=[cx])
                em.mm(pss, pss[:, 0:n], P.ones_f[:], s_[:, 0:n], k == 0, k == 3, reads=[P.ones_f, s_])
            em.act(rstd, rstd[:, 0:n], pss[:, 0:n], AF.Sqrt, reads=[pss, P.epsc], bias=P.epsc[:, 1:2],
                   scale=1.0 / 512)
            em.op("dve", lambda e: e.reciprocal(out=rstd[:, 0:n], in_=rstd[:, 0:n]), reads=[rstd], writes=[rstd])
            for k in range(4):
                em.op("dve", lambda e: e.scalar_tensor_tensor(
                    out=cn[:, k, 0:n], in0=cx[:, k, 0:n], scalar=gains[:, gi, k:k + 1], in1=rstd[:, 0:n],
                    op0=ALU.mult, op1=ALU.mult), reads=[cx, gains, rstd], writes=[cn], partial=True)

        def load_rope(t0, n):
            em.dma("sp", rope[:, :, 0:n], P.ropeT[:, :, t0 - CTX:t0 - CTX + n], writes=[rope])

        def apply_rope(dst, x_, xsw, n, isctx):
            if isctx:
                em.op("dve", lambda e: e.tensor_copy(out=dst[0:64, 0:n], in_=x_[0:64, 0:n]), reads=[x_], writes=[dst])
                return
            em.op("dve", lambda e: e.tensor_tensor(out=dst[0:64, 0:n], in0=x_[0:64, 0:n], in1=rope[:, 0, 0:n],
                                                   op=ALU.mult), reads=[x_, rope], writes=[dst])
            em.op("pool", lambda e: e.tensor_tensor(out=xsw[0:64, 0:n], in0=xsw[0:64, 0:n], in1=rope[:, 1, 0:n],
                                                    op=ALU.mult), reads=[xsw, rope], writes=[xsw])
            em.op("dve", lambda e: e.tensor_tensor(out=dst[0:64, 0:n], in0=dst[0:64, 0:n], in1=xsw[0:64, 0:n],
                                                   op=ALU.add), reads=[dst, xsw], writes=[dst])

        for (t0, n, isctx) in TBLK:
            pp = ppos(t0)
            rmsnorm_block(512, 1, pp, n)
            if not isctx:
                load_rope(t0, n)
            em.dma("sp", krt[:, 0:n], P.pT[8 * 128:8 * 128 + 64, pp:pp + n], writes=[krt])
            em.dma("sp", ksw[:, 0:n], P.pT[9 * 128:9 * 128 + 64, pp:pp + n], writes=[ksw])
            apply_rope(kro, krt, ksw, n, isctx)
            em.act(krb, krb[0:64, 0:n], kro[0:64, 0:n], AF.Copy, reads=[kro], writes=[krb])
            em.dma("pool", P.Kr[:, t0:t0 + n], krb[:, 0:n], reads=[krb], writes=[dummy])
            em.act(sqr, sqr[0:64, 0:n], kro[0:64, 0:n], AF.Square, reads=[kro])
            for h in range(8):
                pk = bank()
                for k in range(4):
                    em.mm(pk, pk[:, 0:n], wkv[:, k, h * 256:h * 256 + 128], cn[:, k, 0:n], k == 0, k == 3,
                          reads=[wkv, cn])
                o_ = obuf()
                em.op("dve", lambda e: e.tensor_copy(out=o_[:, 0:n], in_=pk[:, 0:n]), reads=[pk], writes=[o_])
                em.dma("pool", P.Kn[h * 128:(h + 1) * 128, t0:t0 + n], o_[:, 0:n], reads=[o_], writes=[dummy])
                em.act(sqb, sqb[:, 0:n], o_[:, 0:n], AF.Square, reads=[o_])
                pn = bank()
                em.mm(pn, pn[:, 0:n], P.ones_b[:], sqb[:, 0:n], True, False, reads=[P.ones_b, sqb])
                em.mm(pn, pn[:, 0:n], P.ones_b[:], sqr[:, 0:n], False, True, reads=[P.ones_b, sqr])
                em.op("dve", lambda e: e.tensor_reduce(out=bmax[:], in_=pn[:, 0:n], axis=AX.X, op=ALU.max),
                      reads=[pn], writes=[bmax])
                em.op("dve", lambda e: e.tensor_tensor(out=kmax[:, h:h + 1], in0=kmax[:, h:h + 1], in1=bmax[:],
                                                       op=ALU.max), reads=[kmax, bmax], writes=[kmax])
            for tt in range(n // 128):
                for g in range(2):
                    pv = bank()
                    for k in range(4):
                        em.mm(pv, pv[:, :], cn[:, k, tt * 128:(tt + 1) * 128], wvv[:, k, g * 512:(g + 1) * 512],
                              k == 0, k == 3, reads=[cn, wvv])
                    o_ = obuf()
                    em.act(o_, o_[:], pv[:, :], AF.Copy, reads=[pv])
                    em.dma("pool", P.Vt[t0 + tt * 128:t0 + (tt + 1) * 128, g * 512:(g + 1) * 512], o_[:],
                           reads=[o_], writes=[dummy])
        if getattr(P, "mla_stage", 9) < 1:
            em.barrier()
            return
        nkm = f("nkm", (128, 8))
        em.act(nkm, nkm[:], kmax[:], AF.Sqrt, reads=[kmax])
        em.op("dve", lambda e: e.tensor_scalar_mul(out=nkm[:], in0=nkm[:], scalar1=-1.0), reads=[nkm], writes=[nkm])
        qcnt = 0
        for (t0, n, isctx) in TBLK:
            if isctx and not with_ctx:
                continue
            pp = ppos(t0)
            rmsnorm_block(0, 0, pp, n)
            if not isctx:
                load_rope(t0, n)
            for h in range(8):
                pq, pr, pw = bank(), bank(), bank()
                for k in range(4):
                    em.mm(pq, pq[:, 0:n], wq[:, k, h * 192:h * 192 + 128], cn[:, k, 0:n], k == 0, k == 3,
                          reads=[wq, cn])
                for k in range(4):
                    em.mm(pr, pr[0:64, 0:n], wq[:, k, h * 192 + 128:h * 192 + 192], cn[:, k, 0:n], k == 0, k == 3,
                          reads=[wq, cn])
                if not isctx:
                    for k in range(4):
                        em.mm(pw, pw[0:64, 0:n], wq[:, k, 1536 + h * 64:1536 + (h + 1) * 64], cn[:, k, 0:n],
                              k == 0, k == 3, reads=[wq, cn])
                    em.op("dve", lambda e: e.tensor_copy(out=qs32[0:64, 0:n], in_=pw[0:64, 0:n]), reads=[pw],
                          writes=[qs32])
                em.act(qr32, qr32[0:64, 0:n], pr[0:64, 0:n], AF.Copy, reads=[pr])
                apply_rope(kro, qr32, qs32, n, isctx)
                em.act(sqb, sqb[:, 0:n], pq[:, 0:n], AF.Square, reads=[pq])
                em.act(sqr, sqr[0:64, 0:n], kro[0:64, 0:n], AF.Square, reads=[kro])
                pn = bank()
                em.mm(pn, pn[:, 0:n], P.ones_b[:], sqb[:, 0:n], True, False, reads=[P.ones_b, sqb])
                em.mm(pn, pn[:, 0:n], P.ones_b[:], sqr[:, 0:n], False, True, reads=[P.ones_b, sqr])
                qb = qrb[qcnt % 2]
                qcnt += 1
                em.act(sq[1], sq[1][64:128, 0:n], pn[64:128, 0:n], AF.Sqrt, reads=[pn],
                       scale=ATTN_SCALE * ATTN_SCALE)
                em.op("dve", lambda e: e.tensor_scalar(out=qb[64:128, 0:n], in0=sq[1][64:128, 0:n],
                                                       scalar1=nkm[64:128, h:h + 1], scalar2=sel[64:128, 0:1],
                                                       op0=ALU.mult, op1=ALU.mult),
                      reads=[sq[1], nkm, sel], writes=[qb], partial=True)
                em.act(qb, qb[0:64, 0:n], kro[0:64, 0:n], AF.Copy, reads=[kro], scale=ATTN_SCALE, writes=[qb])
                em.dma("pool", P.Qr[h, :, t0:t0 + n], qb[:, 0:n], reads=[qb], writes=[dummy])
                o_ = obuf()
                em.act(o_, o_[:, 0:n], pq[:, 0:n], AF.Copy, reads=[pq], scale=ATTN_SCALE)
                em.dma("pool", P.Qn[h * 128:(h + 1) * 128, t0:t0 + n], o_[:, 0:n], reads=[o_], writes=[dummy])
        em.barrier()
    if getattr(P, "mla_stage", 9) < 2:
        return
    with ExitStack() as st:
        f = lambda nm, shp, dt=BF16: em.sb("at_" + nm, list(shp), dt, st)
        NKT = TT // 128
        kr = f("kr", (128, TT))
        em.dma("sp", kr[:], P.Kr[:, :], writes=[kr])
        kn = [f("kn%d" % i, (128, TT)) for i in range(2)]
        vv = [f("vv%d" % i, (128, NKT, 128)) for i in range(2)]
        qn = [f("qn%d" % i, (128, 512)) for i in range(2)]
        qr = [f("qr%d" % i, (128, 512)) for i in range(2)]
        pt = [f("pt%d" % i, (128, 512)) for i in range(3)]
        rd = [f("rd%d" % i, (128, 512), F32) for i in range(2)]
        ao = [f("ao%d" % i, (128, 512)) for i in range(2)]
        pS = [em.ps("at_pS%d" % i, [128, 512], F32, st) for i in range(3)]
        pO = [em.ps("at_pO%d" % i, [128, 512], F32, st) for i in range(2)]
        pD = [em.ps("at_pD%d" % i, [128, 512], F32, st) for i in range(2)]
        sc = 0
        qc = 0
        for h in range(8):
            k_n, v_ = kn[h % 2], vv[h % 2]
            em.dma("sp", k_n[:], P.Kn[h * 128:(h + 1) * 128, :], writes=[k_n])
            em.dma("sp", v_[:], P.Vt[:, h * 128:(h + 1) * 128].rearrange("(c p) v -> p c v", p=128), writes=[v_])
            for (t0, n, isctx) in TBLK:
                if isctx and not with_ctx:
                    continue
                q_n, q_r = qn[qc % 2], qr[qc % 2]
                p_O, p_D, r_d, a_o = pO[qc % 2], pD[qc % 2], rd[qc % 2], ao[qc % 2]
                qc += 1
                em.dma("sp", q_n[:, 0:n], P.Qn[h * 128:(h + 1) * 128, t0:t0 + n], writes=[q_n])
                em.dma("sp", q_r[:, 0:n], P.Qr[h, :, t0:t0 + n], writes=[q_r])
                nkt = CTX // 128 if isctx else NKT
                LOOK = 2
                ring = {}

                def scores(kt):
                    nonlocal sc
                    p_S, p_t = pS[sc % 3], pt[sc % 3]
                    sc += 1
                    ks = slice(kt * 128, (kt + 1) * 128)
                    em.mm(p_S, p_S[:, 0:n], k_n[:, ks], q_n[:, 0:n], True, False, reads=[k_n, q_n])
                    em.mm(p_S, p_S[:, 0:n], kr[:, ks], q_r[:, 0:n], False, True, reads=[kr, q_r])
                    em.act(p_t, p_t[:, 0:n], p_S[:, 0:n], AF.Exp, reads=[p_S])
                    ring[kt] = p_t

                for kt in range(min(LOOK, nkt)):
                    scores(kt)
                for kt in range(nkt):
                    p_t = ring.pop(kt)
                    em.mm(p_O, p_O[:, 0:n], v_[:, kt, :], p_t[:, 0:n], kt == 0, kt == nkt - 1, reads=[v_, p_t])
                    em.mm(p_D, p_D[:, 0:n], P.ones_b[:], p_t[:, 0:n], kt == 0, kt == nkt - 1,
                          reads=[P.ones_b, p_t])
                    if kt + LOOK < nkt:
                        scores(kt + LOOK)
                em.op("dve", lambda e: e.reciprocal(out=r_d[:, 0:n], in_=p_D[:, 0:n]), reads=[p_D], writes=[r_d])
                em.op("dve", lambda e: e.tensor_tensor(out=a_o[:, 0:n], in0=p_O[:, 0:n], in1=r_d[:, 0:n],
                                                       op=ALU.mult), reads=[p_O, r_d], writes=[a_o])
                em.dma("sp", P.mixT[h * 128:(h + 1) * 128, t0:t0 + n], a_o[:, 0:n], reads=[a_o], writes=[dummy])
        em.barrier()


def host_mla_inputs(inp, L, seq):
    f32 = np.float32
    perm = np.arange(64).reshape(2, 2, 16)[:, ::-1, :].reshape(64)
    out = {}
    out["mla_w_uq"] = inp["mla_w_uq"][:L]
    cols = np.concatenate([h * 192 + 128 + perm for h in range(8)])
    out["mla_w_uq_sw"] = np.ascontiguousarray(inp["mla_w_uq"][:L][:, :, cols])
    out["mla_w_ukv"] = inp["mla_w_ukv"][:L]
    g = np.stack([inp["mla_q_norm"][:L].reshape(L, 4, 128), inp["mla_kv_norm"][:L].reshape(L, 4, 128)], 1)
    out["mlaT"] = np.ascontiguousarray(g.transpose(3, 0, 1, 2)).astype(f32)
    t = np.arange(seq)
    pos = np.stack([t // 64, t % 64], -1).astype(f32)
    inv = (10000.0 ** (-np.arange(16, dtype=f32) / 16)).astype(f32)
    ang = pos[..., None] * inv
    cos, sin = np.cos(ang), np.sin(ang)
    ct = np.zeros((64, seq), f32)
    stb = np.zeros((64, seq), f32)
    for a in range(2):
        for half in range(2):
            r0 = a * 32 + half * 16
            ct[r0:r0 + 16] = cos[:, a, :].T
            stb[r0:r0 + 16] = (sin[:, a, :].T) * (-1.0 if half == 0 else 1.0)
    out["ropeT"] = np.ascontiguousarray(np.stack([ct, stb], 1))
    return out, perm


def build_program(nl=DEPTH, dbg=()):
    P = Prog(nl=nl, dbg=dbg)
    em = P.em
    declare_io(P)
    declare_dense(P)
    declare_scan(P)
    declare_mla(P)
    setup_consts(P)
    setup_scan_consts(P)
    phase_zero_pads(P)
    phase_cast_in(P)
    phase_cast_dense(P)
    phase_cast_mla(P)
    phase_mod(P)
    nblk = len(TBLK)
    for l in range(nl):
        with_ctx = l < nl - 1
        xsrc = P.xT0 if l == 0 else P.xB
        ft = lambda: [T() for _ in range(nblk)]
        sel_ = getattr(build_program, "phases", "imrgodf")
        if "i" in sel_:
            phase_inproj(P, l, xsrc, ft())
        if "m" in sel_:
            phase_mla(P, l, with_ctx)
        if "r" in sel_:
            phase_rwkv(P, l)
        if "g" in sel_:
            phase_gdn(P, l)
        if "o" in sel_:
            phase_mix_out(P, l, with_ctx)
        P.mixT_t = ft()
        if "d" in sel_:
            phase_outproj(P, l, xsrc, ft(), P.xA, ft(), with_ctx)
        if "f" in sel_:
            phase_ffn(P, l, P.xA, ft(), P.xB, ft(), with_ctx, final=(l == nl - 1))
    em.barrier()
    return P


def host_inputs(inp, b, nl, seq):
    f32 = np.float32
    L = nl
    x = np.asarray(inp["x"][b][:seq], f32)
    ctx = np.asarray(inp["ctx"][b], f32)
    im = {}
    im["xT0"] = np.ascontiguousarray(np.concatenate([ctx, x], 0).T)
    im["cvec"] = np.ascontiguousarray(np.stack([np.asarray(inp["c"][b]).reshape(16, 128).T,
                                                np.asarray(inp["c_ctx"]).reshape(16, 128).T], -1).astype(f32))
    im["w_mod"] = np.asarray(inp["w_mod"][:L], f32)
    im["b_modT"] = np.ascontiguousarray(np.asarray(inp["b_mod"][:L], f32).reshape(L, 96, 128).transpose(2, 0, 1))
    im["w_in"] = np.asarray(inp["w_in"][:L], f32)
    mi, perm = host_mla_inputs(inp, L, seq)
    im["w_in_krsw"] = np.ascontiguousarray(im["w_in"][:, :, 1024 + perm])
    im.update(mi)
    e = np.zeros((128, 1), f32)
    e[64] = 1.0
    im["sel64"] = e
    im["w_out"] = np.asarray(inp["w_out"][:L], f32)
    im["ffn_w_gate"] = np.asarray(inp["ffn_w_gate"][:L], f32)
    im["ffn_w_up"] = np.asarray(inp["ffn_w_up"][:L], f32)
    im["ffn_w_down"] = np.asarray(inp["ffn_w_down"][:L], f32)
    lnT = np.stack([np.asarray(inp[k][:L], f32).reshape(L, 16, 128) for k in ("ln1_g", "ln1_b", "ln2_g", "ln2_b")], 1)
    im["lnT"] = np.ascontiguousarray(lnT.transpose(3, 0, 1, 2))
    im.update(host_consts())
    im.update(host_scan_inputs(inp, L))
    return im


def kernel(**inputs):
    nb = inputs["x"].shape[0]
    P = build_program(DEPTH)
    shared = None
    in_maps = []
    for b in range(nb):
        im = host_inputs(inputs, b, DEPTH, SEQ)
        if shared is None:
            shared = im
        else:
            for k in im:
                if k not in ("xT0", "cvec"):
                    im[k] = shared[k]
        in_maps.append({k: v for k, v in im.items() if k in P.inputs})
    res = run_bass_kernel_spmd(P.nc, in_maps, core_ids=list(range(nb)))
    out = np.stack([np.ascontiguousarray(np.asarray(r["xOut"], np.float32).T) for r in res.results], 0)
    return out
```

```python
import math
from contextlib import ExitStack

import numpy as np
import concourse.bass as bass
import concourse.mybir as mybir
from concourse.bass_utils import run_bass_kernel_spmd

F32 = mybir.dt.float32
BF16 = mybir.dt.bfloat16
AF = mybir.ActivationFunctionType
ALU = mybir.AluOpType
AX = mybir.AxisListType

D = 2048
KD = D // 128
SEQ = 4096
CTX = 256
TT = SEQ + CTX
DEPTH = 4
D_FF = 5632
KF = D_FF // 128
IN_COLS = 4912
ALPHA = (2.0 * DEPTH) ** 0.25
ATTN_SCALE = 192 ** -0.5

CH = []
for i in range(4):
    CH.append(("cq%d" % i, 128 * i, 128))
for i in range(4):
    CH.append(("ckv%d" % i, 512 + 128 * i, 128))
CH += [("kr", 1024, 64), ("krsw", -1, 64), ("wd", 2624, 64), ("ad", 2688, 64)]
CH += [("gd", 2752, 96), ("ab", 4896, 16), ("pad0", -2, 0), ("pad1", -2, 0)]
for nm, c0 in (("r", 1088), ("k", 1600), ("v", 2112), ("gq", 2848), ("gk", 3360), ("gv", 3872), ("z", 4384)):
    for i in range(4):
        CH.append(("%s%d" % (nm, i), c0 + 128 * i, 128))
NCH = len(CH)
CHI = {c[0]: i for i, c in enumerate(CH)}
NCHP = NCH
NFM = 13
TOKC = {"r": 0, "k": 512, "v": 1024, "gq": 1536, "gk": 2048, "gv": 2560, "z": 3072, "ab": 3584}
NTOKC = 3600

TBLK = [(0, CTX, True)] + [(CTX + 512 * j, 512, False) for j in range(SEQ // 512)]
PC0 = 2
PL0 = 2 + CTX + 4
TTP = PL0 + SEQ + 2


def ppos(t):
    return PC0 + t if t < CTX else PL0 + (t - CTX)


def configure(seq):
    global SEQ, TT, TBLK, TTP
    SEQ = seq
    TT = SEQ + CTX
    TBLK = [(0, CTX, True)] + [(CTX + 512 * j, 512, False) for j in range(SEQ // 512)]
    TTP = PL0 + SEQ + 2


class T:
    __slots__ = ("h", "lw", "rd", "pg")

    def __init__(self, h=None):
        self.h = h
        self.lw = {}
        self.rd = {}
        self.pg = {}

    def __getitem__(self, idx):
        return self.h[idx]


class Em:
    NDMA = 6

    def __init__(self, nc, st):
        self.nc = nc
        self.st = st
        self.eng = {"pe": nc.tensor, "act": nc.scalar, "dve": nc.vector, "pool": nc.gpsimd, "sp": nc.sync}
        self.sem = {}
        self.cnt = {}
        for k in ("pe", "act", "dve", "pool", "sp"):
            self.sem[k] = st.enter_context(nc.semaphore("s_" + k))
            self.cnt[k] = 0
        self.dcnt = {"sp": 0, "pool": 0, "act": 0}
        for q in self.dcnt:
            for i in range(self.NDMA):
                self.sem[(q, i)] = st.enter_context(nc.semaphore("d_%s%d" % (q, i)))
        self.seen = {k: {} for k in self.eng}
        self.dmax = {}
        self.ninst = 0
        self.uid = 0

    def sb(self, name, shape, dt, st=None):
        self.uid += 1
        return T((st or self.st).enter_context(self.nc.sbuf_tensor("%s_u%d" % (name, self.uid), list(shape), dt)))

    def ps(self, name, shape, dt=F32, st=None):
        self.uid += 1
        return T((st or self.st).enter_context(self.nc.psum_tensor("%s_u%d" % (name, self.uid), list(shape), dt)))

    def _wait(self, eng, deps):
        e = self.eng[eng]
        seen = self.seen[eng]
        for sk, v in deps.items():
            if sk == "pe" and eng == "pe":
                continue
            if seen.get(sk, 0) < v:
                e.wait_ge(self.sem[sk], v)
                seen[sk] = v
                self.ninst += 1

    @staticmethod
    def _merge(d, s):
        for k, v in s.items():
            if d.get(k, 0) < v:
                d[k] = v

    def _deps(self, reads, writes, partial):
        deps = {}
        for b in reads:
            self._merge(deps, b.lw)
        for b in writes:
            self._merge(deps, b.rd)
            if partial:
                self._merge(deps, b.pg)
            else:
                self._merge(deps, b.lw)
        return deps

    def _record(self, ev, reads, writes, partial):
        sk, v = ev
        for b in reads:
            if b.rd.get(sk, 0) < v:
                b.rd[sk] = v
        for b in writes:
            if partial and not b.rd:
                if b.lw.get(sk, 0) < v:
                    b.lw[sk] = v
            else:
                pg = dict(b.rd)
                self._merge(pg, b.lw)
                b.pg = pg
                b.lw = {sk: v}
                b.rd = {}

    def op(self, eng, fn, reads=(), writes=(), partial=False):
        self._wait(eng, self._deps(reads, writes, partial))
        ins = fn(self.eng[eng])
        self.cnt[eng] += 1
        ins.then_inc(self.sem[eng], 1)
        self.ninst += 1
        self._record((eng, self.cnt[eng]), reads, writes, partial)

    def dma(self, q, out, in_, reads=(), writes=(), partial=False):
        n = self.dcnt[q]
        slot = n % self.NDMA
        sk = (q, slot)
        tgt = 16 * (n // self.NDMA + 1)
        deps = self._deps(reads, writes, partial)
        if tgt > 16:
            deps[sk] = max(deps.get(sk, 0), tgt - 16)
        self._wait(q, deps)
        self.eng[q].dma_start(out=out, in_=in_).then_inc(self.sem[sk], 16)
        self.dcnt[q] = n + 1
        self.dmax[sk] = tgt
        self.ninst += 1
        self._record((sk, tgt), reads, writes, partial)

    def barrier(self):
        allev = {k: v for k, v in self.cnt.items() if v > 0}
        allev.update(self.dmax)
        for eng in self.eng:
            d = {k: v for k, v in allev.items() if k != eng}
            self._wait(eng, d)
        for eng in ("act", "dve", "pool"):
            if self.cnt[eng] > 0:
                self._wait(eng, {eng: self.cnt[eng]})

    def mm(self, out_t, out_ap, lhsT, rhs, start, stop, reads):
        self.op("pe", lambda e: e.matmul(out_ap, lhsT, rhs, start=start, stop=stop),
                reads=reads, writes=[out_t])

    def act(self, eng_out_t, out_ap, in_ap, func, reads, bias=None, scale=None, accum=None, writes=None):
        kw = {}
        if bias is not None:
            kw["bias"] = bias
        if scale is not None:
            kw["scale"] = scale
        if accum is not None:
            kw["accum_out"] = accum
        self.op("act", lambda e: e.activation(out_ap, in_ap, func, **kw), reads=reads,
                writes=writes if writes is not None else [eng_out_t])


def _chunk_rows(ap2d):
    return ap2d.rearrange("(k p) t -> p k t", p=128)


class Prog:
    def __init__(self, nl=DEPTH, dbg=()):
        self.nl = nl
        self.dbg = set(dbg)
        nc = self.nc = bass.Bass("TRN2", target_bir_lowering=False)
        self.st = ExitStack()
        self.em = Em(nc, self.st)
        self.inputs = {}
        self.outs = {}

    def din(self, name, shape, dt=F32):
        self.inputs[name] = (shape, dt)
        return self.nc.dram_tensor(name, list(shape), dt, kind="ExternalInput").ap()

    def dscr(self, name, shape, dt=F32, out=False):
        kind = "ExternalOutput" if (out or name in self.dbg) else "Internal"
        if kind == "ExternalOutput":
            self.outs[name] = (shape, dt)
        return self.nc.dram_tensor(name, list(shape), dt, kind=kind).ap()


def declare_io(P):
    L = P.nl
    P.xT0 = P.din("xT0", [D, TT])
    P.cvec = P.din("cvec", [128, KD, 2])
    P.w_mod = P.din("w_mod", [L, D, 6 * D])
    P.b_modT = P.din("b_modT", [128, L, 96])
    P.w_in = P.din("w_in", [L, D, IN_COLS])
    P.w_in_krsw = P.din("w_in_krsw", [L, D, 64])
    P.wb_in = P.dscr("wb_in", [L, D, NCHP * 128], BF16)
    P.pT = P.dscr("pT", [NFM * 128, TTP])
    P.pTok = P.dscr("pTok", [TTP, NTOKC])


def phase_cast_in(P):
    em = P.em
    P.wb_in_t = [T() for _ in range(P.nl)]
    for l in range(P.nl):
        for i, (nm, c0, w) in enumerate(CH):
            if w == 0:
                continue
            src = P.w_in_krsw[l, :, :] if c0 < 0 else P.w_in[l, :, c0:c0 + w]
            for r0 in range(0, D, 512):
                em.dma("pool", P.wb_in[l, r0:r0 + 512, i * 128:i * 128 + w],
                       src[r0:r0 + 512, :], writes=[P.wb_in_t[l]], partial=True)


def phase_mod(P):
    em = P.em
    L = P.nl
    P.mod = em.sb("mod", [128, L, 96, 2], F32)
    P.mod1 = em.sb("mod1", [128, L, 96, 2], F32)
    P.modg = em.sb("modg", [128, L, 96, 2], F32)
    with ExitStack() as st:
        cv = em.sb("cv", [128, KD, 2], F32, st)
        sc = em.sb("sc", [128, KD, 2], F32, st)
        bm = em.sb("bm", [128, L, 96], F32, st)
        em.dma("sp", cv[:], P.cvec[:, :, :], writes=[cv])
        em.dma("sp", bm[:], P.b_modT[:, :, :], writes=[bm])
        em.act(sc, sc[:], cv[:], AF.Silu, reads=[cv])
        wt = [em.sb("wm%d" % i, [128, KD, 512], F32, st) for i in range(2)]
        pmf = [em.ps("pm%d" % i, [128, 512], F32, st) for i in range(2)]
        g = 0
        for l in range(L):
            wv = P.w_mod[l].rearrange("(k p) c -> p k c", p=128)
            for gi in range(24):
                w = wt[g % 2]
                p = pmf[g % 2]
                g += 1
                em.dma("sp", w[:], wv[:, :, gi * 512:(gi + 1) * 512], writes=[w])
                for c in range(4):
                    for k in range(KD):
                        em.mm(p, p[:, 2 * c:2 * c + 2], w[:, k, c * 128:(c + 1) * 128], sc[:, k, :],
                              k == 0, k == KD - 1, reads=[w, sc])
                em.op("dve", lambda e: e.tensor_tensor(
                    out=P.mod[:, l, gi * 4:(gi + 1) * 4, :], in0=p[:, 0:8].rearrange("p (c j) -> p c j", j=2),
                    in1=bm[:, l, gi * 4:(gi + 1) * 4].unsqueeze(2).to_broadcast([128, 4, 2]),
                    op=ALU.add), reads=[p, bm], writes=[P.mod], partial=True)
        em.op("dve", lambda e: e.tensor_scalar_add(out=P.mod1[:], in0=P.mod[:], scalar1=1.0),
              reads=[P.mod], writes=[P.mod1])
        em.op("dve", lambda e: e.tensor_scalar_mul(out=P.modg[:], in0=P.mod[:], scalar1=1.0 / ALPHA),
              reads=[P.mod], writes=[P.modg])
        em.barrier()


SH_M, SC_M, GT_M, SH_F, SC_F, GT_F = 0, 16, 32, 48, 64, 80


def phase_zero_pads(P):
    em = P.em
    with ExitStack() as st:
        z = em.sb("zpad", [128, NTOKC], F32, st)
        em.op("dve", lambda e: e.memset(z[:], 0.0), writes=[z])
        dummy = T()
        for a, b in ((0, PC0), (PC0 + CTX, PL0), (PL0 + SEQ, TTP)):
            em.dma("sp", P.pTok[a:b, :], z[0:b - a, :], reads=[z], writes=[dummy], partial=True)
            for c in range(NFM):
                em.dma("sp", P.pT[c * 128:(c + 1) * 128, a:b], z[:, 0:b - a], reads=[z], writes=[dummy], partial=True)
        em.barrier()


def phase_inproj(P, l, xsrc, xsrc_t):
    em = P.em
    with ExitStack() as st:
        xs = [em.sb("ip_xs%d" % i, [128, KD, 512], F32, st) for i in range(2)]
        xm = [em.sb("ip_xm%d" % i, [128, KD, 512], BF16, st) for i in range(2)]
        wt = [em.sb("ip_w%d" % i, [128, KD, 512], BF16, st) for i in range(2)]
        ps = [em.ps("ip_ps%d" % i, [128, 512], F32, st) for i in range(4)]
        sg = [em.sb("ip_sg%d" % i, [128, 512], F32, st) for i in range(4)]
        wv = P.wb_in[l].rearrange("(k p) c -> p k c", p=128)
        xv = _chunk_rows(xsrc)
        gcount = 0
        ccount = 0
        dummy = T()

        def evac(p, s, rows, cols):
            nonlocal ccount
            if ccount % 2 == 0:
                em.op("dve", lambda e: e.tensor_copy(out=s[0:rows, 0:cols], in_=p[0:rows, 0:cols]),
                      reads=[p], writes=[s])
            else:
                em.act(s, s[0:rows, 0:cols], p[0:rows, 0:cols], AF.Copy, reads=[p])
            ccount += 1

        for bi, (t0, n, isctx) in enumerate(TBLK):
            j = 1 if isctx else 0
            pp = ppos(t0)
            x_s, x_m = xs[bi % 2], xm[bi % 2]
            em.dma("sp", x_s[:, :, 0:n], xv[:, :, t0:t0 + n], reads=[xsrc_t[bi]], writes=[x_s])
            for k in range(KD):
                em.act(x_m, x_m[:, k, 0:n], x_s[:, k, 0:n], AF.Identity, reads=[x_s, P.mod, P.mod1],
                       bias=P.mod[:, l, SH_M + k, j:j + 1], scale=P.mod1[:, l, SC_M + k, j:j + 1],
                       writes=[x_m])
            for g in range(NCH // 4):
                w = wt[gcount % 2]
                gcount += 1
                em.dma("sp", w[:], wv[:, :, g * 512:(g + 1) * 512], reads=[P.wb_in_t[l]], writes=[w])
                if g < 4:
                    for c in range(4):
                        ci = g * 4 + c
                        nm, c0, wd = CH[ci]
                        if wd == 0:
                            continue
                        if nm == "ab":
                            for tt in range(n // 128):
                                p, s_ = ps[ccount % 4], sg[ccount % 4]
                                for k in range(KD):
                                    em.mm(p, p[:, 0:16], x_m[:, k, tt * 128:(tt + 1) * 128],
                                          w[:, k, c * 128:c * 128 + 16], k == 0, k == KD - 1, reads=[w, x_m])
                                evac(p, s_, 128, 16)
                                em.dma("pool", P.pTok[pp + tt * 128:pp + (tt + 1) * 128, TOKC["ab"]:TOKC["ab"] + 16],
                                       s_[:, 0:16], reads=[s_], writes=[dummy], partial=True)
                            continue
                        fi = ci if ci < 12 else 12
                        p, s_ = ps[ccount % 4], sg[ccount % 4]
                        for k in range(KD):
                            em.mm(p, p[0:wd, 0:n], w[:, k, c * 128:c * 128 + wd], x_m[:, k, 0:n],
                                  k == 0, k == KD - 1, reads=[w, x_m])
                        evac(p, s_, wd, n)
                        em.dma("pool", P.pT[fi * 128:fi * 128 + wd, pp:pp + n], s_[0:wd, 0:n],
                               reads=[s_], writes=[dummy], partial=True)
                else:
                    col0 = (g - 4) * 512
                    for tt in range(n // 128):
                        p, s_ = ps[ccount % 4], sg[ccount % 4]
                        for k in range(KD):
                            em.mm(p, p[:, :], x_m[:, k, tt * 128:(tt + 1) * 128], w[:, k, :],
                                  k == 0, k == KD - 1, reads=[w, x_m])
                        evac(p, s_, 128, 512)
                        em.dma("pool", P.pTok[pp + tt * 128:pp + (tt + 1) * 128, col0:col0 + 512], s_[:, :],
                               reads=[s_], writes=[dummy], partial=True)
        em.barrier()


def declare_dense(P):
    L = P.nl
    P.w_out = P.din("w_out", [L, D, D])
    P.w_gate = P.din("ffn_w_gate", [L, D, D_FF])
    P.w_up = P.din("ffn_w_up", [L, D, D_FF])
    P.w_down = P.din("ffn_w_down", [L, D_FF, D])
    P.lnT = P.din("lnT", [128, L, 4, KD])
    P.wb_out = P.dscr("wb_out", [L, D, D], BF16)
    P.wb_gate = P.dscr("wb_gate", [L, KF // 2, 128, KD, 256], BF16)
    P.wb_up = P.dscr("wb_up", [L, KF // 2, 128, KD, 256], BF16)
    P.wb_down = P.dscr("wb_down", [L, KD, 128, KF, 128], BF16)
    P.mixT = P.dscr("mixT", [D, TT], BF16)
    P.xA = P.dscr("xA", [D, TT])
    P.xB = P.dscr("xB", [D, TT])
    P.xOut = P.dscr("xOut", [D, SEQ], out=True)


def phase_cast_dense(P):
    em = P.em
    P.wb_dense_t = [T() for _ in range(P.nl)]
    for l in range(P.nl):
        for r0 in range(0, D, 256):
            em.dma("pool", P.wb_out[l, r0:r0 + 256, :], P.w_out[l, r0:r0 + 256, :],
                   writes=[P.wb_dense_t[l]], partial=True)
        for src, dst in ((P.w_gate, P.wb_gate), (P.w_up, P.wb_up)):
            for g in range(KF // 2):
                sv = src[l, :, g * 256:(g + 1) * 256].rearrange("(k p) c -> p k c", p=128)
                for k0 in range(0, KD, 8):
                    em.dma("pool", dst[l, g, :, k0:k0 + 8, :], sv[:, k0:k0 + 8, :],
                           writes=[P.wb_dense_t[l]], partial=True)
        for m in range(KD):
            sv = P.w_down[l, :, m * 128:(m + 1) * 128].rearrange("(k p) c -> p k c", p=128)
            for k0 in range(0, KF, 11):
                em.dma("pool", P.wb_down[l, m, :, k0:k0 + 11, :], sv[:, k0:k0 + 11, :],
                       writes=[P.wb_dense_t[l]], partial=True)


def setup_consts(P):
    em = P.em
    P.ones_f = em.sb("ones_f", [128, 128], F32)
    P.ones_b = em.sb("ones_b", [128, 128], BF16)
    em.op("dve", lambda e: e.memset(P.ones_f[:], 1.0), writes=[P.ones_f])
    em.op("dve", lambda e: e.memset(P.ones_b[:], 1.0), writes=[P.ones_b])
    P.epsc = em.sb("epsc", [128, 4], F32)
    em.op("dve", lambda e: e.memset(P.epsc[:, 0:1], 1e-5 / (ALPHA * ALPHA)), writes=[P.epsc])
    em.op("dve", lambda e: e.memset(P.epsc[:, 1:2], 1e-6), writes=[P.epsc])
    em.op("dve", lambda e: e.memset(P.epsc[:, 2:3], 64e-5), writes=[P.epsc])
    em.op("dve", lambda e: e.memset(P.epsc[:, 3:4], 1e-12), writes=[P.epsc])
    P.ln = em.sb("ln", [128, P.nl, 4, KD], F32)
    em.dma("sp", P.ln[:], P.lnT[:, :, :, :], writes=[P.ln])


class ResLN:
    def __init__(self, P, st, tag):
        em = P.em
        self.P = P
        self.s1 = em.ps(tag + "_s1", [128, 512], F32, st)
        self.s2 = em.ps(tag + "_s2", [128, 512], F32, st)
        self.sq = [em.sb(tag + "_sq%d" % i, [128, 512], F32, st) for i in range(2)]
        self.mean = em.sb(tag + "_mean", [128, 512], F32, st)
        self.rstd = em.sb(tag + "_rstd", [128, 512], F32, st)
        self.tmp = [em.sb(tag + "_tmp%d" % i, [128, 512], F32, st) for i in range(2)]
        self.og = [em.sb(tag + "_og%d" % i, [128, 512], F32, st) for i in range(2)]
        self.cnt = 0

    def add_chunk(self, m, psum_t, xblk, n, l, gslot, j):
        P, em = self.P, self.P.em
        em.op("dve", lambda e: e.scalar_tensor_tensor(
            out=xblk[:, m, 0:n], in0=psum_t[:, 0:n], scalar=P.modg[:, l, gslot + m, j:j + 1],
            in1=xblk[:, m, 0:n], op0=ALU.mult, op1=ALU.add), reads=[psum_t, xblk, P.modg], writes=[xblk])
        sq = self.sq[m % 2]
        em.act(sq, sq[:, 0:n], xblk[:, m, 0:n], AF.Square, reads=[xblk])
        em.mm(self.s1, self.s1[:, 0:n], P.ones_f[:], xblk[:, m, 0:n], m == 0, m == KD - 1, reads=[xblk, P.ones_f])
        em.mm(self.s2, self.s2[:, 0:n], P.ones_f[:], sq[:, 0:n], m == 0, m == KD - 1, reads=[sq, P.ones_f])

    def finish(self, xblk, n, l, lnslot, xdst, xdst_t, c0, dst2=None):
        P, em = self.P, self.P.em
        mean, rstd = self.mean, self.rstd
        em.op("dve", lambda e: e.tensor_scalar_mul(out=mean[:, 0:n], in0=self.s1[:, 0:n], scalar1=1.0 / D),
              reads=[self.s1], writes=[mean])
        em.op("dve", lambda e: e.tensor_tensor(out=rstd[:, 0:n], in0=mean[:, 0:n], in1=mean[:, 0:n], op=ALU.mult),
              reads=[mean], writes=[rstd])
        em.op("dve", lambda e: e.scalar_tensor_tensor(
            out=rstd[:, 0:n], in0=self.s2[:, 0:n], scalar=1.0 / D, in1=rstd[:, 0:n],
            op0=ALU.mult, op1=ALU.subtract), reads=[self.s2, rstd], writes=[rstd])
        em.act(rstd, rstd[:, 0:n], rstd[:, 0:n], AF.Sqrt, reads=[rstd, P.epsc], bias=P.epsc[:, 0:1])
        em.op("dve", lambda e: e.reciprocal(out=rstd[:, 0:n], in_=rstd[:, 0:n]), reads=[rstd], writes=[rstd])
        xv = _chunk_rows(xdst)
        for m in range(KD):
            tmp = self.tmp[m % 2]
            og = self.og[m % 2]
            em.op("dve", lambda e: e.tensor_tensor(out=tmp[:, 0:n], in0=xblk[:, m, 0:n], in1=mean[:, 0:n],
                                                   op=ALU.subtract), reads=[xblk, mean], writes=[tmp])
            em.op("pool", lambda e: e.tensor_tensor(out=tmp[:, 0:n], in0=tmp[:, 0:n], in1=rstd[:, 0:n],
                                                    op=ALU.mult), reads=[tmp, rstd], writes=[tmp])
            em.act(og, og[:, 0:n], tmp[:, 0:n], AF.Identity, reads=[tmp, P.ln],
                   bias=P.ln[:, l, lnslot + 1, m:m + 1], scale=P.ln[:, l, lnslot, m:m + 1])
            em.dma("pool", xdst[m * 128:(m + 1) * 128, c0:c0 + n], og[:, 0:n], reads=[og],
                   writes=[xdst_t], partial=True)
            if dst2 is not None:
                d2, d2_t, c2 = dst2
                em.dma("sp", d2[m * 128:(m + 1) * 128, c2:c2 + n], og[:, 0:n], reads=[og],
                       writes=[d2_t], partial=True)


def phase_outproj(P, l, xsrc, xsrc_t, xdst, xdst_t, with_ctx):
    em = P.em
    with ExitStack() as st:
        xs = [em.sb("op_xs%d" % i, [128, KD, 512], F32, st) for i in range(2)]
        am = [em.sb("op_am%d" % i, [128, KD, 512], BF16, st) for i in range(2)]
        wt = [em.sb("op_w%d" % i, [128, KD, 512], BF16, st) for i in range(2)]
        ps = [em.ps("op_ps%d" % i, [128, 512], F32, st) for i in range(2)]
        rl = ResLN(P, st, "op")
        wv = P.wb_out[l].rearrange("(k p) c -> p k c", p=128)
        xv = _chunk_rows(xsrc)
        av = _chunk_rows(P.mixT)
        gc = 0
        cc = 0
        for bi, (t0, n, isctx) in enumerate(TBLK):
            if isctx and not with_ctx:
                continue
            j = 1 if isctx else 0
            x_s, a_m = xs[bi % 2], am[bi % 2]
            em.dma("sp", x_s[:, :, 0:n], xv[:, :, t0:t0 + n], reads=[xsrc_t[bi]], writes=[x_s])
            em.dma("sp", a_m[:, :, 0:n], av[:, :, t0:t0 + n], reads=[P.mixT_t[bi]], writes=[a_m])
            for g in range(4):
                w = wt[gc % 2]
                gc += 1
                em.dma("sp", w[:], wv[:, :, g * 512:(g + 1) * 512], reads=[P.wb_dense_t[l]], writes=[w])
                for c in range(4):
                    m = g * 4 + c
                    p = ps[cc % 2]
                    cc += 1
                    for k in range(KD):
                        em.mm(p, p[:, 0:n], w[:, k, c * 128:(c + 1) * 128], a_m[:, k, 0:n],
                              k == 0, k == KD - 1, reads=[w, a_m])
                    rl.add_chunk(m, p, x_s, n, l, GT_M, j)
            rl.finish(x_s, n, l, 0, xdst, xdst_t[bi], t0)
        em.barrier()


def phase_ffn(P, l, xsrc, xsrc_t, xdst, xdst_t, with_ctx, final=False):
    em = P.em
    with ExitStack() as st:
        xs = em.sb("ff_xs", [128, KD, 512], F32, st)
        xm = em.sb("ff_xm", [128, KD, 512], BF16, st)
        hh = em.sb("ff_h", [128, KF, 512], BF16, st)
        wg = [em.sb("ff_wg%d" % i, [128, KD, 256], BF16, st) for i in range(2)]
        wu = [em.sb("ff_wu%d" % i, [128, KD, 256], BF16, st) for i in range(2)]
        wd = [em.sb("ff_wd%d" % i, [128, KF, 128], BF16, st) for i in range(2)]
        pg = [em.ps("ff_pg%d" % i, [128, 512], F32, st) for i in range(2)]
        pu = [em.ps("ff_pu%d" % i, [128, 512], F32, st) for i in range(2)]
        pd = [em.ps("ff_pd%d" % i, [128, 512], F32, st) for i in range(2)]
        sg = [em.sb("ff_sg%d" % i, [128, 512], F32, st) for i in range(2)]
        rl = ResLN(P, st, "ff")
        xv = _chunk_rows(xsrc)
        gc = 0
        cc = 0
        dc = 0
        for bi, (t0, n, isctx) in enumerate(TBLK):
            if isctx and not with_ctx:
                continue
            j = 1 if isctx else 0
            em.dma("sp", xs[:, :, 0:n], xv[:, :, t0:t0 + n], reads=[xsrc_t[bi]], writes=[xs])
            for k in range(KD):
                em.act(xm, xm[:, k, 0:n], xs[:, k, 0:n], AF.Identity, reads=[xs, P.mod, P.mod1],
                       bias=P.mod[:, l, SH_F + k, j:j + 1], scale=P.mod1[:, l, SC_F + k, j:j + 1])
            for g in range(KF // 2):
                w_g, w_u = wg[gc % 2], wu[gc % 2]
                gc += 1
                em.dma("sp", w_g[:], P.wb_gate[l, g], reads=[P.wb_dense_t[l]], writes=[w_g])
                em.dma("sp", w_u[:], P.wb_up[l, g], reads=[P.wb_dense_t[l]], writes=[w_u])
                for c in range(2):
                    m = g * 2 + c
                    p_g, p_u, s = pg[cc % 2], pu[cc % 2], sg[cc % 2]
                    cc += 1
                    for k in range(KD):
                        em.mm(p_g, p_g[:, 0:n], w_g[:, k, c * 128:(c + 1) * 128], xm[:, k, 0:n],
                              k == 0, k == KD - 1, reads=[w_g, xm])
                    for k in range(KD):
                        em.mm(p_u, p_u[:, 0:n], w_u[:, k, c * 128:(c + 1) * 128], xm[:, k, 0:n],
                              k == 0, k == KD - 1, reads=[w_u, xm])
                    em.act(s, s[:, 0:n], p_g[:, 0:n], AF.Silu, reads=[p_g])
                    em.op("dve", lambda e: e.tensor_tensor(out=hh[:, m, 0:n], in0=s[:, 0:n], in1=p_u[:, 0:n],
                                                           op=ALU.mult), reads=[s, p_u], writes=[hh], partial=True)
            for m in range(KD):
                w_d = wd[dc % 2]
                p_d = pd[dc % 2]
                dc += 1
                em.dma("sp", w_d[:], P.wb_down[l, m], reads=[P.wb_dense_t[l]], writes=[w_d])
                for k in range(KF):
                    em.mm(p_d, p_d[:, 0:n], w_d[:, k, :], hh[:, k, 0:n], k == 0, k == KF - 1, reads=[w_d, hh])
                rl.add_chunk(m, p_d, xs, n, l, GT_F, j)
            if final:
                rl.finish(xs, n, l, 2, P.xOut, xdst_t[bi], t0 - CTX)
            else:
                rl.finish(xs, n, l, 2, xdst, xdst_t[bi], t0)
        em.barrier()


NEG = -1.0e30
PIPE = True


def declare_scan(P):
    L = P.nl
    P.masks_in = P.din("masks", [128, 8, 128])
    P.ident_in = P.din("ident", [128, 128])
    P.tri_in = P.din("tri", [128, 2, 128])
    P.bmasks_in = P.din("bmasks", [128, 4, 128])
    P.rw_row = P.din("rw_row", [L, 11, 512])
    P.rw_mu = P.din("rw_mu", [L, 1536])
    P.rw_muL = P.din("rw_muL", [128, L, 3])
    P.rw_w2 = P.din("rw_w2", [L, 64, 512])
    P.rw_a2 = P.din("rw_a2", [L, 64, 512])
    P.rw_g2 = P.din("rw_g2", [L, 96, 512])
    P.gd_conv = P.din("gd_conv", [L, 5, 1536])
    P.gd_row = P.din("gd_row", [L, 3, 512])
    P.yscr = P.dscr("yscr", [4, TTP, 512])
    P.auxs = P.dscr("auxs", [3, TTP, 512])


def setup_scan_consts(P):
    em = P.em
    P.masks = em.sb("masks_sb", [128, 8, 128], F32)
    P.ident = em.sb("ident", [128, 128], F32)
    P.tri = em.sb("tri", [128, 2, 128], F32)
    em.dma("sp", P.masks[:], P.masks_in[:, :, :], writes=[P.masks])
    em.dma("sp", P.ident[:], P.ident_in[:, :], writes=[P.ident])
    em.dma("sp", P.tri[:], P.tri_in[:, :, :], writes=[P.tri])
    P.bmasks = em.sb("bmasks_sb", [128, 4, 128], F32)
    em.dma("sp", P.bmasks[:], P.bmasks_in[:, :, :], writes=[P.bmasks])


def chunk_order(rev):
    nc_ctx = CTX // 128
    nc_lat = SEQ // 128
    ctx = [128 * i for i in range(nc_ctx)]
    lat = [CTX + 128 * i for i in range(nc_lat)]
    if rev:
        return ctx[::-1] + lat[::-1]
    return ctx + lat


class ScanCore:
    def __init__(self, P, st, N, tag):
        em = P.em
        self.P, self.N, self.H = P, N, 512 // N
        N_, H = N, self.H
        self.pb = [em.ps(tag + "_pb%d" % i, [128, 512], F32, st) for i in range(8)]
        self.pbi = 0
        f = lambda nm, shp: em.sb(tag + "_" + nm, shp, F32, st)
        self.XT = {k: f("xt_" + k, [N_, H, 128]) for k in ("a", "b", "k", "r", "R")}
        self.AT = [f("AT0", [128, H, 128])]
        self.A = [f("A0", [128, H, 128])]
        self.AkT = f("AkT", [128, H, 128])
        self.ArbT = f("ArbT", [128, H, 128])
        self.ArkT = f("ArkT", [128, H, 128])
        self.Z = [f("Z%d" % i, [128, H, 2 * N_]) for i in range(2)]
        self.GH = H // 2
        GH = self.GH
        fb = lambda nm, shp: em.sb(tag + "_" + nm, shp, BF16, st)
        self.J = [[[fb("J%d%d%d" % (gi, i, k), [128, GH, 128]) for k in range(2)] for i in range(2)] for gi in range(2)]
        self.X = [fb("X%d" % gi, [128, GH, 128]) for gi in range(2)]
        self.XTt = [fb("XTt%d" % gi, [128, GH, 128]) for gi in range(2)]
        self.Zb = [fb("Zb%d" % gi, [128, GH, 2 * N_]) for gi in range(2)]
        self.Zt = [f("Zt%d" % gi, [128, GH, 2 * N_]) for gi in range(2)]
        self.WpT = f("WpT", [N_, H, 128])
        self.U = f("U", [128, H, N_])
        self.ST = f("ST", [N_, H, N_])
        self.Ysb = [f("Ysb%d" % i, [128, 512]) for i in range(2)]
        self.yi = 0

    def bank(self):
        b = self.pb[self.pbi % 8]
        self.pbi += 1
        return b

    def reset_state(self):
        em = self.P.em
        em.op("dve", lambda e: e.memset(self.ST[:], 0.0), writes=[self.ST])

    def transpose_to(self, key, src):
        P, em, N, H = self.P, self.P.em, self.N, self.H
        dst = self.XT[key]
        for g in range(H // 4):
            pb = self.bank()
            for hh in range(4):
                h = g * 4 + hh
                em.op("pe", lambda e: e.transpose(pb[0:N, hh * 128:(hh + 1) * 128], src[:, h * N:(h + 1) * N],
                                                  P.ident[:]), reads=[src, P.ident], writes=[pb])
            em.act(dst, dst[:, g * 4:(g + 1) * 4, :], pb[0:N, :].rearrange("p (h t) -> p h t", h=4), AF.Copy,
                   reads=[pb])
        return dst

    def gram(self, lkey, rkey, dst, mask_ap_fn, mask_t):
        P, em, N, H = self.P, self.P.em, self.N, self.H
        L_, R_ = self.XT[lkey], self.XT[rkey]
        for g in range(H // 4):
            pb = self.bank()
            for hh in range(4):
                h = g * 4 + hh
                em.mm(pb, pb[:, hh * 128:(hh + 1) * 128], L_[:, h, :], R_[:, h, :], True, True, reads=[L_, R_])
            em.op("dve", lambda e: e.tensor_tensor(
                out=dst[:, g * 4:(g + 1) * 4, :], in0=pb[:, :].rearrange("p (h t) -> p h t", h=4),
                in1=mask_ap_fn(g), op=ALU.mult), reads=[pb, mask_t], writes=[dst], partial=True)

    def run_chunk(self, rev, ops, WcT, masks, ydst_ap, filler=None):
        P, em, N, H = self.P, self.P.em, self.N, self.H
        hv = lambda t: t[:].rearrange("p (h n) -> p h n", n=N)
        same_bk = ops["gb"] is ops["gk"]
        self.transpose_to("a", ops["ga"])
        self.transpose_to("k", ops["gk"])
        if not same_bk:
            self.transpose_to("b", ops["gb"])
        bkey = "k" if same_bk else "b"
        self.transpose_to("r", ops["gr"])
        if ops["Rtil"] is ops["gr"]:
            Rkey = "r"
        else:
            self.transpose_to("R", ops["Rtil"])
            Rkey = "R"
        AT, A = self.AT[0], self.A[0]
        self.gram(bkey, "a", AT, *masks["AT"])
        self.gram("a", bkey, A, *masks["A"])
        if same_bk:
            AkT, ArbT = AT, None
        else:
            AkT, ArbT = self.AkT, self.ArbT
            self.gram("k", "a", AkT, *masks["AT"])
            self.gram("b", "r", ArbT, *masks["ArT"])
        ArkT = self.ArkT
        self.gram("k", "r", ArkT, *masks["ArT"])
        if same_bk:
            ArbT = ArkT
        V = ops["V"]
        Z = self.Z[0]
        pb = self.bank()
        for h in range(H):
            em.mm(pb, pb[:, h * N:(h + 1) * N], AkT[:, h, :], V[:, h * N:(h + 1) * N], True, True,
                  reads=[AkT, V])
        em.op("dve", lambda e: e.tensor_copy(out=Z[:, :, 0:N], in_=pb[:, :].rearrange("p (h n) -> p h n", n=N)),
              reads=[pb], writes=[Z], partial=True)
        em.op("pool", lambda e: e.tensor_copy(out=Z[:, :, N:2 * N], in_=hv(ops["Atil"])),
              reads=[ops["Atil"]], writes=[Z], partial=True)
        Zf = self.Z[1]
        HB = 512 // (2 * N)
        A0, AT0 = self.A[0], self.AT[0]
        GH = self.GH
        bm = lambda i: P.bmasks[:, i:i + 1, :].to_broadcast([128, GH, 128])
        idb = P.ident[:, :].unsqueeze(1).to_broadcast([128, GH, 128])

        def mmg(dst_pb, lhs_t, rhs_t):
            for hh in range(GH):
                em.mm(dst_pb, dst_pb[:, hh * 128:(hh + 1) * 128], lhs_t[:, hh, :], rhs_t[:, hh, :], True, True,
                      reads=[lhs_t, rhs_t])

        vg = lambda pb_: pb_[:, 0:GH * 128].rearrange("p (h t) -> p h t", h=GH)

        def inv_group(g):
            gs = slice(g * GH, (g + 1) * GH)
            X, XT = self.X[g], self.XTt[g]
            J = self.J[g]
            Ja, JaT = J[0]
            em.op("dve", lambda e: e.tensor_tensor(out=Ja[:], in0=A0[:, gs, :], in1=bm(0), op=ALU.mult),
                  reads=[A0, P.bmasks], writes=[Ja])
            em.op("pool", lambda e: e.tensor_tensor(out=JaT[:], in0=AT0[:, gs, :], in1=bm(0), op=ALU.mult),
                  reads=[AT0, P.bmasks], writes=[JaT])
            em.op("dve", lambda e: e.tensor_tensor(out=X[:], in0=Ja[:], in1=idb, op=ALU.add),
                  reads=[Ja, P.ident], writes=[X])
            em.op("pool", lambda e: e.tensor_tensor(out=XT[:], in0=JaT[:], in1=idb, op=ALU.add),
                  reads=[JaT, P.ident], writes=[XT])
            yield
            cur = 0
            for lev in range(3):
                Jc, JcT = J[cur]
                Jn, JnT = J[1 - cur]
                p1, p2 = self.bank(), self.bank()
                mmg(p1, JcT, Jc)
                mmg(p2, Jc, JcT)
                em.op("dve", lambda e: e.tensor_copy(out=Jn[:], in_=vg(p1)), reads=[p1], writes=[Jn])
                em.act(JnT, JnT[:], vg(p2), AF.Copy, reads=[p2])
                yield
                p3, p4 = self.bank(), self.bank()
                mmg(p3, JnT, X)
                mmg(p4, Jn, XT)
                em.op("dve", lambda e: e.tensor_tensor(out=X[:], in0=vg(p3), in1=X[:], op=ALU.add),
                      reads=[p3, X], writes=[X])
                em.op("dve", lambda e: e.tensor_tensor(out=XT[:], in0=vg(p4), in1=XT[:], op=ALU.add),
                      reads=[p4, XT], writes=[XT])
                yield
                cur = 1 - cur
            for bi in (1, 2, 3):
                Ao, AoT = J[0]
                Y, Y2 = J[1]
                em.op("dve", lambda e: e.tensor_tensor(out=Ao[:], in0=A0[:, gs, :], in1=bm(bi), op=ALU.mult),
                      reads=[A0, P.bmasks], writes=[Ao])
                em.op("pool", lambda e: e.tensor_tensor(out=AoT[:], in0=AT0[:, gs, :], in1=bm(bi), op=ALU.mult),
                      reads=[AT0, P.bmasks], writes=[AoT])
                last = bi == 3
                p2 = self.bank()
                mmg(p2, Ao, XT)
                if not last:
                    p1 = self.bank()
                    mmg(p1, AoT, X)
                    em.op("dve", lambda e: e.tensor_copy(out=Y[:], in_=vg(p1)), reads=[p1], writes=[Y])
                em.act(Y2, Y2[:], vg(p2), AF.Copy, reads=[p2])
                yield
                p4 = self.bank()
                mmg(p4, X, Y2)
                if not last:
                    p3 = self.bank()
                    mmg(p3, XT, Y)
                    em.op("dve", lambda e: e.tensor_tensor(out=X[:], in0=vg(p3), in1=X[:], op=ALU.add),
                          reads=[p3, X], writes=[X])
                em.op("dve", lambda e: e.tensor_tensor(out=XT[:], in0=vg(p4), in1=XT[:], op=ALU.add),
                      reads=[p4, XT], writes=[XT])
                yield
            nsub = max(1, GH // HB)
            hps = GH // nsub
            Zb, Zt = self.Zb[g], self.Zt[g]
            em.op("pool", lambda e: e.tensor_copy(out=Zb[:], in_=Z[:, gs, :]), reads=[Z], writes=[Zb])

            def apply_x(src_b, accumulate):
                for sub in range(nsub):
                    pb = self.bank()
                    for hh in range(hps):
                        h4 = sub * hps + hh
                        em.mm(pb, pb[:, hh * 2 * N:(hh + 1) * 2 * N], XT[:, h4, :], src_b[:, h4, :], True, True,
                              reads=[XT, src_b])
                    h0 = g * GH + sub * hps
                    pv = pb[:, 0:hps * 2 * N].rearrange("p (h n) -> p h n", h=hps)
                    if accumulate:
                        em.op("dve", lambda e: e.tensor_tensor(out=Zf[:, h0:h0 + hps, :], in0=pv,
                                                               in1=Zf[:, h0:h0 + hps, :], op=ALU.add),
                              reads=[pb, Zf], writes=[Zf], partial=True)
                    else:
                        em.act(Zf, Zf[:, h0:h0 + hps, :], pv, AF.Copy, reads=[pb], writes=[Zf])

            apply_x(Zb, False)
            yield
            em.op("pool", lambda e: e.tensor_tensor(out=Zt[:], in0=Z[:, gs, :], in1=Zf[:, gs, :], op=ALU.subtract),
                  reads=[Z, Zf], writes=[Zt])
            for sub in range(nsub):
                pb = self.bank()
                for hh in range(hps):
                    h4 = sub * hps + hh
                    h = g * GH + h4
                    em.mm(pb, pb[:, hh * 2 * N:(hh + 1) * 2 * N], AT0[:, h, :], Zf[:, h, :], True, True,
                          reads=[AT0, Zf])
                h4s = slice(sub * hps, (sub + 1) * hps)
                em.op("dve", lambda e: e.tensor_tensor(
                    out=Zb[:, h4s, :], in0=pb[:, 0:hps * 2 * N].rearrange("p (h n) -> p h n", h=hps),
                    in1=Zt[:, h4s, :], op=ALU.add), reads=[pb, Zt], writes=[Zb], partial=True)
            yield
            apply_x(Zb, True)
            yield

        gens = [inv_group(0), inv_group(1)]
        alive = [True, True]
        while any(alive):
            for gi in range(2):
                if alive[gi]:
                    try:
                        next(gens[gi])
                    except StopIteration:
                        alive[gi] = False
            if filler is not None:
                next(filler, None)
        WpT = self.WpT
        for g in range(H // 4):
            pb = self.bank()
            for hh in range(4):
                h = g * 4 + hh
                em.op("pe", lambda e: e.transpose(pb[0:N, hh * 128:(hh + 1) * 128], Zf[:, h, N:2 * N],
                                                  P.ident[:]), reads=[Zf, P.ident], writes=[pb])
            em.act(WpT, WpT[:, g * 4:(g + 1) * 4, :], pb[0:N, :].rearrange("p (h t) -> p h t", h=4), AF.Copy,
                   reads=[pb], writes=[WpT])
        ST, U = self.ST, self.U
        RT = self.XT[Rkey]
        pb = self.bank()
        for h in range(H):
            em.mm(pb, pb[:, h * N:(h + 1) * N], WpT[:, h, :], ST[:, h, :], True, True, reads=[WpT, ST])
        em.op("dve", lambda e: e.tensor_tensor(out=U[:], in0=pb[:, :].rearrange("p (h n) -> p h n", n=N),
                                               in1=Zf[:, :, 0:N], op=ALU.add), reads=[pb, Zf], writes=[U])
        pb = self.bank()
        for h in range(H):
            o = pb[:, h * N:(h + 1) * N]
            em.mm(pb, o, RT[:, h, :], ST[:, h, :], True, False, reads=[RT, ST])
            em.mm(pb, o, ArbT[:, h, :], U[:, h, :], False, False, reads=[ArbT, U])
            em.mm(pb, o, ArkT[:, h, :], V[:, h * N:(h + 1) * N], False, True, reads=[ArkT, V])
        ysb = self.Ysb[self.yi % 2]
        self.yi += 1
        em.act(ysb, ysb[:], pb[:, :], AF.Copy, reads=[pb])
        em.dma("sp", ydst_ap, ysb[:], reads=[ysb], writes=[T()])
        pb = self.bank()
        Bh, Kh = ops["Bh"], ops["Kh"]
        for h in range(H):
            o = pb[0:N, h * N:(h + 1) * N]
            em.mm(pb, o, Bh[:, h * N:(h + 1) * N], U[:, h, :], True, False, reads=[Bh, U])
            em.mm(pb, o, Kh[:, h * N:(h + 1) * N], V[:, h * N:(h + 1) * N], False, True, reads=[Kh, V])
        em.op("dve", lambda e: e.tensor_tensor(out=ST[:], in0=ST[:],
                                               in1=WcT[:, :].unsqueeze(2).to_broadcast([N, H, N]), op=ALU.mult),
              reads=[ST, WcT], writes=[ST])
        em.op("dve", lambda e: e.tensor_tensor(out=ST[:], in0=pb[0:N, :].rearrange("p (h n) -> p h n", n=N),
                                               in1=ST[:], op=ALU.add), reads=[pb, ST], writes=[ST])


C0 = math.exp(-0.5)


def _bc_row(P, st, name, src_row_ap, width=512):
    em = P.em
    t = em.sb(name, [128, width], F32, st)
    em.dma("sp", t[:], src_row_ap.partition_broadcast(128), writes=[t])
    return t


def phase_rwkv(P, l, want_ctx_out=True):
    em = P.em
    dummy = T()
    with ExitStack() as st:
        core = ScanCore(P, st, 64, "rw")
        f = lambda nm, shp=(128, 512): em.sb("rw_" + nm, list(shp), F32, st)
        prm = {}
        for i, nm in enumerate(["w0_0", "w0_1", "a0_0", "a0_1", "k_k", "k_a", "r_k"]):
            prm[nm] = _bc_row(P, st, "rwp_" + nm, P.rw_row[l, i:i + 1, :])
        omm = f("omm", (128, 1536))
        hmu = f("hmu", (128, 1536))
        em.dma("sp", omm[:], P.rw_mu[l:l + 1, :].partition_broadcast(128), writes=[omm])
        em.op("dve", lambda e: e.tensor_scalar_mul(out=hmu[:], in0=omm[:], scalar1=0.5), reads=[omm], writes=[hmu])
        em.op("dve", lambda e: e.tensor_scalar(out=omm[:], in0=omm[:], scalar1=-1.0, scalar2=1.0, op0=ALU.mult,
                                               op1=ALU.add), reads=[omm], writes=[omm])
        muL = f("muL", (128, 3))
        ommL = f("ommL", (128, 3))
        hmuL = f("hmuL", (128, 3))
        em.dma("sp", muL[:], P.rw_muL[:, l, :], writes=[muL])
        em.op("dve", lambda e: e.tensor_scalar_mul(out=hmuL[:], in0=muL[:], scalar1=0.5), reads=[muL], writes=[hmuL])
        em.op("dve", lambda e: e.tensor_scalar(out=ommL[:], in0=muL[:], scalar1=-1.0, scalar2=1.0, op0=ALU.mult,
                                               op1=ALU.add), reads=[muL], writes=[ommL])
        w2 = f("w2", (64, 512))
        a2 = f("a2", (64, 512))
        g2 = f("g2", (96, 512))
        em.dma("sp", w2[:], P.rw_w2[l, :, :], writes=[w2])
        em.dma("sp", a2[:], P.rw_a2[l, :, :], writes=[a2])
        em.dma("sp", g2[:], P.rw_g2[l, :, :], writes=[g2])
        cen, prv, nxt = f("cen"), f("prv"), f("nxt")
        rp, kp = f("rp"), f("kp")
        vp2 = [f("vp0"), f("vp1")]
        lo = f("lo", (128, 3, 130))
        loP = f("loP", (128, 3, 128))
        lot = f("lot", (128, 128))
        sgd = [f("sgd0"), f("sgd1")]
        alr = [f("alr0"), f("alr1")]
        gg = f("gg")
        kk = f("kk")
        kdir = [f("kdir0"), f("kdir1")]
        t1, t2, t3 = f("t1"), f("t2"), f("t3")
        ss = f("ss", (128, 8))
        E1, E1p, E2, E3 = f("E1"), f("E1p"), f("E2"), f("E3")
        at, bt, kt, rt, bb = f("at"), f("bt"), f("kt"), f("rt"), f("bb")
        Bh2, Kh2 = [f("Bh0"), f("Bh1")], [f("Kh0"), f("Kh1")]
        WcT2 = [f("WcT0", (64, 8)), f("WcT1", (64, 8))]
        aux = f("aux")
        for d in (0, 1):
            rev = d == 1
            core.reset_state()
            if not rev:
                mk = {"AT": (lambda g: P.masks[:, 1:2, :].to_broadcast([128, 4, 128]), P.masks),
                      "A": (lambda g: P.masks[:, 0:1, :].to_broadcast([128, 4, 128]), P.masks),
                      "ArT": (lambda g: P.masks[:, 3:4, :].to_broadcast([128, 4, 128]), P.masks)}
            else:
                mk = {"AT": (lambda g: P.masks[:, 0:1, :].to_broadcast([128, 4, 128]), P.masks),
                      "A": (lambda g: P.masks[:, 1:2, :].to_broadcast([128, 4, 128]), P.masks),
                      "ArT": (lambda g: P.masks[:, 2:3, :].to_broadcast([128, 4, 128]), P.masks)}
            def prep(t0, slot, d=d):
                vp, Bh, Kh, WcT = vp2[slot], Bh2[slot], Kh2[slot], WcT2[slot]
                pp = ppos(t0)
                for ci, dst in enumerate((rp, kp, vp)):
                    c0 = ci * 512
                    em.dma("sp", cen[:], P.pTok[pp:pp + 128, c0:c0 + 512], writes=[cen])
                    em.dma("sp", prv[:], P.pTok[pp - 1:pp + 127, c0:c0 + 512], writes=[prv])
                    em.dma("sp", nxt[:], P.pTok[pp + 1:pp + 129, c0:c0 + 512], writes=[nxt])
                    em.op("pool", lambda e: e.tensor_tensor(out=prv[:], in0=prv[:], in1=nxt[:], op=ALU.add),
                          reads=[prv, nxt], writes=[prv])
                    em.op("pool", lambda e: e.tensor_tensor(out=prv[:], in0=prv[:], in1=hmu[:, c0:c0 + 512],
                                                            op=ALU.mult), reads=[prv, hmu], writes=[prv])
                    em.op("dve", lambda e: e.tensor_tensor(out=cen[:], in0=cen[:], in1=omm[:, c0:c0 + 512],
                                                           op=ALU.mult), reads=[cen, omm], writes=[cen])
                    em.op("dve", lambda e: e.tensor_tensor(out=dst[:], in0=cen[:], in1=prv[:], op=ALU.add),
                          reads=[cen, prv], writes=[dst])
                yield
                for c, (fi, wdt) in enumerate(((10, 64), (11, 64), (12, 96))):
                    em.dma("sp", lo[0:wdt, c, :], P.pT[fi * 128:fi * 128 + wdt, pp - 1:pp + 129], writes=[lo],
                           partial=True)
                for c, wdt in enumerate((64, 64, 96)):
                    em.op("dve", lambda e: e.tensor_tensor(out=lot[0:wdt, :], in0=lo[0:wdt, c, 0:128],
                                                           in1=lo[0:wdt, c, 2:130], op=ALU.add),
                          reads=[lo], writes=[lot])
                    em.op("dve", lambda e: e.tensor_scalar_mul(out=lot[0:wdt, :], in0=lot[0:wdt, :],
                                                               scalar1=hmuL[0:wdt, c:c + 1]),
                          reads=[lot, hmuL], writes=[lot])
                    em.op("dve", lambda e: e.scalar_tensor_tensor(
                        out=loP[0:wdt, c, :], in0=lo[0:wdt, c, 1:129], scalar=ommL[0:wdt, c:c + 1],
                        in1=lot[0:wdt, :], op0=ALU.mult, op1=ALU.add), reads=[lo, ommL, lot], writes=[loP],
                        partial=True)
                em.act(loP, loP[0:64, 0, :], loP[0:64, 0, :], AF.Tanh, reads=[loP])
                em.act(loP, loP[0:96, 2, :], loP[0:96, 2, :], AF.Sigmoid, reads=[loP])
                yield
                for dd in (0, 1):
                    pb = core.bank()
                    em.mm(pb, pb[:, :], loP[dd * 32:(dd + 1) * 32, 0, :], w2[dd * 32:(dd + 1) * 32, :], True, True,
                          reads=[loP, w2])
                    em.op("dve", lambda e: e.tensor_tensor(out=sgd[dd][:], in0=pb[:, :], in1=prm["w0_%d" % dd][:],
                                                           op=ALU.add), reads=[pb, prm["w0_%d" % dd]],
                          writes=[sgd[dd]])
                    em.act(sgd[dd], sgd[dd][:], sgd[dd][:], AF.Sigmoid, reads=[sgd[dd]])
                    pb = core.bank()
                    em.mm(pb, pb[:, :], loP[dd * 32:(dd + 1) * 32, 1, :], a2[dd * 32:(dd + 1) * 32, :], True, True,
                          reads=[loP, a2])
                    em.op("dve", lambda e: e.tensor_tensor(out=alr[dd][:], in0=pb[:, :], in1=prm["a0_%d" % dd][:],
                                                           op=ALU.add), reads=[pb, prm["a0_%d" % dd]],
                          writes=[alr[dd]])
                    em.act(alr[dd], alr[dd][:], alr[dd][:], AF.Sigmoid, reads=[alr[dd]])
                yield
                em.op("dve", lambda e: e.tensor_tensor(out=kk[:], in0=kp[:], in1=prm["k_k"][:], op=ALU.mult),
                      reads=[kp, prm["k_k"]], writes=[kk])
                em.act(t1, t1[:], kk[:], AF.Square, reads=[kk])
                em.op("dve", lambda e: e.tensor_reduce(out=ss[:], in_=t1[:].rearrange("p (h n) -> p h n", n=64),
                                                       axis=AX.X, op=ALU.add), reads=[t1], writes=[ss])
                em.act(ss, ss[:], ss[:], AF.Sqrt, reads=[ss, P.epsc], bias=P.epsc[:, 3:4])
                em.op("dve", lambda e: e.reciprocal(out=ss[:], in_=ss[:]), reads=[ss], writes=[ss])
                em.op("dve", lambda e: e.tensor_tensor(
                    out=kk[:].rearrange("p (h n) -> p h n", n=64), in0=kk[:].rearrange("p (h n) -> p h n", n=64),
                    in1=ss[:, :].unsqueeze(2).to_broadcast([128, 8, 64]), op=ALU.mult), reads=[kk, ss], writes=[kk])
                yield
                for dd in (0, 1):
                    em.op("dve", lambda e: e.scalar_tensor_tensor(
                        out=t1[:], in0=alr[dd][:], scalar=-1.0, in1=prm["k_a"][:], op0=ALU.add, op1=ALU.mult),
                        reads=[alr[dd], prm["k_a"]], writes=[t1])
                    em.op("dve", lambda e: e.scalar_tensor_tensor(
                        out=kdir[dd][:], in0=t1[:], scalar=1.0, in1=kp[:], op0=ALU.add, op1=ALU.mult),
                        reads=[t1, kp], writes=[kdir[dd]])
                yield
                if d == 0:
                    pb = core.bank()
                    em.mm(pb, pb[:, :], loP[0:96, 2, :], g2[0:96, :], True, True, reads=[loP, g2])
                    em.act(gg, gg[:], pb[:, :], AF.Copy, reads=[pb])
                    em.dma("sp", P.auxs[1, pp:pp + 128, :], gg[:], reads=[gg], writes=[dummy])
                    em.op("pool", lambda e: e.tensor_tensor(out=t2[:], in0=kdir[0][:], in1=kdir[1][:], op=ALU.add),
                          reads=[kdir[0], kdir[1]], writes=[t2])
                    em.op("pool", lambda e: e.tensor_tensor(out=t2[:], in0=t2[:], in1=rp[:], op=ALU.mult),
                          reads=[t2, rp], writes=[t2])
                    em.op("pool", lambda e: e.tensor_tensor(out=t2[:], in0=t2[:], in1=prm["r_k"][:], op=ALU.mult),
                          reads=[t2, prm["r_k"]], writes=[t2])
                    em.op("dve", lambda e: e.tensor_reduce(out=ss[:], in_=t2[:].rearrange("p (h n) -> p h n", n=64),
                                                           axis=AX.X, op=ALU.add), reads=[t2], writes=[ss])
                    em.op("dve", lambda e: e.tensor_tensor(
                        out=aux[:].rearrange("p (h n) -> p h n", n=64), in0=vp[:].rearrange("p (h n) -> p h n", n=64),
                        in1=ss[:, :].unsqueeze(2).to_broadcast([128, 8, 64]), op=ALU.mult), reads=[vp, ss],
                        writes=[aux])
                    em.dma("sp", P.auxs[0, pp:pp + 128, :], aux[:], reads=[aux], writes=[dummy])
                yield
                sg_ = sgd[d]
                pc = core.bank()
                em.mm(pc, pc[:, :], P.tri[:, d, :], sg_[:], True, True, reads=[P.tri, sg_])
                ptot = core.bank()
                em.mm(ptot, ptot[:, :], P.ones_f[:], sg_[:], True, True, reads=[P.ones_f, sg_])
                pw = core.bank()
                for h in range(8):
                    em.mm(pw, pw[0:64, h:h + 1], sg_[:, h * 64:(h + 1) * 64], P.ones_f[:, 0:1], True, True,
                          reads=[sg_, P.ones_f])
                em.act(WcT, WcT[:], pw[0:64, 0:8], AF.Exp, reads=[pw], scale=-C0)
                em.op("dve", lambda e: e.tensor_copy(out=t1[:], in_=pc[:, :]), reads=[pc], writes=[t1])
                em.op("dve", lambda e: e.tensor_tensor(out=t2[:], in0=t1[:], in1=sg_[:], op=ALU.subtract),
                      reads=[t1, sg_], writes=[t2])
                em.op("dve", lambda e: e.tensor_tensor(out=t3[:], in0=ptot[:, :], in1=t1[:], op=ALU.subtract),
                      reads=[ptot, t1], writes=[t3])
                yield
                em.act(E1, E1[:], t1[:], AF.Exp, reads=[t1], scale=-C0)
                em.act(E2, E2[:], t1[:], AF.Exp, reads=[t1], scale=C0)
                em.act(E1p, E1p[:], t2[:], AF.Exp, reads=[t2], scale=-C0)
                em.act(E3, E3[:], t3[:], AF.Exp, reads=[t3], scale=-C0)
                yield
                kd, al = kdir[d], alr[d]
                em.op("dve", lambda e: e.scalar_tensor_tensor(out=at[:], in0=kk[:], scalar=-1.0, in1=E1p[:],
                                                              op0=ALU.mult, op1=ALU.mult), reads=[kk, E1p], writes=[at])
                em.op("pool", lambda e: e.tensor_tensor(out=bb[:], in0=kk[:], in1=al[:], op=ALU.mult),
                      reads=[kk, al], writes=[bb])
                em.op("pool", lambda e: e.tensor_tensor(out=bt[:], in0=bb[:], in1=E2[:], op=ALU.mult),
                      reads=[bb, E2], writes=[bt])
                em.op("pool", lambda e: e.tensor_tensor(out=Bh[:], in0=bb[:], in1=E3[:], op=ALU.mult),
                      reads=[bb, E3], writes=[Bh])
                em.op("dve", lambda e: e.tensor_tensor(out=kt[:], in0=kd[:], in1=E2[:], op=ALU.mult),
                      reads=[kd, E2], writes=[kt])
                em.op("pool", lambda e: e.tensor_tensor(out=Kh[:], in0=kd[:], in1=E3[:], op=ALU.mult),
                      reads=[kd, E3], writes=[Kh])
                em.op("dve", lambda e: e.tensor_tensor(out=rt[:], in0=rp[:], in1=E1[:], op=ALU.mult),
                      reads=[rp, E1], writes=[rt])
                yield

            order = chunk_order(rev)
            for _ in prep(order[0], 0):
                pass
            for ci, t0 in enumerate(order):
                slot = ci % 2
                gnext = prep(order[ci + 1], 1 - slot) if ci + 1 < len(order) else None
                ops = {"ga": at, "gb": bt, "gk": kt, "gr": rt, "V": vp2[slot], "Atil": at, "Bh": Bh2[slot],
                       "Kh": Kh2[slot], "Rtil": rt}
                core.run_chunk(rev, ops, WcT2[slot], mk, P.yscr[d, ppos(t0):ppos(t0) + 128, :], filler=(gnext if PIPE else None))
                if gnext is not None:
                    for _ in gnext:
                        pass
        em.barrier()


def host_consts():
    idx = np.arange(128)
    r, c = idx[:, None], idx[None, :]
    m = np.zeros((128, 8, 128), np.float32)
    for i, cond in enumerate((c < r, c > r, c <= r, c >= r)):
        m[:, i, :] = cond.astype(np.float32)
        m[:, 4 + i, :] = np.where(cond, 0.0, NEG).astype(np.float32)
    tri = np.zeros((128, 2, 128), np.float32)
    tri[:, 0, :] = (r <= c)
    tri[:, 1, :] = (r >= c)
    bd = lambda b: ((r // b) == (c // b)).astype(np.float32)
    bmk = np.stack([bd(16), bd(32) - bd(16), bd(64) - bd(32), 1.0 - bd(64)], 1).astype(np.float32)
    return {"masks": m, "ident": np.eye(128, dtype=np.float32), "tri": tri, "bmasks": bmk}


def host_scan_inputs(inp, L):
    f32 = np.float32
    out = {}
    rw = np.zeros((L, 11, 512), f32)
    rw[:, 0] = inp["rwkv_w0"][:L, 0]
    rw[:, 1] = inp["rwkv_w0"][:L, 1]
    rw[:, 2] = inp["rwkv_a0"][:L, 0]
    rw[:, 3] = inp["rwkv_a0"][:L, 1]
    rw[:, 4] = inp["rwkv_k_k"][:L]
    rw[:, 5] = inp["rwkv_k_a"][:L]
    rw[:, 6] = inp["rwkv_r_k"][:L].reshape(L, 512)
    rw[:, 7] = inp["rwkv_gn_g"][:L]
    rw[:, 8] = inp["rwkv_gn_b"][:L]
    out["rw_row"] = rw
    mu = inp["rwkv_mu"][:L]
    out["rw_mu"] = np.ascontiguousarray(mu[:, 0:1536])
    muL = np.zeros((128, L, 3), f32)
    muL[0:64, :, 0] = mu[:, 1536:1600].T
    muL[0:64, :, 1] = mu[:, 1600:1664].T
    muL[0:96, :, 2] = mu[:, 1664:1760].T
    out["rw_muL"] = muL
    out["rw_w2"] = np.ascontiguousarray(inp["rwkv_w2"][:L].reshape(L, 64, 512))
    out["rw_a2"] = np.ascontiguousarray(inp["rwkv_a2"][:L].reshape(L, 64, 512))
    out["rw_g2"] = np.ascontiguousarray(inp["rwkv_g2"][:L])
    out["gd_conv"] = np.ascontiguousarray(inp["gdn_conv"][:L])
    gr = np.zeros((L, 3, 512), f32)
    gr[:, 0] = np.tile(inp["gdn_norm"][:L], (1, 4))
    gr[:, 1, 0:8] = inp["gdn_a_log"][:L].reshape(L, 8)
    gr[:, 1, 8:16] = inp["gdn_dt_bias"][:L].reshape(L, 8)
    out["gd_row"] = gr
    return out


def phase_gdn(P, l):
    em = P.em
    dummy = T()
    with ExitStack() as st:
        core = ScanCore(P, st, 128, "gd")
        f = lambda nm, shp=(128, 512): em.sb("gd_" + nm, list(shp), F32, st)
        cw = []
        for j in range(5):
            t = f("cw%d" % j, (128, 1536))
            em.dma("sp", t[:], P.gd_conv[l, j:j + 1, :].partition_broadcast(128), writes=[t])
            cw.append(t)
        prow = _bc_row(P, st, "gdp_row", P.gd_row[l, 1:2, :], 512)
        negea = f("negea", (128, 8))
        em.act(negea, negea[:], prow[:, 0:8], AF.Exp, reads=[prow])
        em.op("dve", lambda e: e.tensor_scalar_mul(out=negea[:], in0=negea[:], scalar1=-1.0), reads=[negea],
              writes=[negea])
        sh = [f("sh%d" % j) for j in range(5)]
        acc, tmp = f("acc"), f("tmp")
        qkv = [f("q"), f("k"), f("v")]
        ss = f("ss", (128, 4))
        ab = f("ab", (128, 16))
        gx, ge, gl = f("gx", (128, 8)), f("ge", (128, 8)), f("gl", (128, 8))
        gcol, beta = f("gcol", (128, 8)), f("beta", (128, 8))
        Gs, nG, eG, eTG, nb, nbeG = (f(n, (128, 4)) for n in ("Gs", "nG", "eG", "eTG", "nb", "nbeG"))
        ka, Atil, Rtil, zt = f("ka"), f("Atil"), f("Rtil"), f("zt")
        Kh2, Vp2 = [f("Kh0"), f("Kh1")], [f("Vp0"), f("Vp1")]
        etot2 = [f("etot0", (128, 4)), f("etot1", (128, 4))]
        diag = f("diag", (128, 4, 128))
        dtmp = f("dtmp", (128, 4, 128))
        Ds, DTs, DTi = f("Ds", (128, 4, 128)), f("DTs", (128, 4, 128)), f("DTi", (128, 4, 128))
        hv = lambda t: t[:].rearrange("p (h n) -> p h n", n=128)
        bc4 = lambda t: t[:, :].unsqueeze(2).to_broadcast([128, 4, 128])
        for d in (0, 1):
            rev = d == 1
            core.reset_state()
            mA, mAT, mATi = (4, 5, 7) if not rev else (5, 4, 6)
            mk = {"AT": (lambda g: DTs[:, :, :], DTs), "A": (lambda g: Ds[:, :, :], Ds),
                  "ArT": (lambda g: DTi[:, :, :], DTi)}
            def prep(t0, slot, d=d, mA=mA, mAT=mAT, mATi=mATi):
                Kh, Vp, etot = Kh2[slot], Vp2[slot], etot2[slot]
                pp = ppos(t0)
                for ci in range(3):
                    c0 = TOKC["gq"] + ci * 512
                    for j in range(5):
                        em.dma("sp", sh[j][:], P.pTok[pp + j - 2:pp + j - 2 + 128, c0:c0 + 512], writes=[sh[j]])
                    em.op("dve", lambda e: e.tensor_tensor(out=acc[:], in0=sh[0][:], in1=cw[0][:, ci * 512:(ci + 1) * 512],
                                                           op=ALU.mult), reads=[sh[0], cw[0]], writes=[acc])
                    for j in range(1, 5):
                        eng = "pool" if j % 2 else "dve"
                        em.op(eng, lambda e: e.tensor_tensor(out=sh[j][:], in0=sh[j][:],
                                                             in1=cw[j][:, ci * 512:(ci + 1) * 512], op=ALU.mult),
                              reads=[sh[j], cw[j]], writes=[sh[j]])
                        em.op("dve", lambda e: e.tensor_tensor(out=acc[:], in0=acc[:], in1=sh[j][:], op=ALU.add),
                              reads=[acc, sh[j]], writes=[acc])
                    em.act(qkv[ci], qkv[ci][:], acc[:], AF.Silu, reads=[acc])
                    yield
                for ci, sc in ((0, 128 ** -0.5), (1, 1.0)):
                    x_ = qkv[ci]
                    em.act(tmp, tmp[:], x_[:], AF.Square, reads=[x_])
                    em.op("dve", lambda e: e.tensor_reduce(out=ss[:], in_=hv(tmp), axis=AX.X, op=ALU.add),
                          reads=[tmp], writes=[ss])
                    em.act(ss, ss[:], ss[:], AF.Sqrt, reads=[ss, P.epsc], bias=P.epsc[:, 3:4])
                    em.op("dve", lambda e: e.reciprocal(out=ss[:], in_=ss[:]), reads=[ss], writes=[ss])
                    if sc != 1.0:
                        em.op("dve", lambda e: e.tensor_scalar_mul(out=ss[:], in0=ss[:], scalar1=sc), reads=[ss],
                              writes=[ss])
                    em.op("dve", lambda e: e.tensor_tensor(out=hv(x_), in0=hv(x_), in1=bc4(ss), op=ALU.mult),
                          reads=[x_, ss], writes=[x_])
                q_, k_, v_ = qkv
                if d == 0:
                    em.dma("sp", zt[:], P.pTok[pp:pp + 128, TOKC["z"]:TOKC["z"] + 512], writes=[zt])
                    em.act(zt, zt[:], zt[:], AF.Silu, reads=[zt])
                    em.dma("sp", P.auxs[2, pp:pp + 128, :], zt[:], reads=[zt], writes=[dummy])
                yield
                em.dma("sp", ab[:], P.pTok[pp:pp + 128, TOKC["ab"]:TOKC["ab"] + 16], writes=[ab])
                em.op("dve", lambda e: e.tensor_tensor(out=gx[:], in0=ab[:, 0:8], in1=prow[:, 8:16], op=ALU.add),
                      reads=[ab, prow], writes=[gx])
                em.act(ge, ge[:], gx[:], AF.Abs, reads=[gx])
                em.act(ge, ge[:], ge[:], AF.Exp, reads=[ge], scale=-1.0)
                em.act(gl, gl[:], ge[:], AF.Ln, reads=[ge, P.ones_f], bias=P.ones_f[:, 0:1])
                em.op("dve", lambda e: e.scalar_tensor_tensor(out=gcol[:], in0=gx[:], scalar=0.0, in1=gl[:],
                                                              op0=ALU.max, op1=ALU.add), reads=[gx, gl], writes=[gcol])
                em.op("dve", lambda e: e.tensor_tensor(out=gcol[:], in0=gcol[:], in1=negea[:], op=ALU.mult),
                      reads=[gcol, negea], writes=[gcol])
                em.act(beta, beta[:], ab[:, 8:16], AF.Sigmoid, reads=[ab])
                gd_, bd_ = gcol[:, d * 4:(d + 1) * 4], beta[:, d * 4:(d + 1) * 4]
                pG = core.bank()
                em.mm(pG, pG[:, 0:4], P.tri[:, d, :], gd_, True, True, reads=[P.tri, gcol])
                pT_ = core.bank()
                em.mm(pT_, pT_[:, 0:4], P.ones_f[:], gd_, True, True, reads=[P.ones_f, gcol])
                em.op("dve", lambda e: e.tensor_copy(out=Gs[:], in_=pG[:, 0:4]), reads=[pG], writes=[Gs])
                em.op("dve", lambda e: e.tensor_scalar_mul(out=nG[:], in0=Gs[:], scalar1=-1.0), reads=[Gs], writes=[nG])
                em.act(eG, eG[:], Gs[:], AF.Exp, reads=[Gs])
                em.op("dve", lambda e: e.tensor_copy(out=etot[:], in_=pT_[:, 0:4]), reads=[pT_], writes=[etot])
                em.op("dve", lambda e: e.tensor_tensor(out=eTG[:], in0=etot[:], in1=Gs[:], op=ALU.subtract),
                      reads=[etot, Gs], writes=[eTG])
                em.act(etot, etot[:], etot[:], AF.Exp, reads=[etot])
                em.act(eTG, eTG[:], eTG[:], AF.Exp, reads=[eTG])
                yield
                em.op("dve", lambda e: e.tensor_scalar_mul(out=nb[:], in0=bd_, scalar1=-1.0), reads=[beta], writes=[nb])
                em.op("dve", lambda e: e.tensor_tensor(out=nbeG[:], in0=nb[:], in1=eG[:], op=ALU.mult),
                      reads=[nb, eG], writes=[nbeG])
                em.op("dve", lambda e: e.tensor_tensor(out=hv(ka), in0=hv(k_), in1=bc4(nb), op=ALU.mult),
                      reads=[k_, nb], writes=[ka])
                em.op("pool", lambda e: e.tensor_tensor(out=hv(Atil), in0=hv(k_), in1=bc4(nbeG), op=ALU.mult),
                      reads=[k_, nbeG], writes=[Atil])
                em.op("dve", lambda e: e.tensor_tensor(out=hv(Kh), in0=hv(k_), in1=bc4(eTG), op=ALU.mult),
                      reads=[k_, eTG], writes=[Kh])
                em.op("pool", lambda e: e.tensor_tensor(out=hv(Rtil), in0=hv(q_), in1=bc4(eG), op=ALU.mult),
                      reads=[q_, eG], writes=[Rtil])
                em.op("dve", lambda e: e.tensor_tensor(out=hv(Vp), in0=hv(v_),
                                                       in1=beta[:, d * 4:(d + 1) * 4].unsqueeze(2).to_broadcast([128, 4, 128]),
                                                       op=ALU.mult), reads=[v_, beta], writes=[Vp])
                yield
                em.op("dve", lambda e: e.tensor_tensor(out=diag[:], in0=P.ident[:, :].unsqueeze(1).to_broadcast([128, 4, 128]),
                                                       in1=bc4(Gs), op=ALU.mult), reads=[P.ident, Gs], writes=[diag])
                pR = core.bank()
                for h in range(4):
                    em.mm(pR, pR[:, h * 128:(h + 1) * 128], P.ones_f[:], diag[:, h, :], True, True,
                          reads=[P.ones_f, diag])
                pRv = pR[:, :].rearrange("p (h t) -> p h t", h=4)
                for dst, sgn, mi, bias_t in ((Ds, -1.0, mA, Gs), (DTs, 1.0, mAT, nG), (DTi, 1.0, mATi, nG)):
                    em.op("dve", lambda e: e.scalar_tensor_tensor(
                        out=dtmp[:], in0=pRv, scalar=sgn, in1=P.masks[:, mi:mi + 1, :].to_broadcast([128, 4, 128]),
                        op0=ALU.mult, op1=ALU.add), reads=[pR, P.masks], writes=[dtmp])
                    for h in range(4):
                        em.act(dst, dst[:, h, :], dtmp[:, h, :], AF.Exp, reads=[dtmp, bias_t],
                               bias=bias_t[:, h:h + 1], writes=[dst])
                yield

            order = chunk_order(rev)
            for _ in prep(order[0], 0):
                pass
            k_, q_ = qkv[1], qkv[0]
            for ci, t0 in enumerate(order):
                slot = ci % 2
                gnext = prep(order[ci + 1], 1 - slot) if ci + 1 < len(order) else None
                ops = {"ga": ka, "gb": k_, "gk": k_, "gr": q_, "V": Vp2[slot], "Atil": Atil, "Bh": Kh2[slot],
                       "Kh": Kh2[slot], "Rtil": Rtil}
                core.run_chunk(rev, ops, etot2[slot], mk, P.yscr[2 + d, ppos(t0):ppos(t0) + 128, :],
                               filler=(gnext if PIPE else None))
                if gnext is not None:
                    for _ in gnext:
                        pass
        em.barrier()


def phase_mix_out(P, l, with_ctx):
    em = P.em
    dummy = T()
    with ExitStack() as st:
        f = lambda nm, shp=(128, 512): em.sb("mo_" + nm, list(shp), F32, st)
        gn_g = _bc_row(P, st, "mo_gng", P.rw_row[l, 7:8, :])
        gn_b = _bc_row(P, st, "mo_gnb", P.rw_row[l, 8:9, :])
        nrm = _bc_row(P, st, "mo_nrm", P.gd_row[l, 0:1, :])
        sets = [dict(ya=f("ya%d" % i), yb=f("yb%d" % i), bv=f("bv%d" % i), gg=f("gg%d" % i), cen=f("cen%d" % i),
                     sq=f("sq%d" % i), s8=f("s8%d" % i, (128, 8))) for i in range(2)]
        ob = [em.sb("mo_ob%d" % i, [128, 4, 128], BF16, st) for i in range(2)]
        pb = [em.ps("mo_pb%d" % i, [128, 512], F32, st) for i in range(2)]
        cnt = 0
        tiles = ([128 * i for i in range(CTX // 128)] if with_ctx else []) + \
            [CTX + 128 * i for i in range(SEQ // 128)]
        for t0 in tiles:
            pp = ppos(t0)
            for mix, (ia, ib, nh, eps_col) in enumerate(((0, 1, 8, 2), (2, 3, 4, 1))):
                bs = sets[cnt % 2]
                ya, yb, bv, gg, cen, sq, s8 = (bs[k_] for k_ in ("ya", "yb", "bv", "gg", "cen", "sq", "s8"))
                n = 512 // nh
                hv = lambda t: t[:].rearrange("p (h n) -> p h n", n=n)
                bc = lambda t: t[:, 0:nh].unsqueeze(2).to_broadcast([128, nh, n])
                em.dma("sp", ya[:], P.yscr[ia, pp:pp + 128, :], writes=[ya])
                em.dma("sp", yb[:], P.yscr[ib, pp:pp + 128, :], writes=[yb])
                em.op("dve", lambda e: e.tensor_tensor(out=ya[:], in0=ya[:], in1=yb[:], op=ALU.add),
                      reads=[ya, yb], writes=[ya])
                if mix == 0:
                    em.dma("sp", bv[:], P.auxs[0, pp:pp + 128, :], writes=[bv])
                    em.dma("sp", gg[:], P.auxs[1, pp:pp + 128, :], writes=[gg])
                    em.op("dve", lambda e: e.tensor_reduce(out=s8[:, 0:nh], in_=hv(ya), axis=AX.X, op=ALU.add),
                          reads=[ya], writes=[s8])
                    em.op("dve", lambda e: e.tensor_scalar_mul(out=s8[:, 0:nh], in0=s8[:, 0:nh], scalar1=1.0 / n),
                          reads=[s8], writes=[s8])
                    em.op("dve", lambda e: e.tensor_tensor(out=hv(cen), in0=hv(ya), in1=bc(s8), op=ALU.subtract),
                          reads=[ya, s8], writes=[cen])
                else:
                    em.dma("sp", gg[:], P.auxs[2, pp:pp + 128, :], writes=[gg])
                    em.op("dve", lambda e: e.tensor_copy(out=cen[:], in_=ya[:]), reads=[ya], writes=[cen])
                em.act(sq, sq[:], cen[:], AF.Square, reads=[cen])
                em.op("dve", lambda e: e.tensor_reduce(out=s8[:, 0:nh], in_=hv(sq), axis=AX.X, op=ALU.add),
                      reads=[sq], writes=[s8])
                em.act(s8, s8[:, 0:nh], s8[:, 0:nh], AF.Sqrt, reads=[s8, P.epsc], bias=P.epsc[:, eps_col:eps_col + 1],
                       scale=1.0 / n)
                em.op("dve", lambda e: e.reciprocal(out=s8[:, 0:nh], in_=s8[:, 0:nh]), reads=[s8], writes=[s8])
                em.op("dve", lambda e: e.tensor_tensor(out=hv(cen), in0=hv(cen), in1=bc(s8), op=ALU.mult),
                      reads=[cen, s8], writes=[cen])
                if mix == 0:
                    em.op("pool", lambda e: e.tensor_tensor(out=cen[:], in0=cen[:], in1=gn_g[:], op=ALU.mult),
                          reads=[cen, gn_g], writes=[cen])
                    em.op("pool", lambda e: e.tensor_tensor(out=cen[:], in0=cen[:], in1=gn_b[:], op=ALU.add),
                          reads=[cen, gn_b], writes=[cen])
                    em.op("pool", lambda e: e.tensor_tensor(out=cen[:], in0=cen[:], in1=bv[:], op=ALU.add),
                          reads=[cen, bv], writes=[cen])
                else:
                    em.op("pool", lambda e: e.tensor_tensor(out=cen[:], in0=cen[:], in1=nrm[:], op=ALU.mult),
                          reads=[cen, nrm], writes=[cen])
                em.op("dve", lambda e: e.tensor_tensor(out=cen[:], in0=cen[:], in1=gg[:], op=ALU.mult),
                      reads=[cen, gg], writes=[cen])
                p_ = pb[cnt % 2]
                o_ = ob[cnt % 2]
                cnt += 1
                for j in range(4):
                    em.op("pe", lambda e: e.transpose(p_[:, j * 128:(j + 1) * 128], cen[:, j * 128:(j + 1) * 128],
                                                      P.ident[:]), reads=[cen, P.ident], writes=[p_])
                em.act(o_, o_[:], p_[:, :].rearrange("p (j t) -> p j t", j=4), AF.Copy, reads=[p_])
                r0 = 1024 + mix * 512
                em.dma("sp", P.mixT[r0:r0 + 512, t0:t0 + 128].rearrange("(j p) t -> p j t", p=128), o_[:],
                       reads=[o_], writes=[dummy])
        em.barrier()


def declare_mla(P):
    L = P.nl
    P.w_uq = P.din("mla_w_uq", [L, 512, 1536])
    P.w_uq_sw = P.din("mla_w_uq_sw", [L, 512, 512])
    P.w_ukv = P.din("mla_w_ukv", [L, 512, 2048])
    P.mlaT = P.din("mlaT", [128, L, 2, 4])
    P.ropeT = P.din("ropeT", [64, 2, SEQ])
    P.wb_uq = P.dscr("wb_uq", [L, 512, 2048], BF16)
    P.wb_ukv = P.dscr("wb_ukv", [L, 512, 2048], BF16)
    P.Kn = P.dscr("Kn", [1024, TT], BF16)
    P.Kr = P.dscr("Kr", [128, TT], BF16)
    P.sel64_in = P.din("sel64", [128, 1])
    P.Vt = P.dscr("Vt", [TT, 1024], BF16)
    P.Qn = P.dscr("Qn", [1024, TT], BF16)
    P.Qr = P.dscr("Qr", [8, 128, TT], BF16)


def phase_cast_mla(P):
    em = P.em
    P.wb_mla_t = [T() for _ in range(P.nl)]
    for l in range(P.nl):
        d = P.wb_mla_t[l]
        em.dma("pool", P.wb_uq[l, :, 0:1536], P.w_uq[l, :, :], writes=[d], partial=True)
        em.dma("pool", P.wb_uq[l, :, 1536:2048], P.w_uq_sw[l, :, :], writes=[d], partial=True)
        em.dma("pool", P.wb_ukv[l, :, :], P.w_ukv[l, :, :], writes=[d], partial=True)


def phase_mla(P, l, with_ctx):
    em = P.em
    dummy = T()
    with ExitStack() as st:
        f = lambda nm, shp, dt=F32: em.sb("ml_" + nm, list(shp), dt, st)
        gains = f("gains", (128, 2, 4))
        em.dma("sp", gains[:], P.mlaT[:, l, :, :], writes=[gains])
        wkv = f("wkv", (128, 4, 2048), BF16)
        wq = f("wq", (128, 4, 2048), BF16)
        wmt = getattr(P, "wb_mla_t", None)
        wrd = [wmt[l]] if wmt else []
        em.dma("sp", wkv[:], P.wb_ukv[l].rearrange("(k p) c -> p k c", p=128), reads=wrd, writes=[wkv])
        em.dma("sp", wq[:], P.wb_uq[l].rearrange("(k p) c -> p k c", p=128), reads=wrd, writes=[wq])
        wvv = f("wvv", (128, 4, 1024), BF16)
        for k in range(4):
            em.dma("sp", wvv[:, k, :].rearrange("p (h v) -> p h v", v=128),
                   P.wb_ukv[l, k * 128:(k + 1) * 128, :].rearrange("p (h two v) -> p h two v", two=2, v=128)[:, :, 1, :],
                   reads=wrd, writes=[wvv], partial=True)
        cx = f("cx", (128, 4, 512))
        cn = f("cn", (128, 4, 512), BF16)
        sq = [f("sq%d" % i, (128, 512)) for i in range(2)]
        rstd = f("rstd", (128, 512))
        krt, ksw, kro = f("krt", (64, 512)), f("ksw", (64, 512)), f("kro", (64, 512))
        rope = f("rope", (64, 2, 512))
        krb = f("krb", (128, 512), BF16)
        sel = f("sel", (128, 1))
        em.dma("sp", sel[:], P.sel64_in[:, :], writes=[sel])
        zt = f("zt", (128, 512))
        sqr = f("sqr", (128, 512), BF16)
        em.op("dve", lambda e: e.memset(sqr[:], 0.0), writes=[sqr])
        sqb = f("sqb", (128, 512), BF16)
        em.op("dve", lambda e: e.memset(zt[:], 0.0), writes=[zt])
        kmax = f("kmax", (128, 8))
        bmax = f("bmax", (128, 1))
        ob = [f("ob%d" % i, (128, 512), BF16) for i in range(3)]
        qrb = [f("qrb%d" % i, (128, 512), BF16) for i in range(2)]
        qr32, qs32 = f("qr32", (64, 512)), f("qs32", (64, 512))
        ps = [em.ps("ml_ps%d" % i, [128, 512], F32, st) for i in range(6)]
        pi = [0]
        oi = [0]

        def bank():
            pi[0] += 1
            return ps[pi[0] % 6]

        def obuf():
            oi[0] += 1
            return ob[oi[0] % 3]

        em.op("dve", lambda e: e.memset(kmax[:], 0.0), writes=[kmax])
        em.op("dve", lambda e: e.tensor_scalar(out=krb[64:128, :], in0=zt[64:128, :], scalar1=sel[64:128, 0:1],
                                               scalar2=None, op0=ALU.add), reads=[zt, sel], writes=[krb], partial=True)

        def rmsnorm_block(row0, gi, pp, n):
            em.dma("sp", cx[:, :, 0:n], P.pT[row0:row0 + 512, pp:pp + n].rearrange("(k p) t -> p k t", p=128),
                   writes=[cx])
            pss = bank()
            for k in range(4):
                s_ = sq[k % 2]
                em.act(s_, s_[:, 0:n], cx[:, k, 0:n], AF.Square, reads=[cx])
                em.mm(pss, pss[:, 0:n], P.ones_f[:], s_[:, 0:n], k == 0, k == 3, reads=[P.ones_f, s_])
            em.act(rstd, rstd[:, 0:n], pss[:, 0:n], AF.Sqrt, reads=[pss, P.epsc], bias=P.epsc[:, 1:2],
                   scale=1.0 / 512)
            em.op("dve", lambda e: e.reciprocal(out=rstd[:, 0:n], in_=rstd[:, 0:n]), reads=[rstd], writes=[rstd])
            for k in range(4):
                em.op("dve", lambda e: e.scalar_tensor_tensor(
                    out=cn[:, k, 0:n], in0=cx[:, k, 0:n], scalar=gains[:, gi, k:k + 1], in1=rstd[:, 0:n],
                    op0=ALU.mult, op1=ALU.mult), reads=[cx, gains, rstd], writes=[cn], partial=True)

        def load_rope(t0, n):
            em.dma("sp", rope[:, :, 0:n], P.ropeT[:, :, t0 - CTX:t0 - CTX + n], writes=[rope])

        def apply_rope(dst, x_, xsw, n, isctx):
            if isctx:
                em.op("dve", lambda e: e.tensor_copy(out=dst[0:64, 0:n], in_=x_[0:64, 0:n]), reads=[x_], writes=[dst])
                return
            em.op("dve", lambda e: e.tensor_tensor(out=dst[0:64, 0:n], in0=x_[0:64, 0:n], in1=rope[:, 0, 0:n],
                                                   op=ALU.mult), reads=[x_, rope], writes=[dst])
            em.op("pool", lambda e: e.tensor_tensor(out=xsw[0:64, 0:n], in0=xsw[0:64, 0:n], in1=rope[:, 1, 0:n],
                                                    op=ALU.mult), reads=[xsw, rope], writes=[xsw])
            em.op("dve", lambda e: e.tensor_tensor(out=dst[0:64, 0:n], in0=dst[0:64, 0:n], in1=xsw[0:64, 0:n],
                                                   op=ALU.add), reads=[dst, xsw], writes=[dst])

        for (t0, n, isctx) in TBLK:
            pp = ppos(t0)
            rmsnorm_block(512, 1, pp, n)
            if not isctx:
                load_rope(t0, n)
            em.dma("sp", krt[:, 0:n], P.pT[8 * 128:8 * 128 + 64, pp:pp + n], writes=[krt])
            em.dma("sp", ksw[:, 0:n], P.pT[9 * 128:9 * 128 + 64, pp:pp + n], writes=[ksw])
            apply_rope(kro, krt, ksw, n, isctx)
            em.act(krb, krb[0:64, 0:n], kro[0:64, 0:n], AF.Copy, reads=[kro], writes=[krb])
            em.dma("pool", P.Kr[:, t0:t0 + n], krb[:, 0:n], reads=[krb], writes=[dummy])
            em.act(sqr, sqr[0:64, 0:n], kro[0:64, 0:n], AF.Square, reads=[kro])
            for h in range(8):
                pk = bank()
                for k in range(4):
                    em.mm(pk, pk[:, 0:n], wkv[:, k, h * 256:h * 256 + 128], cn[:, k, 0:n], k == 0, k == 3,
                          reads=[wkv, cn])
                o_ = obuf()
                em.op("dve", lambda e: e.tensor_copy(out=o_[:, 0:n], in_=pk[:, 0:n]), reads=[pk], writes=[o_])
                em.dma("pool", P.Kn[h * 128:(h + 1) * 128, t0:t0 + n], o_[:, 0:n], reads=[o_], writes=[dummy])
                em.act(sqb, sqb[:, 0:n], o_[:, 0:n], AF.Square, reads=[o_])
                pn = bank()
                em.mm(pn, pn[:, 0:n], P.ones_b[:], sqb[:, 0:n], True, False, reads=[P.ones_b, sqb])
                em.mm(pn, pn[:, 0:n], P.ones_b[:], sqr[:, 0:n], False, True, reads=[P.ones_b, sqr])
                em.op("dve", lambda e: e.tensor_reduce(out=bmax[:], in_=pn[:, 0:n], axis=AX.X, op=ALU.max),
                      reads=[pn], writes=[bmax])
                em.op("dve", lambda e: e.tensor_tensor(out=kmax[:, h:h + 1], in0=kmax[:, h:h + 1], in1=bmax[:],
                                                       op=ALU.max), reads=[kmax, bmax], writes=[kmax])
            for tt in range(n // 128):
                for g in range(2):
                    pv = bank()
                    for k in range(4):
                        em.mm(pv, pv[:, :], cn[:, k, tt * 128:(tt + 1) * 128], wvv[:, k, g * 512:(g + 1) * 512],
                              k == 0, k == 3, reads=[cn, wvv])
                    o_ = obuf()
                    em.act(o_, o_[:], pv[:, :], AF.Copy, reads=[pv])
                    em.dma("pool", P.Vt[t0 + tt * 128:t0 + (tt + 1) * 128, g * 512:(g + 1) * 512], o_[:],
                           reads=[o_], writes=[dummy])
        if getattr(P, "mla_stage", 9) < 1:
            em.barrier()
            return
        nkm = f("nkm", (128, 8))
        em.act(nkm, nkm[:], kmax[:], AF.Sqrt, reads=[kmax])
        em.op("dve", lambda e: e.tensor_scalar_mul(out=nkm[:], in0=nkm[:], scalar1=-1.0), reads=[nkm], writes=[nkm])
        qcnt = 0
        for (t0, n, isctx) in TBLK:
            if isctx and not with_ctx:
                continue
            pp = ppos(t0)
            rmsnorm_block(0, 0, pp, n)
            if not isctx:
                load_rope(t0, n)
            for h in range(8):
                pq, pr, pw = bank(), bank(), bank()
                for k in range(4):
                    em.mm(pq, pq[:, 0:n], wq[:, k, h * 192:h * 192 + 128], cn[:, k, 0:n], k == 0, k == 3,
                          reads=[wq, cn])
                for k in range(4):
                    em.mm(pr, pr[0:64, 0:n], wq[:, k, h * 192 + 128:h * 192 + 192], cn[:, k, 0:n], k == 0, k == 3,
                          reads=[wq, cn])
                if not isctx:
                    for k in range(4):
                        em.mm(pw, pw[0:64, 0:n], wq[:, k, 1536 + h * 64:1536 + (h + 1) * 64], cn[:, k, 0:n],
                              k == 0, k == 3, reads=[wq, cn])
                    em.op("dve", lambda e: e.tensor_copy(out=qs32[0:64, 0:n], in_=pw[0:64, 0:n]), reads=[pw],
                          writes=[qs32])
                em.act(qr32, qr32[0:64, 0:n], pr[0:64, 0:n], AF.Copy, reads=[pr])
                apply_rope(kro, qr32, qs32, n, isctx)
                em.act(sqb, sqb[:, 0:n], pq[:, 0:n], AF.Square, reads=[pq])
                em.act(sqr, sqr[0:64, 0:n], kro[0:64, 0:n], AF.Square, reads=[kro])
                pn = bank()
                em.mm(pn, pn[:, 0:n], P.ones_b[:], sqb[:, 0:n], True, False, reads=[P.ones_b, sqb])
                em.mm(pn, pn[:, 0:n], P.ones_b[:], sqr[:, 0:n], False, True, reads=[P.ones_b, sqr])
                qb = qrb[qcnt % 2]
                qcnt += 1
                em.act(sq[1], sq[1][64:128, 0:n], pn[64:128, 0:n], AF.Sqrt, reads=[pn],
                       scale=ATTN_SCALE * ATTN_SCALE)
                em.op("dve", lambda e: e.tensor_scalar(out=qb[64:128, 0:n], in0=sq[1][64:128, 0:n],
                                                       scalar1=nkm[64:128, h:h + 1], scalar2=sel[64:128, 0:1],
                                                       op0=ALU.mult, op1=ALU.mult),
                      reads=[sq[1], nkm, sel], writes=[qb], partial=True)
                em.act(qb, qb[0:64, 0:n], kro[0:64, 0:n], AF.Copy, reads=[kro], scale=ATTN_SCALE, writes=[qb])
                em.dma("pool", P.Qr[h, :, t0:t0 + n], qb[:, 0:n], reads=[qb], writes=[dummy])
                o_ = obuf()
                em.act(o_, o_[:, 0:n], pq[:, 0:n], AF.Copy, reads=[pq], scale=ATTN_SCALE)
                em.dma("pool", P.Qn[h * 128:(h + 1) * 128, t0:t0 + n], o_[:, 0:n], reads=[o_], writes=[dummy])
        em.barrier()
    if getattr(P, "mla_stage", 9) < 2:
        return
    with ExitStack() as st:
        f = lambda nm, shp, dt=BF16: em.sb("at_" + nm, list(shp), dt, st)
        NKT = TT // 128
        kr = f("kr", (128, TT))
        em.dma("sp", kr[:], P.Kr[:, :], writes=[kr])
        kn = [f("kn%d" % i, (128, TT)) for i in range(2)]
        vv = [f("vv%d" % i, (128, NKT, 128)) for i in range(2)]
        qn = [f("qn%d" % i, (128, 512)) for i in range(2)]
        qr = [f("qr%d" % i, (128, 512)) for i in range(2)]
        pt = [f("pt%d" % i, (128, 512)) for i in range(3)]
        rd = [f("rd%d" % i, (128, 512), F32) for i in range(2)]
        ao = [f("ao%d" % i, (128, 512)) for i in range(2)]
        pS = [em.ps("at_pS%d" % i, [128, 512], F32, st) for i in range(3)]
        pO = [em.ps("at_pO%d" % i, [128, 512], F32, st) for i in range(2)]
        pD = [em.ps("at_pD%d" % i, [128, 512], F32, st) for i in range(2)]
        sc = 0
        qc = 0
        for h in range(8):
            k_n, v_ = kn[h % 2], vv[h % 2]
            em.dma("sp", k_n[:], P.Kn[h * 128:(h + 1) * 128, :], writes=[k_n])
            em.dma("sp", v_[:], P.Vt[:, h * 128:(h + 1) * 128].rearrange("(c p) v -> p c v", p=128), writes=[v_])
            for (t0, n, isctx) in TBLK:
                if isctx and not with_ctx:
                    continue
                q_n, q_r = qn[qc % 2], qr[qc % 2]
                p_O, p_D, r_d, a_o = pO[qc % 2], pD[qc % 2], rd[qc % 2], ao[qc % 2]
                qc += 1
                em.dma("sp", q_n[:, 0:n], P.Qn[h * 128:(h + 1) * 128, t0:t0 + n], writes=[q_n])
                em.dma("sp", q_r[:, 0:n], P.Qr[h, :, t0:t0 + n], writes=[q_r])
                nkt = CTX // 128 if isctx else NKT
                LOOK = 2
                ring = {}

                def scores(kt):
                    nonlocal sc
                    p_S, p_t = pS[sc % 3], pt[sc % 3]
                    sc += 1
                    ks = slice(kt * 128, (kt + 1) * 128)
                    em.mm(p_S, p_S[:, 0:n], k_n[:, ks], q_n[:, 0:n], True, False, reads=[k_n, q_n])
                    em.mm(p_S, p_S[:, 0:n], kr[:, ks], q_r[:, 0:n], False, True, reads=[kr, q_r])
                    em.act(p_t, p_t[:, 0:n], p_S[:, 0:n], AF.Exp, reads=[p_S])
                    ring[kt] = p_t

                for kt in range(min(LOOK, nkt)):
                    scores(kt)
                for kt in range(nkt):
                    p_t = ring.pop(kt)
                    em.mm(p_O, p_O[:, 0:n], v_[:, kt, :], p_t[:, 0:n], kt == 0, kt == nkt - 1, reads=[v_, p_t])
                    em.mm(p_D, p_D[:, 0:n], P.ones_b[:], p_t[:, 0:n], kt == 0, kt == nkt - 1,
                          reads=[P.ones_b, p_t])
                    if kt + LOOK < nkt:
                        scores(kt + LOOK)
                em.op("dve", lambda e: e.reciprocal(out=r_d[:, 0:n], in_=p_D[:, 0:n]), reads=[p_D], writes=[r_d])
                em.op("dve", lambda e: e.tensor_tensor(out=a_o[:, 0:n], in0=p_O[:, 0:n], in1=r_d[:, 0:n],
                                                       op=ALU.mult), reads=[p_O, r_d], writes=[a_o])
                em.dma("sp", P.mixT[h * 128:(h + 1) * 128, t0:t0 + n], a_o[:, 0:n], reads=[a_o], writes=[dummy])
        em.barrier()


def host_mla_inputs(inp, L, seq):
    f32 = np.float32
    perm = np.arange(64).reshape(2, 2, 16)[:, ::-1, :].reshape(64)
    out = {}
    out["mla_w_uq"] = inp["mla_w_uq"][:L]
    cols = np.concatenate([h * 192 + 128 + perm for h in range(8)])
    out["mla_w_uq_sw"] = np.ascontiguousarray(inp["mla_w_uq"][:L][:, :, cols])
    out["mla_w_ukv"] = inp["mla_w_ukv"][:L]
    g = np.stack([inp["mla_q_norm"][:L].reshape(L, 4, 128), inp["mla_kv_norm"][:L].reshape(L, 4, 128)], 1)
    out["mlaT"] = np.ascontiguousarray(g.transpose(3, 0, 1, 2)).astype(f32)
    t = np.arange(seq)
    pos = np.stack([t // 64, t % 64], -1).astype(f32)
    inv = (10000.0 ** (-np.arange(16, dtype=f32) / 16)).astype(f32)
    ang = pos[..., None] * inv
    cos, sin = np.cos(ang), np.sin(ang)
    ct = np.zeros((64, seq), f32)
    stb = np.zeros((64, seq), f32)
    for a in range(2):
        for half in range(2):
            r0 = a * 32 + half * 16
            ct[r0:r0 + 16] = cos[:, a, :].T
            stb[r0:r0 + 16] = (sin[:, a, :].T) * (-1.0 if half == 0 else 1.0)
    out["ropeT"] = np.ascontiguousarray(np.stack([ct, stb], 1))
    return out, perm


def build_program(nl=DEPTH, dbg=()):
    P = Prog(nl=nl, dbg=dbg)
    em = P.em
    declare_io(P)
    declare_dense(P)
    declare_scan(P)
    declare_mla(P)
    setup_consts(P)
    setup_scan_consts(P)
    phase_zero_pads(P)
    phase_cast_in(P)
    phase_cast_dense(P)
    phase_cast_mla(P)
    phase_mod(P)
    nblk = len(TBLK)
    for l in range(nl):
        with_ctx = l < nl - 1
        xsrc = P.xT0 if l == 0 else P.xB
        ft = lambda: [T() for _ in range(nblk)]
        sel_ = getattr(build_program, "phases", "imrgodf")
        if "i" in sel_:
            phase_inproj(P, l, xsrc, ft())
        if "m" in sel_:
            phase_mla(P, l, with_ctx)
        if "r" in sel_:
            phase_rwkv(P, l)
        if "g" in sel_:
            phase_gdn(P, l)
        if "o" in sel_:
            phase_mix_out(P, l, with_ctx)
        P.mixT_t = ft()
        if "d" in sel_:
            phase_outproj(P, l, xsrc, ft(), P.xA, ft(), with_ctx)
        if "f" in sel_:
            phase_ffn(P, l, P.xA, ft(), P.xB, ft(), with_ctx, final=(l == nl - 1))
    em.barrier()
    return P


def host_inputs(inp, b, nl, seq):
    f32 = np.float32
    L = nl
    x = np.asarray(inp["x"][b][:seq], f32)
    ctx = np.asarray(inp["ctx"][b], f32)
    im = {}
    im["xT0"] = np.ascontiguousarray(np.concatenate([ctx, x], 0).T)
    im["cvec"] = np.ascontiguousarray(np.stack([np.asarray(inp["c"][b]).reshape(16, 128).T,
                                                np.asarray(inp["c_ctx"]).reshape(16, 128).T], -1).astype(f32))
    im["w_mod"] = np.asarray(inp["w_mod"][:L], f32)
    im["b_modT"] = np.ascontiguousarray(np.asarray(inp["b_mod"][:L], f32).reshape(L, 96, 128).transpose(2, 0, 1))
    im["w_in"] = np.asarray(inp["w_in"][:L], f32)
    mi, perm = host_mla_inputs(inp, L, seq)
    im["w_in_krsw"] = np.ascontiguousarray(im["w_in"][:, :, 1024 + perm])
    im.update(mi)
    e = np.zeros((128, 1), f32)
    e[64] = 1.0
    im["sel64"] = e
    im["w_out"] = np.asarray(inp["w_out"][:L], f32)
    im["ffn_w_gate"] = np.asarray(inp["ffn_w_gate"][:L], f32)
    im["ffn_w_up"] = np.asarray(inp["ffn_w_up"][:L], f32)
    im["ffn_w_down"] = np.asarray(inp["ffn_w_down"][:L], f32)
    lnT = np.stack([np.asarray(inp[k][:L], f32).reshape(L, 16, 128) for k in ("ln1_g", "ln1_b", "ln2_g", "ln2_b")], 1)
    im["lnT"] = np.ascontiguousarray(lnT.transpose(3, 0, 1, 2))
    im.update(host_consts())
    im.update(host_scan_inputs(inp, L))
    return im


def kernel(**inputs):
    nb = inputs["x"].shape[0]
    P = build_program(DEPTH)
    shared = None
    in_maps = []
    for b in range(nb):
        im = host_inputs(inputs, b, DEPTH, SEQ)
        if shared is None:
            shared = im
        else:
            for k in im:
                if k not in ("xT0", "cvec"):
                    im[k] = shared[k]
        in_maps.append({k: v for k, v in im.items() if k in P.inputs})
    res = run_bass_kernel_spmd(P.nc, in_maps, core_ids=list(range(nb)))
    out = np.stack([np.ascontiguousarray(np.asarray(r["xOut"], np.float32).T) for r in res.results], 0)
    return out
```

```python
import math
from contextlib import ExitStack

import numpy as np
import concourse.bass as bass
import concourse.mybir as mybir
from concourse.bass_utils import run_bass_kernel_spmd

F32 = mybir.dt.float32
BF16 = mybir.dt.bfloat16
AF = mybir.ActivationFunctionType
ALU = mybir.AluOpType
AX = mybir.AxisListType

D = 2048
KD = D // 128
SEQ = 4096
CTX = 256
TT = SEQ + CTX
DEPTH = 4
D_FF = 5632
KF = D_FF // 128
IN_COLS = 4912
ALPHA = (2.0 * DEPTH) ** 0.25
ATTN_SCALE = 192 ** -0.5

CH = []
for i in range(4):
    CH.append(("cq%d" % i, 128 * i, 128))
for i in range(4):
    CH.append(("ckv%d" % i, 512 + 128 * i, 128))
CH += [("kr", 1024, 64), ("krsw", -1, 64), ("wd", 2624, 64), ("ad", 2688, 64)]
CH += [("gd", 2752, 96), ("ab", 4896, 16), ("pad0", -2, 0), ("pad1", -2, 0)]
for nm, c0 in (("r", 1088), ("k", 1600), ("v", 2112), ("gq", 2848), ("gk", 3360), ("gv", 3872), ("z", 4384)):
    for i in range(4):
        CH.append(("%s%d" % (nm, i), c0 + 128 * i, 128))
NCH = len(CH)
CHI = {c[0]: i for i, c in enumerate(CH)}
NCHP = NCH
NFM = 13
TOKC = {"r": 0, "k": 512, "v": 1024, "gq": 1536, "gk": 2048, "gv": 2560, "z": 3072, "ab": 3584}
NTOKC = 3600

TBLK = [(0, CTX, True)] + [(CTX + 512 * j, 512, False) for j in range(SEQ // 512)]
PC0 = 2
PL0 = 2 + CTX + 4
TTP = PL0 + SEQ + 2


def ppos(t):
    return PC0 + t if t < CTX else PL0 + (t - CTX)


def configure(seq):
    global SEQ, TT, TBLK, TTP
    SEQ = seq
    TT = SEQ + CTX
    TBLK = [(0, CTX, True)] + [(CTX + 512 * j, 512, False) for j in range(SEQ // 512)]
    TTP = PL0 + SEQ + 2


class T:
    __slots__ = ("h", "lw", "rd", "pg")

    def __init__(self, h=None):
        self.h = h
        self.lw = {}
        self.rd = {}
        self.pg = {}

    def __getitem__(self, idx):
        return self.h[idx]


class Em:
    NDMA = 6

    def __init__(self, nc, st):
        self.nc = nc
        self.st = st
        self.eng = {"pe": nc.tensor, "act": nc.scalar, "dve": nc.vector, "pool": nc.gpsimd, "sp": nc.sync}
        self.sem = {}
        self.cnt = {}
        for k in ("pe", "act", "dve", "pool", "sp"):
            self.sem[k] = st.enter_context(nc.semaphore("s_" + k))
            self.cnt[k] = 0
        self.dcnt = {"sp": 0, "pool": 0, "act": 0}
        for q in self.dcnt:
            for i in range(self.NDMA):
                self.sem[(q, i)] = st.enter_context(nc.semaphore("d_%s%d" % (q, i)))
        self.seen = {k: {} for k in self.eng}
        self.dmax = {}
        self.ninst = 0
        self.uid = 0

    def sb(self, name, shape, dt, st=None):
        self.uid += 1
        return T((st or self.st).enter_context(self.nc.sbuf_tensor("%s_u%d" % (name, self.uid), list(shape), dt)))

    def ps(self, name, shape, dt=F32, st=None):
        self.uid += 1
        return T((st or self.st).enter_context(self.nc.psum_tensor("%s_u%d" % (name, self.uid), list(shape), dt)))

    def _wait(self, eng, deps):
        e = self.eng[eng]
        seen = self.seen[eng]
        for sk, v in deps.items():
            if sk == "pe" and eng == "pe":
                continue
            if seen.get(sk, 0) < v:
                e.wait_ge(self.sem[sk], v)
                seen[sk] = v
                self.ninst += 1

    @staticmethod
    def _merge(d, s):
        for k, v in s.items():
            if d.get(k, 0) < v:
                d[k] = v

    def _deps(self, reads, writes, partial):
        deps = {}
        for b in reads:
            self._merge(deps, b.lw)
        for b in writes:
            self._merge(deps, b.rd)
            if partial:
                self._merge(deps, b.pg)
            else:
                self._merge(deps, b.lw)
        return deps

    def _record(self, ev, reads, writes, partial):
        sk, v = ev
        for b in reads:
            if b.rd.get(sk, 0) < v:
                b.rd[sk] = v
        for b in writes:
            if partial and not b.rd:
                if b.lw.get(sk, 0) < v:
                    b.lw[sk] = v
            else:
                pg = dict(b.rd)
                self._merge(pg, b.lw)
                b.pg = pg
                b.lw = {sk: v}
                b.rd = {}

    def op(self, eng, fn, reads=(), writes=(), partial=False):
        self._wait(eng, self._deps(reads, writes, partial))
        ins = fn(self.eng[eng])
        self.cnt[eng] += 1
        ins.then_inc(self.sem[eng], 1)
        self.ninst += 1
        self._record((eng, self.cnt[eng]), reads, writes, partial)

    def dma(self, q, out, in_, reads=(), writes=(), partial=False):
        n = self.dcnt[q]
        slot = n % self.NDMA
        sk = (q, slot)
        tgt = 16 * (n // self.NDMA + 1)
        deps = self._deps(reads, writes, partial)
        if tgt > 16:
            deps[sk] = max(deps.get(sk, 0), tgt - 16)
        self._wait(q, deps)
        self.eng[q].dma_start(out=out, in_=in_).then_inc(self.sem[sk], 16)
        self.dcnt[q] = n + 1
        self.dmax[sk] = tgt
        self.ninst += 1
        self._record((sk, tgt), reads, writes, partial)

    def barrier(self):
        allev = {k: v for k, v in self.cnt.items() if v > 0}
        allev.update(self.dmax)
        for eng in self.eng:
            d = {k: v for k, v in allev.items() if k != eng}
            self._wait(eng, d)
        for eng in ("act", "dve", "pool"):
            if self.cnt[eng] > 0:
                self._wait(eng, {eng: self.cnt[eng]})

    def mm(self, out_t, out_ap, lhsT, rhs, start, stop, reads):
        self.op("pe", lambda e: e.matmul(out_ap, lhsT, rhs, start=start, stop=stop),
                reads=reads, writes=[out_t])

    def act(self, eng_out_t, out_ap, in_ap, func, reads, bias=None, scale=None, accum=None, writes=None):
        kw = {}
        if bias is not None:
            kw["bias"] = bias
        if scale is not None:
            kw["scale"] = scale
        if accum is not None:
            kw["accum_out"] = accum
        self.op("act", lambda e: e.activation(out_ap, in_ap, func, **kw), reads=reads,
                writes=writes if writes is not None else [eng_out_t])


def _chunk_rows(ap2d):
    return ap2d.rearrange("(k p) t -> p k t", p=128)


class Prog:
    def __init__(self, nl=DEPTH, dbg=()):
        self.nl = nl
        self.dbg = set(dbg)
        nc = self.nc = bass.Bass("TRN2", target_bir_lowering=False)
        self.st = ExitStack()
        self.em = Em(nc, self.st)
        self.inputs = {}
        self.outs = {}

    def din(self, name, shape, dt=F32):
        self.inputs[name] = (shape, dt)
        return self.nc.dram_tensor(name, list(shape), dt, kind="ExternalInput").ap()

    def dscr(self, name, shape, dt=F32, out=False):
        kind = "ExternalOutput" if (out or name in self.dbg) else "Internal"
        if kind == "ExternalOutput":
            self.outs[name] = (shape, dt)
        return self.nc.dram_tensor(name, list(shape), dt, kind=kind).ap()


def declare_io(P):
    L = P.nl
    P.xT0 = P.din("xT0", [D, TT])
    P.cvec = P.din("cvec", [128, KD, 2])
    P.w_mod = P.din("w_mod", [L, D, 6 * D])
    P.b_modT = P.din("b_modT", [128, L, 96])
    P.w_in = P.din("w_in", [L, D, IN_COLS])
    P.w_in_krsw = P.din("w_in_krsw", [L, D, 64])
    P.wb_in = P.dscr("wb_in", [L, D, NCHP * 128], BF16)
    P.pT = P.dscr("pT", [NFM * 128, TTP])
    P.pTok = P.dscr("pTok", [TTP, NTOKC])


def phase_cast_in(P):
    em = P.em
    P.wb_in_t = [T() for _ in range(P.nl)]
    for l in range(P.nl):
        for i, (nm, c0, w) in enumerate(CH):
            if w == 0:
                continue
            src = P.w_in_krsw[l, :, :] if c0 < 0 else P.w_in[l, :, c0:c0 + w]
            for r0 in range(0, D, 512):
                em.dma("pool", P.wb_in[l, r0:r0 + 512, i * 128:i * 128 + w],
                       src[r0:r0 + 512, :], writes=[P.wb_in_t[l]], partial=True)


def phase_mod(P):
    em = P.em
    L = P.nl
    P.mod = em.sb("mod", [128, L, 96, 2], F32)
    P.mod1 = em.sb("mod1", [128, L, 96, 2], F32)
    P.modg = em.sb("modg", [128, L, 96, 2], F32)
    with ExitStack() as st:
        cv = em.sb("cv", [128, KD, 2], F32, st)
        sc = em.sb("sc", [128, KD, 2], F32, st)
        bm = em.sb("bm", [128, L, 96], F32, st)
        em.dma("sp", cv[:], P.cvec[:, :, :], writes=[cv])
        em.dma("sp", bm[:], P.b_modT[:, :, :], writes=[bm])
        em.act(sc, sc[:], cv[:], AF.Silu, reads=[cv])
        wt = [em.sb("wm%d" % i, [128, KD, 512], F32, st) for i in range(2)]
        pmf = [em.ps("pm%d" % i, [128, 512], F32, st) for i in range(2)]
        g = 0
        for l in range(L):
            wv = P.w_mod[l].rearrange("(k p) c -> p k c", p=128)
            for gi in range(24):
                w = wt[g % 2]
                p = pmf[g % 2]
                g += 1
                em.dma("sp", w[:], wv[:, :, gi * 512:(gi + 1) * 512], writes=[w])
                for c in range(4):
                    for k in range(KD):
                        em.mm(p, p[:, 2 * c:2 * c + 2], w[:, k, c * 128:(c + 1) * 128], sc[:, k, :],
                              k == 0, k == KD - 1, reads=[w, sc])
                em.op("dve", lambda e: e.tensor_tensor(
                    out=P.mod[:, l, gi * 4:(gi + 1) * 4, :], in0=p[:, 0:8].rearrange("p (c j) -> p c j", j=2),
                    in1=bm[:, l, gi * 4:(gi + 1) * 4].unsqueeze(2).to_broadcast([128, 4, 2]),
                    op=ALU.add), reads=[p, bm], writes=[P.mod], partial=True)
        em.op("dve", lambda e: e.tensor_scalar_add(out=P.mod1[:], in0=P.mod[:], scalar1=1.0),
              reads=[P.mod], writes=[P.mod1])
        em.op("dve", lambda e: e.tensor_scalar_mul(out=P.modg[:], in0=P.mod[:], scalar1=1.0 / ALPHA),
              reads=[P.mod], writes=[P.modg])
        em.barrier()


SH_M, SC_M, GT_M, SH_F, SC_F, GT_F = 0, 16, 32, 48, 64, 80


def phase_zero_pads(P):
    em = P.em
    with ExitStack() as st:
        z = em.sb("zpad", [128, NTOKC], F32, st)
        em.op("dve", lambda e: e.memset(z[:], 0.0), writes=[z])
        dummy = T()
        for a, b in ((0, PC0), (PC0 + CTX, PL0), (PL0 + SEQ, TTP)):
            em.dma("sp", P.pTok[a:b, :], z[0:b - a, :], reads=[z], writes=[dummy], partial=True)
            for c in range(NFM):
                em.dma("sp", P.pT[c * 128:(c + 1) * 128, a:b], z[:, 0:b - a], reads=[z], writes=[dummy], partial=True)
        em.barrier()


def phase_inproj(P, l, xsrc, xsrc_t):
    em = P.em
    with ExitStack() as st:
        xs = [em.sb("ip_xs%d" % i, [128, KD, 512], F32, st) for i in range(2)]
        xm = [em.sb("ip_xm%d" % i, [128, KD, 512], BF16, st) for i in range(2)]
        wt = [em.sb("ip_w%d" % i, [128, KD, 512], BF16, st) for i in range(2)]
        ps = [em.ps("ip_ps%d" % i, [128, 512], F32, st) for i in range(4)]
        sg = [em.sb("ip_sg%d" % i, [128, 512], F32, st) for i in range(4)]
        wv = P.wb_in[l].rearrange("(k p) c -> p k c", p=128)
        xv = _chunk_rows(xsrc)
        gcount = 0
        ccount = 0
        dummy = T()

        def evac(p, s, rows, cols):
            nonlocal ccount
            if ccount % 2 == 0:
                em.op("dve", lambda e: e.tensor_copy(out=s[0:rows, 0:cols], in_=p[0:rows, 0:cols]),
                      reads=[p], writes=[s])
            else:
                em.act(s, s[0:rows, 0:cols], p[0:rows, 0:cols], AF.Copy, reads=[p])
            ccount += 1

        for bi, (t0, n, isctx) in enumerate(TBLK):
            j = 1 if isctx else 0
            pp = ppos(t0)
            x_s, x_m = xs[bi % 2], xm[bi % 2]
            em.dma("sp", x_s[:, :, 0:n], xv[:, :, t0:t0 + n], reads=[xsrc_t[bi]], writes=[x_s])
            for k in range(KD):
                em.act(x_m, x_m[:, k, 0:n], x_s[:, k, 0:n], AF.Identity, reads=[x_s, P.mod, P.mod1],
                       bias=P.mod[:, l, SH_M + k, j:j + 1], scale=P.mod1[:, l, SC_M + k, j:j + 1],
                       writes=[x_m])
            for g in range(NCH // 4):
                w = wt[gcount % 2]
                gcount += 1
                em.dma("sp", w[:], wv[:, :, g * 512:(g + 1) * 512], reads=[P.wb_in_t[l]], writes=[w])
                if g < 4:
                    for c in range(4):
                        ci = g * 4 + c
                        nm, c0, wd = CH[ci]
                        if wd == 0:
                            continue
                        if nm == "ab":
                            for tt in range(n // 128):
                                p, s_ = ps[ccount % 4], sg[ccount % 4]
                                for k in range(KD):
                                    em.mm(p, p[:, 0:16], x_m[:, k, tt * 128:(tt + 1) * 128],
                                          w[:, k, c * 128:c * 128 + 16], k == 0, k == KD - 1, reads=[w, x_m])
                                evac(p, s_, 128, 16)
                                em.dma("pool", P.pTok[pp + tt * 128:pp + (tt + 1) * 128, TOKC["ab"]:TOKC["ab"] + 16],
                                       s_[:, 0:16], reads=[s_], writes=[dummy], partial=True)
                            continue
                        fi = ci if ci < 12 else 12
                        p, s_ = ps[ccount % 4], sg[ccount % 4]
                        for k in range(KD):
                            em.mm(p, p[0:wd, 0:n], w[:, k, c * 128:c * 128 + wd], x_m[:, k, 0:n],
                                  k == 0, k == KD - 1, reads=[w, x_m])
                        evac(p, s_, wd, n)
                        em.dma("pool", P.pT[fi * 128:fi * 128 + wd, pp:pp + n], s_[0:wd, 0:n],
                               reads=[s_], writes=[dummy], partial=True)
                else:
                    col0 = (g - 4) * 512
                    for tt in range(n // 128):
                        p, s_ = ps[ccount % 4], sg[ccount % 4]
                        for k in range(KD):
                            em.mm(p, p[:, :], x_m[:, k, tt * 128:(tt + 1) * 128], w[:, k, :],
                                  k == 0, k == KD - 1, reads=[w, x_m])
                        evac(p, s_, 128, 512)
                        em.dma("pool", P.pTok[pp + tt * 128:pp + (tt + 1) * 128, col0:col0 + 512], s_[:, :],
                               reads=[s_], writes=[dummy], partial=True)
        em.barrier()


def declare_dense(P):
    L = P.nl
    P.w_out = P.din("w_out", [L, D, D])
    P.w_gate = P.din("ffn_w_gate", [L, D, D_FF])
    P.w_up = P.din("ffn_w_up", [L, D, D_FF])
    P.w_down = P.din("ffn_w_down", [L, D_FF, D])
    P.lnT = P.din("lnT", [128, L, 4, KD])
    P.wb_out = P.dscr("wb_out", [L, D, D], BF16)
    P.wb_gate = P.dscr("wb_gate", [L, KF // 2, 128, KD, 256], BF16)
    P.wb_up = P.dscr("wb_up", [L, KF // 2, 128, KD, 256], BF16)
    P.wb_down = P.dscr("wb_down", [L, KD, 128, KF, 128], BF16)
    P.mixT = P.dscr("mixT", [D, TT], BF16)
    P.xA = P.dscr("xA", [D, TT])
    P.xB = P.dscr("xB", [D, TT])
    P.xOut = P.dscr("xOut", [D, SEQ], out=True)


def phase_cast_dense(P):
    em = P.em
    P.wb_dense_t = [T() for _ in range(P.nl)]
    for l in range(P.nl):
        for r0 in range(0, D, 256):
            em.dma("pool", P.wb_out[l, r0:r0 + 256, :], P.w_out[l, r0:r0 + 256, :],
                   writes=[P.wb_dense_t[l]], partial=True)
        for src, dst in ((P.w_gate, P.wb_gate), (P.w_up, P.wb_up)):
            for g in range(KF // 2):
                sv = src[l, :, g * 256:(g + 1) * 256].rearrange("(k p) c -> p k c", p=128)
                for k0 in range(0, KD, 8):
                    em.dma("pool", dst[l, g, :, k0:k0 + 8, :], sv[:, k0:k0 + 8, :],
                           writes=[P.wb_dense_t[l]], partial=True)
        for m in range(KD):
            sv = P.w_down[l, :, m * 128:(m + 1) * 128].rearrange("(k p) c -> p k c", p=128)
            for k0 in range(0, KF, 11):
                em.dma("pool", P.wb_down[l, m, :, k0:k0 + 11, :], sv[:, k0:k0 + 11, :],
                       writes=[P.wb_dense_t[l]], partial=True)


def setup_consts(P):
    em = P.em
    P.ones_f = em.sb("ones_f", [128, 128], F32)
    P.ones_b = em.sb("ones_b", [128, 128], BF16)
    em.op("dve", lambda e: e.memset(P.ones_f[:], 1.0), writes=[P.ones_f])
    em.op("dve", lambda e: e.memset(P.ones_b[:], 1.0), writes=[P.ones_b])
    P.epsc = em.sb("epsc", [128, 4], F32)
    em.op("dve", lambda e: e.memset(P.epsc[:, 0:1], 1e-5 / (ALPHA * ALPHA)), writes=[P.epsc])
    em.op("dve", lambda e: e.memset(P.epsc[:, 1:2], 1e-6), writes=[P.epsc])
    em.op("dve", lambda e: e.memset(P.epsc[:, 2:3], 64e-5), writes=[P.epsc])
    em.op("dve", lambda e: e.memset(P.epsc[:, 3:4], 1e-12), writes=[P.epsc])
    P.ln = em.sb("ln", [128, P.nl, 4, KD], F32)
    em.dma("sp", P.ln[:], P.lnT[:, :, :, :], writes=[P.ln])


class ResLN:
    def __init__(self, P, st, tag):
        em = P.em
        self.P = P
        self.s1 = em.ps(tag + "_s1", [128, 512], F32, st)
        self.s2 = em.ps(tag + "_s2", [128, 512], F32, st)
        self.sq = [em.sb(tag + "_sq%d" % i, [128, 512], F32, st) for i in range(2)]
        self.mean = em.sb(tag + "_mean", [128, 512], F32, st)
        self.rstd = em.sb(tag + "_rstd", [128, 512], F32, st)
        self.tmp = [em.sb(tag + "_tmp%d" % i, [128, 512], F32, st) for i in range(2)]
        self.og = [em.sb(tag + "_og%d" % i, [128, 512], F32, st) for i in range(2)]
        self.cnt = 0

    def add_chunk(self, m, psum_t, xblk, n, l, gslot, j):
        P, em = self.P, self.P.em
        em.op("dve", lambda e: e.scalar_tensor_tensor(
            out=xblk[:, m, 0:n], in0=psum_t[:, 0:n], scalar=P.modg[:, l, gslot + m, j:j + 1],
            in1=xblk[:, m, 0:n], op0=ALU.mult, op1=ALU.add), reads=[psum_t, xblk, P.modg], writes=[xblk])
        sq = self.sq[m % 2]
        em.act(sq, sq[:, 0:n], xblk[:, m, 0:n], AF.Square, reads=[xblk])
        em.mm(self.s1, self.s1[:, 0:n], P.ones_f[:], xblk[:, m, 0:n], m == 0, m == KD - 1, reads=[xblk, P.ones_f])
        em.mm(self.s2, self.s2[:, 0:n], P.ones_f[:], sq[:, 0:n], m == 0, m == KD - 1, reads=[sq, P.ones_f])

    def finish(self, xblk, n, l, lnslot, xdst, xdst_t, c0, dst2=None):
        P, em = self.P, self.P.em
        mean, rstd = self.mean, self.rstd
        em.op("dve", lambda e: e.tensor_scalar_mul(out=mean[:, 0:n], in0=self.s1[:, 0:n], scalar1=1.0 / D),
              reads=[self.s1], writes=[mean])
        em.op("dve", lambda e: e.tensor_tensor(out=rstd[:, 0:n], in0=mean[:, 0:n], in1=mean[:, 0:n], op=ALU.mult),
              reads=[mean], writes=[rstd])
        em.op("dve", lambda e: e.scalar_tensor_tensor(
            out=rstd[:, 0:n], in0=self.s2[:, 0:n], scalar=1.0 / D, in1=rstd[:, 0:n],
            op0=ALU.mult, op1=ALU.subtract), reads=[self.s2, rstd], writes=[rstd])
        em.act(rstd, rstd[:, 0:n], rstd[:, 0:n], AF.Sqrt, reads=[rstd, P.epsc], bias=P.epsc[:, 0:1])
        em.op("dve", lambda e: e.reciprocal(out=rstd[:, 0:n], in_=rstd[:, 0:n]), reads=[rstd], writes=[rstd])
        xv = _chunk_rows(xdst)
        for m in range(KD):
            tmp = self.tmp[m % 2]
            og = self.og[m % 2]
            em.op("dve", lambda e: e.tensor_tensor(out=tmp[:, 0:n], in0=xblk[:, m, 0:n], in1=mean[:, 0:n],
                                                   op=ALU.subtract), reads=[xblk, mean], writes=[tmp])
            em.op("pool", lambda e: e.tensor_tensor(out=tmp[:, 0:n], in0=tmp[:, 0:n], in1=rstd[:, 0:n],
                                                    op=ALU.mult), reads=[tmp, rstd], writes=[tmp])
            em.act(og, og[:, 0:n], tmp[:, 0:n], AF.Identity, reads=[tmp, P.ln],
                   bias=P.ln[:, l, lnslot + 1, m:m + 1], scale=P.ln[:, l, lnslot, m:m + 1])
            em.dma("pool", xdst[m * 128:(m + 1) * 128, c0:c0 + n], og[:, 0:n], reads=[og],
                   writes=[xdst_t], partial=True)
            if dst2 is not None:
                d2, d2_t, c2 = dst2
                em.dma("sp", d2[m * 128:(m + 1) * 128, c2:c2 + n], og[:, 0:n], reads=[og],
                       writes=[d2_t], partial=True)


def phase_outproj(P, l, xsrc, xsrc_t, xdst, xdst_t, with_ctx):
    em = P.em
    with ExitStack() as st:
        xs = [em.sb("op_xs%d" % i, [128, KD, 512], F32, st) for i in range(2)]
        am = [em.sb("op_am%d" % i, [128, KD, 512], BF16, st) for i in range(2)]
        wt = [em.sb("op_w%d" % i, [128, KD, 512], BF16, st) for i in range(2)]
        ps = [em.ps("op_ps%d" % i, [128, 512], F32, st) for i in range(2)]
        rl = ResLN(P, st, "op")
        wv = P.wb_out[l].rearrange("(k p) c -> p k c", p=128)
        xv = _chunk_rows(xsrc)
        av = _chunk_rows(P.mixT)
        gc = 0
        cc = 0
        for bi, (t0, n, isctx) in enumerate(TBLK):
            if isctx and not with_ctx:
                continue
            j = 1 if isctx else 0
            x_s, a_m = xs[bi % 2], am[bi % 2]
            em.dma("sp", x_s[:, :, 0:n], xv[:, :, t0:t0 + n], reads=[xsrc_t[bi]], writes=[x_s])
            em.dma("sp", a_m[:, :, 0:n], av[:, :, t0:t0 + n], reads=[P.mixT_t[bi]], writes=[a_m])
            for g in range(4):
                w = wt[gc % 2]
                gc += 1
                em.dma("sp", w[:], wv[:, :, g * 512:(g + 1) * 512], reads=[P.wb_dense_t[l]], writes=[w])
                for c in range(4):
                    m = g * 4 + c
                    p = ps[cc % 2]
                    cc += 1
                    for k in range(KD):
                        em.mm(p, p[:, 0:n], w[:, k, c * 128:(c + 1) * 128], a_m[:, k, 0:n],
                              k == 0, k == KD - 1, reads=[w, a_m])
                    rl.add_chunk(m, p, x_s, n, l, GT_M, j)
            rl.finish(x_s, n, l, 0, xdst, xdst_t[bi], t0)
        em.barrier()


def phase_ffn(P, l, xsrc, xsrc_t, xdst, xdst_t, with_ctx, final=False):
    em = P.em
    with ExitStack() as st:
        xs = em.sb("ff_xs", [128, KD, 512], F32, st)
        xm = em.sb("ff_xm", [128, KD, 512], BF16, st)
        hh = em.sb("ff_h", [128, KF, 512], BF16, st)
        wg = [em.sb("ff_wg%d" % i, [128, KD, 256], BF16, st) for i in range(2)]
        wu = [em.sb("ff_wu%d" % i, [128, KD, 256], BF16, st) for i in range(2)]
        wd = [em.sb("ff_wd%d" % i, [128, KF, 128], BF16, st) for i in range(2)]
        pg = [em.ps("ff_pg%d" % i, [128, 512], F32, st) for i in range(2)]
        pu = [em.ps("ff_pu%d" % i, [128, 512], F32, st) for i in range(2)]
        pd = [em.ps("ff_pd%d" % i, [128, 512], F32, st) for i in range(2)]
        sg = [em.sb("ff_sg%d" % i, [128, 512], F32, st) for i in range(2)]
        rl = ResLN(P, st, "ff")
        xv = _chunk_rows(xsrc)
        gc = 0
        cc = 0
        dc = 0
        for bi, (t0, n, isctx) in enumerate(TBLK):
            if isctx and not with_ctx:
                continue
            j = 1 if isctx else 0
            em.dma("sp", xs[:, :, 0:n], xv[:, :, t0:t0 + n], reads=[xsrc_t[bi]], writes=[xs])
            for k in range(KD):
                em.act(xm, xm[:, k, 0:n], xs[:, k, 0:n], AF.Identity, reads=[xs, P.mod, P.mod1],
                       bias=P.mod[:, l, SH_F + k, j:j + 1], scale=P.mod1[:, l, SC_F + k, j:j + 1])
            for g in range(KF // 2):
                w_g, w_u = wg[gc % 2], wu[gc % 2]
                gc += 1
                em.dma("sp", w_g[:], P.wb_gate[l, g], reads=[P.wb_dense_t[l]], writes=[w_g])
                em.dma("sp", w_u[:], P.wb_up[l, g], reads=[P.wb_dense_t[l]], writes=[w_u])
                for c in range(2):
                    m = g * 2 + c
                    p_g, p_u, s = pg[cc % 2], pu[cc % 2], sg[cc % 2]
                    cc += 1
                    for k in range(KD):
                        em.mm(p_g, p_g[:, 0:n], w_g[:, k, c * 128:(c + 1) * 128], xm[:, k, 0:n],
                              k == 0, k == KD - 1, reads=[w_g, xm])
                    for k in range(KD):
                        em.mm(p_u, p_u[:, 0:n], w_u[:, k, c * 128:(c + 1) * 128], xm[:, k, 0:n],
                              k == 0, k == KD - 1, reads=[w_u, xm])
                    em.act(s, s[:, 0:n], p_g[:, 0:n], AF.Silu, reads=[p_g])
                    em.op("dve", lambda e: e.tensor_tensor(out=hh[:, m, 0:n], in0=s[:, 0:n], in1=p_u[:, 0:n],
                                                           op=ALU.mult), reads=[s, p_u], writes=[hh], partial=True)
            for m in range(KD):
                w_d = wd[dc % 2]
                p_d = pd[dc % 2]
                dc += 1
                em.dma("sp", w_d[:], P.wb_down[l, m], reads=[P.wb_dense_t[l]], writes=[w_d])
                for k in range(KF):
                    em.mm(p_d, p_d[:, 0:n], w_d[:, k, :], hh[:, k, 0:n], k == 0, k == KF - 1, reads=[w_d, hh])
                rl.add_chunk(m, p_d, xs, n, l, GT_F, j)
            if final:
                rl.finish(xs, n, l, 2, P.xOut, xdst_t[bi], t0 - CTX)
            else:
                rl.finish(xs, n, l, 2, xdst, xdst_t[bi], t0)
        em.barrier()


NEG = -1.0e30
PIPE = True


def declare_scan(P):
    L = P.nl
    P.masks_in = P.din("masks", [128, 8, 128])
    P.ident_in = P.din("ident", [128, 128])
    P.tri_in = P.din("tri", [128, 2, 128])
    P.bmasks_in = P.din("bmasks", [128, 4, 128])
    P.rw_row = P.din("rw_row", [L, 11, 512])
    P.rw_mu = P.din("rw_mu", [L, 1536])
    P.rw_muL = P.din("rw_muL", [128, L, 3])
    P.rw_w2 = P.din("rw_w2", [L, 64, 512])
    P.rw_a2 = P.din("rw_a2", [L, 64, 512])
    P.rw_g2 = P.din("rw_g2", [L, 96, 512])
    P.gd_conv = P.din("gd_conv", [L, 5, 1536])
    P.gd_row = P.din("gd_row", [L, 3, 512])
    P.yscr = P.dscr("yscr", [4, TTP, 512])
    P.auxs = P.dscr("auxs", [3, TTP, 512])


def setup_scan_consts(P):
    em = P.em
    P.masks = em.sb("masks_sb", [128, 8, 128], F32)
    P.ident = em.sb("ident", [128, 128], F32)
    P.tri = em.sb("tri", [128, 2, 128], F32)
    em.dma("sp", P.masks[:], P.masks_in[:, :, :], writes=[P.masks])
    em.dma("sp", P.ident[:], P.ident_in[:, :], writes=[P.ident])
    em.dma("sp", P.tri[:], P.tri_in[:, :, :], writes=[P.tri])
    P.bmasks = em.sb("bmasks_sb", [128, 4, 128], F32)
    em.dma("sp", P.bmasks[:], P.bmasks_in[:, :, :], writes=[P.bmasks])


def chunk_order(rev):
    nc_ctx = CTX // 128
    nc_lat = SEQ // 128
    ctx = [128 * i for i in range(nc_ctx)]
    lat = [CTX + 128 * i for i in range(nc_lat)]
    if rev:
        return ctx[::-1] + lat[::-1]
    return ctx + lat


class ScanCore:
    def __init__(self, P, st, N, tag):
        em = P.em
        self.P, self.N, self.H = P, N, 512 // N
        N_, H = N, self.H
        self.pb = [em.ps(tag + "_pb%d" % i, [128, 512], F32, st) for i in range(8)]
        self.pbi = 0
        f = lambda nm, shp: em.sb(tag + "_" + nm, shp, F32, st)
        self.XT = {k: f("xt_" + k, [N_, H, 128]) for k in ("a", "b", "k", "r", "R")}
        self.AT = [f("AT0", [128, H, 128])]
        self.A = [f("A0", [128, H, 128])]
        self.AkT = f("AkT", [128, H, 128])
        self.ArbT = f("ArbT", [128, H, 128])
        self.ArkT = f("ArkT", [128, H, 128])
        self.Z = [f("Z%d" % i, [128, H, 2 * N_]) for i in range(2)]
        self.GH = H // 2
        GH = self.GH
        fb = lambda nm, shp: em.sb(tag + "_" + nm, shp, BF16, st)
        self.J = [[[fb("J%d%d%d" % (gi, i, k), [128, GH, 128]) for k in range(2)] for i in range(2)] for gi in range(2)]
        self.X = [fb("X%d" % gi, [128, GH, 128]) for gi in range(2)]
        self.XTt = [fb("XTt%d" % gi, [128, GH, 128]) for gi in range(2)]
        self.Zb = [fb("Zb%d" % gi, [128, GH, 2 * N_]) for gi in range(2)]
        self.Zt = [f("Zt%d" % gi, [128, GH, 2 * N_]) for gi in range(2)]
        self.WpT = f("WpT", [N_, H, 128])
        self.U = f("U", [128, H, N_])
        self.ST = f("ST", [N_, H, N_])
        self.Ysb = [f("Ysb%d" % i, [128, 512]) for i in range(2)]
        self.yi = 0

    def bank(self):
        b = self.pb[self.pbi % 8]
        self.pbi += 1
        return b

    def reset_state(self):
        em = self.P.em
        em.op("dve", lambda e: e.memset(self.ST[:], 0.0), writes=[self.ST])

    def transpose_to(self, key, src):
        P, em, N, H = self.P, self.P.em, self.N, self.H
        dst = self.XT[key]
        for g in range(H // 4):
            pb = self.bank()
            for hh in range(4):
                h = g * 4 + hh
                em.op("pe", lambda e: e.transpose(pb[0:N, hh * 128:(hh + 1) * 128], src[:, h * N:(h + 1) * N],
                                                  P.ident[:]), reads=[src, P.ident], writes=[pb])
            em.act(dst, dst[:, g * 4:(g + 1) * 4, :], pb[0:N, :].rearrange("p (h t) -> p h t", h=4), AF.Copy,
                   reads=[pb])
        return dst

    def gram(self, lkey, rkey, dst, mask_ap_fn, mask_t):
        P, em, N, H = self.P, self.P.em, self.N, self.H
        L_, R_ = self.XT[lkey], self.XT[rkey]
        for g in range(H // 4):
            pb = self.bank()
            for hh in range(4):
                h = g * 4 + hh
                em.mm(pb, pb[:, hh * 128:(hh + 1) * 128], L_[:, h, :], R_[:, h, :], True, True, reads=[L_, R_])
            em.op("dve", lambda e: e.tensor_tensor(
                out=dst[:, g * 4:(g + 1) * 4, :], in0=pb[:, :].rearrange("p (h t) -> p h t", h=4),
                in1=mask_ap_fn(g), op=ALU.mult), reads=[pb, mask_t], writes=[dst], partial=True)

    def run_chunk(self, rev, ops, WcT, masks, ydst_ap, filler=None):
        P, em, N, H = self.P, self.P.em, self.N, self.H
        hv = lambda t: t[:].rearrange("p (h n) -> p h n", n=N)
        same_bk = ops["gb"] is ops["gk"]
        self.transpose_to("a", ops["ga"])
        self.transpose_to("k", ops["gk"])
        if not same_bk:
            self.transpose_to("b", ops["gb"])
        bkey = "k" if same_bk else "b"
        self.transpose_to("r", ops["gr"])
        if ops["Rtil"] is ops["gr"]:
            Rkey = "r"
        else:
            self.transpose_to("R", ops["Rtil"])
            Rkey = "R"
        AT, A = self.AT[0], self.A[0]
        self.gram(bkey, "a", AT, *masks["AT"])
        self.gram("a", bkey, A, *masks["A"])
        if same_bk:
            AkT, ArbT = AT, None
        else:
            AkT, ArbT = self.AkT, self.ArbT
            self.gram("k", "a", AkT, *masks["AT"])
            self.gram("b", "r", ArbT, *masks["ArT"])
        ArkT = self.ArkT
        self.gram("k", "r", ArkT, *masks["ArT"])
        if same_bk:
            ArbT = ArkT
        V = ops["V"]
        Z = self.Z[0]
        pb = self.bank()
        for h in range(H):
            em.mm(pb, pb[:, h * N:(h + 1) * N], AkT[:, h, :], V[:, h * N:(h + 1) * N], True, True,
                  reads=[AkT, V])
        em.op("dve", lambda e: e.tensor_copy(out=Z[:, :, 0:N], in_=pb[:, :].rearrange("p (h n) -> p h n", n=N)),
              reads=[pb], writes=[Z], partial=True)
        em.op("pool", lambda e: e.tensor_copy(out=Z[:, :, N:2 * N], in_=hv(ops["Atil"])),
              reads=[ops["Atil"]], writes=[Z], partial=True)
        Zf = self.Z[1]
        HB = 512 // (2 * N)
        A0, AT0 = self.A[0], self.AT[0]
        GH = self.GH
        bm = lambda i: P.bmasks[:, i:i + 1, :].to_broadcast([128, GH, 128])
        idb = P.ident[:, :].unsqueeze(1).to_broadcast([128, GH, 128])

        def mmg(dst_pb, lhs_t, rhs_t):
            for hh in range(GH):
                em.mm(dst_pb, dst_pb[:, hh * 128:(hh + 1) * 128], lhs_t[:, hh, :], rhs_t[:, hh, :], True, True,
                      reads=[lhs_t, rhs_t])

        vg = lambda pb_: pb_[:, 0:GH * 128].rearrange("p (h t) -> p h t", h=GH)

        def inv_group(g):
            gs = slice(g * GH, (g + 1) * GH)
            X, XT = self.X[g], self.XTt[g]
            J = self.J[g]
            Ja, JaT = J[0]
            em.op("dve", lambda e: e.tensor_tensor(out=Ja[:], in0=A0[:, gs, :], in1=bm(0), op=ALU.mult),
                  reads=[A0, P.bmasks], writes=[Ja])
            em.op("pool", lambda e: e.tensor_tensor(out=JaT[:], in0=AT0[:, gs, :], in1=bm(0), op=ALU.mult),
                  reads=[AT0, P.bmasks], writes=[JaT])
            em.op("dve", lambda e: e.tensor_tensor(out=X[:], in0=Ja[:], in1=idb, op=ALU.add),
                  reads=[Ja, P.ident], writes=[X])
            em.op("pool", lambda e: e.tensor_tensor(out=XT[:], in0=JaT[:], in1=idb, op=ALU.add),
                  reads=[JaT, P.ident], writes=[XT])
            yield
            cur = 0
            for lev in range(3):
                Jc, JcT = J[cur]
                Jn, JnT = J[1 - cur]
                p1, p2 = self.bank(), self.bank()
                mmg(p1, JcT, Jc)
                mmg(p2, Jc, JcT)
                em.op("dve", lambda e: e.tensor_copy(out=Jn[:], in_=vg(p1)), reads=[p1], writes=[Jn])
                em.act(JnT, JnT[:], vg(p2), AF.Copy, reads=[p2])
                yield
                p3, p4 = self.bank(), self.bank()
                mmg(p3, JnT, X)
                mmg(p4, Jn, XT)
                em.op("dve", lambda e: e.tensor_tensor(out=X[:], in0=vg(p3), in1=X[:], op=ALU.add),
                      reads=[p3, X], writes=[X])
                em.op("dve", lambda e: e.tensor_tensor(out=XT[:], in0=vg(p4), in1=XT[:], op=ALU.add),
                      reads=[p4, XT], writes=[XT])
                yield
                cur = 1 - cur
            for bi in (1, 2, 3):
                Ao, AoT = J[0]
                Y, Y2 = J[1]
                em.op("dve", lambda e: e.tensor_tensor(out=Ao[:], in0=A0[:, gs, :], in1=bm(bi), op=ALU.mult),
                      reads=[A0, P.bmasks], writes=[Ao])
                em.op("pool", lambda e: e.tensor_tensor(out=AoT[:], in0=AT0[:, gs, :], in1=bm(bi), op=ALU.mult),
                      reads=[AT0, P.bmasks], writes=[AoT])
                last = bi == 3
                p2 = self.bank()
                mmg(p2, Ao, XT)
                if not last:
                    p1 = self.bank()
                    mmg(p1, AoT, X)
                    em.op("dve", lambda e: e.tensor_copy(out=Y[:], in_=vg(p1)), reads=[p1], writes=[Y])
                em.act(Y2, Y2[:], vg(p2), AF.Copy, reads=[p2])
                yield
                p4 = self.bank()
                mmg(p4, X, Y2)
                if not last:
                    p3 = self.bank()
                    mmg(p3, XT, Y)
                    em.op("dve", lambda e: e.tensor_tensor(out=X[:], in0=vg(p3), in1=X[:], op=ALU.add),
                          reads=[p3, X], writes=[X])
                em.op("dve", lambda e: e.tensor_tensor(out=XT[:], in0=vg(p4), in1=XT[:], op=ALU.add),
                      reads=[p4, XT], writes=[XT])
                yield
            nsub = max(1, GH // HB)
            hps = GH // nsub
            Zb, Zt = self.Zb[g], self.Zt[g]
            em.op("pool", lambda e: e.tensor_copy(out=Zb[:], in_=Z[:, gs, :]), reads=[Z], writes=[Zb])

            def apply_x(src_b, accumulate):
                for sub in range(nsub):
                    pb = self.bank()
                    for hh in range(hps):
                        h4 = sub * hps + hh
                        em.mm(pb, pb[:, hh * 2 * N:(hh + 1) * 2 * N], XT[:, h4, :], src_b[:, h4, :], True, True,
                              reads=[XT, src_b])
                    h0 = g * GH + sub * hps
                    pv = pb[:, 0:hps * 2 * N].rearrange("p (h n) -> p h n", h=hps)
                    if accumulate:
                        em.op("dve", lambda e: e.tensor_tensor(out=Zf[:, h0:h0 + hps, :], in0=pv,
                                                               in1=Zf[:, h0:h0 + hps, :], op=ALU.add),
                              reads=[pb, Zf], writes=[Zf], partial=True)
                    else:
                        em.act(Zf, Zf[:, h0:h0 + hps, :], pv, AF.Copy, reads=[pb], writes=[Zf])

            apply_x(Zb, False)
            yield
            em.op("pool", lambda e: e.tensor_tensor(out=Zt[:], in0=Z[:, gs, :], in1=Zf[:, gs, :], op=ALU.subtract),
                  reads=[Z, Zf], writes=[Zt])
            for sub in range(nsub):
                pb = self.bank()
                for hh in range(hps):
                    h4 = sub * hps + hh
                    h = g * GH + h4
                    em.mm(pb, pb[:, hh * 2 * N:(hh + 1) * 2 * N], AT0[:, h, :], Zf[:, h, :], True, True,
                          reads=[AT0, Zf])
                h4s = slice(sub * hps, (sub + 1) * hps)
                em.op("dve", lambda e: e.tensor_tensor(
                    out=Zb[:, h4s, :], in0=pb[:, 0:hps * 2 * N].rearrange("p (h n) -> p h n", h=hps),
                    in1=Zt[:, h4s, :], op=ALU.add), reads=[pb, Zt], writes=[Zb], partial=True)
            yield
            apply_x(Zb, True)
            yield

        gens = [inv_group(0), inv_group(1)]
        alive = [True, True]
        while any(alive):
            for gi in range(2):
                if alive[gi]:
                    try:
                        next(gens[gi])
                    except StopIteration:
                        alive[gi] = False
            if filler is not None:
                next(filler, None)
        WpT = self.WpT
        for g in range(H // 4):
            pb = self.bank()
            for hh in range(4):
                h = g * 4 + hh
                em.op("pe", lambda e: e.transpose(pb[0:N, hh * 128:(hh + 1) * 128], Zf[:, h, N:2 * N],
                                                  P.ident[:]), reads=[Zf, P.ident], writes=[pb])
            em.act(WpT, WpT[:, g * 4:(g + 1) * 4, :], pb[0:N, :].rearrange("p (h t) -> p h t", h=4), AF.Copy,
                   reads=[pb], writes=[WpT])
        ST, U = self.ST, self.U
        RT = self.XT[Rkey]
        pb = self.bank()
        for h in range(H):
            em.mm(pb, pb[:, h * N:(h + 1) * N], WpT[:, h, :], ST[:, h, :], True, True, reads=[WpT, ST])
        em.op("dve", lambda e: e.tensor_tensor(out=U[:], in0=pb[:, :].rearrange("p (h n) -> p h n", n=N),
                                               in1=Zf[:, :, 0:N], op=ALU.add), reads=[pb, Zf], writes=[U])
        pb = self.bank()
        for h in range(H):
            o = pb[:, h * N:(h + 1) * N]
            em.mm(pb, o, RT[:, h, :], ST[:, h, :], True, False, reads=[RT, ST])
            em.mm(pb, o, ArbT[:, h, :], U[:, h, :], False, False, reads=[ArbT, U])
            em.mm(pb, o, ArkT[:, h, :], V[:, h * N:(h + 1) * N], False, True, reads=[ArkT, V])
        ysb = self.Ysb[self.yi % 2]
        self.yi += 1
        em.act(ysb, ysb[:], pb[:, :], AF.Copy, reads=[pb])
        em.dma("sp", ydst_ap, ysb[:], reads=[ysb], writes=[T()])
        pb = self.bank()
        Bh, Kh = ops["Bh"], ops["Kh"]
        for h in range(H):
            o = pb[0:N, h * N:(h + 1) * N]
            em.mm(pb, o, Bh[:, h * N:(h + 1) * N], U[:, h, :], True, False, reads=[Bh, U])
            em.mm(pb, o, Kh[:, h * N:(h + 1) * N], V[:, h * N:(h + 1) * N], False, True, reads=[Kh, V])
        em.op("dve", lambda e: e.tensor_tensor(out=ST[:], in0=ST[:],
                                               in1=WcT[:, :].unsqueeze(2).to_broadcast([N, H, N]), op=ALU.mult),
              reads=[ST, WcT], writes=[ST])
        em.op("dve", lambda e: e.tensor_tensor(out=ST[:], in0=pb[0:N, :].rearrange("p (h n) -> p h n", n=N),
                                               in1=ST[:], op=ALU.add), reads=[pb, ST], writes=[ST])


C0 = math.exp(-0.5)


def _bc_row(P, st, name, src_row_ap, width=512):
    em = P.em
    t = em.sb(name, [128, width], F32, st)
    em.dma("sp", t[:], src_row_ap.partition_broadcast(128), writes=[t])
    return t


def phase_rwkv(P, l, want_ctx_out=True):
    em = P.em
    dummy = T()
    with ExitStack() as st:
        core = ScanCore(P, st, 64, "rw")
        f = lambda nm, shp=(128, 512): em.sb("rw_" + nm, list(shp), F32, st)
        prm = {}
        for i, nm in enumerate(["w0_0", "w0_1", "a0_0", "a0_1", "k_k", "k_a", "r_k"]):
            prm[nm] = _bc_row(P, st, "rwp_" + nm, P.rw_row[l, i:i + 1, :])
        omm = f("omm", (128, 1536))
        hmu = f("hmu", (128, 1536))
        em.dma("sp", omm[:], P.rw_mu[l:l + 1, :].partition_broadcast(128), writes=[omm])
        em.op("dve", lambda e: e.tensor_scalar_mul(out=hmu[:], in0=omm[:], scalar1=0.5), reads=[omm], writes=[hmu])
        em.op("dve", lambda e: e.tensor_scalar(out=omm[:], in0=omm[:], scalar1=-1.0, scalar2=1.0, op0=ALU.mult,
                                               op1=ALU.add), reads=[omm], writes=[omm])
        muL = f("muL", (128, 3))
        ommL = f("ommL", (128, 3))
        hmuL = f("hmuL", (128, 3))
        em.dma("sp", muL[:], P.rw_muL[:, l, :], writes=[muL])
        em.op("dve", lambda e: e.tensor_scalar_mul(out=hmuL[:], in0=muL[:], scalar1=0.5), reads=[muL], writes=[hmuL])
        em.op("dve", lambda e: e.tensor_scalar(out=ommL[:], in0=muL[:], scalar1=-1.0, scalar2=1.0, op0=ALU.mult,
                                               op1=ALU.add), reads=[muL], writes=[ommL])
        w2 = f("w2", (64, 512))
        a2 = f("a2", (64, 512))
        g2 = f("g2", (96, 512))
        em.dma("sp", w2[:], P.rw_w2[l, :, :], writes=[w2])
        em.dma("sp", a2[:], P.rw_a2[l, :, :], writes=[a2])
        em.dma("sp", g2[:], P.rw_g2[l, :, :], writes=[g2])
        cen, prv, nxt = f("cen"), f("prv"), f("nxt")
        rp, kp = f("rp"), f("kp")
        vp2 = [f("vp0"), f("vp1")]
        lo = f("lo", (128, 3, 130))
        loP = f("loP", (128, 3, 128))
        lot = f("lot", (128, 128))
        sgd = [f("sgd0"), f("sgd1")]
        alr = [f("alr0"), f("alr1")]
        gg = f("gg")
        kk = f("kk")
        kdir = [f("kdir0"), f("kdir1")]
        t1, t2, t3 = f("t1"), f("t2"), f("t3")
        ss = f("ss", (128, 8))
        E1, E1p, E2, E3 = f("E1"), f("E1p"), f("E2"), f("E3")
        at, bt, kt, rt, bb = f("at"), f("bt"), f("kt"), f("rt"), f("bb")
        Bh2, Kh2 = [f("Bh0"), f("Bh1")], [f("Kh0"), f("Kh1")]
        WcT2 = [f("WcT0", (64, 8)), f("WcT1", (64, 8))]
        aux = f("aux")
        for d in (0, 1):
            rev = d == 1
            core.reset_state()
            if not rev:
                mk = {"AT": (lambda g: P.masks[:, 1:2, :].to_broadcast([128, 4, 128]), P.masks),
                      "A": (lambda g: P.masks[:, 0:1, :].to_broadcast([128, 4, 128]), P.masks),
                      "ArT": (lambda g: P.masks[:, 3:4, :].to_broadcast([128, 4, 128]), P.masks)}
            else:
                mk = {"AT": (lambda g: P.masks[:, 0:1, :].to_broadcast([128, 4, 128]), P.masks),
                      "A": (lambda g: P.masks[:, 1:2, :].to_broadcast([128, 4, 128]), P.masks),
                      "ArT": (lambda g: P.masks[:, 2:3, :].to_broadcast([128, 4, 128]), P.masks)}
            def prep(t0, slot, d=d):
                vp, Bh, Kh, WcT = vp2[slot], Bh2[slot], Kh2[slot], WcT2[slot]
                pp = ppos(t0)
                for ci, dst in enumerate((rp, kp, vp)):
                    c0 = ci * 512
                    em.dma("sp", cen[:], P.pTok[pp:pp + 128, c0:c0 + 512], writes=[cen])
                    em.dma("sp", prv[:], P.pTok[pp - 1:pp + 127, c0:c0 + 512], writes=[prv])
                    em.dma("sp", nxt[:], P.pTok[pp + 1:pp + 129, c0:c0 + 512], writes=[nxt])
                    em.op("pool", lambda e: e.tensor_tensor(out=prv[:], in0=prv[:], in1=nxt[:], op=ALU.add),
                          reads=[prv, nxt], writes=[prv])
                    em.op("pool", lambda e: e.tensor_tensor(out=prv[:], in0=prv[:], in1=hmu[:, c0:c0 + 512],
                                                            op=ALU.mult), reads=[prv, hmu], writes=[prv])
                    em.op("dve", lambda e: e.tensor_tensor(out=cen[:], in0=cen[:], in1=omm[:, c0:c0 + 512],
                                                           op=ALU.mult), reads=[cen, omm], writes=[cen])
                    em.op("dve", lambda e: e.tensor_tensor(out=dst[:], in0=cen[:], in1=prv[:], op=ALU.add),
                          reads=[cen, prv], writes=[dst])
                yield
                for c, (fi, wdt) in enumerate(((10, 64), (11, 64), (12, 96))):
                    em.dma("sp", lo[0:wdt, c, :], P.pT[fi * 128:fi * 128 + wdt, pp - 1:pp + 129], writes=[lo],
                           partial=True)
                for c, wdt in enumerate((64, 64, 96)):
                    em.op("dve", lambda e: e.tensor_tensor(out=lot[0:wdt, :], in0=lo[0:wdt, c, 0:128],
                                                           in1=lo[0:wdt, c, 2:130], op=ALU.add),
                          reads=[lo], writes=[lot])
                    em.op("dve", lambda e: e.tensor_scalar_mul(out=lot[0:wdt, :], in0=lot[0:wdt, :],
                                                               scalar1=hmuL[0:wdt, c:c + 1]),
                          reads=[lot, hmuL], writes=[lot])
                    em.op("dve", lambda e: e.scalar_tensor_tensor(
                        out=loP[0:wdt, c, :], in0=lo[0:wdt, c, 1:129], scalar=ommL[0:wdt, c:c + 1],
                        in1=lot[0:wdt, :], op0=ALU.mult, op1=ALU.add), reads=[lo, ommL, lot], writes=[loP],
                        partial=True)
                em.act(loP, loP[0:64, 0, :], loP[0:64, 0, :], AF.Tanh, reads=[loP])
                em.act(loP, loP[0:96, 2, :], loP[0:96, 2, :], AF.Sigmoid, reads=[loP])
                yield
                for dd in (0, 1):
                    pb = core.bank()
                    em.mm(pb, pb[:, :], loP[dd * 32:(dd + 1) * 32, 0, :], w2[dd * 32:(dd + 1) * 32, :], True, True,
                          reads=[loP, w2])
                    em.op("dve", lambda e: e.tensor_tensor(out=sgd[dd][:], in0=pb[:, :], in1=prm["w0_%d" % dd][:],
                                                           op=ALU.add), reads=[pb, prm["w0_%d" % dd]],
                          writes=[sgd[dd]])
                    em.act(sgd[dd], sgd[dd][:], sgd[dd][:], AF.Sigmoid, reads=[sgd[dd]])
                    pb = core.bank()
                    em.mm(pb, pb[:, :], loP[dd * 32:(dd + 1) * 32, 1, :], a2[dd * 32:(dd + 1) * 32, :], True, True,
                          reads=[loP, a2])
                    em.op("dve", lambda e: e.tensor_tensor(out=alr[dd][:], in0=pb[:, :], in1=prm["a0_%d" % dd][:],
                                                           op=ALU.add), reads=[pb, prm["a0_%d" % dd]],
                          writes=[alr[dd]])
                    em.act(alr[dd], alr[dd][:], alr[dd][:], AF.Sigmoid, reads=[alr[dd]])
                yield
                em.op("dve", lambda e: e.tensor_tensor(out=kk[:], in0=kp[:], in1=prm["k_k"][:], op=ALU.mult),
                      reads=[kp, prm["k_k"]], writes=[kk])
                em.act(t1, t1[:], kk[:], AF.Square, reads=[kk])
                em.op("dve", lambda e: e.tensor_reduce(out=ss[:], in_=t1[:].rearrange("p (h n) -> p h n", n=64),
                                                       axis=AX.X, op=ALU.add), reads=[t1], writes=[ss])
                em.act(ss, ss[:], ss[:], AF.Sqrt, reads=[ss, P.epsc], bias=P.epsc[:, 3:4])
                em.op("dve", lambda e: e.reciprocal(out=ss[:], in_=ss[:]), reads=[ss], writes=[ss])
                em.op("dve", lambda e: e.tensor_tensor(
                    out=kk[:].rearrange("p (h n) -> p h n", n=64), in0=kk[:].rearrange("p (h n) -> p h n", n=64),
                    in1=ss[:, :].unsqueeze(2).to_broadcast([128, 8, 64]), op=ALU.mult), reads=[kk, ss], writes=[kk])
                yield
                for dd in (0, 1):
                    em.op("dve", lambda e: e.scalar_tensor_tensor(
                        out=t1[:], in0=alr[dd][:], scalar=-1.0, in1=prm["k_a"][:], op0=ALU.add, op1=ALU.mult),
                        reads=[alr[dd], prm["k_a"]], writes=[t1])
                    em.op("dve", lambda e: e.scalar_tensor_tensor(
                        out=kdir[dd][:], in0=t1[:], scalar=1.0, in1=kp[:], op0=ALU.add, op1=ALU.mult),
                        reads=[t1, kp], writes=[kdir[dd]])
                yield
                if d == 0:
                    pb = core.bank()
                    em.mm(pb, pb[:, :], loP[0:96, 2, :], g2[0:96, :], True, True, reads=[loP, g2])
                    em.act(gg, gg[:], pb[:, :], AF.Copy, reads=[pb])
                    em.dma("sp", P.auxs[1, pp:pp + 128, :], gg[:], reads=[gg], writes=[dummy])
                    em.op("pool", lambda e: e.tensor_tensor(out=t2[:], in0=kdir[0][:], in1=kdir[1][:], op=ALU.add),
                          reads=[kdir[0], kdir[1]], writes=[t2])
                    em.op("pool", lambda e: e.tensor_tensor(out=t2[:], in0=t2[:], in1=rp[:], op=ALU.mult),
                          reads=[t2, rp], writes=[t2])
                    em.op("pool", lambda e: e.tensor_tensor(out=t2[:], in0=t2[:], in1=prm["r_k"][:], op=ALU.mult),
                          reads=[t2, prm["r_k"]], writes=[t2])
                    em.op("dve", lambda e: e.tensor_reduce(out=ss[:], in_=t2[:].rearrange("p (h n) -> p h n", n=64),
                                                           axis=AX.X, op=ALU.add), reads=[t2], writes=[ss])
                    em.op("dve", lambda e: e.tensor_tensor(
                        out=aux[:].rearrange("p (h n) -> p h n", n=64), in0=vp[:].rearrange("p (h n) -> p h n", n=64),
                        in1=ss[:, :].unsqueeze(2).to_broadcast([128, 8, 64]), op=ALU.mult), reads=[vp, ss],
                        writes=[aux])
                    em.dma("sp", P.auxs[0, pp:pp + 128, :], aux[:], reads=[aux], writes=[dummy])
                yield
                sg_ = sgd[d]
                pc = core.bank()
                em.mm(pc, pc[:, :], P.tri[:, d, :], sg_[:], True, True, reads=[P.tri, sg_])
                ptot = core.bank()
                em.mm(ptot, ptot[:, :], P.ones_f[:], sg_[:], True, True, reads=[P.ones_f, sg_])
                pw = core.bank()
                for h in range(8):
                    em.mm(pw, pw[0:64, h:h + 1], sg_[:, h * 64:(h + 1) * 64], P.ones_f[:, 0:1], True, True,
                          reads=[sg_, P.ones_f])
                em.act(WcT, WcT[:], pw[0:64, 0:8], AF.Exp, reads=[pw], scale=-C0)
                em.op("dve", lambda e: e.tensor_copy(out=t1[:], in_=pc[:, :]), reads=[pc], writes=[t1])
                em.op("dve", lambda e: e.tensor_tensor(out=t2[:], in0=t1[:], in1=sg_[:], op=ALU.subtract),
                      reads=[t1, sg_], writes=[t2])
                em.op("dve", lambda e: e.tensor_tensor(out=t3[:], in0=ptot[:, :], in1=t1[:], op=ALU.subtract),
                      reads=[ptot, t1], writes=[t3])
                yield
                em.act(E1, E1[:], t1[:], AF.Exp, reads=[t1], scale=-C0)
                em.act(E2, E2[:], t1[:], AF.Exp, reads=[t1], scale=C0)
                em.act(E1p, E1p[:], t2[:], AF.Exp, reads=[t2], scale=-C0)
                em.act(E3, E3[:], t3[:], AF.Exp, reads=[t3], scale=-C0)
                yield
                kd, al = kdir[d], alr[d]
                em.op("dve", lambda e: e.scalar_tensor_tensor(out=at[:], in0=kk[:], scalar=-1.0, in1=E1p[:],
                                                              op0=ALU.mult, op1=ALU.mult), reads=[kk, E1p], writes=[at])
                em.op("pool", lambda e: e.tensor_tensor(out=bb[:], in0=kk[:], in1=al[:], op=ALU.mult),
                      reads=[kk, al], writes=[bb])
                em.op("pool", lambda e: e.tensor_tensor(out=bt[:], in0=bb[:], in1=E2[:], op=ALU.mult),
                      reads=[bb, E2], writes=[bt])
                em.op("pool", lambda e: e.tensor_tensor(out=Bh[:], in0=bb[:], in1=E3[:], op=ALU.mult),
                      reads=[bb, E3], writes=[Bh])
                em.op("dve", lambda e: e.tensor_tensor(out=kt[:], in0=kd[:], in1=E2[:], op=ALU.mult),
                      reads=[kd, E2], writes=[kt])
                em.op("pool", lambda e: e.tensor_tensor(out=Kh[:], in0=kd[:], in1=E3[:], op=ALU.mult),
                      reads=[kd, E3], writes=[Kh])
                em.op("dve", lambda e: e.tensor_tensor(out=rt[:], in0=rp[:], in1=E1[:], op=ALU.mult),
                      reads=[rp, E1], writes=[rt])
                yield

            order = chunk_order(rev)
            for _ in prep(order[0], 0):
                pass
            for ci, t0 in enumerate(order):
                slot = ci % 2
                gnext = prep(order[ci + 1], 1 - slot) if ci + 1 < len(order) else None
                ops = {"ga": at, "gb": bt, "gk": kt, "gr": rt, "V": vp2[slot], "Atil": at, "Bh": Bh2[slot],
                       "Kh": Kh2[slot], "Rtil": rt}
                core.run_chunk(rev, ops, WcT2[slot], mk, P.yscr[d, ppos(t0):ppos(t0) + 128, :], filler=(gnext if PIPE else None))
                if gnext is not None:
                    for _ in gnext:
                        pass
        em.barrier()


def host_consts():
    idx = np.arange(128)
    r, c = idx[:, None], idx[None, :]
    m = np.zeros((128, 8, 128), np.float32)
    for i, cond in enumerate((c < r, c > r, c <= r, c >= r)):
        m[:, i, :] = cond.astype(np.float32)
        m[:, 4 + i, :] = np.where(cond, 0.0, NEG).astype(np.float32)
    tri = np.zeros((128, 2, 128), np.float32)
    tri[:, 0, :] = (r <= c)
    tri[:, 1, :] = (r >= c)
    bd = lambda b: ((r // b) == (c // b)).astype(np.float32)
    bmk = np.stack([bd(16), bd(32) - bd(16), bd(64) - bd(32), 1.0 - bd(64)], 1).astype(np.float32)
    return {"masks": m, "ident": np.eye(128, dtype=np.float32), "tri": tri, "bmasks": bmk}


def host_scan_inputs(inp, L):
    f32 = np.float32
    out = {}
    rw = np.zeros((L, 11, 512), f32)
    rw[:, 0] = inp["rwkv_w0"][:L, 0]
    rw[:, 1] = inp["rwkv_w0"][:L, 1]
    rw[:, 2] = inp["rwkv_a0"][:L, 0]
    rw[:, 3] = inp["rwkv_a0"][:L, 1]
    rw[:, 4] = inp["rwkv_k_k"][:L]
    rw[:, 5] = inp["rwkv_k_a"][:L]
    rw[:, 6] = inp["rwkv_r_k"][:L].reshape(L, 512)
    rw[:, 7] = inp["rwkv_gn_g"][:L]
    rw[:, 8] = inp["rwkv_gn_b"][:L]
    out["rw_row"] = rw
    mu = inp["rwkv_mu"][:L]
    out["rw_mu"] = np.ascontiguousarray(mu[:, 0:1536])
    muL = np.zeros((128, L, 3), f32)
    muL[0:64, :, 0] = mu[:, 1536:1600].T
    muL[0:64, :, 1] = mu[:, 1600:1664].T
    muL[0:96, :, 2] = mu[:, 1664:1760].T
    out["rw_muL"] = muL
    out["rw_w2"] = np.ascontiguousarray(inp["rwkv_w2"][:L].reshape(L, 64, 512))
    out["rw_a2"] = np.ascontiguousarray(inp["rwkv_a2"][:L].reshape(L, 64, 512))
    out["rw_g2"] = np.ascontiguousarray(inp["rwkv_g2"][:L])
    out["gd_conv"] = np.ascontiguousarray(inp["gdn_conv"][:L])
    gr = np.zeros((L, 3, 512), f32)
    gr[:, 0] = np.tile(inp["gdn_norm"][:L], (1, 4))
    gr[:, 1, 0:8] = inp["gdn_a_log"][:L].reshape(L, 8)
    gr[:, 1, 8:16] = inp["gdn_dt_bias"][:L].reshape(L, 8)
    out["gd_row"] = gr
    return out


def phase_gdn(P, l):
    em = P.em
    dummy = T()
    with ExitStack() as st:
        core = ScanCore(P, st, 128, "gd")
        f = lambda nm, shp=(128, 512): em.sb("gd_" + nm, list(shp), F32, st)
        cw = []
        for j in range(5):
            t = f("cw%d" % j, (128, 1536))
            em.dma("sp", t[:], P.gd_conv[l, j:j + 1, :].partition_broadcast(128), writes=[t])
            cw.append(t)
        prow = _bc_row(P, st, "gdp_row", P.gd_row[l, 1:2, :], 512)
        negea = f("negea", (128, 8))
        em.act(negea, negea[:], prow[:, 0:8], AF.Exp, reads=[prow])
        em.op("dve", lambda e: e.tensor_scalar_mul(out=negea[:], in0=negea[:], scalar1=-1.0), reads=[negea],
              writes=[negea])
        shs = [[f("sh%d_%d" % (j, i)) for j in range(5)] for i in range(2)]
        shc = [0]
        acc, tmp = f("acc"), f("tmp")
        qkv = [f("q"), f("k"), f("v")]
        ss = f("ss", (128, 4))
        ab = f("ab", (128, 16))
        gx, ge, gl = f("gx", (128, 8)), f("ge", (128, 8)), f("gl", (128, 8))
        gcol, beta = f("gcol", (128, 8)), f("beta", (128, 8))
        Gs, nG, eG, eTG, nb, nbeG = (f(n, (128, 4)) for n in ("Gs", "nG", "eG", "eTG", "nb", "nbeG"))
        ka, Atil, Rtil, zt = f("ka"), f("Atil"), f("Rtil"), f("zt")
        Kh2, Vp2 = [f("Kh0"), f("Kh1")], [f("Vp0"), f("Vp1")]
        etot2 = [f("etot0", (128, 4)), f("etot1", (128, 4))]
        diag = f("diag", (128, 4, 128))
        dtmp = f("dtmp", (128, 4, 128))
        Ds, DTs, DTi = f("Ds", (128, 4, 128)), f("DTs", (128, 4, 128)), f("DTi", (128, 4, 128))
        hv = lambda t: t[:].rearrange("p (h n) -> p h n", n=128)
        bc4 = lambda t: t[:, :].unsqueeze(2).to_broadcast([128, 4, 128])
        for d in (0, 1):
            rev = d == 1
            core.reset_state()
            mA, mAT, mATi = (4, 5, 7) if not rev else (5, 4, 6)
            mk = {"AT": (lambda g: DTs[:, :, :], DTs), "A": (lambda g: Ds[:, :, :], Ds),
                  "ArT": (lambda g: DTi[:, :, :], DTi)}
            def prep(t0, slot, d=d, mA=mA, mAT=mAT, mATi=mATi):
                Kh, Vp, etot = Kh2[slot], Vp2[slot], etot2[slot]
                pp = ppos(t0)
                for ci in range(3):
                    c0 = TOKC["gq"] + ci * 512
                    sh = shs[shc[0] % 2]
                    shc[0] += 1
                    for j in range(5):
                        em.dma("sp", sh[j][:], P.pTok[pp + j - 2:pp + j - 2 + 128, c0:c0 + 512], writes=[sh[j]])
                    em.op("dve", lambda e: e.tensor_tensor(out=acc[:], in0=sh[0][:], in1=cw[0][:, ci * 512:(ci + 1) * 512],
                                                           op=ALU.mult), reads=[sh[0], cw[0]], writes=[acc])
                    for j in range(1, 5):
                        eng = "pool" if j % 2 else "dve"
                        em.op(eng, lambda e: e.tensor_tensor(out=sh[j][:], in0=sh[j][:],
                                                             in1=cw[j][:, ci * 512:(ci + 1) * 512], op=ALU.mult),
                              reads=[sh[j], cw[j]], writes=[sh[j]])
                        em.op("dve", lambda e: e.tensor_tensor(out=acc[:], in0=acc[:], in1=sh[j][:], op=ALU.add),
                              reads=[acc, sh[j]], writes=[acc])
                    em.act(qkv[ci], qkv[ci][:], acc[:], AF.Silu, reads=[acc])
                    yield
                for ci, sc in ((0, 128 ** -0.5), (1, 1.0)):
                    x_ = qkv[ci]
                    em.act(tmp, tmp[:], x_[:], AF.Square, reads=[x_])
                    em.op("dve", lambda e: e.tensor_reduce(out=ss[:], in_=hv(tmp), axis=AX.X, op=ALU.add),
                          reads=[tmp], writes=[ss])
                    em.act(ss, ss[:], ss[:], AF.Sqrt, reads=[ss, P.epsc], bias=P.epsc[:, 3:4])
                    em.op("dve", lambda e: e.reciprocal(out=ss[:], in_=ss[:]), reads=[ss], writes=[ss])
                    if sc != 1.0:
                        em.op("dve", lambda e: e.tensor_scalar_mul(out=ss[:], in0=ss[:], scalar1=sc), reads=[ss],
                              writes=[ss])
                    em.op("dve", lambda e: e.tensor_tensor(out=hv(x_), in0=hv(x_), in1=bc4(ss), op=ALU.mult),
                          reads=[x_, ss], writes=[x_])
                q_, k_, v_ = qkv
                if d == 0:
                    em.dma("sp", zt[:], P.pTok[pp:pp + 128, TOKC["z"]:TOKC["z"] + 512], writes=[zt])
                    em.act(zt, zt[:], zt[:], AF.Silu, reads=[zt])
                    em.dma("sp", P.auxs[2, pp:pp + 128, :], zt[:], reads=[zt], writes=[dummy])
                yield
                em.dma("sp", ab[:], P.pTok[pp:pp + 128, TOKC["ab"]:TOKC["ab"] + 16], writes=[ab])
                em.op("dve", lambda e: e.tensor_tensor(out=gx[:], in0=ab[:, 0:8], in1=prow[:, 8:16], op=ALU.add),
                      reads=[ab, prow], writes=[gx])
                em.act(ge, ge[:], gx[:], AF.Abs, reads=[gx])
                em.act(ge, ge[:], ge[:], AF.Exp, reads=[ge], scale=-1.0)
                em.act(gl, gl[:], ge[:], AF.Ln, reads=[ge, P.ones_f], bias=P.ones_f[:, 0:1])
                em.op("dve", lambda e: e.scalar_tensor_tensor(out=gcol[:], in0=gx[:], scalar=0.0, in1=gl[:],
                                                              op0=ALU.max, op1=ALU.add), reads=[gx, gl], writes=[gcol])
                em.op("dve", lambda e: e.tensor_tensor(out=gcol[:], in0=gcol[:], in1=negea[:], op=ALU.mult),
                      reads=[gcol, negea], writes=[gcol])
                em.act(beta, beta[:], ab[:, 8:16], AF.Sigmoid, reads=[ab])
                gd_, bd_ = gcol[:, d * 4:(d + 1) * 4], beta[:, d * 4:(d + 1) * 4]
                pG = core.bank()
                em.mm(pG, pG[:, 0:4], P.tri[:, d, :], gd_, True, True, reads=[P.tri, gcol])
                pT_ = core.bank()
                em.mm(pT_, pT_[:, 0:4], P.ones_f[:], gd_, True, True, reads=[P.ones_f, gcol])
                em.op("dve", lambda e: e.tensor_copy(out=Gs[:], in_=pG[:, 0:4]), reads=[pG], writes=[Gs])
                em.op("dve", lambda e: e.tensor_scalar_mul(out=nG[:], in0=Gs[:], scalar1=-1.0), reads=[Gs], writes=[nG])
                em.act(eG, eG[:], Gs[:], AF.Exp, reads=[Gs])
                em.op("dve", lambda e: e.tensor_copy(out=etot[:], in_=pT_[:, 0:4]), reads=[pT_], writes=[etot])
                em.op("dve", lambda e: e.tensor_tensor(out=eTG[:], in0=etot[:], in1=Gs[:], op=ALU.subtract),
                      reads=[etot, Gs], writes=[eTG])
                em.act(etot, etot[:], etot[:], AF.Exp, reads=[etot])
                em.act(eTG, eTG[:], eTG[:], AF.Exp, reads=[eTG])
                yield
                em.op("dve", lambda e: e.tensor_scalar_mul(out=nb[:], in0=bd_, scalar1=-1.0), reads=[beta], writes=[nb])
                em.op("dve", lambda e: e.tensor_tensor(out=nbeG[:], in0=nb[:], in1=eG[:], op=ALU.mult),
                      reads=[nb, eG], writes=[nbeG])
                em.op("dve", lambda e: e.tensor_tensor(out=hv(ka), in0=hv(k_), in1=bc4(nb), op=ALU.mult),
                      reads=[k_, nb], writes=[ka])
                em.op("pool", lambda e: e.tensor_tensor(out=hv(Atil), in0=hv(k_), in1=bc4(nbeG), op=ALU.mult),
                      reads=[k_, nbeG], writes=[Atil])
                em.op("dve", lambda e: e.tensor_tensor(out=hv(Kh), in0=hv(k_), in1=bc4(eTG), op=ALU.mult),
                      reads=[k_, eTG], writes=[Kh])
                em.op("pool", lambda e: e.tensor_tensor(out=hv(Rtil), in0=hv(q_), in1=bc4(eG), op=ALU.mult),
                      reads=[q_, eG], writes=[Rtil])
                em.op("dve", lambda e: e.tensor_tensor(out=hv(Vp), in0=hv(v_),
                                                       in1=beta[:, d * 4:(d + 1) * 4].unsqueeze(2).to_broadcast([128, 4, 128]),
                                                       op=ALU.mult), reads=[v_, beta], writes=[Vp])
                yield
                em.op("dve", lambda e: e.tensor_tensor(out=diag[:], in0=P.ident[:, :].unsqueeze(1).to_broadcast([128, 4, 128]),
                                                       in1=bc4(Gs), op=ALU.mult), reads=[P.ident, Gs], writes=[diag])
                pR = core.bank()
                for h in range(4):
                    em.mm(pR, pR[:, h * 128:(h + 1) * 128], P.ones_f[:], diag[:, h, :], True, True,
                          reads=[P.ones_f, diag])
                pRv = pR[:, :].rearrange("p (h t) -> p h t", h=4)
                for dst, sgn, mi, bias_t in ((Ds, -1.0, mA, Gs), (DTs, 1.0, mAT, nG), (DTi, 1.0, mATi, nG)):
                    em.op("dve", lambda e: e.scalar_tensor_tensor(
                        out=dtmp[:], in0=pRv, scalar=sgn, in1=P.masks[:, mi:mi + 1, :].to_broadcast([128, 4, 128]),
                        op0=ALU.mult, op1=ALU.add), reads=[pR, P.masks], writes=[dtmp])
                    for h in range(4):
                        em.act(dst, dst[:, h, :], dtmp[:, h, :], AF.Exp, reads=[dtmp, bias_t],
                               bias=bias_t[:, h:h + 1], writes=[dst])
                yield

            order = chunk_order(rev)
            for _ in prep(order[0], 0):
                pass
            k_, q_ = qkv[1], qkv[0]
            for ci, t0 in enumerate(order):
                slot = ci % 2
                gnext = prep(order[ci + 1], 1 - slot) if ci + 1 < len(order) else None
                ops = {"ga": ka, "gb": k_, "gk": k_, "gr": q_, "V": Vp2[slot], "Atil": Atil, "Bh": Kh2[slot],
                       "Kh": Kh2[slot], "Rtil": Rtil}
                core.run_chunk(rev, ops, etot2[slot], mk, P.yscr[2 + d, ppos(t0):ppos(t0) + 128, :],
                               filler=(gnext if PIPE else None))
                if gnext is not None:
                    for _ in gnext:
                        pass
        em.barrier()


def phase_mix_out(P, l, with_ctx):
    em = P.em
    dummy = T()
    with ExitStack() as st:
        f = lambda nm, shp=(128, 512): em.sb("mo_" + nm, list(shp), F32, st)
        gn_g = _bc_row(P, st, "mo_gng", P.rw_row[l, 7:8, :])
        gn_b = _bc_row(P, st, "mo_gnb", P.rw_row[l, 8:9, :])
        nrm = _bc_row(P, st, "mo_nrm", P.gd_row[l, 0:1, :])
        sets = [dict(ya=f("ya%d" % i), yb=f("yb%d" % i), bv=f("bv%d" % i), gg=f("gg%d" % i), cen=f("cen%d" % i),
                     sq=f("sq%d" % i), s8=f("s8%d" % i, (128, 8))) for i in range(2)]
        ob = [em.sb("mo_ob%d" % i, [128, 4, 128], BF16, st) for i in range(2)]
        pb = [em.ps("mo_pb%d" % i, [128, 512], F32, st) for i in range(2)]
        cnt = 0
        tiles = ([128 * i for i in range(CTX // 128)] if with_ctx else []) + \
            [CTX + 128 * i for i in range(SEQ // 128)]
        for t0 in tiles:
            pp = ppos(t0)
            for mix, (ia, ib, nh, eps_col) in enumerate(((0, 1, 8, 2), (2, 3, 4, 1))):
                bs = sets[cnt % 2]
                ya, yb, bv, gg, cen, sq, s8 = (bs[k_] for k_ in ("ya", "yb", "bv", "gg", "cen", "sq", "s8"))
                n = 512 // nh
                hv = lambda t: t[:].rearrange("p (h n) -> p h n", n=n)
                bc = lambda t: t[:, 0:nh].unsqueeze(2).to_broadcast([128, nh, n])
                em.dma("sp", ya[:], P.yscr[ia, pp:pp + 128, :], writes=[ya])
                em.dma("sp", yb[:], P.yscr[ib, pp:pp + 128, :], writes=[yb])
                em.op("dve", lambda e: e.tensor_tensor(out=ya[:], in0=ya[:], in1=yb[:], op=ALU.add),
                      reads=[ya, yb], writes=[ya])
                if mix == 0:
                    em.dma("sp", bv[:], P.auxs[0, pp:pp + 128, :], writes=[bv])
                    em.dma("sp", gg[:], P.auxs[1, pp:pp + 128, :], writes=[gg])
                    em.op("dve", lambda e: e.tensor_reduce(out=s8[:, 0:nh], in_=hv(ya), axis=AX.X, op=ALU.add),
                          reads=[ya], writes=[s8])
                    em.op("dve", lambda e: e.tensor_scalar_mul(out=s8[:, 0:nh], in0=s8[:, 0:nh], scalar1=1.0 / n),
                          reads=[s8], writes=[s8])
                    em.op("dve", lambda e: e.tensor_tensor(out=hv(cen), in0=hv(ya), in1=bc(s8), op=ALU.subtract),
                          reads=[ya, s8], writes=[cen])
                else:
                    em.dma("sp", gg[:], P.auxs[2, pp:pp + 128, :], writes=[gg])
                    em.op("dve", lambda e: e.tensor_copy(out=cen[:], in_=ya[:]), reads=[ya], writes=[cen])
                em.act(sq, sq[:], cen[:], AF.Square, reads=[cen])
                em.op("dve", lambda e: e.tensor_reduce(out=s8[:, 0:nh], in_=hv(sq), axis=AX.X, op=ALU.add),
                      reads=[sq], writes=[s8])
                em.act(s8, s8[:, 0:nh], s8[:, 0:nh], AF.Sqrt, reads=[s8, P.epsc], bias=P.epsc[:, eps_col:eps_col + 1],
                       scale=1.0 / n)
                em.op("dve", lambda e: e.reciprocal(out=s8[:, 0:nh], in_=s8[:, 0:nh]), reads=[s8], writes=[s8])
                em.op("dve", lambda e: e.tensor_tensor(out=hv(cen), in0=hv(cen), in1=bc(s8), op=ALU.mult),
                      reads=[cen, s8], writes=[cen])
                if mix == 0:
                    em.op("pool", lambda e: e.tensor_tensor(out=cen[:], in0=cen[:], in1=gn_g[:], op=ALU.mult),
                          reads=[cen, gn_g], writes=[cen])
                    em.op("pool", lambda e: e.tensor_tensor(out=cen[:], in0=cen[:], in1=gn_b[:], op=ALU.add),
                          reads=[cen, gn_b], writes=[cen])
                    em.op("pool", lambda e: e.tensor_tensor(out=cen[:], in0=cen[:], in1=bv[:], op=ALU.add),
                          reads=[cen, bv], writes=[cen])
                else:
                    em.op("pool", lambda e: e.tensor_tensor(out=cen[:], in0=cen[:], in1=nrm[:], op=ALU.mult),
                          reads=[cen, nrm], writes=[cen])
                em.op("dve", lambda e: e.tensor_tensor(out=cen[:], in0=cen[:], in1=gg[:], op=ALU.mult),
                      reads=[cen, gg], writes=[cen])
                p_ = pb[cnt % 2]
                o_ = ob[cnt % 2]
                cnt += 1
                for j in range(4):
                    em.op("pe", lambda e: e.transpose(p_[:, j * 128:(j + 1) * 128], cen[:, j * 128:(j + 1) * 128],
                                                      P.ident[:]), reads=[cen, P.ident], writes=[p_])
                em.act(o_, o_[:], p_[:, :].rearrange("p (j t) -> p j t", j=4), AF.Copy, reads=[p_])
                r0 = 1024 + mix * 512
                em.dma("sp", P.mixT[r0:r0 + 512, t0:t0 + 128].rearrange("(j p) t -> p j t", p=128), o_[:],
                       reads=[o_], writes=[dummy])
        em.barrier()


def declare_mla(P):
    L = P.nl
    P.w_uq = P.din("mla_w_uq", [L, 512, 1536])
    P.w_uq_sw = P.din("mla_w_uq_sw", [L, 512, 512])
    P.w_ukv = P.din("mla_w_ukv", [L, 512, 2048])
    P.mlaT = P.din("mlaT", [128, L, 2, 4])
    P.ropeT = P.din("ropeT", [64, 2, SEQ])
    P.wb_uq = P.dscr("wb_uq", [L, 512, 2048], BF16)
    P.wb_ukv = P.dscr("wb_ukv", [L, 512, 2048], BF16)
    P.Kn = P.dscr("Kn", [1024, TT], BF16)
    P.Kr = P.dscr("Kr", [128, TT], BF16)
    P.sel64_in = P.din("sel64", [128, 1])
    P.Vt = P.dscr("Vt", [TT, 1024], BF16)
    P.Qn = P.dscr("Qn", [1024, TT], BF16)
    P.Qr = P.dscr("Qr", [8, 128, TT], BF16)


def phase_cast_mla(P):
    em = P.em
    P.wb_mla_t = [T() for _ in range(P.nl)]
    for l in range(P.nl):
        d = P.wb_mla_t[l]
        em.dma("pool", P.wb_uq[l, :, 0:1536], P.w_uq[l, :, :], writes=[d], partial=True)
        em.dma("pool", P.wb_uq[l, :, 1536:2048], P.w_uq_sw[l, :, :], writes=[d], partial=True)
        em.dma("pool", P.wb_ukv[l, :, :], P.w_ukv[l, :, :], writes=[d], partial=True)


def phase_mla(P, l, with_ctx):
    em = P.em
    dummy = T()
    with ExitStack() as st:
        f = lambda nm, shp, dt=F32: em.sb("ml_" + nm, list(shp), dt, st)
        gains = f("gains", (128, 2, 4))
        em.dma("sp", gains[:], P.mlaT[:, l, :, :], writes=[gains])
        wkv = f("wkv", (128, 4, 2048), BF16)
        wq = f("wq", (128, 4, 2048), BF16)
        wmt = getattr(P, "wb_mla_t", None)
        wrd = [wmt[l]] if wmt else []
        em.dma("sp", wkv[:], P.wb_ukv[l].rearrange("(k p) c -> p k c", p=128), reads=wrd, writes=[wkv])
        em.dma("sp", wq[:], P.wb_uq[l].rearrange("(k p) c -> p k c", p=128), reads=wrd, writes=[wq])
        wvv = f("wvv", (128, 4, 1024), BF16)
        for k in range(4):
            em.dma("sp", wvv[:, k, :].rearrange("p (h v) -> p h v", v=128),
                   P.wb_ukv[l, k * 128:(k + 1) * 128, :].rearrange("p (h two v) -> p h two v", two=2, v=128)[:, :, 1, :],
                   reads=wrd, writes=[wvv], partial=True)
        cx = f("cx", (128, 4, 512))
        cn = f("cn", (128, 4, 512), BF16)
        sq = [f("sq%d" % i, (128, 512)) for i in range(2)]
        rstd = f("rstd", (128, 512))
        krt, ksw, kro = f("krt", (64, 512)), f("ksw", (64, 512)), f("kro", (64, 512))
        rope = f("rope", (64, 2, 512))
        krb = f("krb", (128, 512), BF16)
        sel = f("sel", (128, 1))
        em.dma("sp", sel[:], P.sel64_in[:, :], writes=[sel])
        zt = f("zt", (128, 512))
        sqr = f("sqr", (128, 512), BF16)
        em.op("dve", lambda e: e.memset(sqr[:], 0.0), writes=[sqr])
        sqb = f("sqb", (128, 512), BF16)
        em.op("dve", lambda e: e.memset(zt[:], 0.0), writes=[zt])
        kmax = f("kmax", (128, 8))
        bmax = f("bmax", (128, 1))
        ob = [f("ob%d" % i, (128, 512), BF16) for i in range(3)]
        qrb = [f("qrb%d" % i, (128, 512), BF16) for i in range(2)]
        qr32, qs32 = f("qr32", (64, 512)), f("qs32", (64, 512))
        ps = [em.ps("ml_ps%d" % i, [128, 512], F32, st) for i in range(6)]
        pi = [0]
        oi = [0]

        def bank():
            pi[0] += 1
            return ps[pi[0] % 6]

        def obuf():
            oi[0] += 1
            return ob[oi[0] % 3]

        em.op("dve", lambda e: e.memset(kmax[:], 0.0), writes=[kmax])
        em.op("dve", lambda e: e.tensor_scalar(out=krb[64:128, :], in0=zt[64:128, :], scalar1=sel[64:128, 0:1],
                                               scalar2=None, op0=ALU.add), reads=[zt, sel], writes=[krb], partial=True)

        def rmsnorm_block(row0, gi, pp, n):
            em.dma("sp", cx[:, :, 0:n], P.pT[row0:row0 + 512, pp:pp + n].rearrange("(k p) t -> p k t", p=128),
                   writes=[cx])
            pss = bank()
            for k in range(4):
                s_ = sq[k % 2]
                em.act(s_, s_[:, 0:n], cx[:, k, 0:n], AF.Square, reads=[cx])
                em.mm(pss, pss[:, 0:n], P.ones_f[:], s_[:, 0:n], k == 0, k == 3, reads=[P.ones_f, s_])
            em.act(rstd, rstd[:, 0:n], pss[:, 0:n], AF.Sqrt, reads=[pss, P.epsc], bias=P.epsc[:, 1:2],
                   scale=1.0 / 512)
            em.op("dve", lambda e: e.reciprocal(out=rstd[:, 0:n], in_=rstd[:, 0:n]), reads=[rstd], writes=[rstd])
            for k in range(4):
                em.op("dve", lambda e: e.scalar_tensor_tensor(
                    out=cn[:, k, 0:n], in0=cx[:, k, 0:n], scalar=gains[:, gi, k:k + 1], in1=rstd[:, 0:n],
                    op0=ALU.mult, op1=ALU.mult), reads=[cx, gains, rstd], writes=[cn], partial=True)

        def load_rope(t0, n):
            em.dma("sp", rope[:, :, 0:n], P.ropeT[:, :, t0 - CTX:t0 - CTX + n], writes=[rope])

        def apply_rope(dst, x_, xsw, n, isctx):
            if isctx:
                em.op("dve", lambda e: e.tensor_copy(out=dst[0:64, 0:n], in_=x_[0:64, 0:n]), reads=[x_], writes=[dst])
                return
            em.op("dve", lambda e: e.tensor_tensor(out=dst[0:64, 0:n], in0=x_[0:64, 0:n], in1=rope[:, 0, 0:n],
                                                   op=ALU.mult), reads=[x_, rope], writes=[dst])
            em.op("pool", lambda e: e.tensor_tensor(out=xsw[0:64, 0:n], in0=xsw[0:64, 0:n], in1=rope[:, 1, 0:n],
                                                    op=ALU.mult), reads=[xsw, rope], writes=[xsw])
            em.op("dve", lambda e: e.tensor_tensor(out=dst[0:64, 0:n], in0=dst[0:64, 0:n], in1=xsw[0:64, 0:n],
                                                   op=ALU.add), reads=[dst, xsw], writes=[dst])

        for (t0, n, isctx) in TBLK:
            pp = ppos(t0)
            rmsnorm_block(512, 1, pp, n)
            if not isctx:
                load_rope(t0, n)
            em.dma("sp", krt[:, 0:n], P.pT[8 * 128:8 * 128 + 64, pp:pp + n], writes=[krt])
            em.dma("sp", ksw[:, 0:n], P.pT[9 * 128:9 * 128 + 64, pp:pp + n], writes=[ksw])
            apply_rope(kro, krt, ksw, n, isctx)
            em.act(krb, krb[0:64, 0:n], kro[0:64, 0:n], AF.Copy, reads=[kro], writes=[krb])
            em.dma("pool", P.Kr[:, t0:t0 + n], krb[:, 0:n], reads=[krb], writes=[dummy])
            em.act(sqr, sqr[0:64, 0:n], kro[0:64, 0:n], AF.Square, reads=[kro])
            for h in range(8):
                pk = bank()
                for k in range(4):
                    em.mm(pk, pk[:, 0:n], wkv[:, k, h * 256:h * 256 + 128], cn[:, k, 0:n], k == 0, k == 3,
                          reads=[wkv, cn])
                o_ = obuf()
                em.op("dve", lambda e: e.tensor_copy(out=o_[:, 0:n], in_=pk[:, 0:n]), reads=[pk], writes=[o_])
                em.dma("pool", P.Kn[h * 128:(h + 1) * 128, t0:t0 + n], o_[:, 0:n], reads=[o_], writes=[dummy])
                em.act(sqb, sqb[:, 0:n], o_[:, 0:n], AF.Square, reads=[o_])
                pn = bank()
                em.mm(pn, pn[:, 0:n], P.ones_b[:], sqb[:, 0:n], True, False, reads=[P.ones_b, sqb])
                em.mm(pn, pn[:, 0:n], P.ones_b[:], sqr[:, 0:n], False, True, reads=[P.ones_b, sqr])
                em.op("dve", lambda e: e.tensor_reduce(out=bmax[:], in_=pn[:, 0:n], axis=AX.X, op=ALU.max),
                      reads=[pn], writes=[bmax])
                em.op("dve", lambda e: e.tensor_tensor(out=kmax[:, h:h + 1], in0=kmax[:, h:h + 1], in1=bmax[:],
                                                       op=ALU.max), reads=[kmax, bmax], writes=[kmax])
            for tt in range(n // 128):
                for g in range(2):
                    pv = bank()
                    for k in range(4):
                        em.mm(pv, pv[:, :], cn[:, k, tt * 128:(tt + 1) * 128], wvv[:, k, g * 512:(g + 1) * 512],
                              k == 0, k == 3, reads=[cn, wvv])
                    o_ = obuf()
                    em.act(o_, o_[:], pv[:, :], AF.Copy, reads=[pv])
                    em.dma("pool", P.Vt[t0 + tt * 128:t0 + (tt + 1) * 128, g * 512:(g + 1) * 512], o_[:],
                           reads=[o_], writes=[dummy])
        if getattr(P, "mla_stage", 9) < 1:
            em.barrier()
            return
        nkm = f("nkm", (128, 8))
        em.act(nkm, nkm[:], kmax[:], AF.Sqrt, reads=[kmax])
        em.op("dve", lambda e: e.tensor_scalar_mul(out=nkm[:], in0=nkm[:], scalar1=-1.0), reads=[nkm], writes=[nkm])
        qcnt = 0
        for (t0, n, isctx) in TBLK:
            if isctx and not with_ctx:
                continue
            pp = ppos(t0)
            rmsnorm_block(0, 0, pp, n)
            if not isctx:
                load_rope(t0, n)
            for h in range(8):
                pq, pr, pw = bank(), bank(), bank()
                for k in range(4):
                    em.mm(pq, pq[:, 0:n], wq[:, k, h * 192:h * 192 + 128], cn[:, k, 0:n], k == 0, k == 3,
                          reads=[wq, cn])
                for k in range(4):
                    em.mm(pr, pr[0:64, 0:n], wq[:, k, h * 192 + 128:h * 192 + 192], cn[:, k, 0:n], k == 0, k == 3,
                          reads=[wq, cn])
                if not isctx:
                    for k in range(4):
                        em.mm(pw, pw[0:64, 0:n], wq[:, k, 1536 + h * 64:1536 + (h + 1) * 64], cn[:, k, 0:n],
                              k == 0, k == 3, reads=[wq, cn])
                    em.op("dve", lambda e: e.tensor_copy(out=qs32[0:64, 0:n], in_=pw[0:64, 0:n]), reads=[pw],
                          writes=[qs32])
                em.act(qr32, qr32[0:64, 0:n], pr[0:64, 0:n], AF.Copy, reads=[pr])
                apply_rope(kro, qr32, qs32, n, isctx)
                em.act(sqb, sqb[:, 0:n], pq[:, 0:n], AF.Square, reads=[pq])
                em.act(sqr, sqr[0:64, 0:n], kro[0:64, 0:n], AF.Square, reads=[kro])
                pn = bank()
                em.mm(pn, pn[:, 0:n], P.ones_b[:], sqb[:, 0:n], True, False, reads=[P.ones_b, sqb])
                em.mm(pn, pn[:, 0:n], P.ones_b[:], sqr[:, 0:n], False, True, reads=[P.ones_b, sqr])
                qb = qrb[qcnt % 2]
                qcnt += 1
                em.act(sq[1], sq[1][64:128, 0:n], pn[64:128, 0:n], AF.Sqrt, reads=[pn],
                       scale=ATTN_SCALE * ATTN_SCALE)
                em.op("dve", lambda e: e.tensor_scalar(out=qb[64:128, 0:n], in0=sq[1][64:128, 0:n],
                                                       scalar1=nkm[64:128, h:h + 1], scalar2=sel[64:128, 0:1],
                                                       op0=ALU.mult, op1=ALU.mult),
                      reads=[sq[1], nkm, sel], writes=[qb], partial=True)
                em.act(qb, qb[0:64, 0:n], kro[0:64, 0:n], AF.Copy, reads=[kro], scale=ATTN_SCALE, writes=[qb])
                em.dma("pool", P.Qr[h, :, t0:t0 + n], qb[:, 0:n], reads=[qb], writes=[dummy])
                o_ = obuf()
                em.act(o_, o_[:, 0:n], pq[:, 0:n], AF.Copy, reads=[pq], scale=ATTN_SCALE)
                em.dma("pool", P.Qn[h * 128:(h + 1) * 128, t0:t0 + n], o_[:, 0:n], reads=[o_], writes=[dummy])
        em.barrier()
    if getattr(P, "mla_stage", 9) < 2:
        return
    with ExitStack() as st:
        f = lambda nm, shp, dt=BF16: em.sb("at_" + nm, list(shp), dt, st)
        NKT = TT // 128
        kr = f("kr", (128, TT))
        em.dma("sp", kr[:], P.Kr[:, :], writes=[kr])
        kn = [f("kn%d" % i, (128, TT)) for i in range(2)]
        vv = [f("vv%d" % i, (128, NKT, 128)) for i in range(2)]
        qn = [f("qn%d" % i, (128, 512)) for i in range(2)]
        qr = [f("qr%d" % i, (128, 512)) for i in range(2)]
        pt = [f("pt%d" % i, (128, 512)) for i in range(3)]
        rd = [f("rd%d" % i, (128, 512), F32) for i in range(2)]
        ao = [f("ao%d" % i, (128, 512)) for i in range(2)]
        pS = [em.ps("at_pS%d" % i, [128, 512], F32, st) for i in range(3)]
        pO = [em.ps("at_pO%d" % i, [128, 512], F32, st) for i in range(2)]
        pD = [em.ps("at_pD%d" % i, [128, 512], F32, st) for i in range(2)]
        sc = 0
        qc = 0
        for h in range(8):
            k_n, v_ = kn[h % 2], vv[h % 2]
            em.dma("sp", k_n[:], P.Kn[h * 128:(h + 1) * 128, :], writes=[k_n])
            em.dma("sp", v_[:], P.Vt[:, h * 128:(h + 1) * 128].rearrange("(c p) v -> p c v", p=128), writes=[v_])
            for (t0, n, isctx) in TBLK:
                if isctx and not with_ctx:
                    continue
                q_n, q_r = qn[qc % 2], qr[qc % 2]
                p_O, p_D, r_d, a_o = pO[qc % 2], pD[qc % 2], rd[qc % 2], ao[qc % 2]
                qc += 1
                em.dma("sp", q_n[:, 0:n], P.Qn[h * 128:(h + 1) * 128, t0:t0 + n], writes=[q_n])
                em.dma("sp", q_r[:, 0:n], P.Qr[h, :, t0:t0 + n], writes=[q_r])
                nkt = CTX // 128 if isctx else NKT
                LOOK = 2
                ring = {}

                def scores(kt):
                    nonlocal sc
                    p_S, p_t = pS[sc % 3], pt[sc % 3]
                    sc += 1
                    ks = slice(kt * 128, (kt + 1) * 128)
                    em.mm(p_S, p_S[:, 0:n], k_n[:, ks], q_n[:, 0:n], True, False, reads=[k_n, q_n])
                    em.mm(p_S, p_S[:, 0:n], kr[:, ks], q_r[:, 0:n], False, True, reads=[kr, q_r])
                    em.act(p_t, p_t[:, 0:n], p_S[:, 0:n], AF.Exp, reads=[p_S])
                    ring[kt] = p_t

                for kt in range(min(LOOK, nkt)):
                    scores(kt)
                for kt in range(nkt):
                    p_t = ring.pop(kt)
                    em.mm(p_O, p_O[:, 0:n], v_[:, kt, :], p_t[:, 0:n], kt == 0, kt == nkt - 1, reads=[v_, p_t])
                    em.mm(p_D, p_D[:, 0:n], P.ones_b[:], p_t[:, 0:n], kt == 0, kt == nkt - 1,
                          reads=[P.ones_b, p_t])
                    if kt + LOOK < nkt:
                        scores(kt + LOOK)
                em.op("dve", lambda e: e.reciprocal(out=r_d[:, 0:n], in_=p_D[:, 0:n]), reads=[p_D], writes=[r_d])
                em.op("dve", lambda e: e.tensor_tensor(out=a_o[:, 0:n], in0=p_O[:, 0:n], in1=r_d[:, 0:n],
                                                       op=ALU.mult), reads=[p_O, r_d], writes=[a_o])
                em.dma("sp", P.mixT[h * 128:(h + 1) * 128, t0:t0 + n], a_o[:, 0:n], reads=[a_o], writes=[dummy])
        em.barrier()


def host_mla_inputs(inp, L, seq):
    f32 = np.float32
    perm = np.arange(64).reshape(2, 2, 16)[:, ::-1, :].reshape(64)
    out = {}
    out["mla_w_uq"] = inp["mla_w_uq"][:L]
    cols = np.concatenate([h * 192 + 128 + perm for h in range(8)])
    out["mla_w_uq_sw"] = np.ascontiguousarray(inp["mla_w_uq"][:L][:, :, cols])
    out["mla_w_ukv"] = inp["mla_w_ukv"][:L]
    g = np.stack([inp["mla_q_norm"][:L].reshape(L, 4, 128), inp["mla_kv_norm"][:L].reshape(L, 4, 128)], 1)
    out["mlaT"] = np.ascontiguousarray(g.transpose(3, 0, 1, 2)).astype(f32)
    t = np.arange(seq)
    pos = np.stack([t // 64, t % 64], -1).astype(f32)
    inv = (10000.0 ** (-np.arange(16, dtype=f32) / 16)).astype(f32)
    ang = pos[..., None] * inv
    cos, sin = np.cos(ang), np.sin(ang)
    ct = np.zeros((64, seq), f32)
    stb = np.zeros((64, seq), f32)
    for a in range(2):
        for half in range(2):
            r0 = a * 32 + half * 16
            ct[r0:r0 + 16] = cos[:, a, :].T
            stb[r0:r0 + 16] = (sin[:, a, :].T) * (-1.0 if half == 0 else 1.0)
    out["ropeT"] = np.ascontiguousarray(np.stack([ct, stb], 1))
    return out, perm


def build_program(nl=DEPTH, dbg=()):
    P = Prog(nl=nl, dbg=dbg)
    em = P.em
    declare_io(P)
    declare_dense(P)
    declare_scan(P)
    declare_mla(P)
    setup_consts(P)
    setup_scan_consts(P)
    phase_zero_pads(P)
    phase_cast_in(P)
    phase_cast_dense(P)
    phase_cast_mla(P)
    phase_mod(P)
    nblk = len(TBLK)
    for l in range(nl):
        with_ctx = l < nl - 1
        xsrc = P.xT0 if l == 0 else P.xB
        ft = lambda: [T() for _ in range(nblk)]
        sel_ = getattr(build_program, "phases", "imrgodf")
        if "i" in sel_:
            phase_inproj(P, l, xsrc, ft())
        if "m" in sel_:
            phase_mla(P, l, with_ctx)
        if "r" in sel_:
            phase_rwkv(P, l)
        if "g" in sel_:
            phase_gdn(P, l)
        if "o" in sel_:
            phase_mix_out(P, l, with_ctx)
        P.mixT_t = ft()
        if "d" in sel_:
            phase_outproj(P, l, xsrc, ft(), P.xA, ft(), with_ctx)
        if "f" in sel_:
            phase_ffn(P, l, P.xA, ft(), P.xB, ft(), with_ctx, final=(l == nl - 1))
    em.barrier()
    return P


def host_inputs(inp, b, nl, seq):
    f32 = np.float32
    L = nl
    x = np.asarray(inp["x"][b][:seq], f32)
    ctx = np.asarray(inp["ctx"][b], f32)
    im = {}
    im["xT0"] = np.ascontiguousarray(np.concatenate([ctx, x], 0).T)
    im["cvec"] = np.ascontiguousarray(np.stack([np.asarray(inp["c"][b]).reshape(16, 128).T,
                                                np.asarray(inp["c_ctx"]).reshape(16, 128).T], -1).astype(f32))
    im["w_mod"] = np.asarray(inp["w_mod"][:L], f32)
    im["b_modT"] = np.ascontiguousarray(np.asarray(inp["b_mod"][:L], f32).reshape(L, 96, 128).transpose(2, 0, 1))
    im["w_in"] = np.asarray(inp["w_in"][:L], f32)
    mi, perm = host_mla_inputs(inp, L, seq)
    im["w_in_krsw"] = np.ascontiguousarray(im["w_in"][:, :, 1024 + perm])
    im.update(mi)
    e = np.zeros((128, 1), f32)
    e[64] = 1.0
    im["sel64"] = e
    im["w_out"] = np.asarray(inp["w_out"][:L], f32)
    im["ffn_w_gate"] = np.asarray(inp["ffn_w_gate"][:L], f32)
    im["ffn_w_up"] = np.asarray(inp["ffn_w_up"][:L], f32)
    im["ffn_w_down"] = np.asarray(inp["ffn_w_down"][:L], f32)
    lnT = np.stack([np.asarray(inp[k][:L], f32).reshape(L, 16, 128) for k in ("ln1_g", "ln1_b", "ln2_g", "ln2_b")], 1)
    im["lnT"] = np.ascontiguousarray(lnT.transpose(3, 0, 1, 2))
    im.update(host_consts())
    im.update(host_scan_inputs(inp, L))
    return im


def kernel(**inputs):
    nb = inputs["x"].shape[0]
    P = build_program(DEPTH)
    shared = None
    in_maps = []
    for b in range(nb):
        im = host_inputs(inputs, b, DEPTH, SEQ)
        if shared is None:
            shared = im
        else:
            for k in im:
                if k not in ("xT0", "cvec"):
                    im[k] = shared[k]
        in_maps.append({k: v for k, v in im.items() if k in P.inputs})
    res = run_bass_kernel_spmd(P.nc, in_maps, core_ids=list(range(nb)))
    out = np.stack([np.ascontiguousarray(np.asarray(r["xOut"], np.float32).T) for r in res.results], 0)
    return out
```
